# Optimizing a Trainium2 kernel written in Bass

```python
import jax
import jax.numpy as jnp
from jax import lax
import numpy as np

D_MODEL = 1024
BATCH = 8
SEQ = 2048
DEPTH = 4
DEC_BATCH = 128
DEC_SEQ = 4
PAST_LEN = 8192
PAGE_SIZE = 128

D_MIX = D_MODEL
HEAD_DIM = 64
W_A = D_MIX // 4
W_B = D_MIX // 4
W_C = D_MIX // 4
W_D = D_MIX - W_A - W_B - W_C
N_Q = W_A // HEAD_DIM
N_KV = N_Q // 2
GQA_G = N_Q // N_KV
WINDOW = 128
Q_BLOCK = 128
N_HG = W_B // HEAD_DIM
HG_K = HEAD_DIM
HG_V = HEAD_DIM
HG_CHUNK = 64
LB_FLOOR = 1e-30
CONV_W = 3
POOL_WINDOWS = (2, 4, 8, 16)
N_POOL = 4
POOL_GW = W_D // N_POOL
POOL_BUF = 15
N_MEM = 256
N_XH = 4
X_HD = 64
D_X = N_XH * X_HD
D_FF = ((8 * D_MODEL + 3 * 256 - 1) // (3 * 256)) * 256
ALPHA = (2 * DEPTH) ** 0.25
BETA = (8 * DEPTH) ** -0.25
LN_EPS = 1e-5
RMS_EPS = 1e-6
MASK_VALUE = -1e30
SPLIT_SIZES = (W_A, N_KV * HEAD_DIM, N_KV * HEAD_DIM, W_B, W_B, W_B, W_B, W_C, W_C, W_C, W_D)
D_IN = W_A + 2 * N_KV * HEAD_DIM + 4 * W_B + 3 * W_C + W_D

kernel_name = 'hybrid_parallel_heads_decode_step'


def split_points():
    pts, acc = [], 0
    for s in SPLIT_SIZES[:-1]:
        acc += s
        pts.append(acc)
    return pts


def layer_norm(x, g, b):
    xf = x.astype(jnp.float32)
    mu = xf.mean(-1, keepdims=True)
    var = jnp.square(xf - mu).mean(-1, keepdims=True)
    return ((xf - mu) * lax.rsqrt(var + LN_EPS) * g + b).astype(x.dtype)


def alibi_slopes():
    h = jnp.arange(N_Q, dtype=jnp.float32) + 1.0
    return jnp.exp2(-8.0 * h / N_Q).reshape(N_KV, GQA_G)


def sliding_window_attention(q, k, v, k_buf, v_buf, pos0, sink, slopes):
    B, T = q.shape[:2]
    k_ext = jnp.concatenate([k_buf.astype(k.dtype), k], axis=1)
    v_ext = jnp.concatenate([v_buf.astype(v.dtype), v], axis=1)
    qb_len = Q_BLOCK if T % Q_BLOCK == 0 else T
    nb = T // qb_len
    idx = (jnp.arange(nb) * qb_len)[:, None] + jnp.arange(qb_len + WINDOW)[None, :]
    kb = k_ext[:, idx]
    vb = v_ext[:, idx]
    qb = q.reshape(B, nb, qb_len, N_KV, GQA_G, HEAD_DIM)
    s = jnp.einsum('bnqkgd,bnskd->bnkgqs', qb, kb).astype(jnp.float32) * (HEAD_DIM ** -0.5)
    k_pos = pos0 - WINDOW + idx
    q_pos = pos0 + jnp.arange(T).reshape(nb, qb_len)
    rel = q_pos[:, :, None] - k_pos[:, None, :]
    valid = (rel >= 0) & (rel <= WINDOW) & (k_pos[:, None, :] >= 0)
    s = s - slopes[None, None, :, :, None, None] * rel[None, :, None, None].astype(jnp.float32)
    s = jnp.where(valid[None, :, None, None], s, MASK_VALUE)
    sk = sink.astype(jnp.float32).reshape(N_KV, GQA_G)[None, None, :, :, None]
    m = jnp.maximum(jnp.max(s, axis=-1), sk)
    p = jnp.exp(s - m[..., None])
    denom = p.sum(-1) + jnp.exp(sk - m)
    p = p / denom[..., None]
    o = jnp.einsum('bnkgqs,bnskd->bnqkgd', p.astype(v.dtype), vb).reshape(B, T, N_Q * HEAD_DIM)
    return o, k_ext[:, -WINDOW:], v_ext[:, -WINDOW:]


def hgrn_lower_bounds(lb_param):
    p = jax.nn.softmax(lb_param.astype(jnp.float32), axis=0)
    return jnp.cumsum(p, axis=0) - p[0:1]


def hgrn2(q, f_pre, i, g, s0, lb, norm_g):
    B, T, _ = q.shape
    f32 = jnp.float32
    qh = jax.nn.silu(q.astype(f32)).reshape(B, T, N_HG, HG_K)
    fp = f_pre.astype(f32).reshape(B, T, N_HG, HG_K)
    lbh = lb.astype(f32).reshape(N_HG, HG_K)
    log_f = jnp.logaddexp(jax.nn.log_sigmoid(fp), jnp.log(jnp.maximum(lbh, LB_FLOOR)) + jax.nn.log_sigmoid(-fp))
    kh = (1.0 - lbh) * jax.nn.sigmoid(-fp)
    vh = i.astype(f32).reshape(B, T, N_HG, HG_V)
    c = HG_CHUNK if T % HG_CHUNK == 0 else T
    nc = T // c

    def to_chunks(a):
        return a.reshape(B, nc, c, N_HG, a.shape[-1]).swapaxes(0, 1)

    tril = jnp.tril(jnp.ones((c, c), dtype=bool))[None, :, :, None, None]

    def step(S, inp):
        qc, kc, vc, lfc = inp
        cum = jnp.cumsum(lfc, axis=1)
        o = jnp.einsum('bthk,bhkv->bthv', qc * jnp.exp(cum), S)
        diff = cum[:, :, None] - cum[:, None, :]
        decay = jnp.where(tril, jnp.exp(jnp.where(tril, diff, 0.0)), 0.0)
        a = jnp.einsum('bthk,bshk,btshk->bhts', qc, kc, decay)
        o = o + jnp.einsum('bhts,bshv->bthv', a, vc)
        last = cum[:, -1]
        S = jnp.exp(last)[..., None] * S + jnp.einsum('bshk,bshv->bhkv', kc * jnp.exp(last[:, None] - cum), vc)
        return S, o

    S, o = lax.scan(step, s0.astype(f32), (to_chunks(qh), to_chunks(kh), to_chunks(vh), to_chunks(log_f)))
    o = o.swapaxes(0, 1).reshape(B, T, N_HG, HG_V)
    o = o * lax.rsqrt(jnp.mean(jnp.square(o), axis=-1, keepdims=True) + RMS_EPS) * norm_g.astype(f32).reshape(N_HG, HG_V)
    o = o.reshape(B, T, W_B) * jax.nn.silu(g.astype(f32))
    return o.astype(q.dtype), S.astype(s0.dtype)


def short_conv(b_gate, c_gate, h, buf, conv_w):
    T = h.shape[1]
    u = c_gate * h
    ext = jnp.concatenate([buf.astype(u.dtype), u], axis=1)
    y = ext[:, 0:T] * conv_w[0]
    for j in range(1, CONV_W):
        y = y + ext[:, j:j + T] * conv_w[j]
    return b_gate * y, ext[:, -(CONV_W - 1):]


def pool_mixer(v, buf, pos0, pool_w, pool_scale):
    B, T, _ = v.shape
    ext_raw = jnp.concatenate([buf.astype(v.dtype), v], axis=1)
    ext = ext_raw.astype(jnp.float32)
    cs = jnp.concatenate([jnp.zeros((B, 1, W_D), jnp.float32), jnp.cumsum(ext, axis=1)], axis=1)
    pos = pos0 + jnp.arange(T)
    outs = []
    for gi, w in enumerate(POOL_WINDOWS):
        lo, hi = gi * POOL_GW, (gi + 1) * POOL_GW
        win_sum = cs[:, POOL_BUF + 1:POOL_BUF + 1 + T, lo:hi] - cs[:, POOL_BUF + 1 - w:POOL_BUF + 1 - w + T, lo:hi]
        cnt = jnp.minimum(pos + 1, w).astype(jnp.float32)
        outs.append(win_sum / cnt[None, :, None])
    pooled = (jnp.concatenate(outs, axis=-1) - v.astype(jnp.float32)).reshape(B, T, N_POOL, POOL_GW)
    y = jnp.einsum('btgc,gcd->btgd', pooled, pool_w.astype(jnp.float32)).reshape(B, T, W_D) * pool_scale
    return y.astype(v.dtype), ext_raw[:, -POOL_BUF:]


def memory_attention(x, mem_k, mem_v, w_xq, w_xo):
    B, T, _ = x.shape
    q = (x @ w_xq).reshape(B, T, N_XH, X_HD)
    s = jnp.einsum('bthd,bmhd->bhtm', q, mem_k).astype(jnp.float32) * (X_HD ** -0.5)
    p = jax.nn.softmax(s, axis=-1)
    o = jnp.einsum('bhtm,bmhd->bthd', p.astype(mem_v.dtype), mem_v).reshape(B, T, D_X)
    return o @ w_xo


def decoder_layer(x, k_buf, v_buf, s_hgrn, conv_buf, pool_buf, mem_k, mem_v, pos0,
                  w_in, attn_sink, lb, hgrn_norm_g, conv_w, pool_w, pool_scale, w_o,
                  ln1_g, ln1_b, w_xq, w_xo, ln2_g, ln2_b, w_gate, w_up, w_down, ln3_g, ln3_b):
    B, T, _ = x.shape
    proj = x @ w_in
    a_q, a_k, a_v, b_q, b_f, b_i, b_g, c_b, c_c, c_h, d_v = jnp.split(proj, split_points(), axis=-1)
    o_a, k_new, v_new = sliding_window_attention(
        a_q.reshape(B, T, N_Q, HEAD_DIM), a_k.reshape(B, T, N_KV, HEAD_DIM), a_v.reshape(B, T, N_KV, HEAD_DIM),
        k_buf, v_buf, pos0, attn_sink, alibi_slopes())
    o_b, s_new = hgrn2(b_q, b_f, b_i, b_g, s_hgrn, lb, hgrn_norm_g)
    o_c, conv_new = short_conv(c_b, c_c, c_h, conv_buf, conv_w)
    o_d, pool_new = pool_mixer(d_v, pool_buf, pos0, pool_w, pool_scale)
    mix = jnp.concatenate([o_a, o_b, o_c, o_d], axis=-1) @ w_o
    x = layer_norm(ALPHA * x + mix, ln1_g, ln1_b)
    x = layer_norm(ALPHA * x + memory_attention(x, mem_k, mem_v, w_xq, w_xo), ln2_g, ln2_b)
    ffn = (jax.nn.silu(x @ w_gate) * (x @ w_up)) @ w_down
    x = layer_norm(ALPHA * x + ffn, ln3_g, ln3_b)
    return x, k_new, v_new, s_new, conv_new, pool_new


def setup_inputs(seed: int = 0) -> dict:
    key = jax.random.key(seed)
    ks = iter(jax.random.split(key, 64))
    f32 = jnp.float32

    def nrm(shape, scale=1.0):
        return jax.random.normal(next(ks), shape, f32) * scale

    def gain(shape):
        return 1.0 + nrm(shape, 0.05)

    return {
        'x_prompt': nrm((BATCH, SEQ, D_MODEL)),
        'x_sample': nrm((DEC_BATCH, DEC_SEQ, D_MODEL)),
        'cache_swa_k': nrm((DEPTH, DEC_BATCH, WINDOW, N_KV, HEAD_DIM)),
        'cache_swa_v': nrm((DEPTH, DEC_BATCH, WINDOW, N_KV, HEAD_DIM)),
        'state_hgrn': nrm((DEPTH, DEC_BATCH, N_HG, HG_K, HG_V), 0.5),
        'state_conv': nrm((DEPTH, DEC_BATCH, CONV_W - 1, W_C)),
        'state_pool': nrm((DEPTH, DEC_BATCH, POOL_BUF, W_D)),
        'cache_mem_k': nrm((DEPTH, DEC_BATCH, N_MEM, N_XH, X_HD)),
        'cache_mem_v': nrm((DEPTH, DEC_BATCH, N_MEM, N_XH, X_HD)),
        'mem_prompt': nrm((BATCH, N_MEM, D_MODEL)),
        'emb_ln_g': gain((D_MODEL,)),
        'emb_ln_b': nrm((D_MODEL,), 0.05),
        'w_in': nrm((DEPTH, D_MODEL, D_IN), D_MODEL ** -0.5),
        'attn_sink': nrm((DEPTH, N_Q), 0.5),
        'hgrn_lb': nrm((DEPTH, W_B), 0.5),
        'hgrn_norm_g': gain((DEPTH, W_B)),
        'conv_w': nrm((DEPTH, CONV_W, W_C), CONV_W ** -0.5),
        'pool_w': nrm((DEPTH, N_POOL, POOL_GW, POOL_GW), POOL_GW ** -0.5),
        'pool_scale': 1.0 + nrm((DEPTH, W_D), 0.1),
        'w_o': nrm((DEPTH, D_MIX, D_MODEL), BETA * D_MIX ** -0.5),
        'ln1_g': gain((DEPTH, D_MODEL)),
        'ln1_b': nrm((DEPTH, D_MODEL), 0.05),
        'w_xq': nrm((DEPTH, D_MODEL, D_X), D_MODEL ** -0.5),
        'w_xk': nrm((DEPTH, D_MODEL, D_X), D_MODEL ** -0.5),
        'w_xv': nrm((DEPTH, D_MODEL, D_X), D_MODEL ** -0.5),
        'w_xo': nrm((DEPTH, D_X, D_MODEL), BETA * D_X ** -0.5),
        'ln2_g': gain((DEPTH, D_MODEL)),
        'ln2_b': nrm((DEPTH, D_MODEL), 0.05),
        'w_gate': nrm((DEPTH, D_MODEL, D_FF), D_MODEL ** -0.5),
        'w_up': nrm((DEPTH, D_MODEL, D_FF), D_MODEL ** -0.5),
        'w_down': nrm((DEPTH, D_FF, D_MODEL), BETA * D_FF ** -0.5),
        'ln3_g': gain((DEPTH, D_MODEL)),
        'ln3_b': nrm((DEPTH, D_MODEL), 0.05),
    }


def reference(x_prompt, x_sample, cache_swa_k, cache_swa_v, state_hgrn, state_conv, state_pool,
              cache_mem_k, cache_mem_v, mem_prompt, emb_ln_g, emb_ln_b, w_in, attn_sink, hgrn_lb,
              hgrn_norm_g, conv_w, pool_w, pool_scale, w_o, ln1_g, ln1_b, w_xq, w_xk, w_xv, w_xo,
              ln2_g, ln2_b, w_gate, w_up, w_down, ln3_g, ln3_b):
    lb_all = hgrn_lower_bounds(hgrn_lb)
    hp = layer_norm(x_prompt, emb_ln_g, emb_ln_b)
    hs = layer_norm(x_sample, emb_ln_g, emb_ln_b)
    bp = x_prompt.shape[0]
    dt = x_prompt.dtype
    z_kv = jnp.zeros((bp, WINDOW, N_KV, HEAD_DIM), dt)
    z_s = jnp.zeros((bp, N_HG, HG_K, HG_V), jnp.float32)
    z_c = jnp.zeros((bp, CONV_W - 1, W_C), dt)
    z_p = jnp.zeros((bp, POOL_BUF, W_D), dt)
    pk, pv, ps, pc, pp, pmk, pmv = [], [], [], [], [], [], []
    sk, sv, ss, sc, sp = [], [], [], [], []
    for l in range(DEPTH):
        mk = (mem_prompt @ w_xk[l]).reshape(bp, N_MEM, N_XH, X_HD)
        mv = (mem_prompt @ w_xv[l]).reshape(bp, N_MEM, N_XH, X_HD)
        lw = (w_in[l], attn_sink[l], lb_all[l], hgrn_norm_g[l], conv_w[l], pool_w[l], pool_scale[l], w_o[l],
              ln1_g[l], ln1_b[l], w_xq[l], w_xo[l], ln2_g[l], ln2_b[l], w_gate[l], w_up[l], w_down[l],
              ln3_g[l], ln3_b[l])
        hp, k1, v1, s1, c1, p1 = decoder_layer(hp, z_kv, z_kv, z_s, z_c, z_p, mk, mv, 0, *lw)
        hs, k2, v2, s2, c2, p2 = decoder_layer(hs, cache_swa_k[l], cache_swa_v[l], state_hgrn[l], state_conv[l],
                                               state_pool[l], cache_mem_k[l], cache_mem_v[l], PAST_LEN, *lw)
        pk.append(k1); pv.append(v1); ps.append(s1); pc.append(c1); pp.append(p1); pmk.append(mk); pmv.append(mv)
        sk.append(k2); sv.append(v2); ss.append(s2); sc.append(c2); sp.append(p2)
    return (hp, hs,
            jnp.stack(pk), jnp.stack(pv), jnp.stack(ps), jnp.stack(pc), jnp.stack(pp), jnp.stack(pmk), jnp.stack(pmv),
            jnp.stack(sk), jnp.stack(sv), jnp.stack(ss), jnp.stack(sc), jnp.stack(sp))
```

```python
import numpy as np
import concourse.bass as bass
import concourse.mybir as mybir

F32 = mybir.dt.float32
BF16 = mybir.dt.bfloat16
ALU = mybir.AluOpType
AF = mybir.ActivationFunctionType
AX = mybir.AxisListType

import os as _os
SERIAL = bool(_os.environ.get('SERIAL'))
NS = 8
ND = 12


class Buf:
    __slots__ = ("lo", "hi", "name")

    def __init__(self, name, lo, hi):
        self.name, self.lo, self.hi = name, lo, hi


class _Op:
    __slots__ = ("eng", "fn", "idx", "is_dma", "waits", "needs_inc", "seq", "dsem", "dval")

    def __init__(self, eng, fn, is_dma):
        self.eng = eng
        self.fn = fn
        self.is_dma = is_dma
        self.waits = []
        self.needs_inc = False
        self.seq = -1
        self.idx = -1
        self.dsem = -1
        self.dval = 0


class Sched:
    ENGS = ("pe", "act", "dve", "pool", "sp")

    def __init__(self):
        self.ops = {e: [] for e in self.ENGS}
        self.last_w = {}
        self.readers = {}
        self.live = []
        self.known = {e: {f: -1 for f in self.ENGS} for e in self.ENGS}
        self.known_dma = {e: set() for e in self.ENGS}
        self.dma_count = {"sp": 0, "pool": 0}
        self.dma_hist = {"sp": {}, "pool": {}}

    def _overl(self, b):
        return [r for r in self.live if r is not b and r.lo < b.hi and b.lo < r.hi]

    def op(self, eng, fn, reads=(), writes=(), dma=False):
        o = _Op(eng, fn, dma)
        o.idx = len(self.ops[eng])
        deps = {}
        for k in reads:
            w = self.last_w.get(k)
            if w is not None:
                deps[w] = True
            if isinstance(k, str) and k[:2] in ("ps", "pT"):
                for r in self.readers.get(k, ()):
                    if r.eng != eng and r not in deps:
                        deps[r] = False
            if isinstance(k, Buf):
                for r in self._overl(k):
                    w = self.last_w.get(r)
                    if w is not None:
                        deps[w] = True
        for k in writes:
            ks = [k] + (self._overl(k) if isinstance(k, Buf) else [])
            for kk in ks:
                w = self.last_w.get(kk)
                if w is not None:
                    deps[w] = True
                for r in self.readers.get(kk, ()):
                    if r not in deps:
                        deps[r] = False
        if SERIAL:
            for e2 in self.ENGS:
                if self.ops[e2]:
                    deps.setdefault(self.ops[e2][-1], True)
        if dma:
            q = eng
            j = self.dma_count[q]
            self.dma_count[q] += 1
            o.dsem = j % ND
            o.dval = 16 * (j // ND + 1)
            prev = self.dma_hist[q].get(o.dsem)
            if prev is not None:
                deps.setdefault(prev, True)
            self.dma_hist[q][o.dsem] = o
            o.needs_inc = True
        kn = self.known[eng]
        for d, hard in deps.items():
            if d is o:
                continue
            if d.is_dma:
                if d in self.known_dma[eng]:
                    continue
                self.known_dma[eng].add(d)
                o.waits.append(d)
                continue
            if d.eng == eng and (not hard or eng == "pe"):
                continue
            if d.idx <= kn[d.eng]:
                continue
            kn[d.eng] = d.idx
            d.needs_inc = True
            o.waits.append(d)
        for k in reads:
            self.readers.setdefault(k, []).append(o)
            if isinstance(k, Buf) and k not in self.live:
                self.live.append(k)
        for k in writes:
            if isinstance(k, Buf):
                for r in self._overl(k):
                    self.live.remove(r)
                    self.last_w.pop(r, None)
                    self.readers.pop(r, None)
                if k not in self.live:
                    self.live.append(k)
            self.last_w[k] = o
            self.readers[k] = []
        self.ops[eng].append(o)
        return o

    def final(self):
        o = _Op("sp", lambda h: h.nop(), False)
        o.idx = len(self.ops["sp"])
        for q in self.dma_hist:
            for d in self.dma_hist[q].values():
                o.waits.append(d)
        self.ops["sp"].append(o)

    def emit(self, nc, sems, dsems, block):
        for e in self.ENGS:
            m = 0
            for o in self.ops[e]:
                if o.needs_inc and not o.is_dma:
                    o.seq = m
                    m += 1

        def run(e, h):
            for o in self.ops[e]:
                for d in o.waits:
                    if d.is_dma:
                        h.wait_ge(dsems[d.eng][d.dsem], d.dval)
                    else:
                        h.wait_ge(sems[d.eng][d.seq % NS], d.seq // NS + 1)
                ins = o.fn(h)
                if o.needs_inc:
                    if o.is_dma:
                        ins.then_inc(dsems[e][o.dsem], 16)
                    else:
                        ins.then_inc(sems[e][o.seq % NS], 1)

        @block.tensor
        def _(h):
            run("pe", h)

        @block.scalar
        def _(h):
            run("act", h)

        @block.vector
        def _(h):
            run("dve", h)

        @block.gpsimd
        def _(h):
            run("pool", h)

        @block.sync
        def _(h):
            run("sp", h)
from contextlib import ExitStack
import threading
from concourse.bass_utils import run_bass_kernel_spmd

import os
DBG_SKIP = bool(os.environ.get('DBG_SKIP'))
DBGP = os.environ.get('DBGP', '123')
D = 1024; DEPTH = 4; NT = 17; TS = 64; DFF = 2816
ALPHA = (2 * DEPTH) ** 0.25
LN_EPS = 1e-5; RMS_EPS = 1e-6
cA, cB, cBQ, cBF, cCB, cK, cV, cI, cG, cCC, cCH, cDV = 0, 128, 256, 512, 768, 1024, 1152, 1280, 1536, 1792, 2048, 2304


def _perm():
    q = np.arange(256).reshape(2, 2, 64)
    qa = q[:, 0, :].reshape(-1); qb = q[:, 1, :].reshape(-1)
    r = lambda a, b: np.arange(a, b)
    return np.concatenate([qa, qb, r(512, 768), r(768, 1024), r(1536, 1792), r(256, 384), r(384, 512),
                           r(1024, 1280), r(1280, 1536), r(1792, 2048), r(2048, 2304), r(2304, 2560)])


def _consts():
    c = {}
    s = np.arange(128)[:, None]; q = np.arange(128)[None, :]
    slopes = [2.0 ** (-2.0 * (h + 1)) for h in range(4)]
    NEG = -30000.0
    bt = np.zeros((128, 2, 2, 2, 128), np.float32)
    bsc = np.zeros((128, 2, 2, 64), np.float32)
    bsn = np.full((128, 2, 2, 64), NEG, np.float32)
    for g in range(2):
        for kv in range(2):
            sl = slopes[kv * 2 + g]
            bt[:, 0, g, kv, :] = np.where(s <= q, -sl * (q - s), NEG)
            bt[:, 1, g, kv, :] = np.where(s >= q, -sl * (128 + q - s), NEG)
            for b in range(16):
                for t in range(4):
                    col = b * 4 + t
                    bsc[:, g, kv, col] = np.where(np.arange(128) >= t, -sl * (t + 128 - np.arange(128)), NEG)
                    for t2 in range(t + 1):
                        bsn[b * 4 + t2, g, kv, col] = -sl * (t - t2)
    c["bt"] = bt.reshape(128, 1024)
    global _STAB
    _STAB = np.ascontiguousarray(np.concatenate([bsc.reshape(128, 256), bsn.reshape(128, 256)], 1))
    tri = (np.arange(64)[:, None] <= np.arange(64)[None, :]).astype(np.float32)
    c["tri2"] = np.concatenate([tri, tri], 0)
    bdc = np.zeros((128, 64), np.float32); mind = np.zeros((128, 16), np.float32)
    for b in range(16):
        mind[b * 4:(b + 1) * 4, b] = 1.0
        for t in range(4):
            bdc[b * 4:b * 4 + t + 1, b * 4 + t] = 1.0
    c["bdc"] = bdc; c["mind"] = mind
    ic = np.zeros((128, 2, 16), np.float32)
    for hc in range(2):
        for half in range(2):
            w = (2, 4, 8, 16)[hc * 2 + half]
            ic[half * 64:(half + 1) * 64, hc, :] = 1.0 / np.minimum(np.arange(16) + 1, w)
    c["icnt"] = ic.reshape(128, 32)
    c["ident"] = np.eye(128, dtype=np.float32)
    names = list(c.keys()); offs = {}; o = 0
    for n in names:
        offs[n] = (o, c[n].shape[1]); o += c[n].shape[1]
    return np.concatenate([c[n] for n in names], 1), offs


_STAB = None
_CT, _CO = _consts()


def build(nl=DEPTH, dbg=False):
    nc = bass.Bass("TRN2", target_bir_lowering=False)
    I = lambda n, s: nc.dram_tensor(n, list(s), F32, kind="ExternalInput").ap()
    O = lambda n, s: nc.dram_tensor(n, list(s), F32, kind="ExternalOutput").ap()
    xp = I("xp", (2048, D)); xs = I("xs", (64, D))
    ck = I("ck", (nl, 16, 128, 128)); cv = I("cv", (nl, 16, 128, 128))
    sh = I("sh", (nl, 16, 4, 64, 64)); scv = I("scv", (nl, 32, 256)); spl = I("spl", (nl, 16, 15, 256))
    cmk = I("cmk", (nl, 16, 256, 256)); cmv = I("cmv", (nl, 16, 256, 256)); memp = I("memp", (256, D))
    win = I("win", (nl, 128, 8, 2560)); wo = I("wo", (nl, 128, 8, 1024))
    wxq = I("wxq", (nl, 128, 8, 256)); wxk = I("wxk", (nl, 128, 8, 256)); wxv = I("wxv", (nl, 128, 8, 256))
    wxo = I("wxo", (nl, 128, 2, 1024)); wg = I("wg", (nl, 128, 8, DFF)); wu = I("wu", (nl, 128, 8, DFF))
    wd = I("wd", (nl, 128, 22, 1024))
    lncols = I("lncols", (128, 13 * 16)); lnrows = I("lnrows", (13, 2, 1024))
    sink = I("sink", (1, 16)); hlb = I("hlb", (128, 8)); hng = I("hng", (4, 256))
    cw = I("cw", (128, 24)); pw = I("pw", (4, 4, 64, 64)); psc = I("psc", (128, 8))
    ctab = I("ctab", _CT.shape); stab = I("stab", (128, 512))
    yp = O("yp", (2048, D)); ys = O("ys", (64, D))
    okp = O("okp", (nl, 128, 128)); ovp = O("ovp", (nl, 128, 128)); ohp = O("ohp", (nl, 4, 64, 64))
    ocp = O("ocp", (nl, 2, 256)); opp = O("opp", (nl, 15, 256)); omk = O("omk", (nl, 256, 256)); omv = O("omv", (nl, 256, 256))
    oks = O("oks", (nl, 16, 128, 128)); ovs = O("ovs", (nl, 16, 128, 128)); ohs = O("ohs", (nl, 16, 4, 64, 64))
    ocs = O("ocs", (nl, 16, 2, 256)); ops_ = O("ops", (nl, 16, 15, 256))
    dbgo = O("dbgo", (1024, 2112)) if dbg else None
    dbg2 = None
    S = Sched()
    es = ExitStack()
    with es:
        def sb(name, shape, dt=F32):
            return es.enter_context(nc.sbuf_tensor(name, list(shape), dt))
        x_tok = sb("x_tok", (128, NT, D)); xT = sb("xT", (128, 8, 2112), BF16)
        WA = sb("WA", (128, 28672), BF16)
        WK = None
        ct = sb("ct", (128, _CT.shape[1])); identb = sb("identb", (128, 128), BF16)
        lnc = sb("lnc", (128, 13, 2, 8)); Grow = sb("Grow", (128, D)); Brow = sb("Brow", (128, D))
        esink = sb("esink", (128, 16)); lbt = sb("lbt", (128, 2, 4)); oml = sb("oml", (128, 2, 4)); lbe = sb("lbe", (128, 2, 4)); lbs = sb("lbs", (128, 2, 1))
        normg = sb("normg", (128, 256)); cwt = sb("cwt", (128, 4, 2, 3)); psct = sb("psct", (128, 4, 2)); wbd = sb("wbd", (128, 2, 128), BF16)
        ones = sb("ones", (128, 64))
        ps = [es.enter_context(nc.psum_tensor(f"ps{i}", [128, 512], F32)) for i in range(8)]
        sems = {e: [es.enter_context(nc.semaphore(f"s_{e}{i}")) for i in range(NS)] for e in Sched.ENGS}
        dsems = {e: [es.enter_context(nc.semaphore(f"d_{e}{i}")) for i in range(ND)] for e in ("sp", "pool")}
        block = es.enter_context(nc.Block())

        C = lambda n: ct[:, _CO[n][0]:_CO[n][0] + _CO[n][1]]
        ident = C("ident")

        tl = threading.local()
        pools = {"ALL": list(range(8)), "A": [0, 1], "B": [2, 3], "Bt": [4], "C": [5], "D": [6], "E": [7], "P0": [0, 1, 2, 3], "P1": [4, 5, 6, 7]}
        ppos = {kk: 0 for kk in pools}

        def _next(pname):
            lst = pools[pname]; i = lst[ppos[pname] % len(lst)]; ppos[pname] += 1
            return i

        def bank():
            i = _next(getattr(tl, "pool", "ALL"))
            return ps[i], f"ps{i}"

        def tbank():
            i = _next(getattr(tl, "tpool", None) or getattr(tl, "pool", "ALL"))
            return ps[i][:].bitcast(BF16), f"ps{i}"

        def run_interleaved(funcs, pnames, tpnames=None):
            nf = len(funcs); turn = [0]; alive = [True] * nf; cv = threading.Condition(); errs = []

            def handoff(i):
                j = (i + 1) % nf; c_ = 0
                while not alive[j] and c_ < nf:
                    j = (j + 1) % nf; c_ += 1
                turn[0] = j; cv.notify_all()

            def step(i):
                with cv:
                    handoff(i)
                    while turn[0] != i:
                        cv.wait()

            def worker(i):
                tl.pool = pnames[i]; tl.tpool = tpnames[i] if tpnames else None
                tl.step = lambda: step(i)
                with cv:
                    while turn[0] != i:
                        cv.wait()
                try:
                    funcs[i]()
                except BaseException as e:
                    errs.append(e)
                finally:
                    with cv:
                        alive[i] = False
                        handoff(i)
            ths = [threading.Thread(target=worker, args=(i,)) for i in range(nf)]
            for th in ths:
                th.start()
            for th in ths:
                th.join()
            if errs:
                raise errs[0]

        def _y():
            st_ = getattr(tl, "step", None)
            if st_:
                st_()

        def warena(name, lo, n):
            return Buf(name, 2 * lo, 2 * (lo + n)), WA[:, lo:lo + n]

        def work(name, lo, n, dt=F32):
            v = WK[:, lo:lo + n]
            if dt == BF16:
                v = v.bitcast(BF16)
            return Buf(name, 1000000 + 4 * lo, 1000000 + 4 * (lo + n)), v

        def _mk(eng):
            def f(fn, r=(), w=()):
                S.op(eng, fn, r, w); _y()
            return f
        dve = _mk("dve"); act = _mk("act"); pe = _mk("pe"); pool = _mk("pool")

        def dsp(o, i, r=(), w=()):
            S.op("sp", lambda h: h.dma_start(out=o, in_=i), r, w, dma=True); _y()

        def dpl(o, i, r=(), w=()):
            S.op("pool", lambda h: h.dma_start(out=o, in_=i), r, w, dma=True); _y()

        dsp(ct[:], ctab[:], w=["ct"])
        dsp(lnc[:].rearrange("p a b c -> p (a b c)"), lncols[:], w=["lnc"])
        dsp(cwt[:].rearrange("p a b c -> p (a b c)"), cw[:], w=["cwt"])
        dsp(psct[:].rearrange("p a b -> p (a b)"), psc[:], w=["psct"])
        dsp(lbt[:].rearrange("p a b -> p (a b)"), hlb[:], w=["lbt"])
        dsp(esink[:], sink[:].partition_broadcast(128), w=["esink"])
        dve(lambda h: h.tensor_copy(out=identb[:], in_=ident), ["ct"], ["identb"])
        dve(lambda h: h.memset(ones[:], 1.0), w=["ones"])
        act(lambda h: h.activation(out=esink[:], in_=esink[:], func=AF.Exp), ["esink"], ["esink"])
        act(lambda h: h.activation(out=lbe[:], in_=lbt[:], func=AF.Exp), ["lbt"], ["lbe"])
        dve(lambda h: h.tensor_reduce(out=lbs[:], in_=lbe[:], axis=AX.X, op=ALU.add), ["lbe"], ["lbs"])
        dve(lambda h: h.reciprocal(out=lbs[:], in_=lbs[:]), ["lbs"], ["lbs"])
        dve(lambda h: h.tensor_tensor(out=lbe[:], in0=lbe[:], in1=lbs[:].to_broadcast([128, 2, 4]), op=ALU.mult), ["lbe", "lbs"], ["lbe"])
        dve(lambda h: h.memset(lbt[:, :, 0:1], 0.0), ["lbe"], ["lbt"])
        for l in range(1, 4):
            dve(lambda h, l=l: h.tensor_tensor(out=lbt[:, :, l:l + 1], in0=lbt[:, :, l - 1:l], in1=lbe[:, :, l:l + 1], op=ALU.add), ["lbt", "lbe"], ["lbt"])
        dve(lambda h: h.tensor_scalar(out=oml[:], in0=lbt[:], scalar1=-1.0, scalar2=1.0, op0=ALU.mult, op1=ALU.add), ["lbt"], ["oml"])

        for t in range(16):
            dsp(x_tok[:, t, :], xp[t * 128:(t + 1) * 128, :], w=[f"x{t}"])
        dsp(x_tok[0:64, 16, :], xs[:, :], w=["x16"])

        def ntok(t):
            return 64 if t == 16 else 128

        lnst = sb("lnst", (128, 2, 6)); lnmv = sb("lnmv", (128, 2)); lnr = sb("lnr", (128, 2)); zbf = sb("zbf", (128, D), BF16)

        def load_ln(i):
            dsp(Grow[:], lnrows[i, 0:1, :].partition_broadcast(128), w=["Grow"])
            dsp(Brow[:], lnrows[i, 1:2, :].partition_broadcast(128), w=["Brow"])

        def ln_tile(t, i, banks, first=True):
            n = ntok(t); xk = f"x{t}"
            xt = x_tok[0:n, t, :]
            if banks is not None:
                for nh, (b, bk) in enumerate(banks):
                    if first:
                        dve(lambda h, b=b, nh=nh: h.scalar_tensor_tensor(out=xt[:, nh * 512:(nh + 1) * 512], in0=xt[:, nh * 512:(nh + 1) * 512], scalar=ALPHA, in1=b[0:n, :], op0=ALU.mult, op1=ALU.add), [xk, bk], [xk])
                    else:
                        dve(lambda h, b=b, nh=nh: h.tensor_tensor(out=xt[:, nh * 512:(nh + 1) * 512], in0=xt[:, nh * 512:(nh + 1) * 512], in1=b[0:n, :], op=ALU.add), [xk, bk], [xk])
            if i is None:
                return
            for a in range(2):
                dve(lambda h, a=a: h.bn_stats(out=lnst[0:n, a, :], in_=xt[:, a * 512:(a + 1) * 512]), [xk], ["lnst"])
            dve(lambda h: h.bn_aggr(out=lnmv[0:n], in_=lnst[0:n].rearrange("p a b -> p (a b)")), ["lnst"], ["lnmv"])
            act(lambda h: h.activation(out=lnr[0:n, 0:1], in_=lnmv[0:n, 1:2], func=AF.Ln, bias=LN_EPS, scale=1.0), ["lnmv"], ["lnr"])
            act(lambda h: h.activation(out=lnr[0:n, 0:1], in_=lnr[0:n, 0:1], func=AF.Exp, scale=-0.5), ["lnr"], ["lnr"])
            dve(lambda h: h.scalar_tensor_tensor(out=lnr[0:n, 1:2], in0=lnmv[0:n, 0:1], scalar=-1.0, in1=lnr[0:n, 0:1], op0=ALU.mult, op1=ALU.mult), ["lnr", "lnmv"], ["lnr2"])
            act(lambda h: h.activation(out=zbf[0:n], in_=xt, func=AF.Identity, scale=lnr[0:n, 0:1], bias=lnr[0:n, 1:2]), [xk, "lnr", "lnr2"], ["zbf"])
            dve(lambda h: h.tensor_scalar(out=xt, in0=xt, scalar1=lnr[0:n, 0:1], scalar2=lnr[0:n, 1:2], op0=ALU.mult, op1=ALU.add), [xk, "lnr", "lnr2"], [xk])
            dve(lambda h: h.tensor_tensor(out=xt, in0=xt, in1=Grow[0:n], op=ALU.mult), [xk, "Grow"], [xk])
            pool(lambda h: h.tensor_tensor(out=xt, in0=xt, in1=Brow[0:n], op=ALU.add), [xk, "Brow"], [xk])
            tb, tk = tbank()
            for c in range(8):
                pe(lambda h, c=c: h.transpose(tb[:, c * 128:c * 128 + n], zbf[0:n, c * 128:(c + 1) * 128], identb[0:n, 0:n]), ["zbf", "identb"], [tk])
            for c in range(8):
                if c % 2 == 0:
                    act(lambda h, c=c: h.activation(out=xT[:, c, t * 128:t * 128 + n], in_=tb[:, c * 128:c * 128 + n], func=AF.Identity, scale=lnc[:, i, 0, c:c + 1], bias=lnc[:, i, 1, c:c + 1]), [tk, "lnc"], [f"xT{t}"])
                else:
                    dve(lambda h, c=c: h.tensor_scalar(out=xT[:, c, t * 128:t * 128 + n], in0=tb[:, c * 128:c * 128 + n], scalar1=lnc[:, i, 0, c:c + 1], scalar2=lnc[:, i, 1, c:c + 1], op0=ALU.mult, op1=ALU.add), [tk, "lnc"], [f"xT{t}"])

        W_in = [warena(f"win{k}", k * 2560, 2560) for k in range(8)]
        W_o = warena("wo", 20480, 8192)
        for k in range(8):
            dpl(W_in[k][1], win[0, :, k, :], w=[W_in[k][0]])
        dpl(W_o[1].rearrange("p (a b) -> p a b", b=1024), wo[0], w=[W_o[0]])
        load_ln(0)
        for t in range(NT):
            ln_tile(t, 0, None)

        W_xq = warena("wxq", 0, 2048); W_xo = warena("wxo", 2048, 2048); W_xk = warena("wxk", 4096, 2048); W_xv = warena("wxv", 6144, 2048)
        ring = [8192, 14336, 20480, 0]

        mixTs = [sb("mixT0", (128, 8, 128), BF16), sb("mixT1", (128, 8, 128), BF16)]
        qZ = sb("qZ", (128, 2, 2, 128), BF16); kTr = sb("kTr", (128, 2, 128), BF16); Vaug = sb("Vaug", (128, 2, 2, 65), BF16)
        tmpS = sb("tmpS", (128, 512)); PT = sb("PT", (128, 2, 512), BF16); otok = sb("otok", (128, 256), BF16)
        den = sb("den", (128, 4)); reca = sb("reca", (128, 4))
        uext = sb("uext", (128, 2, 130)); chs = sb("chs", (128, 128)); ycv = sb("ycv", (128, 128))
        ext = sb("ext", (128, 2, 144)); s2 = sb("s2", (128, 144)); s4 = sb("s4", (128, 144)); s8 = sb("s8", (128, 144)); s16 = sb("s16", (128, 144))
        pooled = sb("pooled", (128, 2, 128), BF16)
        hE = sb("hE", (128, 128)); hL1 = sb("hL1", (128, 128)); hL2 = sb("hL2", (128, 128)); hCU = sb("hCU", (128, 128))
        hEC = sb("hEC", (128, 128)); hEN = sb("hEN", (128, 128)); hEQ = sb("hEQ", (128, 128))
        qSZ = sb("qSZ", (128, 2, 2, 2, 64), BF16); kAb = sb("kAb", (128, 2, 128), BF16); kAtokZ = sb("kAtokZ", (128, 2, 256), BF16)
        Vh = sb("Vh", (128, 256), BF16); Sst = sb("Sst", (128, 2, 64)); Sbf = sb("Sbf", (128, 2, 2, 64), BF16); ECL = sb("ECL", (128, 2, 2))
        AtZ = sb("AtZ", (128, 2, 4, 64), BF16); sqb = sb("sqb", (128, 256)); sgb = sb("sgb", (128, 256)); otok2 = sb("otok2", (128, 256), BF16)
        ssq = sb("ssq", (128, 4)); rsq = sb("rsq", (128, 4))
        pool(lambda h: h.memset(qSZ[:], 0.0), w=["qSZ"])
        pool(lambda h: h.memset(kAtokZ[:], 0.0), w=["kAtokZ"])
        pool(lambda h: h.memset(AtZ[:], 0.0), w=["AtZ"])
        ptmp = sb("ptmp", (128, 16)); uxs = sb("uxs", (128, 2, 16, 6)); exs = sb("exs", (128, 2, 16, 20)); ECLs = sb("ECLs", (128, 2, 16))
        dbgb = None
        print("SBUF bytes remaining:", nc.sbuf_bytes_remaining)
        pool(lambda h: h.memset(qZ[:], 0.0), w=["qZ"])
        pool(lambda h: h.memset(Vaug[:], 1.0), w=["Va0", "Va1"])
        mva = sb("mva", (128, 2, 4, 65), BF16)
        sgt = hE; hT = hL1[:].bitcast(BF16).rearrange("p (a b) -> p a b", b=128); hT2 = hL2[:].bitcast(BF16).rearrange("p (a b) -> p a b", b=128)
        mkT = kAtokZ; qxT = kAb; oxt = otok; oxT = otok2[:].rearrange("p (a b) -> p a b", b=128); rec = reca
        PX = PT[:].rearrange("p a (b c) -> p a b c", c=128)
        memTb, memT_ = warena("memT", 8192, 2048)
        memT = memT_.rearrange("p (c m) -> p c m", m=256)
        dve(lambda h: h.memset(mva[:], 1.0), w=["mva"])
        MEMT = [memTb]

        def layer(l):
            if l > 0:
                for k in range(8):
                    dpl(W_in[k][1], win[l, :, k, :], w=[W_in[k][0]])
                dpl(W_o[1].rearrange("p (a b) -> p a b", b=1024), wo[l], w=[W_o[0]])
            load_ln(1 + 3 * l)
            WIN = [w[0] for w in W_in]
            pool(lambda h: h.memset(Sst[:], 0.0), w=["Sst"])
            pool(lambda h: h.memset(exs[:], 0.0), w=["exs0", "exs1"])
            pool(lambda h: h.memset(kAtokZ[:], 0.0), w=["kAtokZ"])
            pool(lambda h: h.memset(uext[:, :, 0:2], 0.0), w=["uext0", "uext1"])
            pool(lambda h: h.memset(ext[:, :, 0:16], 0.0), w=["ext0", "ext1"])
            pool(lambda h: h.memset(wbd[:], 0.0), w=["wbd"])
            for g_ in range(4):
                dpl(wbd[64 * (g_ % 2):64 * (g_ % 2) + 64, g_ // 2, 64 * (g_ % 2):64 * (g_ % 2) + 64], pw[l, g_], w=["wbd"])
            dsp(normg[:], hng[l:l + 1, :].partition_broadcast(128), w=["normg"])
            def p1(t, prev_post):
                n = ntok(t); c0 = t * 128
                mixT = mixTs[t % 2]; mk = f"mixT{t % 2}"
                b1, k1 = bank()
                for k in range(8):
                    pe(lambda h, k=k, b1=b1: h.matmul(b1[0:n, :], lhsT=xT[:, k, c0:c0 + n], rhs=W_in[k][1][:, cK:cK + 512], start=(k == 0), stop=(k == 7)), [f"xT{t}"] + WIN, [k1])
                act(lambda h: h.copy(out=Vh[0:n], in_=b1[0:n, 256:512]), [k1], ["Vh"])
                act(lambda h: h.copy(out=Vaug[0:n, t % 2, :, 0:64], in_=b1[0:n, 128:256].rearrange("p (kv d) -> p kv d", d=64)), [k1], [f"Va{t % 2}"])
                if t >= 15:
                    dve(lambda h, b1=b1: h.tensor_copy(out=sqb[0:n, 0:256], in_=b1[0:n, 0:256]), [k1], ["sqb"])
                    if t == 15:
                        dsp(okp[l], sqb[:, 0:128], ["sqb"]); dsp(ovp[l], sqb[:, 128:256], ["sqb"])
                    elif not DBG_SKIP:
                        dsp(oks[l, :, 0:124, :], ck[l, :, 4:128, :]); dsp(ovs[l, :, 0:124, :], cv[l, :, 4:128, :])
                        for tt in range(4):
                            dsp(oks[l, :, 124 + tt, :], sqb[tt:64:4, 0:128], ["sqb"]); dsp(ovs[l, :, 124 + tt, :], sqb[tt:64:4, 128:256], ["sqb"])
                    b2, k2 = bank(); b3, k3 = bank()
                    for k in range(8):
                        pe(lambda h, k=k, b2=b2: h.matmul(b2[0:n, :], lhsT=xT[:, k, c0:c0 + n], rhs=W_in[k][1][:, cCC:cCC + 512], start=(k == 0), stop=(k == 7)), [f"xT{t}"] + WIN, [k2])
                    for k in range(8):
                        pe(lambda h, k=k, b3=b3: h.matmul(b3[0:n, 0:256], lhsT=xT[:, k, c0:c0 + n], rhs=W_in[k][1][:, cDV:cDV + 256], start=(k == 0), stop=(k == 7)), [f"xT{t}"] + WIN, [k3])
                    act(lambda h, b2=b2: h.copy(out=sgb[0:n, :], in_=b2[0:n, 256:512]), [k2], ["sgb"])
                    dve(lambda h, b2=b2: h.tensor_tensor(out=sgb[0:n, :], in0=b2[0:n, 0:256], in1=sgb[0:n, :], op=ALU.mult), [k2, "sgb"], ["sgb"])
                    act(lambda h, b3=b3: h.copy(out=tmpS[0:n, 0:256], in_=b3[0:n, 0:256]), [k3], ["tmpS"])
                    if t == 15:
                        dsp(ocp[l], sgb[126:128, :], ["sgb"]); dsp(opp[l], tmpS[113:128, 0:256], ["tmpS"])
                    elif not DBG_SKIP:
                        dsp(ops_[l, :, 0:11, :], spl[l, :, 4:15, :])
                        for tt in range(4):
                            dsp(ops_[l, :, 11 + tt, :], tmpS[tt:64:4, 0:256], ["tmpS"])
                            if tt >= 2:
                                dsp(ocs[l, :, tt - 2, :], sgb[tt:64:4, :], ["sgb"])
                def fm(col):
                    b, bk = bank()
                    for k in range(8):
                        pe(lambda h, k=k: h.matmul(b[:, 0:n], lhsT=W_in[k][1][:, col:col + 128], rhs=xT[:, k, c0:c0 + n], start=(k == 0), stop=(k == 7)), [f"xT{t}"] + WIN, [bk])
                    return b, bk
                sl = t % 2
                v3 = lambda ap: ap.rearrange("p (b t) -> p b t", t=4)

                def attn_out(bo_, ko_):
                    ov = bo_[0:n, 0:260].rearrange("p (h e) -> p h e", e=65)
                    dve(lambda h: h.tensor_tensor(out=den[0:n], in0=ov[:, :, 64], in1=esink[0:n, l * 4:(l + 1) * 4], op=ALU.add), [ko_, "esink"], ["den"])
                    dve(lambda h: h.reciprocal(out=reca[0:n], in_=den[0:n]), ["den"], ["reca"])
                    dve(lambda h: h.tensor_tensor(out=otok[0:n].rearrange("p (h d) -> p h d", d=64), in0=ov[:, :, 0:64], in1=reca[0:n].unsqueeze(2).to_broadcast([n, 4, 64]), op=ALU.mult), [ko_, "reca"], ["otok"])
                    tb, tk = tbank()
                    for c in range(2):
                        pe(lambda h, c=c: h.transpose(tb[:, c * 128:c * 128 + n], otok[0:n, c * 128:(c + 1) * 128], identb[0:n, 0:n]), ["otok", "identb"], [tk])
                    dve(lambda h: h.tensor_copy(out=mixT[:, 0:2, 0:n], in_=tb[:, 0:256].rearrange("p (a b) -> p a b", b=128)[:, :, 0:n]), [tk], [mk])

                def hgrn_gate():
                    bg, kg = bank()
                    for k_ in range(8):
                        pe(lambda h, k_=k_: h.matmul(bg[0:n, 0:256], lhsT=xT[:, k_, c0:c0 + n], rhs=W_in[k_][1][:, cG:cG + 256], start=(k_ == 0), stop=(k_ == 7)), [f"xT{t}"] + WIN, [kg])
                    act(lambda h: h.activation(out=sgb[0:n], in_=bg[0:n, 0:256], func=AF.Exp, scale=-1.0), [kg], ["sgb"])
                    act(lambda h: h.activation(out=sgb[0:n], in_=sgb[0:n], func=AF.Ln, scale=1.0, bias=1.0), ["sgb"], ["sgb"])
                    act(lambda h: h.activation(out=sgb[0:n], in_=sgb[0:n], func=AF.Exp, scale=-1.0), ["sgb"], ["sgb"])
                    dve(lambda h: h.tensor_tensor(out=sgb[0:n], in0=bg[0:n, 0:256], in1=sgb[0:n], op=ALU.mult), [kg, "sgb"], ["sgb"])

                def hgrn_out(bO, kO):
                    act(lambda h: h.activation(out=sqb[0:n], in_=bO[0:n, 0:256], func=AF.Square), [kO], ["sqb"])
                    dve(lambda h: h.tensor_reduce(out=ssq[0:n], in_=sqb[0:n].rearrange("p (a v) -> p a v", v=64), axis=AX.X, op=ALU.add), ["sqb"], ["ssq"])
                    act(lambda h: h.activation(out=rsq[0:n], in_=ssq[0:n], func=AF.Ln, scale=1.0 / 64, bias=RMS_EPS), ["ssq"], ["rsq"])
                    act(lambda h: h.activation(out=rsq[0:n], in_=rsq[0:n], func=AF.Exp, scale=-0.5), ["rsq"], ["rsq"])
                    dve(lambda h: h.tensor_tensor(out=sqb[0:n].rearrange("p (a v) -> p a v", v=64), in0=bO[0:n, 0:256].rearrange("p (a v) -> p a v", v=64), in1=rsq[0:n].unsqueeze(2).to_broadcast([n, 4, 64]), op=ALU.mult), [kO, "rsq", "sqb"], ["sqb"])
                    dve(lambda h: h.tensor_tensor(out=sqb[0:n], in0=sqb[0:n], in1=normg[0:n], op=ALU.mult), ["sqb", "normg"], ["sqb"])
                    dve(lambda h: h.tensor_tensor(out=otok2[0:n], in0=sqb[0:n], in1=sgb[0:n], op=ALU.mult), ["sqb", "sgb"], ["otok2"])
                    tb2, tk2 = tbank()
                    for c in range(2):
                        pe(lambda h, c=c: h.transpose(tb2[:, c * 128:c * 128 + n], otok2[0:n, c * 128:(c + 1) * 128], identb[0:n, 0:n]), ["otok2", "identb"], [tk2])
                    dve(lambda h: h.tensor_copy(out=mixT[:, 2:4, 0:n], in_=tb2[:, 0:256].rearrange("p (a b) -> p a b", b=128)[:, :, 0:n]), [tk2], [mk])
                if t == 16:
                    dsp(sqb[0:32, :], scv[l], w=["sqb"])
                    for hc in range(2):
                        bT, kT_ = bank()
                        pe(lambda h, hc=hc, bT=bT: h.transpose(bT[:, 0:32], sqb[0:32, hc * 128:(hc + 1) * 128], ident[0:32, 0:32]), ["sqb", "ct"], [kT_])
                        act(lambda h, hc=hc, bT=bT: h.copy(out=uxs[:, hc, :, 0:2], in_=bT[:, 0:32].rearrange("p (b j) -> p b j", j=2)), [kT_], [f"uxs{hc}"])
                    for half in range(2):
                        dsp(sqb[0:120, :], spl[l, 8 * half:8 * half + 8].rearrange("b j c -> (b j) c"), w=["sqb"])
                        for hc in range(2):
                            bT, kT_ = bank()
                            pe(lambda h, hc=hc, bT=bT: h.transpose(bT[:, 0:120], sqb[0:120, hc * 128:(hc + 1) * 128], ident[0:120, 0:120]), ["sqb", "ct"], [kT_])
                            act(lambda h, hc=hc, bT=bT, half=half: h.copy(out=exs[:, hc, 8 * half:8 * half + 8, 1:16], in_=bT[:, 0:120].rearrange("p (b j) -> p b j", j=15)), [kT_], [f"exs{hc}"])
                def secA():
                    for g, col in enumerate((cA, cB)):
                        b, bk = fm(col)
                        for kv in range(2):
                            act(lambda h, b=b, g=g, kv=kv: h.copy(out=qZ[64 * kv:64 * kv + 64, g, kv, 0:n], in_=b[64 * kv:64 * kv + 64, 0:n]), [bk], ["qZ"])
                    bK, kK_ = fm(cK)
                    act(lambda h: h.copy(out=kTr[:, sl, 0:n], in_=bK[:, 0:n]), [kK_], [f"kT{sl}"])
                    if t < 16:
                        blks = [0] if t == 0 else [1, 0]
                        for blk in blks:
                            ksl = sl if blk == 0 else 1 - sl
                            bs, ks = bank()
                            for g in range(2):
                                pe(lambda h, g=g, bs=bs, ksl=ksl: h.matmul(bs[:, g * 256:(g + 1) * 256], lhsT=kTr[:, ksl, :], rhs=qZ[:, g, :, :].rearrange("p a q -> p (a q)"), start=True, stop=True), [f"kT{ksl}", "qZ"], [ks])
                            dve(lambda h, bs=bs, blk=blk: h.scalar_tensor_tensor(out=tmpS[:], in0=bs[:, :], scalar=0.125, in1=C("bt")[:, blk * 512:(blk + 1) * 512], op0=ALU.mult, op1=ALU.add), [ks, "ct"], ["tmpS"])
                            act(lambda h, blk=blk: h.activation(out=PT[:, blk, :], in_=tmpS[:], func=AF.Exp), ["tmpS"], [f"PT{blk}"])
                        bo_, ko_ = bank()
                        for kv in range(2):
                            for g in range(2):
                                hh = kv * 2 + g; pc = (g * 2 + kv) * 128
                                if t > 0:
                                    pe(lambda h, hh=hh, pc=pc, kv=kv: h.matmul(bo_[:, hh * 65:(hh + 1) * 65], lhsT=PT[:, 1, pc:pc + 128], rhs=Vaug[:, 1 - sl, kv, :], start=True, stop=False), ["PT1", f"Va{1 - sl}"], [ko_])
                                pe(lambda h, hh=hh, pc=pc, kv=kv: h.matmul(bo_[:, hh * 65:(hh + 1) * 65], lhsT=PT[:, 0, pc:pc + 128], rhs=Vaug[:, sl, kv, :], start=(t == 0), stop=True), ["PT0", f"Va{sl}"], [ko_])
                        attn_out(bo_, ko_)
                def secB():
                    if t < 16:
                        tbH, tkH = tbank()
                        for hc in range(2):
                            bq, kq = fm(cBQ + hc * 128); bf, kf = fm(cBF + hc * 128)
                            act(lambda h, bf=bf: h.activation(out=hE[:, 0:n], in_=bf[:, 0:n], func=AF.Exp, scale=-1.0), [kf], ["hE"])
                            act(lambda h, hc=hc: h.activation(out=hL1[:, 0:n], in_=hE[:, 0:n], func=AF.Ln, scale=lbt[:, hc, l:l + 1], bias=1.0), ["hE", "lbt"], ["hL1"])
                            act(lambda h: h.activation(out=hL2[:, 0:n], in_=hE[:, 0:n], func=AF.Ln, scale=1.0, bias=1.0), ["hE"], ["hL2"])
                            dve(lambda h: h.tensor_tensor(out=hL1[:, 0:n], in0=hL1[:, 0:n], in1=hL2[:, 0:n], op=ALU.subtract), ["hL1", "hL2"], ["hL1"])
                            act(lambda h: h.activation(out=hL2[:, 0:n], in_=hL2[:, 0:n], func=AF.Exp, scale=-1.0), ["hL2"], ["hL2"])
                            dve(lambda h, hc=hc: h.scalar_tensor_tensor(out=hE[:, 0:n], in0=hE[:, 0:n], scalar=oml[:, hc, l:l + 1], in1=hL2[:, 0:n], op0=ALU.mult, op1=ALU.mult), ["hE", "hL2", "oml"], ["hE"])
                            for c2 in range(2):
                                dve(lambda h, c2=c2: h.tensor_tensor_scan(out=hCU[:, c2 * 64:(c2 + 1) * 64], data0=ones[:, 0:64], data1=hL1[:, c2 * 64:(c2 + 1) * 64], initial=0.0, op0=ALU.mult, op1=ALU.add), ["hL1", "ones"], ["hCU"])
                            act(lambda h: h.activation(out=hEC[:, 0:n], in_=hCU[:, 0:n], func=AF.Exp), ["hCU"], ["hEC"])
                            act(lambda h: h.activation(out=hEN[:, 0:n], in_=hCU[:, 0:n], func=AF.Exp, scale=-1.0), ["hCU"], ["hEN"])
                            act(lambda h, bq=bq: h.activation(out=hEQ[:, 0:n], in_=bq[:, 0:n], func=AF.Exp, scale=-1.0), [kq], ["hEQ"])
                            act(lambda h: h.activation(out=hEQ[:, 0:n], in_=hEQ[:, 0:n], func=AF.Ln, scale=1.0, bias=1.0), ["hEQ"], ["hEQ"])
                            act(lambda h: h.activation(out=hEQ[:, 0:n], in_=hEQ[:, 0:n], func=AF.Exp, scale=-1.0), ["hEQ"], ["hEQ"])
                            dve(lambda h, bq=bq: h.tensor_tensor(out=hEQ[:, 0:n], in0=bq[:, 0:n], in1=hEQ[:, 0:n], op=ALU.mult), [kq, "hEQ"], ["hEQ"])
                            for hp in range(2):
                                rw = slice(64 * hp, 64 * hp + 64)
                                dve(lambda h, hc=hc, hp=hp, rw=rw: h.tensor_tensor(out=qSZ[rw, hc, :, hp, :], in0=hEQ[rw, :].rearrange("p (c t) -> p c t", t=64), in1=hEC[rw, :].rearrange("p (c t) -> p c t", t=64), op=ALU.mult), ["hEQ", "hEC"], ["qSZ"])
                            dve(lambda h, hc=hc: h.tensor_tensor(out=kAb[:, hc, :], in0=hE[:, 0:n], in1=hEN[:, 0:n], op=ALU.mult), ["hE", "hEN"], [f"kAb{hc}"])
                            act(lambda h, hc=hc: h.copy(out=ECL[:, hc, :], in_=hEC[:, 63:128:64]), ["hEC"], ["ECL"])
                            pe(lambda h, hc=hc: h.transpose(tbH[:, hc * 128:(hc + 1) * 128], kAb[:, hc, :], identb[:]), [f"kAb{hc}", "identb"], [tkH])
                        for c2 in range(2):
                            rw = slice(64 * c2, 64 * c2 + 64)
                            dve(lambda h, c2=c2, rw=rw: h.tensor_copy(out=kAtokZ[rw, c2, :], in_=tbH[rw, 0:256]), [tkH], ["kAtokZ"])
                        bU, kU = bank()
                        for c2 in range(2):
                            for hc in range(2):
                                pe(lambda h, c2=c2, hc=hc: h.matmul(bU[:, (c2 * 2 + hc) * 128:(c2 * 2 + hc + 1) * 128], lhsT=kAtokZ[:, c2, hc * 128:(hc + 1) * 128], rhs=Vh[:, hc * 128:(hc + 1) * 128], start=True, stop=True), ["kAtokZ", "Vh"], [kU])
                        for c2 in range(2):
                            act(lambda h, c2=c2: h.copy(out=Sbf[:, :, c2, :], in_=Sst[:]), ["Sst"], ["Sbf"])
                            for hp in range(2):
                                rw = slice(64 * hp, 64 * hp + 64)
                                dve(lambda h, c2=c2, hp=hp, rw=rw: h.tensor_tensor(out=Sst[rw], in0=Sst[rw], in1=bU[rw, c2 * 256:(c2 + 1) * 256].rearrange("p (hc hh v) -> p hc hh v", hc=2, hh=2)[:, :, hp, :], op=ALU.add), ["Sst", kU], ["Sst"])
                            dve(lambda h, c2=c2: h.tensor_tensor(out=Sst[:], in0=Sst[:], in1=ECL[:, :, c2:c2 + 1].to_broadcast([128, 2, 64]), op=ALU.mult), ["Sst", "ECL"], ["Sst"])
                        bA, kA_ = bank()
                        for c2 in range(2):
                            for hc in range(2):
                                pe(lambda h, c2=c2, hc=hc: h.matmul(bA[64 * c2:64 * c2 + 64, hc * 128:(hc + 1) * 128], lhsT=kAb[:, hc, c2 * 64:(c2 + 1) * 64], rhs=qSZ[:, hc, c2, :, :].rearrange("p a t -> p (a t)"), start=True, stop=True), [f"kAb{hc}", "qSZ"], [kA_])
                        for c2 in range(2):
                            rw = slice(64 * c2, 64 * c2 + 64)
                            dve(lambda h, c2=c2, rw=rw: h.tensor_tensor(out=AtZ[rw, c2, :, :], in0=bA[rw, 0:256].rearrange("p (a t) -> p a t", t=64), in1=C("tri2")[rw, :].unsqueeze(1).to_broadcast([64, 4, 64]), op=ALU.mult), [kA_, "ct"], ["AtZ"])
                        bO, kO = bank()
                        for c2 in range(2):
                            for hh in range(4):
                                hc, hp = hh // 2, hh % 2
                                pe(lambda h, c2=c2, hh=hh, hc=hc, hp=hp: h.matmul(bO[64 * c2:64 * c2 + 64, hh * 64:(hh + 1) * 64], lhsT=qSZ[:, hc, c2, hp, :], rhs=Sbf[:, hc, c2, :], start=True, stop=False), ["qSZ", "Sbf"], [kO])
                                pe(lambda h, c2=c2, hh=hh: h.matmul(bO[64 * c2:64 * c2 + 64, hh * 64:(hh + 1) * 64], lhsT=AtZ[:, c2, hh, :], rhs=Vh[:, hh * 64:(hh + 1) * 64], start=False, stop=True), ["AtZ", "Vh"], [kO])
                        hgrn_gate()
                        hgrn_out(bO, kO)
                        if t == int(os.environ.get('DBG_ST', '15')):
                            dsp(ohp[l].rearrange("(hc hp) k v -> (hp k) hc v", hc=2), Sst[:], ["Sst"])
                    else:
                        tbH, tkH = tbank()
                        for hc in range(2):
                            bq, kq = fm(cBQ + hc * 128); bf, kf = fm(cBF + hc * 128)
                            act(lambda h, bf=bf: h.activation(out=hE[:, 0:n], in_=bf[:, 0:n], func=AF.Exp, scale=-1.0), [kf], ["hE"])
                            act(lambda h, hc=hc: h.activation(out=hL1[:, 0:n], in_=hE[:, 0:n], func=AF.Ln, scale=lbt[:, hc, l:l + 1], bias=1.0), ["hE", "lbt"], ["hL1"])
                            act(lambda h: h.activation(out=hL2[:, 0:n], in_=hE[:, 0:n], func=AF.Ln, scale=1.0, bias=1.0), ["hE"], ["hL2"])
                            dve(lambda h: h.tensor_tensor(out=hL1[:, 0:n], in0=hL1[:, 0:n], in1=hL2[:, 0:n], op=ALU.subtract), ["hL1", "hL2"], ["hL1"])
                            act(lambda h: h.activation(out=hL2[:, 0:n], in_=hL2[:, 0:n], func=AF.Exp, scale=-1.0), ["hL2"], ["hL2"])
                            dve(lambda h, hc=hc: h.scalar_tensor_tensor(out=hE[:, 0:n], in0=hE[:, 0:n], scalar=oml[:, hc, l:l + 1], in1=hL2[:, 0:n], op0=ALU.mult, op1=ALU.mult), ["hE", "hL2", "oml"], ["hE"])
                            dve(lambda h: h.tensor_copy(out=hCU[:, 0:n], in_=hL1[:, 0:n]), ["hL1"], ["hCU"])
                            for tt in range(1, 4):
                                dve(lambda h, tt=tt: h.tensor_tensor(out=v3(hCU[:, 0:n])[:, :, tt:tt + 1], in0=v3(hCU[:, 0:n])[:, :, tt - 1:tt], in1=v3(hL1[:, 0:n])[:, :, tt:tt + 1], op=ALU.add), ["hCU", "hL1"], ["hCU"])
                            act(lambda h: h.activation(out=hEC[:, 0:n], in_=hCU[:, 0:n], func=AF.Exp), ["hCU"], ["hEC"])
                            act(lambda h: h.activation(out=hEN[:, 0:n], in_=hCU[:, 0:n], func=AF.Exp, scale=-1.0), ["hCU"], ["hEN"])
                            act(lambda h, bq=bq: h.activation(out=hEQ[:, 0:n], in_=bq[:, 0:n], func=AF.Exp, scale=-1.0), [kq], ["hEQ"])
                            act(lambda h: h.activation(out=hEQ[:, 0:n], in_=hEQ[:, 0:n], func=AF.Ln, scale=1.0, bias=1.0), ["hEQ"], ["hEQ"])
                            act(lambda h: h.activation(out=hEQ[:, 0:n], in_=hEQ[:, 0:n], func=AF.Exp, scale=-1.0), ["hEQ"], ["hEQ"])
                            dve(lambda h, bq=bq: h.tensor_tensor(out=hEQ[:, 0:n], in0=bq[:, 0:n], in1=hEQ[:, 0:n], op=ALU.mult), [kq, "hEQ"], ["hEQ"])
                            for hp in range(2):
                                rw = slice(64 * hp, 64 * hp + 64)
                                dve(lambda h, hc=hc, hp=hp, rw=rw: h.tensor_tensor(out=qSZ[rw, hc, 0, hp, :], in0=hEQ[rw, 0:n], in1=hEC[rw, 0:n], op=ALU.mult), ["hEQ", "hEC"], ["qSZ"])
                            dve(lambda h, hc=hc: h.tensor_tensor(out=kAb[:, hc, 0:n], in0=hE[:, 0:n], in1=hEN[:, 0:n], op=ALU.mult), ["hE", "hEN"], [f"kAb{hc}"])
                            act(lambda h, hc=hc: h.copy(out=ECLs[:, hc, :], in_=v3(hEC[:, 0:n])[:, :, 3]), ["hEC"], ["ECLs"])
                            dve(lambda h, hc=hc: h.tensor_tensor(out=v3(kAb[:, hc, 64:128]), in0=v3(kAb[:, hc, 0:64]), in1=ECLs[:, hc, :].unsqueeze(2).to_broadcast([128, 16, 4]), op=ALU.mult), [f"kAb{hc}", "ECLs"], [f"kAb{hc}"])
                            pe(lambda h, hc=hc: h.transpose(tbH[0:64, hc * 128:(hc + 1) * 128], kAb[:, hc, 64:128], identb[:]), [f"kAb{hc}", "identb"], [tkH])
                        dve(lambda h: h.tensor_copy(out=kAtokZ[0:64, 0, :], in_=tbH[0:64, 0:256]), [tkH], ["kAtokZ"])
                        bA, kA_ = bank()
                        for hc in range(2):
                            pe(lambda h, hc=hc: h.matmul(bA[0:64, hc * 128:(hc + 1) * 128], lhsT=kAb[:, hc, 0:64], rhs=qSZ[:, hc, 0, :, :].rearrange("p a t -> p (a t)"), start=True, stop=True), [f"kAb{hc}", "qSZ"], [kA_])
                        dve(lambda h: h.tensor_tensor(out=AtZ[0:64, 0, :, :], in0=bA[0:64, 0:256].rearrange("p (a t) -> p a t", t=64), in1=C("bdc")[0:64, :].unsqueeze(1).to_broadcast([64, 4, 64]), op=ALU.mult), [kA_, "ct"], ["AtZ"])
                        hgrn_gate()
                def secC():
                    for hc in range(2):
                        if t < 16:
                            bch, kch = fm(cCH + hc * 128)
                            act(lambda h, bch=bch: h.copy(out=chs[:, 0:n], in_=bch[:, 0:n]), [kch], ["chs"])
                            bcc, kcc = fm(cCC + hc * 128)
                            dve(lambda h, bcc=bcc, hc=hc: h.tensor_tensor(out=uext[:, hc, 2:2 + n], in0=bcc[:, 0:n], in1=chs[:, 0:n], op=ALU.mult), [kcc, "chs"], [f"uext{hc}"])
                            dve(lambda h, hc=hc: h.tensor_scalar(out=ycv[:, 0:n], in0=uext[:, hc, 2:2 + n], scalar1=cwt[:, l, hc, 2:3], scalar2=None, op0=ALU.mult), [f"uext{hc}", "cwt"], ["ycv"])
                            for j in (1, 0):
                                dve(lambda h, hc=hc, j=j: h.scalar_tensor_tensor(out=ycv[:, 0:n], in0=uext[:, hc, j:j + n], scalar=cwt[:, l, hc, j:j + 1], in1=ycv[:, 0:n], op0=ALU.mult, op1=ALU.add), [f"uext{hc}", "cwt", "ycv"], ["ycv"])
                            bcb, kcb = fm(cCB + hc * 128)
                            dve(lambda h, hc=hc, bcb=bcb: h.tensor_tensor(out=mixT[:, 4 + hc, 0:n], in0=ycv[:, 0:n], in1=bcb[:, 0:n], op=ALU.mult), ["ycv", kcb], [mk])
                            act(lambda h, hc=hc: h.copy(out=uext[:, hc, 0:2], in_=uext[:, hc, n:n + 2]), [f"uext{hc}"], [f"uext{hc}"])
                        else:
                            v3 = lambda ap: ap.rearrange("p (b t) -> p b t", t=4)
                            bch, kch = fm(cCH + hc * 128)
                            act(lambda h, bch=bch: h.copy(out=chs[:, 0:n], in_=bch[:, 0:n]), [kch], ["chs"])
                            bcc, kcc = fm(cCC + hc * 128)
                            dve(lambda h, bcc=bcc, hc=hc: h.tensor_tensor(out=uxs[:, hc, :, 2:6], in0=v3(bcc[:, 0:n]), in1=v3(chs[:, 0:n]), op=ALU.mult), [kcc, "chs"], [f"uxs{hc}"])
                            dve(lambda h, hc=hc: h.tensor_scalar(out=v3(ycv[:, 0:n]), in0=uxs[:, hc, :, 2:6], scalar1=cwt[:, l, hc, 2:3], scalar2=None, op0=ALU.mult), [f"uxs{hc}", "cwt"], ["ycv"])
                            for j in (1, 0):
                                dve(lambda h, hc=hc, j=j: h.scalar_tensor_tensor(out=v3(ycv[:, 0:n]), in0=uxs[:, hc, :, j:j + 4], scalar=cwt[:, l, hc, j:j + 1], in1=v3(ycv[:, 0:n]), op0=ALU.mult, op1=ALU.add), [f"uxs{hc}", "cwt", "ycv"], ["ycv"])
                            bcb, kcb = fm(cCB + hc * 128)
                            dve(lambda h, hc=hc, bcb=bcb: h.tensor_tensor(out=mixT[:, 4 + hc, 0:n], in0=ycv[:, 0:n], in1=bcb[:, 0:n], op=ALU.mult), ["ycv", kcb], [mk])
                def secD():
                    for hc in range(2):
                        bdv, kdv = fm(cDV + hc * 128)
                        if t < 16:
                            W_ = 16 + n
                            act(lambda h, bdv=bdv, hc=hc: h.copy(out=ext[:, hc, 16:16 + n], in_=bdv[:, 0:n]), [kdv], [f"ext{hc}"])
                            pool(lambda h, hc=hc: h.tensor_tensor(out=s2[:, 1:W_], in0=ext[:, hc, 1:W_], in1=ext[:, hc, 0:W_ - 1], op=ALU.add), [f"ext{hc}"], ["s2"])
                            pool(lambda h: h.tensor_tensor(out=s4[:, 3:W_], in0=s2[:, 3:W_], in1=s2[:, 1:W_ - 2], op=ALU.add), ["s2"], ["s4"])
                            if hc == 1:
                                pool(lambda h: h.tensor_tensor(out=s8[:, 7:W_], in0=s4[:, 7:W_], in1=s4[:, 3:W_ - 4], op=ALU.add), ["s4"], ["s8"])
                                pool(lambda h: h.tensor_tensor(out=s16[:, 15:W_], in0=s8[:, 15:W_], in1=s8[:, 7:W_ - 8], op=ALU.add), ["s8"], ["s16"])
                            sel = [(s2, "s2", 2.0), (s4, "s4", 4.0)] if hc == 0 else [(s8, "s8", 8.0), (s16, "s16", 16.0)]
                            for half, (sv, sk, w_) in enumerate(sel):
                                rows = slice(64 * half, 64 * half + 64)
                                dve(lambda h, sv=sv, rows=rows, w_=w_, hc=hc: h.scalar_tensor_tensor(out=pooled[rows, hc, 0:n], in0=sv[rows, 16:W_], scalar=1.0 / w_, in1=ext[rows, hc, 16:W_], op0=ALU.mult, op1=ALU.subtract), [sk, f"ext{hc}"], [f"pooled{hc}"])
                                if t == 0:
                                    dve(lambda h, sv=sv, rows=rows, hc=hc: h.tensor_tensor(out=ptmp[rows, 0:16], in0=sv[rows, 16:32], in1=C("icnt")[rows, hc * 16:(hc + 1) * 16], op=ALU.mult), [sk, "ct"], ["ptmp"])
                                    dve(lambda h, rows=rows, hc=hc: h.tensor_tensor(out=pooled[rows, hc, 0:16], in0=ptmp[rows, 0:16], in1=ext[rows, hc, 16:32], op=ALU.subtract), ["ptmp", f"ext{hc}", f"pooled{hc}"], [f"pooled{hc}"])
                            by, ky = bank()
                            pe(lambda h, hc=hc, by=by: h.matmul(by[:, 0:n], lhsT=wbd[:, hc, :], rhs=pooled[:, hc, 0:n], start=True, stop=True), ["wbd", f"pooled{hc}"], [ky])
                            act(lambda h, hc=hc, by=by: h.activation(out=mixT[:, 6 + hc, 0:n], in_=by[:, 0:n], func=AF.Copy, scale=psct[:, l, hc:hc + 1]), [ky, "psct"], [mk])
                            act(lambda h, hc=hc: h.copy(out=ext[:, hc, 0:16], in_=ext[:, hc, n:n + 16]), [f"ext{hc}"], [f"ext{hc}"])
                        else:
                            v3 = lambda ap: ap.rearrange("p (b t) -> p b t", t=4)
                            act(lambda h, bdv=bdv, hc=hc: h.copy(out=exs[:, hc, :, 16:20], in_=v3(bdv[:, 0:n])), [kdv], [f"exs{hc}"])
                            for half in range(2):
                                w_ = (2, 4, 8, 16)[hc * 2 + half]; rows = slice(64 * half, 64 * half + 64)
                                dve(lambda h, rows=rows, hc=hc: h.tensor_copy(out=v3(s2[rows, 0:n]), in_=exs[rows, hc, :, 16:20]), [f"exs{hc}"], ["s2"])
                                for j in range(1, w_):
                                    dve(lambda h, rows=rows, hc=hc, j=j: h.tensor_tensor(out=v3(s2[rows, 0:n]), in0=v3(s2[rows, 0:n]), in1=exs[rows, hc, :, 16 - j:20 - j], op=ALU.add), ["s2", f"exs{hc}"], ["s2"])
                                dve(lambda h, rows=rows, hc=hc, w_=w_: h.scalar_tensor_tensor(out=v3(pooled[rows, hc, 0:n]), in0=v3(s2[rows, 0:n]), scalar=1.0 / w_, in1=exs[rows, hc, :, 16:20], op0=ALU.mult, op1=ALU.subtract), ["s2", f"exs{hc}"], [f"pooled{hc}"])
                            by, ky = bank()
                            pe(lambda h, hc=hc, by=by: h.matmul(by[:, 0:n], lhsT=wbd[:, hc, :], rhs=pooled[:, hc, 0:n], start=True, stop=True), ["wbd", f"pooled{hc}"], [ky])
                            act(lambda h, hc=hc, by=by: h.activation(out=mixT[:, 6 + hc, 0:n], in_=by[:, 0:n], func=AF.Copy, scale=psct[:, l, hc:hc + 1]), [ky, "psct"], [mk])
                if os.environ.get('NO_ILV'):
                    secA(); secB(); secC(); secD()
                    if prev_post:
                        prev_post()
                else:
                    fs = [secA, secB, secC, secD] + ([prev_post] if prev_post else [])
                    run_interleaved(fs, ["A", "B", "C", "D", "E"][:len(fs)], [None, "Bt", None, None, None][:len(fs)])
                if t == 16:
                    PS_ = 28672
                    kc_b, kc_ = warena("kc", 0, 2048); kc = kc_.rearrange("p (b c) -> p b c", c=128)
                    kcT_b, kcT_ = warena("kcT", 2048, 2048); kcT = kcT_.rearrange("p (b c) -> p b c", c=128)
                    Vc_b, Vc_ = warena("Vc", 4096, 2080); Vc = Vc_.rearrange("p (b kv e) -> p b kv e", kv=2, e=65)
                    Pb_b, Pb_ = warena("Pb", 6400, 4096); Pb = Pb_.rearrange("p (g kv b q) -> p g kv b q", g=2, kv=2, b=16)
                    tabs_b, tabs_ = warena("stab", 18944, 1024); tabs = tabs_.bitcast(F32)
                    dsp(tabs, stab[:, :], w=[tabs_b])
                    dpl(kc, ck[l].rearrange("b s c -> s b c"), w=[kc_b])
                    pool(lambda h: h.memset(Vc_, 1.0), w=[Vc_b])
                    for kv in range(2):
                        dpl(Vc[:, :, kv, 0:64], cv[l, :, :, kv * 64:(kv + 1) * 64].rearrange("b s d -> s b d"), w=[Vc_b])
                    pool(lambda h: h.memset(Pb_, 0.0), w=[Pb_b])
                    for half in range(2):
                        tbk, tkk = tbank()
                        for j in range(8):
                            pe(lambda h, j=j, half=half, tbk=tbk: h.transpose(tbk[:, j * 128:(j + 1) * 128], kc[:, half * 8 + j, :], identb[:]), [kc_b, "identb"], [tkk])
                        dve(lambda h, half=half, tbk=tbk: h.tensor_copy(out=kcT[:, half * 8:(half + 1) * 8, :], in_=tbk[:, :].rearrange("p (b c) -> p b c", c=128)), [tkk], [kcT_b])
                    bs, ks = bank()
                    for b_ in range(16):
                        for g in range(2):
                            pe(lambda h, b_=b_, g=g: h.matmul(bs[:, g * 128:(g + 1) * 128].rearrange("p (kv q) -> p kv q", kv=2)[:, :, b_ * 4:b_ * 4 + 4], lhsT=kcT[:, b_, :], rhs=qZ[:, g, :, b_ * 4:b_ * 4 + 4], start=True, stop=True), [kcT_b, "qZ"], [ks])
                    bn, kn = bank()
                    for g in range(2):
                        pe(lambda h, g=g: h.matmul(bn[0:64, g * 128:(g + 1) * 128].rearrange("p (kv q) -> p kv q", kv=2), lhsT=kTr[:, 0, 0:64], rhs=qZ[:, g, :, 0:64], start=True, stop=True), ["kT0", "qZ"], [kn])
                    dve(lambda h: h.scalar_tensor_tensor(out=tmpS[:, 0:256], in0=bs[:, 0:256], scalar=0.125, in1=tabs[:, 0:256], op0=ALU.mult, op1=ALU.add), [ks, tabs_b], ["tmpS"])
                    dve(lambda h: h.scalar_tensor_tensor(out=tmpS[0:64, 256:512], in0=bn[0:64, 0:256], scalar=0.125, in1=tabs[0:64, 256:512], op0=ALU.mult, op1=ALU.add), [kn, tabs_b, "tmpS"], ["tmpS"])
                    for gk in range(4):
                        act(lambda h, gk=gk: h.activation(out=bass.AP(WA, 6400 + gk * 1024, [[PS_, 128], [68, 16], [1, 4]]), in_=v3(tmpS[:, gk * 64:(gk + 1) * 64]), func=AF.Exp), ["tmpS"], [Pb_b])
                    act(lambda h: h.activation(out=PT[0:64, 0, 0:256], in_=tmpS[0:64, 256:512], func=AF.Exp), ["tmpS"], ["PT0"])
                    bo_, ko_ = bank()
                    for kv in range(2):
                        for g in range(2):
                            hh = kv * 2 + g; pc = (g * 2 + kv) * 64
                            for b_ in range(16):
                                pe(lambda h, hh=hh, g=g, kv=kv, b_=b_: h.matmul(bo_[0:64, hh * 65:(hh + 1) * 65], lhsT=Pb[:, g, kv, b_, :], rhs=Vc[:, b_, kv, :], start=(b_ == 0), stop=False), [Pb_b, Vc_b], [ko_])
                            pe(lambda h, hh=hh, pc=pc, kv=kv: h.matmul(bo_[0:64, hh * 65:(hh + 1) * 65], lhsT=PT[0:64, 0, pc:pc + 64], rhs=Vaug[0:64, 0, kv, :], start=False, stop=True), ["PT0", "Va0"], [ko_])
                    attn_out(bo_, ko_)
                    S0f_b, S0f_ = warena("S0f", 10496, 4096); S0f = S0f_.bitcast(F32).rearrange("p (hc b v) -> p hc b v", hc=2, b=16)
                    S0b_b, S0b_ = warena("S0b", 14592, 2048); S0b = S0b_.rearrange("p (hc b v) -> p hc b v", hc=2, b=16)
                    qSb_b, qSb_ = warena("qSb", 16640, 2048); qSbig = qSb_.rearrange("p (hp b q) -> p hp b q", hp=2, b=16)
                    Vbg_b, Vbg_ = warena("Vbg", 0, 4096); Vbig = Vbg_.rearrange("p (b c) -> p b c", c=256)
                    for hc in range(2):
                        for hp in range(2):
                            srcS = sh[l, :, hc * 2 + hp].rearrange("b k v -> k b v")
                            dsp(S0f[64 * hp:64 * hp + 64, hc], srcS, w=[S0f_b]); dpl(S0b[64 * hp:64 * hp + 64, hc], srcS, w=[S0b_b])
                    pool(lambda h: h.memset(qSb_, 0.0), w=[qSb_b])
                    dve(lambda h: h.tensor_tensor(out=Vbig[0:64], in0=Vh[0:64, :].unsqueeze(1).to_broadcast([64, 16, 256]), in1=C("mind")[0:64, :].unsqueeze(2).to_broadcast([64, 16, 256]), op=ALU.mult), ["Vh", "ct"], [Vbg_b])
                    bO, kO = bank()
                    for hc in range(2):
                        for hp in range(2):
                            rw = slice(64 * hp, 64 * hp + 64)
                            dve(lambda h, hc=hc, hp=hp, rw=rw: h.tensor_copy(out=bass.AP(WA, 64 * hp * PS_ + 16640 + hp * 1024, [[PS_, 64], [68, 16], [1, 4]]), in_=v3(qSZ[rw, hc, 0, hp, :])), ["qSZ"], [qSb_b])
                        for hp in range(2):
                            hh = hc * 2 + hp
                            for b_ in range(16):
                                pe(lambda h, hh=hh, hc=hc, hp=hp, b_=b_: h.matmul(bO[0:64, hh * 64:(hh + 1) * 64], lhsT=qSbig[:, hp, b_, :], rhs=S0b[:, hc, b_, :], start=(b_ == 0), stop=False), [qSb_b, S0b_b], [kO])
                            pe(lambda h, hh=hh: h.matmul(bO[0:64, hh * 64:(hh + 1) * 64], lhsT=AtZ[:, 0, hh, :], rhs=Vh[:, hh * 64:(hh + 1) * 64], start=False, stop=True), ["AtZ", "Vh"], [kO])
                    hgrn_out(bO, kO)
                    for hc in range(2):
                        for bg_ in range(4):
                            bU, kU = bank()
                            pe(lambda h, hc=hc, bg_=bg_, bU=bU: h.matmul(bU[:, 0:512].rearrange("p (b c) -> p b c", c=128), lhsT=kAtokZ[0:64, 0, hc * 128:(hc + 1) * 128], rhs=Vbig[0:64, bg_ * 4:(bg_ + 1) * 4, hc * 128:(hc + 1) * 128], start=True, stop=True), ["kAtokZ", Vbg_b], [kU])
                            for hp in range(2):
                                rw = slice(64 * hp, 64 * hp + 64); b4 = slice(bg_ * 4, bg_ * 4 + 4)
                                dve(lambda h, hc=hc, rw=rw, b4=b4: h.tensor_tensor(out=S0f[rw, hc, b4, :], in0=S0f[rw, hc, b4, :], in1=ECLs[rw, hc, b4].unsqueeze(2).to_broadcast([64, 4, 64]), op=ALU.mult), [S0f_b, "ECLs"], [S0f_b])
                                dve(lambda h, hc=hc, hp=hp, rw=rw, b4=b4, bU=bU: h.tensor_tensor(out=S0f[rw, hc, b4, :], in0=S0f[rw, hc, b4, :], in1=bU[rw, 0:512].rearrange("p (b c) -> p b c", c=128)[:, :, hp * 64:(hp + 1) * 64], op=ALU.add), [S0f_b, kU], [S0f_b])
                    for hc in range(2):
                        for hp in range(2):
                            dsp(ohs[l, :, hc * 2 + hp].rearrange("b k v -> k b v"), S0f[64 * hp:64 * hp + 64, hc], [S0f_b])
                def post():
                    if dbg:
                        for c in range(8):
                            dpl(dbgo[c * 128:(c + 1) * 128, c0:c0 + n], mixT[:, c, 0:n], [mk])
                    for nh in range(2):
                        bo_h, bok_h = bank()
                        for k in range(8):
                            pe(lambda h, k=k, nh=nh, bo_h=bo_h: h.matmul(bo_h[0:n, :], lhsT=mixT[:, k, 0:n], rhs=W_o[1][:, k * 1024 + nh * 512:k * 1024 + (nh + 1) * 512], start=(k == 0), stop=(k == 7)), [mk, W_o[0]], [bok_h])
                        dve(lambda h, nh=nh, bo_h=bo_h: h.scalar_tensor_tensor(out=x_tok[0:n, t, nh * 512:(nh + 1) * 512], in0=x_tok[0:n, t, nh * 512:(nh + 1) * 512], scalar=ALPHA, in1=bo_h[0:n, :], op0=ALU.mult, op1=ALU.add), [f"x{t}", bok_h], [f"x{t}"])
                    ln_tile(t, 1 + 3 * l, None)
                return post
            prev = None
            for t in (range(NT) if '1' in DBGP else []):
                prev = p1(t, prev)
            if prev:
                prev()
            dpl(W_xq[1].rearrange("p (a b) -> p a b", b=256), wxq[l], w=[W_xq[0]])
            dpl(W_xo[1].rearrange("p (a b) -> p a b", b=1024), wxo[l], w=[W_xo[0]])
            dpl(W_xk[1].rearrange("p (a b) -> p a b", b=256), wxk[l], w=[W_xk[0]])
            dpl(W_xv[1].rearrange("p (a b) -> p a b", b=256), wxv[l], w=[W_xv[0]])
            load_ln(2 + 3 * l)
            for mt in range(2):
                for hf in range(2):
                    dsp(tmpS[:], memp[mt * 128:(mt + 1) * 128, hf * 512:(hf + 1) * 512], w=["tmpS"])
                    act(lambda h, hf=hf: h.copy(out=zbf[:, hf * 512:(hf + 1) * 512], in_=tmpS[:]), ["tmpS"], ["zbf"])
                tbm, tkm = tbank()
                for c in range(8):
                    pe(lambda h, c=c, tbm=tbm: h.transpose(tbm[:, c * 128:(c + 1) * 128], zbf[:, c * 128:(c + 1) * 128], identb[:]), ["zbf", "identb"], [tkm])
                dve(lambda h, mt=mt, tbm=tbm: h.tensor_copy(out=memT[:, :, mt * 128:(mt + 1) * 128], in_=tbm[:].rearrange("p (c m) -> p c m", m=128)), [tkm], [memTb])
            def p2m(mt):
                bk_, kk_ = bank()
                for k in range(8):
                    pe(lambda h, k=k, mt=mt, b=bk_: h.matmul(b[:, 0:256], lhsT=memT[:, k, mt * 128:(mt + 1) * 128], rhs=W_xk[1][:, k * 256:(k + 1) * 256], start=(k == 0), stop=(k == 7)), MEMT + [W_xk[0]], [kk_])
                for k in range(8):
                    pe(lambda h, k=k, mt=mt, b=bk_: h.matmul(b[:, 256:512], lhsT=memT[:, k, mt * 128:(mt + 1) * 128], rhs=W_xv[1][:, k * 256:(k + 1) * 256], start=(k == 0), stop=(k == 7)), MEMT + [W_xv[0]], [kk_])
                dve(lambda h, b=bk_: h.tensor_copy(out=tmpS[:, 0:512], in_=b[:, :]), [kk_], ["tmpS"])
                act(lambda h, b=bk_, mt=mt: h.copy(out=mva[:, mt, :, 0:64], in_=b[:, 256:512].rearrange("p (h d) -> p h d", d=64)), [kk_], ["mva"])
                dsp(omk[l, mt * 128:(mt + 1) * 128, :], tmpS[:, 0:256], ["tmpS"]); dsp(omv[l, mt * 128:(mt + 1) * 128, :], tmpS[:, 256:512], ["tmpS"])
            for mt in (range(2) if '2' in DBGP else []):
                p2m(mt)
            for hc in range(2):
                bk_, kk_ = bank()
                for k in range(8):
                    pe(lambda h, k=k, hc=hc, b=bk_: h.matmul(b[:, 0:256], lhsT=W_xk[1][:, k * 256 + hc * 128:k * 256 + (hc + 1) * 128], rhs=memT[:, k, :], start=(k == 0), stop=(k == 7)), MEMT + [W_xk[0]], [kk_])
                act(lambda h, hc=hc, b=bk_: h.copy(out=mkT[:, hc, :], in_=b[:, 0:256]), [kk_], ["kAtokZ"])
            qxT2 = hEQ[:].bitcast(BF16).rearrange("p (a b) -> p a b", b=128)
            PX2 = tmpS[:].bitcast(BF16).rearrange("p (a b c) -> p a b c", a=2, c=128)

            def p2(t):
                n = 128; c0 = t * 128
                par = t % 2
                qx = (qxT, qxT2)[par]; qk = (["kAb0", "kAb1"], ["hEQ", "hEQ"])[par]
                px = (PX, PX2)[par]; pk = (["PT0", "PT1"], ["tmpS", "tmpS"])[par]
                for hc in range(2):
                    bq, kq = bank()
                    for k in range(8):
                        pe(lambda h, k=k, hc=hc, bq=bq: h.matmul(bq[:, 0:n], lhsT=W_xq[1][:, k * 256 + hc * 128:k * 256 + (hc + 1) * 128], rhs=xT[:, k, c0:c0 + n], start=(k == 0), stop=(k == 7)), [f"xT{t}", W_xq[0]], [kq])
                    act(lambda h, hc=hc, bq=bq: h.copy(out=qx[:, hc, :], in_=bq[:, 0:n]), [kq], [qk[hc]])
                bsc = [bank(), bank()]
                for mt in range(2):
                    for hc in range(2):
                        for hp in range(2):
                            pe(lambda h, hc=hc, hp=hp, mt=mt: h.matmul(bsc[hp][0][:, (mt * 2 + hc) * 128:(mt * 2 + hc + 1) * 128], lhsT=mkT[64 * hp:64 * hp + 64, hc, mt * 128:(mt + 1) * 128], rhs=qx[64 * hp:64 * hp + 64, hc, :], start=True, stop=True), ["kAtokZ"] + qk, [bsc[hp][1]])
                for hp in range(2):
                    act(lambda h, hp=hp: h.activation(out=px[:, hp, :, :].rearrange("p a q -> p (a q)"), in_=bsc[hp][0][:, :], func=AF.Exp, scale=0.125), [bsc[hp][1]], [pk[hp]])

                def stage_b():
                    bo_, ko_ = bank()
                    for hh in range(4):
                        hc, hp = hh // 2, hh % 2
                        for mt in range(2):
                            pe(lambda h, hh=hh, hc=hc, hp=hp, mt=mt: h.matmul(bo_[:, hh * 65:(hh + 1) * 65], lhsT=px[:, hp, mt * 2 + hc, :], rhs=mva[:, mt, hh, :], start=(mt == 0), stop=(mt == 1)), [pk[hp], "mva"], [ko_])
                    ov = bo_[:, 0:260].rearrange("p (h e) -> p h e", e=65)
                    dve(lambda h: h.reciprocal(out=rec[:], in_=ov[:, :, 64]), [ko_], ["reca"])
                    dve(lambda h: h.tensor_tensor(out=oxt[:].rearrange("p (h d) -> p h d", d=64), in0=ov[:, :, 0:64], in1=rec[:].unsqueeze(2).to_broadcast([128, 4, 64]), op=ALU.mult), [ko_, "reca"], ["otok"])
                    tb, tk = tbank()
                    for hc in range(2):
                        pe(lambda h, hc=hc: h.transpose(tb[:, hc * 128:(hc + 1) * 128], oxt[:, hc * 128:(hc + 1) * 128], identb[:]), ["otok", "identb"], [tk])
                    dve(lambda h: h.tensor_copy(out=oxT[:].rearrange("p a b -> p (a b)"), in_=tb[:, 0:256]), [tk], ["otok2"])
                    bo = [bank(), bank()]
                    for nh in range(2):
                        for hc in range(2):
                            pe(lambda h, hc=hc, nh=nh: h.matmul(bo[nh][0][:, :], lhsT=oxT[:, hc, :], rhs=W_xo[1][:, hc * 1024 + nh * 512:hc * 1024 + (nh + 1) * 512], start=(hc == 0), stop=(hc == 1)), ["otok2", W_xo[0]], [bo[nh][1]])
                    ln_tile(t, 2 + 3 * l, bo)
                return stage_b
            prevB = None
            for t in (range(16) if '2' in DBGP else []):
                sb_ = p2(t)
                if prevB:
                    prevB()
                prevB = sb_
            if prevB:
                prevB()
            def p2s():
                t = 16; n = 64; c0 = 2048; PS_ = 28672
                v3 = lambda ap: ap.rearrange("p (b t) -> p b t", t=4)
                for hc in range(2):
                    bq, kq = bank()
                    for k_ in range(8):
                        pe(lambda h, k_=k_, hc=hc, bq=bq: h.matmul(bq[:, 0:n], lhsT=W_xq[1][:, k_ * 256 + hc * 128:k_ * 256 + (hc + 1) * 128], rhs=xT[:, k_, c0:c0 + n], start=(k_ == 0), stop=(k_ == 7)), ["xT16", W_xq[0]], [kq])
                    for hp in range(2):
                        act(lambda h, hc=hc, hp=hp, bq=bq: h.copy(out=qZ[64 * hp:64 * hp + 64, hc, hp, 0:n], in_=bq[64 * hp:64 * hp + 64, 0:n]), [kq], ["qZ"])
                def xhalf(hi_, half):
                    b0 = 8 * half
                    Kc_b, Kc_ = warena("xKc", 10240, 4096); Kc = Kc_.rearrange("p (b mt c) -> p b mt c", b=8, mt=2)
                    KcT_b, KcT_ = warena("xKcT", 14336, 4096); KcT = KcT_.rearrange("p (b hc m) -> p b hc m", b=8, hc=2)
                    Vx_b, Vx_ = warena("xVc", 18432, 4160); Vx = Vx_.rearrange("p (b mt hh e) -> p b mt hh e", b=8, mt=2, hh=4)
                    Px_b, Px_ = warena("xPb", 22592, 4096); Px = Px_.rearrange("p (mt hh b q) -> p mt hh b q", mt=2, hh=4, b=8)
                    pool(lambda h, Vx_=Vx_: h.memset(Vx_, 1.0), w=[Vx_b])
                    pool(lambda h, Px_=Px_: h.memset(Px_, 0.0), w=[Px_b])
                    for mt in range(2):
                        dpl(Kc[:, :, mt, :], cmk[l, b0:b0 + 8, mt * 128:(mt + 1) * 128, :].rearrange("b m c -> m b c"), w=[Kc_b])
                        for hh in range(4):
                            dpl(Vx[:, :, mt, hh, 0:64], cmv[l, b0:b0 + 8, mt * 128:(mt + 1) * 128, hh * 64:(hh + 1) * 64].rearrange("b m d -> m b d"), w=[Vx_b])
                    for bp in range(4):
                        tbk, tkk = tbank()
                        for j in range(8):
                            bl, hc, mt = bp * 2 + j // 4, (j // 2) % 2, j % 2
                            pe(lambda h, j=j, bl=bl, hc=hc, mt=mt, tbk=tbk, Kc=Kc: h.transpose(tbk[:, j * 128:(j + 1) * 128], Kc[:, bl, mt, hc * 128:(hc + 1) * 128], identb[:]), [Kc_b, "identb"], [tkk])
                        dve(lambda h, bp=bp, tbk=tbk, KcT_=KcT_: h.tensor_copy(out=KcT_[:, bp * 1024:(bp + 1) * 1024], in_=tbk[:, :]), [tkk], [KcT_b])
                    bs, ks = bank()
                    for bl in range(8):
                        for mt in range(2):
                            for hc in range(2):
                                pe(lambda h, bl=bl, mt=mt, hc=hc, KcT=KcT: h.matmul(bs[:, (mt * 2 + hc) * 64:(mt * 2 + hc + 1) * 64].rearrange("p (hp q) -> p hp q", hp=2)[:, :, bl * 4:bl * 4 + 4], lhsT=KcT[:, bl, hc, mt * 128:(mt + 1) * 128], rhs=qZ[:, hc, :, (b0 + bl) * 4:(b0 + bl) * 4 + 4], start=True, stop=True), [KcT_b, "qZ"], [ks])
                    for mt in range(2):
                        for hh in range(4):
                            act(lambda h, mt=mt, hh=hh, half=half: h.activation(out=bass.AP(WA, 22592 + (mt * 4 + hh) * 512 + half * 32, [[PS_, 128], [68, 8], [1, 4]]), in_=v3(bs[:, mt * 128 + hh * 32:mt * 128 + hh * 32 + 32]), func=AF.Exp, scale=0.125), [ks], [Px_b])
                    bo_, ko_ = bank()
                    for hh in range(4):
                        for bl in range(8):
                            for mt in range(2):
                                pe(lambda h, hh=hh, bl=bl, mt=mt, Px=Px, Vx=Vx: h.matmul(bo_[0:64, hh * 65:(hh + 1) * 65], lhsT=Px[:, mt, hh, bl, :], rhs=Vx[:, bl, mt, hh, :], start=(bl == 0 and mt == 0), stop=(bl == 7 and mt == 1)), [Px_b, Vx_b], [ko_])
                    if hi_ == 0:
                        dve(lambda h, bo_=bo_: h.tensor_copy(out=tmpS[0:64, 0:260], in_=bo_[0:64, 0:260]), [ko_], ["tmpS"])
                    else:
                        dve(lambda h, bo_=bo_: h.tensor_tensor(out=tmpS[0:64, 0:260], in0=tmpS[0:64, 0:260], in1=bo_[0:64, 0:260], op=ALU.add), [ko_, "tmpS"], ["tmpS"])
                for hi_, half in enumerate((0, 1)):
                    xhalf(hi_, half)
                ov = tmpS[0:64, 0:260].rearrange("p (h e) -> p h e", e=65)
                dve(lambda h: h.reciprocal(out=reca[0:64], in_=ov[:, :, 64]), ["tmpS"], ["reca"])
                dve(lambda h: h.tensor_tensor(out=otok[0:64].rearrange("p (h d) -> p h d", d=64), in0=ov[:, :, 0:64], in1=reca[0:64].unsqueeze(2).to_broadcast([64, 4, 64]), op=ALU.mult), ["tmpS", "reca"], ["otok"])
                if dbg:
                    dpl(dbgo[0:64, 0:256], otok[0:64, 0:256], ["otok"])
                tb, tk = tbank()
                for hc in range(2):
                    pe(lambda h, hc=hc: h.transpose(tb[:, hc * 128:hc * 128 + 64], otok[0:64, hc * 128:(hc + 1) * 128], identb[0:64, 0:64]), ["otok", "identb"], [tk])
                dve(lambda h: h.tensor_copy(out=oxT[:, :, 0:64], in_=tb[:, 0:256].rearrange("p (a b) -> p a b", b=128)[:, :, 0:64]), [tk], ["otok2"])
                bo = [bank(), bank()]
                for nh in range(2):
                    for hc in range(2):
                        pe(lambda h, hc=hc, nh=nh: h.matmul(bo[nh][0][0:64, :], lhsT=oxT[:, hc, 0:64], rhs=W_xo[1][:, hc * 1024 + nh * 512:hc * 1024 + (nh + 1) * 512], start=(hc == 0), stop=(hc == 1)), ["otok2", W_xo[0]], [bo[nh][1]])
                ln_tile(16, 2 + 3 * l, bo)
            if '2' in DBGP:
                p2s()
            load_ln(3 + 3 * l)
            def p3(j):
                r0 = ring[j % 4]
                Wg_ = warena(f"fg{j % 4}", r0, 2048); Wu_ = warena(f"fu{j % 4}", r0 + 2048, 2048); Wd_ = warena(f"fd{j % 4}", r0 + 4096, 2048)
                dpl(Wg_[1].rearrange("p (a b) -> p a b", b=256), wg[l, :, :, j * 256:(j + 1) * 256], w=[Wg_[0]])
                dpl(Wu_[1].rearrange("p (a b) -> p a b", b=256), wu[l, :, :, j * 256:(j + 1) * 256], w=[Wu_[0]])
                dpl(Wd_[1].rearrange("p (a b) -> p a b", b=1024), wd[l, :, 2 * j:2 * j + 2, :], w=[Wd_[0]])
                def p3t(t):
                    n = ntok(t); c0 = t * 128
                    for cc in range(2):
                        bg, kg = bank(); bu, ku = bank()
                        for k in range(8):
                            pe(lambda h, k=k, cc=cc, bg=bg: h.matmul(bg[:, 0:n], lhsT=Wg_[1][:, k * 256 + cc * 128:k * 256 + (cc + 1) * 128], rhs=xT[:, k, c0:c0 + n], start=(k == 0), stop=(k == 7)), [f"xT{t}", Wg_[0]], [kg])
                        for k in range(8):
                            pe(lambda h, k=k, cc=cc, bu=bu: h.matmul(bu[:, 0:n], lhsT=Wu_[1][:, k * 256 + cc * 128:k * 256 + (cc + 1) * 128], rhs=xT[:, k, c0:c0 + n], start=(k == 0), stop=(k == 7)), [f"xT{t}", Wu_[0]], [ku])
                        sg_, sgk = ((hE, "hE"), (hCU, "hCU"), (hEC, "hEC"), (hEN, "hEN"))[(t % 2) * 2 + cc]
                        hT_, hTk = ((hT, "hL1"), (hT2, "hL2"))[t % 2]
                        act(lambda h, bg=bg, sg_=sg_: h.activation(out=sg_[:, 0:n], in_=bg[:, 0:n], func=AF.Silu), [kg], [sgk])
                        dve(lambda h, bu=bu, cc=cc, sg_=sg_, hT_=hT_: h.tensor_tensor(out=hT_[:, cc, 0:n], in0=sg_[:, 0:n], in1=bu[:, 0:n], op=ALU.mult), [sgk, ku], [hTk])
                    def down():
                        bo = [bank(), bank()]
                        for nh in range(2):
                            for cc in range(2):
                                pe(lambda h, cc=cc, nh=nh: h.matmul(bo[nh][0][0:n, :], lhsT=hT_[:, cc, 0:n], rhs=Wd_[1][:, cc * 1024 + nh * 512:cc * 1024 + (nh + 1) * 512], start=(cc == 0), stop=(cc == 1)), [hTk, Wd_[0]], [bo[nh][1]])
                        ln_tile(t, (3 + 3 * l) if j == 10 else None, bo, first=(j == 0))
                    return down
                prevD = None
                for t in range(NT):
                    d_ = p3t(t)
                    if prevD:
                        prevD()
                    prevD = d_
                prevD()
            for j in (range(11) if '3' in DBGP else []):
                p3(j)
        for l_ in range(nl):
            layer(l_)
        for t in range(16):
            dsp(yp[t * 128:(t + 1) * 128, :], x_tok[:, t, :], [f"x{t}"])
        dsp(ys[:, :], x_tok[0:64, 16, :], ["x16"])
        S.final()
        S.emit(nc, sems, dsems, block)
    return nc


def _host_layout(inp, nl=4, cores=range(8)):
    f = lambda a: np.ascontiguousarray(np.asarray(a, dtype=np.float32))
    pk = lambda w: f(np.asarray(w)[:nl].reshape(nl, -1, 128, np.asarray(w).shape[-1]).transpose(0, 2, 1, 3))
    sh = {}
    sh["win"] = pk(np.asarray(inp["w_in"])[:, :, _perm()])
    sh["wo"] = pk(inp["w_o"]); sh["wxq"] = pk(inp["w_xq"]); sh["wxk"] = pk(inp["w_xk"]); sh["wxv"] = pk(inp["w_xv"])
    sh["wxo"] = pk(inp["w_xo"]); sh["wg"] = pk(inp["w_gate"]); sh["wu"] = pk(inp["w_up"]); sh["wd"] = pk(inp["w_down"])
    gs = [inp["emb_ln_g"]]; bs = [inp["emb_ln_b"]]
    for l in range(4):
        for k in (1, 2, 3):
            gs.append(np.asarray(inp[f"ln{k}_g"])[l]); bs.append(np.asarray(inp[f"ln{k}_b"])[l])
    rows = np.stack([np.stack([np.asarray(g), np.asarray(b)]) for g, b in zip(gs, bs)])
    sh["lnrows"] = f(rows)
    sh["lncols"] = f(rows.reshape(13, 2, 8, 128).transpose(3, 0, 1, 2).reshape(128, 13 * 16))
    sh["sink"] = f(np.asarray(inp["attn_sink"]).reshape(1, 16))
    sh["hlb"] = f(np.asarray(inp["hgrn_lb"]).reshape(4, 2, 128).transpose(2, 1, 0).reshape(128, 8))
    sh["hng"] = f(inp["hgrn_norm_g"])
    sh["cw"] = f(np.asarray(inp["conv_w"]).reshape(4, 3, 2, 128).transpose(3, 0, 2, 1).reshape(128, 24))
    sh["pw"] = f(inp["pool_w"])
    sh["psc"] = f(np.asarray(inp["pool_scale"]).reshape(4, 2, 128).transpose(2, 0, 1).reshape(128, 8))
    sh["ctab"] = _CT; sh["stab"] = _STAB
    maps = []
    for c in cores:
        m = dict(sh)
        b = slice(16 * c, 16 * c + 16)
        m["xp"] = f(inp["x_prompt"][c]); m["xs"] = f(np.asarray(inp["x_sample"])[b].reshape(64, D))
        m["ck"] = f(np.asarray(inp["cache_swa_k"])[:nl, b].reshape(nl, 16, 128, 128))
        m["cv"] = f(np.asarray(inp["cache_swa_v"])[:nl, b].reshape(nl, 16, 128, 128))
        m["sh"] = f(np.asarray(inp["state_hgrn"])[:nl, b])
        m["scv"] = f(np.asarray(inp["state_conv"])[:nl, b].reshape(nl, 32, 256))
        m["spl"] = f(np.asarray(inp["state_pool"])[:nl, b])
        m["cmk"] = f(np.asarray(inp["cache_mem_k"])[:nl, b].reshape(nl, 16, 256, 256))
        m["cmv"] = f(np.asarray(inp["cache_mem_v"])[:nl, b].reshape(nl, 16, 256, 256))
        m["memp"] = f(inp["mem_prompt"][c])
        maps.append(m)
    return maps


_NC = None


def kernel(**inputs):
    global _NC
    if _NC is None:
        _NC = build()
    maps = _host_layout(inputs, 4, range(8))
    res = run_bass_kernel_spmd(_NC, maps, core_ids=list(range(8))).results
    st = lambda k: np.stack([r[k] for r in res])
    cat = lambda k: np.concatenate([r[k] for r in res], 1)
    return (st("yp"), np.concatenate([r["ys"].reshape(16, 4, D) for r in res], 0),
            st("okp").transpose(1, 0, 2, 3).reshape(4, 8, 128, 2, 64), st("ovp").transpose(1, 0, 2, 3).reshape(4, 8, 128, 2, 64),
            st("ohp").transpose(1, 0, 2, 3, 4), st("ocp").transpose(1, 0, 2, 3), st("opp").transpose(1, 0, 2, 3),
            st("omk").transpose(1, 0, 2, 3).reshape(4, 8, 256, 4, 64), st("omv").transpose(1, 0, 2, 3).reshape(4, 8, 256, 4, 64),
            cat("oks").reshape(4, 128, 128, 2, 64), cat("ovs").reshape(4, 128, 128, 2, 64), cat("ohs"), cat("ocs"), cat("ops"))
```

```python
import numpy as np
import concourse.bass as bass
import concourse.mybir as mybir

F32 = mybir.dt.float32
BF16 = mybir.dt.bfloat16
ALU = mybir.AluOpType
AF = mybir.ActivationFunctionType
AX = mybir.AxisListType

import os as _os
SERIAL = bool(_os.environ.get('SERIAL'))
NOSAME = bool(_os.environ.get('NOSAME'))
NS = 8
ND = 12


class Buf:
    __slots__ = ("lo", "hi", "name")

    def __init__(self, name, lo, hi):
        self.name, self.lo, self.hi = name, lo, hi


class _Op:
    __slots__ = ("eng", "fn", "idx", "is_dma", "waits", "needs_inc", "seq", "dsem", "dval")

    def __init__(self, eng, fn, is_dma):
        self.eng = eng
        self.fn = fn
        self.is_dma = is_dma
        self.waits = []
        self.needs_inc = False
        self.seq = -1
        self.idx = -1
        self.dsem = -1
        self.dval = 0


class Sched:
    ENGS = ("pe", "act", "dve", "pool", "sp")

    def __init__(self):
        self.ops = {e: [] for e in self.ENGS}
        self.last_w = {}
        self.readers = {}
        self.live = []
        self.known = {e: {f: -1 for f in self.ENGS} for e in self.ENGS}
        self.known_dma = {e: set() for e in self.ENGS}
        self.dma_count = {"sp": 0, "pool": 0}
        self.dma_hist = {"sp": {}, "pool": {}}

    def _overl(self, b):
        return [r for r in self.live if r is not b and r.lo < b.hi and b.lo < r.hi]

    def op(self, eng, fn, reads=(), writes=(), dma=False):
        o = _Op(eng, fn, dma)
        o.idx = len(self.ops[eng])
        deps = {}
        for k in reads:
            w = self.last_w.get(k)
            if w is not None:
                deps[w] = True
            if isinstance(k, str) and k[:2] in ("ps", "pT"):
                for r in self.readers.get(k, ()):
                    if r.eng != eng and r not in deps:
                        deps[r] = False
            if isinstance(k, Buf):
                for r in self._overl(k):
                    w = self.last_w.get(r)
                    if w is not None:
                        deps[w] = True
        for k in writes:
            ks = [k] + (self._overl(k) if isinstance(k, Buf) else [])
            for kk in ks:
                w = self.last_w.get(kk)
                if w is not None:
                    deps[w] = True
                for r in self.readers.get(kk, ()):
                    if r not in deps:
                        deps[r] = False
        if SERIAL:
            for e2 in self.ENGS:
                if self.ops[e2]:
                    deps.setdefault(self.ops[e2][-1], True)
        if dma:
            q = eng
            j = self.dma_count[q]
            self.dma_count[q] += 1
            o.dsem = j % ND
            o.dval = 16 * (j // ND + 1)
            prev = self.dma_hist[q].get(o.dsem)
            if prev is not None:
                deps.setdefault(prev, True)
            self.dma_hist[q][o.dsem] = o
            o.needs_inc = True
        kn = self.known[eng]
        for d, hard in deps.items():
            if d is o:
                continue
            if d.is_dma:
                if d in self.known_dma[eng]:
                    continue
                self.known_dma[eng].add(d)
                o.waits.append(d)
                continue
            if d.eng == eng and (not hard or eng == "pe" or NOSAME):
                continue
            if d.idx <= kn[d.eng]:
                continue
            kn[d.eng] = d.idx
            d.needs_inc = True
            o.waits.append(d)
        for k in reads:
            self.readers.setdefault(k, []).append(o)
            if isinstance(k, Buf) and k not in self.live:
                self.live.append(k)
        for k in writes:
            if isinstance(k, Buf):
                for r in self._overl(k):
                    self.live.remove(r)
                    self.last_w.pop(r, None)
                    self.readers.pop(r, None)
                if k not in self.live:
                    self.live.append(k)
            self.last_w[k] = o
            self.readers[k] = []
        self.ops[eng].append(o)
        return o

    def final(self):
        o = _Op("sp", lambda h: h.nop(), False)
        o.idx = len(self.ops["sp"])
        for q in self.dma_hist:
            for d in self.dma_hist[q].values():
                o.waits.append(d)
        self.ops["sp"].append(o)

    def emit(self, nc, sems, dsems, block):
        for e in self.ENGS:
            m = 0
            for o in self.ops[e]:
                if o.needs_inc and not o.is_dma:
                    o.seq = m
                    m += 1

        def run(e, h):
            for o in self.ops[e]:
                for d in o.waits:
                    if d.is_dma:
                        h.wait_ge(dsems[d.eng][d.dsem], d.dval)
                    else:
                        h.wait_ge(sems[d.eng][d.seq % NS], d.seq // NS + 1)
                ins = o.fn(h)
                if o.needs_inc:
                    if o.is_dma:
                        ins.then_inc(dsems[e][o.dsem], 16)
                    else:
                        ins.then_inc(sems[e][o.seq % NS], 1)

        @block.tensor
        def _(h):
            run("pe", h)

        @block.scalar
        def _(h):
            run("act", h)

        @block.vector
        def _(h):
            run("dve", h)

        @block.gpsimd
        def _(h):
            run("pool", h)

        @block.sync
        def _(h):
            run("sp", h)
from contextlib import ExitStack
import threading
from concourse.bass_utils import run_bass_kernel_spmd

import os
DBG_SKIP = bool(os.environ.get('DBG_SKIP'))
DBGP = os.environ.get('DBGP', '123')
D = 1024; DEPTH = 4; NT = 17; TS = 64; DFF = 2816
ALPHA = (2 * DEPTH) ** 0.25
LN_EPS = 1e-5; RMS_EPS = 1e-6
cA, cB, cBQ, cBF, cCB, cK, cV, cI, cG, cCC, cCH, cDV = 0, 128, 256, 512, 768, 1024, 1152, 1280, 1536, 1792, 2048, 2304


def _perm():
    q = np.arange(256).reshape(2, 2, 64)
    qa = q[:, 0, :].reshape(-1); qb = q[:, 1, :].reshape(-1)
    r = lambda a, b: np.arange(a, b)
    return np.concatenate([qa, qb, r(512, 768), r(768, 1024), r(1536, 1792), r(256, 384), r(384, 512),
                           r(1024, 1280), r(1280, 1536), r(1792, 2048), r(2048, 2304), r(2304, 2560)])


def _consts():
    c = {}
    s = np.arange(128)[:, None]; q = np.arange(128)[None, :]
    slopes = [2.0 ** (-2.0 * (h + 1)) for h in range(4)]
    NEG = -30000.0
    bt = np.zeros((128, 2, 2, 2, 128), np.float32)
    bsc = np.zeros((128, 2, 2, 64), np.float32)
    bsn = np.full((128, 2, 2, 64), NEG, np.float32)
    for g in range(2):
        for kv in range(2):
            sl = slopes[kv * 2 + g]
            bt[:, 0, g, kv, :] = np.where(s <= q, -sl * (q - s), NEG)
            bt[:, 1, g, kv, :] = np.where(s >= q, -sl * (128 + q - s), NEG)
            for b in range(16):
                for t in range(4):
                    col = b * 4 + t
                    bsc[:, g, kv, col] = np.where(np.arange(128) >= t, -sl * (t + 128 - np.arange(128)), NEG)
                    for t2 in range(t + 1):
                        bsn[b * 4 + t2, g, kv, col] = -sl * (t - t2)
    c["bt"] = bt.reshape(128, 1024)
    global _STAB
    _STAB = np.ascontiguousarray(np.concatenate([bsc.reshape(128, 256), bsn.reshape(128, 256)], 1))
    tri = (np.arange(64)[:, None] <= np.arange(64)[None, :]).astype(np.float32)
    c["tri2"] = np.concatenate([tri, tri], 0)
    bdc = np.zeros((128, 64), np.float32); mind = np.zeros((128, 16), np.float32)
    for b in range(16):
        mind[b * 4:(b + 1) * 4, b] = 1.0
        for t in range(4):
            bdc[b * 4:b * 4 + t + 1, b * 4 + t] = 1.0
    c["bdc"] = bdc; c["mind"] = mind
    ic = np.zeros((128, 2, 16), np.float32)
    for hc in range(2):
        for half in range(2):
            w = (2, 4, 8, 16)[hc * 2 + half]
            ic[half * 64:(half + 1) * 64, hc, :] = 1.0 / np.minimum(np.arange(16) + 1, w)
    c["icnt"] = ic.reshape(128, 32)
    c["ident"] = np.eye(128, dtype=np.float32)
    names = list(c.keys()); offs = {}; o = 0
    for n in names:
        offs[n] = (o, c[n].shape[1]); o += c[n].shape[1]
    return np.concatenate([c[n] for n in names], 1), offs


_STAB = None
_CT, _CO = _consts()


def build(nl=DEPTH, dbg=False):
    nc = bass.Bass("TRN2", target_bir_lowering=False)
    I = lambda n, s: nc.dram_tensor(n, list(s), F32, kind="ExternalInput").ap()
    O = lambda n, s: nc.dram_tensor(n, list(s), F32, kind="ExternalOutput").ap()
    xp = I("xp", (2048, D)); xs = I("xs", (64, D))
    ck = I("ck", (nl, 16, 128, 128)); cv = I("cv", (nl, 16, 128, 128))
    sh = I("sh", (nl, 16, 4, 64, 64)); scv = I("scv", (nl, 32, 256)); spl = I("spl", (nl, 16, 15, 256))
    cmk = I("cmk", (nl, 16, 256, 256)); cmv = I("cmv", (nl, 16, 256, 256)); memp = I("memp", (256, D))
    win = I("win", (nl, 128, 8, 2560)); wo = I("wo", (nl, 128, 8, 1024))
    wxq = I("wxq", (nl, 128, 8, 256)); wxk = I("wxk", (nl, 128, 8, 256)); wxv = I("wxv", (nl, 128, 8, 256))
    wxo = I("wxo", (nl, 128, 2, 1024)); wg = I("wg", (nl, 128, 8, DFF)); wu = I("wu", (nl, 128, 8, DFF))
    wd = I("wd", (nl, 128, 22, 1024))
    lncols = I("lncols", (128, 13 * 16)); lnrows = I("lnrows", (13, 2, 1024))
    sink = I("sink", (1, 16)); hlb = I("hlb", (128, 8)); hng = I("hng", (4, 256))
    cw = I("cw", (128, 24)); pw = I("pw", (4, 4, 64, 64)); psc = I("psc", (128, 8))
    ctab = I("ctab", _CT.shape); stab = I("stab", (128, 512))
    yp = O("yp", (2048, D)); ys = O("ys", (64, D))
    okp = O("okp", (nl, 128, 128)); ovp = O("ovp", (nl, 128, 128)); ohp = O("ohp", (nl, 4, 64, 64))
    ocp = O("ocp", (nl, 2, 256)); opp = O("opp", (nl, 15, 256)); omk = O("omk", (nl, 256, 256)); omv = O("omv", (nl, 256, 256))
    oks = O("oks", (nl, 16, 128, 128)); ovs = O("ovs", (nl, 16, 128, 128)); ohs = O("ohs", (nl, 16, 4, 64, 64))
    ocs = O("ocs", (nl, 16, 2, 256)); ops_ = O("ops", (nl, 16, 15, 256))
    dbgo = O("dbgo", (1024, 2112)) if dbg else None
    dbg2 = None
    S = Sched()
    es = ExitStack()
    with es:
        def sb(name, shape, dt=F32):
            return es.enter_context(nc.sbuf_tensor(name, list(shape), dt))
        x_tok = sb("x_tok", (128, NT, D)); xT = sb("xT", (128, 8, 2112), BF16)
        WA = sb("WA", (128, 28672), BF16)
        WK = None
        ct = sb("ct", (128, _CT.shape[1])); identb = sb("identb", (128, 128), BF16)
        lnc = sb("lnc", (128, 13, 2, 8)); Grow = sb("Grow", (128, D)); Brow = sb("Brow", (128, D))
        esink = sb("esink", (128, 16)); lbt = sb("lbt", (128, 2, 4)); oml = sb("oml", (128, 2, 4)); lbe = sb("lbe", (128, 2, 4)); lbs = sb("lbs", (128, 2, 1))
        normg = sb("normg", (128, 256)); cwt = sb("cwt", (128, 4, 2, 3)); psct = sb("psct", (128, 4, 2)); wbd = sb("wbd", (128, 2, 128), BF16)
        ones = sb("ones", (128, 64))
        ps = [es.enter_context(nc.psum_tensor(f"ps{i}", [128, 512], F32)) for i in range(8)]
        sems = {e: [es.enter_context(nc.semaphore(f"s_{e}{i}")) for i in range(NS)] for e in Sched.ENGS}
        dsems = {e: [es.enter_context(nc.semaphore(f"d_{e}{i}")) for i in range(ND)] for e in ("sp", "pool")}
        block = es.enter_context(nc.Block())

        C = lambda n: ct[:, _CO[n][0]:_CO[n][0] + _CO[n][1]]
        ident = C("ident")

        tl = threading.local()
        pools = {"ALL": list(range(8)), "A": [0, 1], "B": [2, 3], "Bt": [4], "C": [5], "D": [6], "E": [7], "P0": [0, 1, 2, 3], "P1": [4, 5, 6, 7]}
        ppos = {kk: 0 for kk in pools}

        def _next(pname):
            lst = pools[pname]; i = lst[ppos[pname] % len(lst)]; ppos[pname] += 1
            return i

        def bank():
            i = _next(getattr(tl, "pool", "ALL"))
            return ps[i], f"ps{i}"

        def tbank():
            i = _next(getattr(tl, "tpool", None) or getattr(tl, "pool", "ALL"))
            return ps[i][:].bitcast(BF16), f"ps{i}"

        _w = os.environ.get("WTS", "A1B2C1D1E1")
        wts = {_w[i]: int(_w[i + 1]) for i in range(0, len(_w), 2)}

        def run_interleaved(funcs, pnames, tpnames=None):
            nf = len(funcs); turn = [0]; alive = [True] * nf; cv = threading.Condition(); errs = []

            def handoff(i):
                j = (i + 1) % nf; c_ = 0
                while not alive[j] and c_ < nf:
                    j = (j + 1) % nf; c_ += 1
                turn[0] = j; cv.notify_all()

            cnt = [0] * nf

            def step(i):
                cnt[i] += 1
                if cnt[i] % wts.get(pnames[i], 1) != 0:
                    return
                with cv:
                    handoff(i)
                    while turn[0] != i:
                        cv.wait()

            def worker(i):
                tl.pool = pnames[i]; tl.tpool = tpnames[i] if tpnames else None
                tl.step = lambda: step(i)
                with cv:
                    while turn[0] != i:
                        cv.wait()
                try:
                    funcs[i]()
                except BaseException as e:
                    errs.append(e)
                finally:
                    with cv:
                        alive[i] = False
                        handoff(i)
            ths = [threading.Thread(target=worker, args=(i,)) for i in range(nf)]
            for th in ths:
                th.start()
            for th in ths:
                th.join()
            if errs:
                raise errs[0]

        def _y():
            st_ = getattr(tl, "step", None)
            if st_:
                st_()

        def warena(name, lo, n):
            return Buf(name, 2 * lo, 2 * (lo + n)), WA[:, lo:lo + n]

        def work(name, lo, n, dt=F32):
            v = WK[:, lo:lo + n]
            if dt == BF16:
                v = v.bitcast(BF16)
            return Buf(name, 1000000 + 4 * lo, 1000000 + 4 * (lo + n)), v

        def _mk(eng):
            def f(fn, r=(), w=()):
                S.op(eng, fn, r, w); _y()
            return f
        dve = _mk("dve"); act = _mk("act"); pe = _mk("pe"); pool = _mk("pool")

        def dsp(o, i, r=(), w=()):
            S.op("sp", lambda h: h.dma_start(out=o, in_=i), r, w, dma=True); _y()

        def dpl(o, i, r=(), w=()):
            S.op("pool", lambda h: h.dma_start(out=o, in_=i), r, w, dma=True); _y()

        dsp(ct[:], ctab[:], w=["ct"])
        dsp(lnc[:].rearrange("p a b c -> p (a b c)"), lncols[:], w=["lnc"])
        dsp(cwt[:].rearrange("p a b c -> p (a b c)"), cw[:], w=["cwt"])
        dsp(psct[:].rearrange("p a b -> p (a b)"), psc[:], w=["psct"])
        dsp(lbt[:].rearrange("p a b -> p (a b)"), hlb[:], w=["lbt"])
        dsp(esink[:], sink[:].partition_broadcast(128), w=["esink"])
        dve(lambda h: h.tensor_copy(out=identb[:], in_=ident), ["ct"], ["identb"])
        dve(lambda h: h.memset(ones[:], 1.0), w=["ones"])
        act(lambda h: h.activation(out=esink[:], in_=esink[:], func=AF.Exp), ["esink"], ["esink"])
        act(lambda h: h.activation(out=lbe[:], in_=lbt[:], func=AF.Exp), ["lbt"], ["lbe"])
        dve(lambda h: h.tensor_reduce(out=lbs[:], in_=lbe[:], axis=AX.X, op=ALU.add), ["lbe"], ["lbs"])
        dve(lambda h: h.reciprocal(out=lbs[:], in_=lbs[:]), ["lbs"], ["lbs"])
        dve(lambda h: h.tensor_tensor(out=lbe[:], in0=lbe[:], in1=lbs[:].to_broadcast([128, 2, 4]), op=ALU.mult), ["lbe", "lbs"], ["lbe"])
        dve(lambda h: h.memset(lbt[:, :, 0:1], 0.0), ["lbe"], ["lbt"])
        for l in range(1, 4):
            dve(lambda h, l=l: h.tensor_tensor(out=lbt[:, :, l:l + 1], in0=lbt[:, :, l - 1:l], in1=lbe[:, :, l:l + 1], op=ALU.add), ["lbt", "lbe"], ["lbt"])
        dve(lambda h: h.tensor_scalar(out=oml[:], in0=lbt[:], scalar1=-1.0, scalar2=1.0, op0=ALU.mult, op1=ALU.add), ["lbt"], ["oml"])

        for t in range(16):
            dsp(x_tok[:, t, :], xp[t * 128:(t + 1) * 128, :], w=[f"x{t}"])
        dsp(x_tok[0:64, 16, :], xs[:, :], w=["x16"])

        def ntok(t):
            return 64 if t == 16 else 128

        lnst = sb("lnst", (128, 2, 6)); lnmv = sb("lnmv", (128, 2)); lnr = sb("lnr", (128, 2)); zbf = sb("zbf", (128, D), BF16)

        def load_ln(i):
            dsp(Grow[:], lnrows[i, 0:1, :].partition_broadcast(128), w=["Grow"])
            dsp(Brow[:], lnrows[i, 1:2, :].partition_broadcast(128), w=["Brow"])

        def ln_tile(t, i, banks, first=True):
            n = ntok(t); xk = f"x{t}"
            xt = x_tok[0:n, t, :]
            if banks is not None:
                for nh, (b, bk) in enumerate(banks):
                    if first:
                        dve(lambda h, b=b, nh=nh: h.scalar_tensor_tensor(out=xt[:, nh * 512:(nh + 1) * 512], in0=xt[:, nh * 512:(nh + 1) * 512], scalar=ALPHA, in1=b[0:n, :], op0=ALU.mult, op1=ALU.add), [xk, bk], [xk])
                    else:
                        dve(lambda h, b=b, nh=nh: h.tensor_tensor(out=xt[:, nh * 512:(nh + 1) * 512], in0=xt[:, nh * 512:(nh + 1) * 512], in1=b[0:n, :], op=ALU.add), [xk, bk], [xk])
            if i is None:
                return
            for a in range(2):
                dve(lambda h, a=a: h.bn_stats(out=lnst[0:n, a, :], in_=xt[:, a * 512:(a + 1) * 512]), [xk], ["lnst"])
            dve(lambda h: h.bn_aggr(out=lnmv[0:n], in_=lnst[0:n].rearrange("p a b -> p (a b)")), ["lnst"], ["lnmv"])
            act(lambda h: h.activation(out=lnr[0:n, 0:1], in_=lnmv[0:n, 1:2], func=AF.Ln, bias=LN_EPS, scale=1.0), ["lnmv"], ["lnr"])
            act(lambda h: h.activation(out=lnr[0:n, 0:1], in_=lnr[0:n, 0:1], func=AF.Exp, scale=-0.5), ["lnr"], ["lnr"])
            dve(lambda h: h.scalar_tensor_tensor(out=lnr[0:n, 1:2], in0=lnmv[0:n, 0:1], scalar=-1.0, in1=lnr[0:n, 0:1], op0=ALU.mult, op1=ALU.mult), ["lnr", "lnmv"], ["lnr2"])
            act(lambda h: h.activation(out=zbf[0:n], in_=xt, func=AF.Identity, scale=lnr[0:n, 0:1], bias=lnr[0:n, 1:2]), [xk, "lnr", "lnr2"], ["zbf"])
            dve(lambda h: h.tensor_scalar(out=xt, in0=xt, scalar1=lnr[0:n, 0:1], scalar2=lnr[0:n, 1:2], op0=ALU.mult, op1=ALU.add), [xk, "lnr", "lnr2"], [xk])
            dve(lambda h: h.tensor_tensor(out=xt, in0=xt, in1=Grow[0:n], op=ALU.mult), [xk, "Grow"], [xk])
            pool(lambda h: h.tensor_tensor(out=xt, in0=xt, in1=Brow[0:n], op=ALU.add), [xk, "Brow"], [xk])
            tb, tk = tbank()
            for c in range(8):
                pe(lambda h, c=c: h.transpose(tb[:, c * 128:c * 128 + n], zbf[0:n, c * 128:(c + 1) * 128], identb[0:n, 0:n]), ["zbf", "identb"], [tk])
            for c in range(8):
                if c % 2 == 0:
                    act(lambda h, c=c: h.activation(out=xT[:, c, t * 128:t * 128 + n], in_=tb[:, c * 128:c * 128 + n], func=AF.Identity, scale=lnc[:, i, 0, c:c + 1], bias=lnc[:, i, 1, c:c + 1]), [tk, "lnc"], [f"xT{t}"])
                else:
                    dve(lambda h, c=c: h.tensor_scalar(out=xT[:, c, t * 128:t * 128 + n], in0=tb[:, c * 128:c * 128 + n], scalar1=lnc[:, i, 0, c:c + 1], scalar2=lnc[:, i, 1, c:c + 1], op0=ALU.mult, op1=ALU.add), [tk, "lnc"], [f"xT{t}"])

        W_in = [warena(f"win{k}", k * 2560, 2560) for k in range(8)]
        W_o = warena("wo", 20480, 8192)
        for k in range(8):
            dpl(W_in[k][1], win[0, :, k, :], w=[W_in[k][0]])
        dpl(W_o[1].rearrange("p (a b) -> p a b", b=1024), wo[0], w=[W_o[0]])
        load_ln(0)
        for t in range(NT):
            ln_tile(t, 0, None)

        W_xq = warena("wxq", 0, 2048); W_xo = warena("wxo", 2048, 2048); W_xk = warena("wxk", 4096, 2048); W_xv = warena("wxv", 6144, 2048)
        ring = [8192, 14336, 20480, 0]

        mixTs = [sb("mixT0", (128, 8, 128), BF16), sb("mixT1", (128, 8, 128), BF16)]
        qZ = sb("qZ", (128, 2, 2, 128), BF16); kTr = sb("kTr", (128, 2, 128), BF16); Vaug = sb("Vaug", (128, 2, 2, 65), BF16)
        tmpS = sb("tmpS", (128, 512)); PT = sb("PT", (128, 2, 512), BF16); otok = sb("otok", (128, 256), BF16)
        den = sb("den", (128, 4)); reca = sb("reca", (128, 4))
        uext = sb("uext", (128, 2, 130)); chs = sb("chs", (128, 128)); ycv = sb("ycv", (128, 128))
        ext = sb("ext", (128, 2, 144)); s2 = sb("s2", (128, 144)); s4 = sb("s4", (128, 144)); s8 = sb("s8", (128, 144)); s16 = sb("s16", (128, 144))
        pooled = sb("pooled", (128, 2, 128), BF16)
        hE = sb("hE", (128, 128)); hL1 = sb("hL1", (128, 128)); hL2 = sb("hL2", (128, 128)); hCU = sb("hCU", (128, 128))
        hEC = sb("hEC", (128, 128)); hEN = sb("hEN", (128, 128)); hEQ = sb("hEQ", (128, 128))
        qSZ = sb("qSZ", (128, 2, 2, 2, 64), BF16); kAb = sb("kAb", (128, 2, 128), BF16); kAtokZ = sb("kAtokZ", (128, 2, 256), BF16)
        Vh = sb("Vh", (128, 256), BF16); Sst = sb("Sst", (128, 2, 64)); Sbf = sb("Sbf", (128, 2, 2, 64), BF16); ECL = sb("ECL", (128, 2, 2))
        AtZ = sb("AtZ", (128, 2, 4, 64), BF16); sqb = sb("sqb", (128, 256)); sgb = sb("sgb", (128, 256)); otok2 = sb("otok2", (128, 256), BF16)
        ssq = sb("ssq", (128, 4)); rsq = sb("rsq", (128, 4))
        pool(lambda h: h.memset(qSZ[:], 0.0), w=["qSZ"])
        pool(lambda h: h.memset(kAtokZ[:], 0.0), w=["kAtokZ"])
        pool(lambda h: h.memset(AtZ[:], 0.0), w=["AtZ"])
        ptmp = sb("ptmp", (128, 16)); uxs = sb("uxs", (128, 2, 16, 6)); exs = sb("exs", (128, 2, 16, 20)); ECLs = sb("ECLs", (128, 2, 16))
        dbgb = None
        print("SBUF bytes remaining:", nc.sbuf_bytes_remaining)
        pool(lambda h: h.memset(qZ[:], 0.0), w=["qZ"])
        pool(lambda h: h.memset(Vaug[:], 1.0), w=["Va0", "Va1"])
        mva = sb("mva", (128, 2, 4, 65), BF16)
        sgt = hE; hT = hL1[:].bitcast(BF16).rearrange("p (a b) -> p a b", b=128); hT2 = hL2[:].bitcast(BF16).rearrange("p (a b) -> p a b", b=128)
        mkT = kAtokZ; qxT = kAb; oxt = otok; oxT = otok2[:].rearrange("p (a b) -> p a b", b=128); rec = reca
        PX = PT[:].rearrange("p a (b c) -> p a b c", c=128)
        memTb, memT_ = warena("memT", 8192, 2048)
        memT = memT_.rearrange("p (c m) -> p c m", m=256)
        dve(lambda h: h.memset(mva[:], 1.0), w=["mva"])
        MEMT = [memTb]

        def layer(l):
            if l > 0:
                for k in range(8):
                    dpl(W_in[k][1], win[l, :, k, :], w=[W_in[k][0]])
                dpl(W_o[1].rearrange("p (a b) -> p a b", b=1024), wo[l], w=[W_o[0]])
            load_ln(1 + 3 * l)
            WIN = [w[0] for w in W_in]
            pool(lambda h: h.memset(Sst[:], 0.0), w=["Sst"])
            pool(lambda h: h.memset(exs[:], 0.0), w=["exs0", "exs1"])
            pool(lambda h: h.memset(kAtokZ[:], 0.0), w=["kAtokZ"])
            pool(lambda h: h.memset(uext[:, :, 0:2], 0.0), w=["uext0", "uext1"])
            pool(lambda h: h.memset(ext[:, :, 0:16], 0.0), w=["ext0", "ext1"])
            pool(lambda h: h.memset(wbd[:], 0.0), w=["wbd"])
            for g_ in range(4):
                dpl(wbd[64 * (g_ % 2):64 * (g_ % 2) + 64, g_ // 2, 64 * (g_ % 2):64 * (g_ % 2) + 64], pw[l, g_], w=["wbd"])
            dsp(normg[:], hng[l:l + 1, :].partition_broadcast(128), w=["normg"])
            def p1(t, prev_post):
                n = ntok(t); c0 = t * 128
                mixT = mixTs[t % 2]; mk = f"mixT{t % 2}"
                b1, k1 = bank()
                for k in range(8):
                    pe(lambda h, k=k, b1=b1: h.matmul(b1[0:n, :], lhsT=xT[:, k, c0:c0 + n], rhs=W_in[k][1][:, cK:cK + 512], start=(k == 0), stop=(k == 7)), [f"xT{t}"] + WIN, [k1])
                act(lambda h: h.copy(out=Vh[0:n], in_=b1[0:n, 256:512]), [k1], ["Vh"])
                act(lambda h: h.copy(out=Vaug[0:n, t % 2, :, 0:64], in_=b1[0:n, 128:256].rearrange("p (kv d) -> p kv d", d=64)), [k1], [f"Va{t % 2}"])
                if t >= 15:
                    dve(lambda h, b1=b1: h.tensor_copy(out=sqb[0:n, 0:256], in_=b1[0:n, 0:256]), [k1], ["sqb"])
                    if t == 15:
                        dsp(okp[l], sqb[:, 0:128], ["sqb"]); dsp(ovp[l], sqb[:, 128:256], ["sqb"])
                    elif not DBG_SKIP:
                        dsp(oks[l, :, 0:124, :], ck[l, :, 4:128, :]); dsp(ovs[l, :, 0:124, :], cv[l, :, 4:128, :])
                        for tt in range(4):
                            dsp(oks[l, :, 124 + tt, :], sqb[tt:64:4, 0:128], ["sqb"]); dsp(ovs[l, :, 124 + tt, :], sqb[tt:64:4, 128:256], ["sqb"])
                    b2, k2 = bank(); b3, k3 = bank()
                    for k in range(8):
                        pe(lambda h, k=k, b2=b2: h.matmul(b2[0:n, :], lhsT=xT[:, k, c0:c0 + n], rhs=W_in[k][1][:, cCC:cCC + 512], start=(k == 0), stop=(k == 7)), [f"xT{t}"] + WIN, [k2])
                    for k in range(8):
                        pe(lambda h, k=k, b3=b3: h.matmul(b3[0:n, 0:256], lhsT=xT[:, k, c0:c0 + n], rhs=W_in[k][1][:, cDV:cDV + 256], start=(k == 0), stop=(k == 7)), [f"xT{t}"] + WIN, [k3])
                    act(lambda h, b2=b2: h.copy(out=sgb[0:n, :], in_=b2[0:n, 256:512]), [k2], ["sgb"])
                    dve(lambda h, b2=b2: h.tensor_tensor(out=sgb[0:n, :], in0=b2[0:n, 0:256], in1=sgb[0:n, :], op=ALU.mult), [k2, "sgb"], ["sgb"])
                    act(lambda h, b3=b3: h.copy(out=tmpS[0:n, 0:256], in_=b3[0:n, 0:256]), [k3], ["tmpS"])
                    if t == 15:
                        dsp(ocp[l], sgb[126:128, :], ["sgb"]); dsp(opp[l], tmpS[113:128, 0:256], ["tmpS"])
                    elif not DBG_SKIP:
                        dsp(ops_[l, :, 0:11, :], spl[l, :, 4:15, :])
                        for tt in range(4):
                            dsp(ops_[l, :, 11 + tt, :], tmpS[tt:64:4, 0:256], ["tmpS"])
                            if tt >= 2:
                                dsp(ocs[l, :, tt - 2, :], sgb[tt:64:4, :], ["sgb"])
                def fm(col):
                    b, bk = bank()
                    for k in range(8):
                        pe(lambda h, k=k: h.matmul(b[:, 0:n], lhsT=W_in[k][1][:, col:col + 128], rhs=xT[:, k, c0:c0 + n], start=(k == 0), stop=(k == 7)), [f"xT{t}"] + WIN, [bk])
                    return b, bk
                sl = t % 2
                v3 = lambda ap: ap.rearrange("p (b t) -> p b t", t=4)

                def attn_out(bo_, ko_):
                    ov = bo_[0:n, 0:260].rearrange("p (h e) -> p h e", e=65)
                    dve(lambda h: h.tensor_tensor(out=den[0:n], in0=ov[:, :, 64], in1=esink[0:n, l * 4:(l + 1) * 4], op=ALU.add), [ko_, "esink"], ["den"])
                    dve(lambda h: h.reciprocal(out=reca[0:n], in_=den[0:n]), ["den"], ["reca"])
                    dve(lambda h: h.tensor_tensor(out=otok[0:n].rearrange("p (h d) -> p h d", d=64), in0=ov[:, :, 0:64], in1=reca[0:n].unsqueeze(2).to_broadcast([n, 4, 64]), op=ALU.mult), [ko_, "reca"], ["otok"])
                    tb, tk = tbank()
                    for c in range(2):
                        pe(lambda h, c=c: h.transpose(tb[:, c * 128:c * 128 + n], otok[0:n, c * 128:(c + 1) * 128], identb[0:n, 0:n]), ["otok", "identb"], [tk])
                    dve(lambda h: h.tensor_copy(out=mixT[:, 0:2, 0:n], in_=tb[:, 0:256].rearrange("p (a b) -> p a b", b=128)[:, :, 0:n]), [tk], [mk])

                def hgrn_gate():
                    bg, kg = bank()
                    for k_ in range(8):
                        pe(lambda h, k_=k_: h.matmul(bg[0:n, 0:256], lhsT=xT[:, k_, c0:c0 + n], rhs=W_in[k_][1][:, cG:cG + 256], start=(k_ == 0), stop=(k_ == 7)), [f"xT{t}"] + WIN, [kg])
                    act(lambda h: h.activation(out=sgb[0:n], in_=bg[0:n, 0:256], func=AF.Exp, scale=-1.0), [kg], ["sgb"])
                    act(lambda h: h.activation(out=sgb[0:n], in_=sgb[0:n], func=AF.Ln, scale=1.0, bias=1.0), ["sgb"], ["sgb"])
                    act(lambda h: h.activation(out=sgb[0:n], in_=sgb[0:n], func=AF.Exp, scale=-1.0), ["sgb"], ["sgb"])
                    dve(lambda h: h.tensor_tensor(out=sgb[0:n], in0=bg[0:n, 0:256], in1=sgb[0:n], op=ALU.mult), [kg, "sgb"], ["sgb"])

                def hgrn_out(bO, kO):
                    act(lambda h: h.activation(out=sqb[0:n], in_=bO[0:n, 0:256], func=AF.Square), [kO], ["sqb"])
                    dve(lambda h: h.tensor_reduce(out=ssq[0:n], in_=sqb[0:n].rearrange("p (a v) -> p a v", v=64), axis=AX.X, op=ALU.add), ["sqb"], ["ssq"])
                    act(lambda h: h.activation(out=rsq[0:n], in_=ssq[0:n], func=AF.Ln, scale=1.0 / 64, bias=RMS_EPS), ["ssq"], ["rsq"])
                    act(lambda h: h.activation(out=rsq[0:n], in_=rsq[0:n], func=AF.Exp, scale=-0.5), ["rsq"], ["rsq"])
                    dve(lambda h: h.tensor_tensor(out=sqb[0:n].rearrange("p (a v) -> p a v", v=64), in0=bO[0:n, 0:256].rearrange("p (a v) -> p a v", v=64), in1=rsq[0:n].unsqueeze(2).to_broadcast([n, 4, 64]), op=ALU.mult), [kO, "rsq", "sqb"], ["sqb"])
                    dve(lambda h: h.tensor_tensor(out=sqb[0:n], in0=sqb[0:n], in1=normg[0:n], op=ALU.mult), ["sqb", "normg"], ["sqb"])
                    dve(lambda h: h.tensor_tensor(out=otok2[0:n], in0=sqb[0:n], in1=sgb[0:n], op=ALU.mult), ["sqb", "sgb"], ["otok2"])
                    tb2, tk2 = tbank()
                    for c in range(2):
                        pe(lambda h, c=c: h.transpose(tb2[:, c * 128:c * 128 + n], otok2[0:n, c * 128:(c + 1) * 128], identb[0:n, 0:n]), ["otok2", "identb"], [tk2])
                    dve(lambda h: h.tensor_copy(out=mixT[:, 2:4, 0:n], in_=tb2[:, 0:256].rearrange("p (a b) -> p a b", b=128)[:, :, 0:n]), [tk2], [mk])
                if t == 16:
                    dsp(sqb[0:32, :], scv[l], w=["sqb"])
                    for hc in range(2):
                        bT, kT_ = bank()
                        pe(lambda h, hc=hc, bT=bT: h.transpose(bT[:, 0:32], sqb[0:32, hc * 128:(hc + 1) * 128], ident[0:32, 0:32]), ["sqb", "ct"], [kT_])
                        act(lambda h, hc=hc, bT=bT: h.copy(out=uxs[:, hc, :, 0:2], in_=bT[:, 0:32].rearrange("p (b j) -> p b j", j=2)), [kT_], [f"uxs{hc}"])
                    for half in range(2):
                        dsp(sqb[0:120, :], spl[l, 8 * half:8 * half + 8].rearrange("b j c -> (b j) c"), w=["sqb"])
                        for hc in range(2):
                            bT, kT_ = bank()
                            pe(lambda h, hc=hc, bT=bT: h.transpose(bT[:, 0:120], sqb[0:120, hc * 128:(hc + 1) * 128], ident[0:120, 0:120]), ["sqb", "ct"], [kT_])
                            act(lambda h, hc=hc, bT=bT, half=half: h.copy(out=exs[:, hc, 8 * half:8 * half + 8, 1:16], in_=bT[:, 0:120].rearrange("p (b j) -> p b j", j=15)), [kT_], [f"exs{hc}"])
                def secA():
                    for g, col in enumerate((cA, cB)):
                        b, bk = fm(col)
                        for kv in range(2):
                            act(lambda h, b=b, g=g, kv=kv: h.copy(out=qZ[64 * kv:64 * kv + 64, g, kv, 0:n], in_=b[64 * kv:64 * kv + 64, 0:n]), [bk], ["qZ"])
                    bK, kK_ = fm(cK)
                    act(lambda h: h.copy(out=kTr[:, sl, 0:n], in_=bK[:, 0:n]), [kK_], [f"kT{sl}"])
                    if t < 16:
                        blks = [0] if t == 0 else [1, 0]
                        for blk in blks:
                            ksl = sl if blk == 0 else 1 - sl
                            bs, ks = bank()
                            for g in range(2):
                                pe(lambda h, g=g, bs=bs, ksl=ksl: h.matmul(bs[:, g * 256:(g + 1) * 256], lhsT=kTr[:, ksl, :], rhs=qZ[:, g, :, :].rearrange("p a q -> p (a q)"), start=True, stop=True), [f"kT{ksl}", "qZ"], [ks])
                            dve(lambda h, bs=bs, blk=blk: h.scalar_tensor_tensor(out=tmpS[:], in0=bs[:, :], scalar=0.125, in1=C("bt")[:, blk * 512:(blk + 1) * 512], op0=ALU.mult, op1=ALU.add), [ks, "ct"], ["tmpS"])
                            act(lambda h, blk=blk: h.activation(out=PT[:, blk, :], in_=tmpS[:], func=AF.Exp), ["tmpS"], [f"PT{blk}"])
                        bo_, ko_ = bank()
                        for kv in range(2):
                            for g in range(2):
                                hh = kv * 2 + g; pc = (g * 2 + kv) * 128
                                if t > 0:
                                    pe(lambda h, hh=hh, pc=pc, kv=kv: h.matmul(bo_[:, hh * 65:(hh + 1) * 65], lhsT=PT[:, 1, pc:pc + 128], rhs=Vaug[:, 1 - sl, kv, :], start=True, stop=False), ["PT1", f"Va{1 - sl}"], [ko_])
                                pe(lambda h, hh=hh, pc=pc, kv=kv: h.matmul(bo_[:, hh * 65:(hh + 1) * 65], lhsT=PT[:, 0, pc:pc + 128], rhs=Vaug[:, sl, kv, :], start=(t == 0), stop=True), ["PT0", f"Va{sl}"], [ko_])
                        attn_out(bo_, ko_)
                def secB():
                    if t < 16:
                        tbH, tkH = tbank()
                        for hc in range(2):
                            bq, kq = fm(cBQ + hc * 128); bf, kf = fm(cBF + hc * 128)
                            act(lambda h, bf=bf: h.activation(out=hE[:, 0:n], in_=bf[:, 0:n], func=AF.Exp, scale=-1.0), [kf], ["hE"])
                            act(lambda h, hc=hc: h.activation(out=hL1[:, 0:n], in_=hE[:, 0:n], func=AF.Ln, scale=lbt[:, hc, l:l + 1], bias=1.0), ["hE", "lbt"], ["hL1"])
                            act(lambda h: h.activation(out=hL2[:, 0:n], in_=hE[:, 0:n], func=AF.Ln, scale=1.0, bias=1.0), ["hE"], ["hL2"])
                            dve(lambda h: h.tensor_tensor(out=hL1[:, 0:n], in0=hL1[:, 0:n], in1=hL2[:, 0:n], op=ALU.subtract), ["hL1", "hL2"], ["hL1"])
                            act(lambda h: h.activation(out=hL2[:, 0:n], in_=hL2[:, 0:n], func=AF.Exp, scale=-1.0), ["hL2"], ["hL2"])
                            dve(lambda h, hc=hc: h.scalar_tensor_tensor(out=hE[:, 0:n], in0=hE[:, 0:n], scalar=oml[:, hc, l:l + 1], in1=hL2[:, 0:n], op0=ALU.mult, op1=ALU.mult), ["hE", "hL2", "oml"], ["hE"])
                            for c2 in range(2):
                                dve(lambda h, c2=c2: h.tensor_tensor_scan(out=hCU[:, c2 * 64:(c2 + 1) * 64], data0=ones[:, 0:64], data1=hL1[:, c2 * 64:(c2 + 1) * 64], initial=0.0, op0=ALU.mult, op1=ALU.add), ["hL1", "ones"], ["hCU"])
                            act(lambda h: h.activation(out=hEC[:, 0:n], in_=hCU[:, 0:n], func=AF.Exp), ["hCU"], ["hEC"])
                            act(lambda h: h.activation(out=hEN[:, 0:n], in_=hCU[:, 0:n], func=AF.Exp, scale=-1.0), ["hCU"], ["hEN"])
                            act(lambda h, bq=bq: h.activation(out=hEQ[:, 0:n], in_=bq[:, 0:n], func=AF.Exp, scale=-1.0), [kq], ["hEQ"])
                            act(lambda h: h.activation(out=hEQ[:, 0:n], in_=hEQ[:, 0:n], func=AF.Ln, scale=1.0, bias=1.0), ["hEQ"], ["hEQ"])
                            act(lambda h: h.activation(out=hEQ[:, 0:n], in_=hEQ[:, 0:n], func=AF.Exp, scale=-1.0), ["hEQ"], ["hEQ"])
                            dve(lambda h, bq=bq: h.tensor_tensor(out=hEQ[:, 0:n], in0=bq[:, 0:n], in1=hEQ[:, 0:n], op=ALU.mult), [kq, "hEQ"], ["hEQ"])
                            for hp in range(2):
                                rw = slice(64 * hp, 64 * hp + 64)
                                dve(lambda h, hc=hc, hp=hp, rw=rw: h.tensor_tensor(out=qSZ[rw, hc, :, hp, :], in0=hEQ[rw, :].rearrange("p (c t) -> p c t", t=64), in1=hEC[rw, :].rearrange("p (c t) -> p c t", t=64), op=ALU.mult), ["hEQ", "hEC"], ["qSZ"])
                            dve(lambda h, hc=hc: h.tensor_tensor(out=kAb[:, hc, :], in0=hE[:, 0:n], in1=hEN[:, 0:n], op=ALU.mult), ["hE", "hEN"], [f"kAb{hc}"])
                            act(lambda h, hc=hc: h.copy(out=ECL[:, hc, :], in_=hEC[:, 63:128:64]), ["hEC"], ["ECL"])
                            pe(lambda h, hc=hc: h.transpose(tbH[:, hc * 128:(hc + 1) * 128], kAb[:, hc, :], identb[:]), [f"kAb{hc}", "identb"], [tkH])
                        for c2 in range(2):
                            rw = slice(64 * c2, 64 * c2 + 64)
                            dve(lambda h, c2=c2, rw=rw: h.tensor_copy(out=kAtokZ[rw, c2, :], in_=tbH[rw, 0:256]), [tkH], ["kAtokZ"])
                        bU, kU = bank()
                        for c2 in range(2):
                            for hc in range(2):
                                pe(lambda h, c2=c2, hc=hc: h.matmul(bU[:, (c2 * 2 + hc) * 128:(c2 * 2 + hc + 1) * 128], lhsT=kAtokZ[:, c2, hc * 128:(hc + 1) * 128], rhs=Vh[:, hc * 128:(hc + 1) * 128], start=True, stop=True), ["kAtokZ", "Vh"], [kU])
                        for c2 in range(2):
                            act(lambda h, c2=c2: h.copy(out=Sbf[:, :, c2, :], in_=Sst[:]), ["Sst"], ["Sbf"])
                            for hp in range(2):
                                rw = slice(64 * hp, 64 * hp + 64)
                                dve(lambda h, c2=c2, hp=hp, rw=rw: h.tensor_tensor(out=Sst[rw], in0=Sst[rw], in1=bU[rw, c2 * 256:(c2 + 1) * 256].rearrange("p (hc hh v) -> p hc hh v", hc=2, hh=2)[:, :, hp, :], op=ALU.add), ["Sst", kU], ["Sst"])
                            dve(lambda h, c2=c2: h.tensor_tensor(out=Sst[:], in0=Sst[:], in1=ECL[:, :, c2:c2 + 1].to_broadcast([128, 2, 64]), op=ALU.mult), ["Sst", "ECL"], ["Sst"])
                        bA, kA_ = bank()
                        for c2 in range(2):
                            for hc in range(2):
                                pe(lambda h, c2=c2, hc=hc: h.matmul(bA[64 * c2:64 * c2 + 64, hc * 128:(hc + 1) * 128], lhsT=kAb[:, hc, c2 * 64:(c2 + 1) * 64], rhs=qSZ[:, hc, c2, :, :].rearrange("p a t -> p (a t)"), start=True, stop=True), [f"kAb{hc}", "qSZ"], [kA_])
                        for c2 in range(2):
                            rw = slice(64 * c2, 64 * c2 + 64)
                            dve(lambda h, c2=c2, rw=rw: h.tensor_tensor(out=AtZ[rw, c2, :, :], in0=bA[rw, 0:256].rearrange("p (a t) -> p a t", t=64), in1=C("tri2")[rw, :].unsqueeze(1).to_broadcast([64, 4, 64]), op=ALU.mult), [kA_, "ct"], ["AtZ"])
                        bO, kO = bank()
                        for c2 in range(2):
                            for hh in range(4):
                                hc, hp = hh // 2, hh % 2
                                pe(lambda h, c2=c2, hh=hh, hc=hc, hp=hp: h.matmul(bO[64 * c2:64 * c2 + 64, hh * 64:(hh + 1) * 64], lhsT=qSZ[:, hc, c2, hp, :], rhs=Sbf[:, hc, c2, :], start=True, stop=False), ["qSZ", "Sbf"], [kO])
                                pe(lambda h, c2=c2, hh=hh: h.matmul(bO[64 * c2:64 * c2 + 64, hh * 64:(hh + 1) * 64], lhsT=AtZ[:, c2, hh, :], rhs=Vh[:, hh * 64:(hh + 1) * 64], start=False, stop=True), ["AtZ", "Vh"], [kO])
                        hgrn_gate()
                        hgrn_out(bO, kO)
                        if t == int(os.environ.get('DBG_ST', '15')):
                            dsp(ohp[l].rearrange("(hc hp) k v -> (hp k) hc v", hc=2), Sst[:], ["Sst"])
                    else:
                        tbH, tkH = tbank()
                        for hc in range(2):
                            bq, kq = fm(cBQ + hc * 128); bf, kf = fm(cBF + hc * 128)
                            act(lambda h, bf=bf: h.activation(out=hE[:, 0:n], in_=bf[:, 0:n], func=AF.Exp, scale=-1.0), [kf], ["hE"])
                            act(lambda h, hc=hc: h.activation(out=hL1[:, 0:n], in_=hE[:, 0:n], func=AF.Ln, scale=lbt[:, hc, l:l + 1], bias=1.0), ["hE", "lbt"], ["hL1"])
                            act(lambda h: h.activation(out=hL2[:, 0:n], in_=hE[:, 0:n], func=AF.Ln, scale=1.0, bias=1.0), ["hE"], ["hL2"])
                            dve(lambda h: h.tensor_tensor(out=hL1[:, 0:n], in0=hL1[:, 0:n], in1=hL2[:, 0:n], op=ALU.subtract), ["hL1", "hL2"], ["hL1"])
                            act(lambda h: h.activation(out=hL2[:, 0:n], in_=hL2[:, 0:n], func=AF.Exp, scale=-1.0), ["hL2"], ["hL2"])
                            dve(lambda h, hc=hc: h.scalar_tensor_tensor(out=hE[:, 0:n], in0=hE[:, 0:n], scalar=oml[:, hc, l:l + 1], in1=hL2[:, 0:n], op0=ALU.mult, op1=ALU.mult), ["hE", "hL2", "oml"], ["hE"])
                            dve(lambda h: h.tensor_copy(out=hCU[:, 0:n], in_=hL1[:, 0:n]), ["hL1"], ["hCU"])
                            for tt in range(1, 4):
                                dve(lambda h, tt=tt: h.tensor_tensor(out=v3(hCU[:, 0:n])[:, :, tt:tt + 1], in0=v3(hCU[:, 0:n])[:, :, tt - 1:tt], in1=v3(hL1[:, 0:n])[:, :, tt:tt + 1], op=ALU.add), ["hCU", "hL1"], ["hCU"])
                            act(lambda h: h.activation(out=hEC[:, 0:n], in_=hCU[:, 0:n], func=AF.Exp), ["hCU"], ["hEC"])
                            act(lambda h: h.activation(out=hEN[:, 0:n], in_=hCU[:, 0:n], func=AF.Exp, scale=-1.0), ["hCU"], ["hEN"])
                            act(lambda h, bq=bq: h.activation(out=hEQ[:, 0:n], in_=bq[:, 0:n], func=AF.Exp, scale=-1.0), [kq], ["hEQ"])
                            act(lambda h: h.activation(out=hEQ[:, 0:n], in_=hEQ[:, 0:n], func=AF.Ln, scale=1.0, bias=1.0), ["hEQ"], ["hEQ"])
                            act(lambda h: h.activation(out=hEQ[:, 0:n], in_=hEQ[:, 0:n], func=AF.Exp, scale=-1.0), ["hEQ"], ["hEQ"])
                            dve(lambda h, bq=bq: h.tensor_tensor(out=hEQ[:, 0:n], in0=bq[:, 0:n], in1=hEQ[:, 0:n], op=ALU.mult), [kq, "hEQ"], ["hEQ"])
                            for hp in range(2):
                                rw = slice(64 * hp, 64 * hp + 64)
                                dve(lambda h, hc=hc, hp=hp, rw=rw: h.tensor_tensor(out=qSZ[rw, hc, 0, hp, :], in0=hEQ[rw, 0:n], in1=hEC[rw, 0:n], op=ALU.mult), ["hEQ", "hEC"], ["qSZ"])
                            dve(lambda h, hc=hc: h.tensor_tensor(out=kAb[:, hc, 0:n], in0=hE[:, 0:n], in1=hEN[:, 0:n], op=ALU.mult), ["hE", "hEN"], [f"kAb{hc}"])
                            act(lambda h, hc=hc: h.copy(out=ECLs[:, hc, :], in_=v3(hEC[:, 0:n])[:, :, 3]), ["hEC"], ["ECLs"])
                            dve(lambda h, hc=hc: h.tensor_tensor(out=v3(kAb[:, hc, 64:128]), in0=v3(kAb[:, hc, 0:64]), in1=ECLs[:, hc, :].unsqueeze(2).to_broadcast([128, 16, 4]), op=ALU.mult), [f"kAb{hc}", "ECLs"], [f"kAb{hc}"])
                            pe(lambda h, hc=hc: h.transpose(tbH[0:64, hc * 128:(hc + 1) * 128], kAb[:, hc, 64:128], identb[:]), [f"kAb{hc}", "identb"], [tkH])
                        dve(lambda h: h.tensor_copy(out=kAtokZ[0:64, 0, :], in_=tbH[0:64, 0:256]), [tkH], ["kAtokZ"])
                        bA, kA_ = bank()
                        for hc in range(2):
                            pe(lambda h, hc=hc: h.matmul(bA[0:64, hc * 128:(hc + 1) * 128], lhsT=kAb[:, hc, 0:64], rhs=qSZ[:, hc, 0, :, :].rearrange("p a t -> p (a t)"), start=True, stop=True), [f"kAb{hc}", "qSZ"], [kA_])
                        dve(lambda h: h.tensor_tensor(out=AtZ[0:64, 0, :, :], in0=bA[0:64, 0:256].rearrange("p (a t) -> p a t", t=64), in1=C("bdc")[0:64, :].unsqueeze(1).to_broadcast([64, 4, 64]), op=ALU.mult), [kA_, "ct"], ["AtZ"])
                        hgrn_gate()
                def secC():
                    for hc in range(2):
                        if t < 16:
                            bch, kch = fm(cCH + hc * 128)
                            act(lambda h, bch=bch: h.copy(out=chs[:, 0:n], in_=bch[:, 0:n]), [kch], ["chs"])
                            bcc, kcc = fm(cCC + hc * 128)
                            dve(lambda h, bcc=bcc, hc=hc: h.tensor_tensor(out=uext[:, hc, 2:2 + n], in0=bcc[:, 0:n], in1=chs[:, 0:n], op=ALU.mult), [kcc, "chs"], [f"uext{hc}"])
                            dve(lambda h, hc=hc: h.tensor_scalar(out=ycv[:, 0:n], in0=uext[:, hc, 2:2 + n], scalar1=cwt[:, l, hc, 2:3], scalar2=None, op0=ALU.mult), [f"uext{hc}", "cwt"], ["ycv"])
                            for j in (1, 0):
                                dve(lambda h, hc=hc, j=j: h.scalar_tensor_tensor(out=ycv[:, 0:n], in0=uext[:, hc, j:j + n], scalar=cwt[:, l, hc, j:j + 1], in1=ycv[:, 0:n], op0=ALU.mult, op1=ALU.add), [f"uext{hc}", "cwt", "ycv"], ["ycv"])
                            bcb, kcb = fm(cCB + hc * 128)
                            dve(lambda h, hc=hc, bcb=bcb: h.tensor_tensor(out=mixT[:, 4 + hc, 0:n], in0=ycv[:, 0:n], in1=bcb[:, 0:n], op=ALU.mult), ["ycv", kcb], [mk])
                            act(lambda h, hc=hc: h.copy(out=uext[:, hc, 0:2], in_=uext[:, hc, n:n + 2]), [f"uext{hc}"], [f"uext{hc}"])
                        else:
                            v3 = lambda ap: ap.rearrange("p (b t) -> p b t", t=4)
                            bch, kch = fm(cCH + hc * 128)
                            act(lambda h, bch=bch: h.copy(out=chs[:, 0:n], in_=bch[:, 0:n]), [kch], ["chs"])
                            bcc, kcc = fm(cCC + hc * 128)
                            dve(lambda h, bcc=bcc, hc=hc: h.tensor_tensor(out=uxs[:, hc, :, 2:6], in0=v3(bcc[:, 0:n]), in1=v3(chs[:, 0:n]), op=ALU.mult), [kcc, "chs"], [f"uxs{hc}"])
                            dve(lambda h, hc=hc: h.tensor_scalar(out=v3(ycv[:, 0:n]), in0=uxs[:, hc, :, 2:6], scalar1=cwt[:, l, hc, 2:3], scalar2=None, op0=ALU.mult), [f"uxs{hc}", "cwt"], ["ycv"])
                            for j in (1, 0):
                                dve(lambda h, hc=hc, j=j: h.scalar_tensor_tensor(out=v3(ycv[:, 0:n]), in0=uxs[:, hc, :, j:j + 4], scalar=cwt[:, l, hc, j:j + 1], in1=v3(ycv[:, 0:n]), op0=ALU.mult, op1=ALU.add), [f"uxs{hc}", "cwt", "ycv"], ["ycv"])
                            bcb, kcb = fm(cCB + hc * 128)
                            dve(lambda h, hc=hc, bcb=bcb: h.tensor_tensor(out=mixT[:, 4 + hc, 0:n], in0=ycv[:, 0:n], in1=bcb[:, 0:n], op=ALU.mult), ["ycv", kcb], [mk])
                def secD():
                    for hc in range(2):
                        bdv, kdv = fm(cDV + hc * 128)
                        if t < 16:
                            W_ = 16 + n
                            act(lambda h, bdv=bdv, hc=hc: h.copy(out=ext[:, hc, 16:16 + n], in_=bdv[:, 0:n]), [kdv], [f"ext{hc}"])
                            pool(lambda h, hc=hc: h.tensor_tensor(out=s2[:, 1:W_], in0=ext[:, hc, 1:W_], in1=ext[:, hc, 0:W_ - 1], op=ALU.add), [f"ext{hc}"], ["s2"])
                            pool(lambda h: h.tensor_tensor(out=s4[:, 3:W_], in0=s2[:, 3:W_], in1=s2[:, 1:W_ - 2], op=ALU.add), ["s2"], ["s4"])
                            if hc == 1:
                                pool(lambda h: h.tensor_tensor(out=s8[:, 7:W_], in0=s4[:, 7:W_], in1=s4[:, 3:W_ - 4], op=ALU.add), ["s4"], ["s8"])
                                pool(lambda h: h.tensor_tensor(out=s16[:, 15:W_], in0=s8[:, 15:W_], in1=s8[:, 7:W_ - 8], op=ALU.add), ["s8"], ["s16"])
                            sel = [(s2, "s2", 2.0), (s4, "s4", 4.0)] if hc == 0 else [(s8, "s8", 8.0), (s16, "s16", 16.0)]
                            for half, (sv, sk, w_) in enumerate(sel):
                                rows = slice(64 * half, 64 * half + 64)
                                dve(lambda h, sv=sv, rows=rows, w_=w_, hc=hc: h.scalar_tensor_tensor(out=pooled[rows, hc, 0:n], in0=sv[rows, 16:W_], scalar=1.0 / w_, in1=ext[rows, hc, 16:W_], op0=ALU.mult, op1=ALU.subtract), [sk, f"ext{hc}"], [f"pooled{hc}"])
                                if t == 0:
                                    dve(lambda h, sv=sv, rows=rows, hc=hc: h.tensor_tensor(out=ptmp[rows, 0:16], in0=sv[rows, 16:32], in1=C("icnt")[rows, hc * 16:(hc + 1) * 16], op=ALU.mult), [sk, "ct"], ["ptmp"])
                                    dve(lambda h, rows=rows, hc=hc: h.tensor_tensor(out=pooled[rows, hc, 0:16], in0=ptmp[rows, 0:16], in1=ext[rows, hc, 16:32], op=ALU.subtract), ["ptmp", f"ext{hc}", f"pooled{hc}"], [f"pooled{hc}"])
                            by, ky = bank()
                            pe(lambda h, hc=hc, by=by: h.matmul(by[:, 0:n], lhsT=wbd[:, hc, :], rhs=pooled[:, hc, 0:n], start=True, stop=True), ["wbd", f"pooled{hc}"], [ky])
                            act(lambda h, hc=hc, by=by: h.activation(out=mixT[:, 6 + hc, 0:n], in_=by[:, 0:n], func=AF.Copy, scale=psct[:, l, hc:hc + 1]), [ky, "psct"], [mk])
                            act(lambda h, hc=hc: h.copy(out=ext[:, hc, 0:16], in_=ext[:, hc, n:n + 16]), [f"ext{hc}"], [f"ext{hc}"])
                        else:
                            v3 = lambda ap: ap.rearrange("p (b t) -> p b t", t=4)
                            act(lambda h, bdv=bdv, hc=hc: h.copy(out=exs[:, hc, :, 16:20], in_=v3(bdv[:, 0:n])), [kdv], [f"exs{hc}"])
                            for half in range(2):
                                w_ = (2, 4, 8, 16)[hc * 2 + half]; rows = slice(64 * half, 64 * half + 64)
                                dve(lambda h, rows=rows, hc=hc: h.tensor_copy(out=v3(s2[rows, 0:n]), in_=exs[rows, hc, :, 16:20]), [f"exs{hc}"], ["s2"])
                                for j in range(1, w_):
                                    dve(lambda h, rows=rows, hc=hc, j=j: h.tensor_tensor(out=v3(s2[rows, 0:n]), in0=v3(s2[rows, 0:n]), in1=exs[rows, hc, :, 16 - j:20 - j], op=ALU.add), ["s2", f"exs{hc}"], ["s2"])
                                dve(lambda h, rows=rows, hc=hc, w_=w_: h.scalar_tensor_tensor(out=v3(pooled[rows, hc, 0:n]), in0=v3(s2[rows, 0:n]), scalar=1.0 / w_, in1=exs[rows, hc, :, 16:20], op0=ALU.mult, op1=ALU.subtract), ["s2", f"exs{hc}"], [f"pooled{hc}"])
                            by, ky = bank()
                            pe(lambda h, hc=hc, by=by: h.matmul(by[:, 0:n], lhsT=wbd[:, hc, :], rhs=pooled[:, hc, 0:n], start=True, stop=True), ["wbd", f"pooled{hc}"], [ky])
                            act(lambda h, hc=hc, by=by: h.activation(out=mixT[:, 6 + hc, 0:n], in_=by[:, 0:n], func=AF.Copy, scale=psct[:, l, hc:hc + 1]), [ky, "psct"], [mk])
                if os.environ.get('NO_ILV'):
                    secA(); secB(); secC(); secD()
                    if prev_post:
                        prev_post()
                else:
                    fs = [secA, secB, secC, secD] + ([prev_post] if prev_post else [])
                    run_interleaved(fs, ["A", "B", "C", "D", "E"][:len(fs)], [None, "Bt", None, None, None][:len(fs)])
                if t == 16:
                    PS_ = 28672
                    kc_b, kc_ = warena("kc", 0, 2048); kc = kc_.rearrange("p (b c) -> p b c", c=128)
                    kcT_b, kcT_ = warena("kcT", 2048, 2048); kcT = kcT_.rearrange("p (b c) -> p b c", c=128)
                    Vc_b, Vc_ = warena("Vc", 4096, 2080); Vc = Vc_.rearrange("p (b kv e) -> p b kv e", kv=2, e=65)
                    Pb_b, Pb_ = warena("Pb", 6400, 4096); Pb = Pb_.rearrange("p (g kv b q) -> p g kv b q", g=2, kv=2, b=16)
                    tabs_b, tabs_ = warena("stab", 18944, 1024); tabs = tabs_.bitcast(F32)
                    dsp(tabs, stab[:, :], w=[tabs_b])
                    dpl(kc, ck[l].rearrange("b s c -> s b c"), w=[kc_b])
                    pool(lambda h: h.memset(Vc_, 1.0), w=[Vc_b])
                    for kv in range(2):
                        dpl(Vc[:, :, kv, 0:64], cv[l, :, :, kv * 64:(kv + 1) * 64].rearrange("b s d -> s b d"), w=[Vc_b])
                    pool(lambda h: h.memset(Pb_, 0.0), w=[Pb_b])
                    for half in range(2):
                        tbk, tkk = tbank()
                        for j in range(8):
                            pe(lambda h, j=j, half=half, tbk=tbk: h.transpose(tbk[:, j * 128:(j + 1) * 128], kc[:, half * 8 + j, :], identb[:]), [kc_b, "identb"], [tkk])
                        dve(lambda h, half=half, tbk=tbk: h.tensor_copy(out=kcT[:, half * 8:(half + 1) * 8, :], in_=tbk[:, :].rearrange("p (b c) -> p b c", c=128)), [tkk], [kcT_b])
                    bs, ks = bank()
                    for b_ in range(16):
                        for g in range(2):
                            pe(lambda h, b_=b_, g=g: h.matmul(bs[:, g * 128:(g + 1) * 128].rearrange("p (kv q) -> p kv q", kv=2)[:, :, b_ * 4:b_ * 4 + 4], lhsT=kcT[:, b_, :], rhs=qZ[:, g, :, b_ * 4:b_ * 4 + 4], start=True, stop=True), [kcT_b, "qZ"], [ks])
                    bn, kn = bank()
                    for g in range(2):
                        pe(lambda h, g=g: h.matmul(bn[0:64, g * 128:(g + 1) * 128].rearrange("p (kv q) -> p kv q", kv=2), lhsT=kTr[:, 0, 0:64], rhs=qZ[:, g, :, 0:64], start=True, stop=True), ["kT0", "qZ"], [kn])
                    dve(lambda h: h.scalar_tensor_tensor(out=tmpS[:, 0:256], in0=bs[:, 0:256], scalar=0.125, in1=tabs[:, 0:256], op0=ALU.mult, op1=ALU.add), [ks, tabs_b], ["tmpS"])
                    dve(lambda h: h.scalar_tensor_tensor(out=tmpS[0:64, 256:512], in0=bn[0:64, 0:256], scalar=0.125, in1=tabs[0:64, 256:512], op0=ALU.mult, op1=ALU.add), [kn, tabs_b, "tmpS"], ["tmpS"])
                    for gk in range(4):
                        act(lambda h, gk=gk: h.activation(out=bass.AP(WA, 6400 + gk * 1024, [[PS_, 128], [68, 16], [1, 4]]), in_=v3(tmpS[:, gk * 64:(gk + 1) * 64]), func=AF.Exp), ["tmpS"], [Pb_b])
                    act(lambda h: h.activation(out=PT[0:64, 0, 0:256], in_=tmpS[0:64, 256:512], func=AF.Exp), ["tmpS"], ["PT0"])
                    bo_, ko_ = bank()
                    for kv in range(2):
                        for g in range(2):
                            hh = kv * 2 + g; pc = (g * 2 + kv) * 64
                            for b_ in range(16):
                                pe(lambda h, hh=hh, g=g, kv=kv, b_=b_: h.matmul(bo_[0:64, hh * 65:(hh + 1) * 65], lhsT=Pb[:, g, kv, b_, :], rhs=Vc[:, b_, kv, :], start=(b_ == 0), stop=False), [Pb_b, Vc_b], [ko_])
                            pe(lambda h, hh=hh, pc=pc, kv=kv: h.matmul(bo_[0:64, hh * 65:(hh + 1) * 65], lhsT=PT[0:64, 0, pc:pc + 64], rhs=Vaug[0:64, 0, kv, :], start=False, stop=True), ["PT0", "Va0"], [ko_])
                    attn_out(bo_, ko_)
                    S0f_b, S0f_ = warena("S0f", 10496, 4096); S0f = S0f_.bitcast(F32).rearrange("p (hc b v) -> p hc b v", hc=2, b=16)
                    S0b_b, S0b_ = warena("S0b", 14592, 2048); S0b = S0b_.rearrange("p (hc b v) -> p hc b v", hc=2, b=16)
                    qSb_b, qSb_ = warena("qSb", 16640, 2048); qSbig = qSb_.rearrange("p (hp b q) -> p hp b q", hp=2, b=16)
                    Vbg_b, Vbg_ = warena("Vbg", 0, 4096); Vbig = Vbg_.rearrange("p (b c) -> p b c", c=256)
                    for hc in range(2):
                        for hp in range(2):
                            srcS = sh[l, :, hc * 2 + hp].rearrange("b k v -> k b v")
                            dsp(S0f[64 * hp:64 * hp + 64, hc], srcS, w=[S0f_b]); dpl(S0b[64 * hp:64 * hp + 64, hc], srcS, w=[S0b_b])
                    pool(lambda h: h.memset(qSb_, 0.0), w=[qSb_b])
                    dve(lambda h: h.tensor_tensor(out=Vbig[0:64], in0=Vh[0:64, :].unsqueeze(1).to_broadcast([64, 16, 256]), in1=C("mind")[0:64, :].unsqueeze(2).to_broadcast([64, 16, 256]), op=ALU.mult), ["Vh", "ct"], [Vbg_b])
                    bO, kO = bank()
                    for hc in range(2):
                        for hp in range(2):
                            rw = slice(64 * hp, 64 * hp + 64)
                            dve(lambda h, hc=hc, hp=hp, rw=rw: h.tensor_copy(out=bass.AP(WA, 64 * hp * PS_ + 16640 + hp * 1024, [[PS_, 64], [68, 16], [1, 4]]), in_=v3(qSZ[rw, hc, 0, hp, :])), ["qSZ"], [qSb_b])
                        for hp in range(2):
                            hh = hc * 2 + hp
                            for b_ in range(16):
                                pe(lambda h, hh=hh, hc=hc, hp=hp, b_=b_: h.matmul(bO[0:64, hh * 64:(hh + 1) * 64], lhsT=qSbig[:, hp, b_, :], rhs=S0b[:, hc, b_, :], start=(b_ == 0), stop=False), [qSb_b, S0b_b], [kO])
                            pe(lambda h, hh=hh: h.matmul(bO[0:64, hh * 64:(hh + 1) * 64], lhsT=AtZ[:, 0, hh, :], rhs=Vh[:, hh * 64:(hh + 1) * 64], start=False, stop=True), ["AtZ", "Vh"], [kO])
                    hgrn_out(bO, kO)
                    for hc in range(2):
                        for bg_ in range(4):
                            bU, kU = bank()
                            pe(lambda h, hc=hc, bg_=bg_, bU=bU: h.matmul(bU[:, 0:512].rearrange("p (b c) -> p b c", c=128), lhsT=kAtokZ[0:64, 0, hc * 128:(hc + 1) * 128], rhs=Vbig[0:64, bg_ * 4:(bg_ + 1) * 4, hc * 128:(hc + 1) * 128], start=True, stop=True), ["kAtokZ", Vbg_b], [kU])
                            for hp in range(2):
                                rw = slice(64 * hp, 64 * hp + 64); b4 = slice(bg_ * 4, bg_ * 4 + 4)
                                dve(lambda h, hc=hc, rw=rw, b4=b4: h.tensor_tensor(out=S0f[rw, hc, b4, :], in0=S0f[rw, hc, b4, :], in1=ECLs[rw, hc, b4].unsqueeze(2).to_broadcast([64, 4, 64]), op=ALU.mult), [S0f_b, "ECLs"], [S0f_b])
                                dve(lambda h, hc=hc, hp=hp, rw=rw, b4=b4, bU=bU: h.tensor_tensor(out=S0f[rw, hc, b4, :], in0=S0f[rw, hc, b4, :], in1=bU[rw, 0:512].rearrange("p (b c) -> p b c", c=128)[:, :, hp * 64:(hp + 1) * 64], op=ALU.add), [S0f_b, kU], [S0f_b])
                    for hc in range(2):
                        for hp in range(2):
                            dsp(ohs[l, :, hc * 2 + hp].rearrange("b k v -> k b v"), S0f[64 * hp:64 * hp + 64, hc], [S0f_b])
                def post():
                    if dbg:
                        for c in range(8):
                            dpl(dbgo[c * 128:(c + 1) * 128, c0:c0 + n], mixT[:, c, 0:n], [mk])
                    for nh in range(2):
                        bo_h, bok_h = bank()
                        for k in range(8):
                            pe(lambda h, k=k, nh=nh, bo_h=bo_h: h.matmul(bo_h[0:n, :], lhsT=mixT[:, k, 0:n], rhs=W_o[1][:, k * 1024 + nh * 512:k * 1024 + (nh + 1) * 512], start=(k == 0), stop=(k == 7)), [mk, W_o[0]], [bok_h])
                        dve(lambda h, nh=nh, bo_h=bo_h: h.scalar_tensor_tensor(out=x_tok[0:n, t, nh * 512:(nh + 1) * 512], in0=x_tok[0:n, t, nh * 512:(nh + 1) * 512], scalar=ALPHA, in1=bo_h[0:n, :], op0=ALU.mult, op1=ALU.add), [f"x{t}", bok_h], [f"x{t}"])
                    ln_tile(t, 1 + 3 * l, None)
                return post
            prev = None
            for t in (range(NT) if '1' in DBGP else []):
                prev = p1(t, prev)
            if prev:
                prev()
            dpl(W_xq[1].rearrange("p (a b) -> p a b", b=256), wxq[l], w=[W_xq[0]])
            dpl(W_xo[1].rearrange("p (a b) -> p a b", b=1024), wxo[l], w=[W_xo[0]])
            dpl(W_xk[1].rearrange("p (a b) -> p a b", b=256), wxk[l], w=[W_xk[0]])
            dpl(W_xv[1].rearrange("p (a b) -> p a b", b=256), wxv[l], w=[W_xv[0]])
            load_ln(2 + 3 * l)
            for mt in range(2):
                for hf in range(2):
                    dsp(tmpS[:], memp[mt * 128:(mt + 1) * 128, hf * 512:(hf + 1) * 512], w=["tmpS"])
                    act(lambda h, hf=hf: h.copy(out=zbf[:, hf * 512:(hf + 1) * 512], in_=tmpS[:]), ["tmpS"], ["zbf"])
                tbm, tkm = tbank()
                for c in range(8):
                    pe(lambda h, c=c, tbm=tbm: h.transpose(tbm[:, c * 128:(c + 1) * 128], zbf[:, c * 128:(c + 1) * 128], identb[:]), ["zbf", "identb"], [tkm])
                dve(lambda h, mt=mt, tbm=tbm: h.tensor_copy(out=memT[:, :, mt * 128:(mt + 1) * 128], in_=tbm[:].rearrange("p (c m) -> p c m", m=128)), [tkm], [memTb])
            def p2m(mt):
                bk_, kk_ = bank()
                for k in range(8):
                    pe(lambda h, k=k, mt=mt, b=bk_: h.matmul(b[:, 0:256], lhsT=memT[:, k, mt * 128:(mt + 1) * 128], rhs=W_xk[1][:, k * 256:(k + 1) * 256], start=(k == 0), stop=(k == 7)), MEMT + [W_xk[0]], [kk_])
                for k in range(8):
                    pe(lambda h, k=k, mt=mt, b=bk_: h.matmul(b[:, 256:512], lhsT=memT[:, k, mt * 128:(mt + 1) * 128], rhs=W_xv[1][:, k * 256:(k + 1) * 256], start=(k == 0), stop=(k == 7)), MEMT + [W_xv[0]], [kk_])
                dve(lambda h, b=bk_: h.tensor_copy(out=tmpS[:, 0:512], in_=b[:, :]), [kk_], ["tmpS"])
                act(lambda h, b=bk_, mt=mt: h.copy(out=mva[:, mt, :, 0:64], in_=b[:, 256:512].rearrange("p (h d) -> p h d", d=64)), [kk_], ["mva"])
                dsp(omk[l, mt * 128:(mt + 1) * 128, :], tmpS[:, 0:256], ["tmpS"]); dsp(omv[l, mt * 128:(mt + 1) * 128, :], tmpS[:, 256:512], ["tmpS"])
            for mt in (range(2) if '2' in DBGP else []):
                p2m(mt)
            for hc in range(2):
                bk_, kk_ = bank()
                for k in range(8):
                    pe(lambda h, k=k, hc=hc, b=bk_: h.matmul(b[:, 0:256], lhsT=W_xk[1][:, k * 256 + hc * 128:k * 256 + (hc + 1) * 128], rhs=memT[:, k, :], start=(k == 0), stop=(k == 7)), MEMT + [W_xk[0]], [kk_])
                act(lambda h, hc=hc, b=bk_: h.copy(out=mkT[:, hc, :], in_=b[:, 0:256]), [kk_], ["kAtokZ"])
            qxT2 = hEQ[:].bitcast(BF16).rearrange("p (a b) -> p a b", b=128)
            PX2 = tmpS[:].bitcast(BF16).rearrange("p (a b c) -> p a b c", a=2, c=128)

            def p2(t):
                n = 128; c0 = t * 128
                par = t % 2
                qx = (qxT, qxT2)[par]; qk = (["kAb0", "kAb1"], ["hEQ", "hEQ"])[par]
                px = (PX, PX2)[par]; pk = (["PT0", "PT1"], ["tmpS", "tmpS"])[par]
                for hc in range(2):
                    bq, kq = bank()
                    for k in range(8):
                        pe(lambda h, k=k, hc=hc, bq=bq: h.matmul(bq[:, 0:n], lhsT=W_xq[1][:, k * 256 + hc * 128:k * 256 + (hc + 1) * 128], rhs=xT[:, k, c0:c0 + n], start=(k == 0), stop=(k == 7)), [f"xT{t}", W_xq[0]], [kq])
                    act(lambda h, hc=hc, bq=bq: h.copy(out=qx[:, hc, :], in_=bq[:, 0:n]), [kq], [qk[hc]])
                bsc = [bank(), bank()]
                for mt in range(2):
                    for hc in range(2):
                        for hp in range(2):
                            pe(lambda h, hc=hc, hp=hp, mt=mt: h.matmul(bsc[hp][0][:, (mt * 2 + hc) * 128:(mt * 2 + hc + 1) * 128], lhsT=mkT[64 * hp:64 * hp + 64, hc, mt * 128:(mt + 1) * 128], rhs=qx[64 * hp:64 * hp + 64, hc, :], start=True, stop=True), ["kAtokZ"] + qk, [bsc[hp][1]])
                for hp in range(2):
                    act(lambda h, hp=hp: h.activation(out=px[:, hp, :, :].rearrange("p a q -> p (a q)"), in_=bsc[hp][0][:, :], func=AF.Exp, scale=0.125), [bsc[hp][1]], [pk[hp]])

                def stage_b():
                    bo_, ko_ = bank()
                    for hh in range(4):
                        hc, hp = hh // 2, hh % 2
                        for mt in range(2):
                            pe(lambda h, hh=hh, hc=hc, hp=hp, mt=mt: h.matmul(bo_[:, hh * 65:(hh + 1) * 65], lhsT=px[:, hp, mt * 2 + hc, :], rhs=mva[:, mt, hh, :], start=(mt == 0), stop=(mt == 1)), [pk[hp], "mva"], [ko_])
                    ov = bo_[:, 0:260].rearrange("p (h e) -> p h e", e=65)
                    dve(lambda h: h.reciprocal(out=rec[:], in_=ov[:, :, 64]), [ko_], ["reca"])
                    dve(lambda h: h.tensor_tensor(out=oxt[:].rearrange("p (h d) -> p h d", d=64), in0=ov[:, :, 0:64], in1=rec[:].unsqueeze(2).to_broadcast([128, 4, 64]), op=ALU.mult), [ko_, "reca"], ["otok"])
                    tb, tk = tbank()
                    for hc in range(2):
                        pe(lambda h, hc=hc: h.transpose(tb[:, hc * 128:(hc + 1) * 128], oxt[:, hc * 128:(hc + 1) * 128], identb[:]), ["otok", "identb"], [tk])
                    dve(lambda h: h.tensor_copy(out=oxT[:].rearrange("p a b -> p (a b)"), in_=tb[:, 0:256]), [tk], ["otok2"])
                    bo = [bank(), bank()]
                    for nh in range(2):
                        for hc in range(2):
                            pe(lambda h, hc=hc, nh=nh: h.matmul(bo[nh][0][:, :], lhsT=oxT[:, hc, :], rhs=W_xo[1][:, hc * 1024 + nh * 512:hc * 1024 + (nh + 1) * 512], start=(hc == 0), stop=(hc == 1)), ["otok2", W_xo[0]], [bo[nh][1]])
                    ln_tile(t, 2 + 3 * l, bo)
                return stage_b
            prevB = None
            for t in (range(16) if '2' in DBGP else []):
                sb_ = p2(t)
                if prevB:
                    prevB()
                prevB = sb_
            if prevB:
                prevB()
            def p2s():
                t = 16; n = 64; c0 = 2048; PS_ = 28672
                v3 = lambda ap: ap.rearrange("p (b t) -> p b t", t=4)
                for hc in range(2):
                    bq, kq = bank()
                    for k_ in range(8):
                        pe(lambda h, k_=k_, hc=hc, bq=bq: h.matmul(bq[:, 0:n], lhsT=W_xq[1][:, k_ * 256 + hc * 128:k_ * 256 + (hc + 1) * 128], rhs=xT[:, k_, c0:c0 + n], start=(k_ == 0), stop=(k_ == 7)), ["xT16", W_xq[0]], [kq])
                    for hp in range(2):
                        act(lambda h, hc=hc, hp=hp, bq=bq: h.copy(out=qZ[64 * hp:64 * hp + 64, hc, hp, 0:n], in_=bq[64 * hp:64 * hp + 64, 0:n]), [kq], ["qZ"])
                def xhalf(hi_, half):
                    b0 = 8 * half
                    Kc_b, Kc_ = warena("xKc", 10240, 4096); Kc = Kc_.rearrange("p (b mt c) -> p b mt c", b=8, mt=2)
                    KcT_b, KcT_ = warena("xKcT", 14336, 4096); KcT = KcT_.rearrange("p (b hc m) -> p b hc m", b=8, hc=2)
                    Vx_b, Vx_ = warena("xVc", 18432, 4160); Vx = Vx_.rearrange("p (b mt hh e) -> p b mt hh e", b=8, mt=2, hh=4)
                    Px_b, Px_ = warena("xPb", 22592, 4096); Px = Px_.rearrange("p (mt hh b q) -> p mt hh b q", mt=2, hh=4, b=8)
                    pool(lambda h, Vx_=Vx_: h.memset(Vx_, 1.0), w=[Vx_b])
                    pool(lambda h, Px_=Px_: h.memset(Px_, 0.0), w=[Px_b])
                    for mt in range(2):
                        dpl(Kc[:, :, mt, :], cmk[l, b0:b0 + 8, mt * 128:(mt + 1) * 128, :].rearrange("b m c -> m b c"), w=[Kc_b])
                        for hh in range(4):
                            dpl(Vx[:, :, mt, hh, 0:64], cmv[l, b0:b0 + 8, mt * 128:(mt + 1) * 128, hh * 64:(hh + 1) * 64].rearrange("b m d -> m b d"), w=[Vx_b])
                    for bp in range(4):
                        tbk, tkk = tbank()
                        for j in range(8):
                            bl, hc, mt = bp * 2 + j // 4, (j // 2) % 2, j % 2
                            pe(lambda h, j=j, bl=bl, hc=hc, mt=mt, tbk=tbk, Kc=Kc: h.transpose(tbk[:, j * 128:(j + 1) * 128], Kc[:, bl, mt, hc * 128:(hc + 1) * 128], identb[:]), [Kc_b, "identb"], [tkk])
                        dve(lambda h, bp=bp, tbk=tbk, KcT_=KcT_: h.tensor_copy(out=KcT_[:, bp * 1024:(bp + 1) * 1024], in_=tbk[:, :]), [tkk], [KcT_b])
                    bs, ks = bank()
                    for bl in range(8):
                        for mt in range(2):
                            for hc in range(2):
                                pe(lambda h, bl=bl, mt=mt, hc=hc, KcT=KcT: h.matmul(bs[:, (mt * 2 + hc) * 64:(mt * 2 + hc + 1) * 64].rearrange("p (hp q) -> p hp q", hp=2)[:, :, bl * 4:bl * 4 + 4], lhsT=KcT[:, bl, hc, mt * 128:(mt + 1) * 128], rhs=qZ[:, hc, :, (b0 + bl) * 4:(b0 + bl) * 4 + 4], start=True, stop=True), [KcT_b, "qZ"], [ks])
                    for mt in range(2):
                        for hh in range(4):
                            act(lambda h, mt=mt, hh=hh, half=half: h.activation(out=bass.AP(WA, 22592 + (mt * 4 + hh) * 512 + half * 32, [[PS_, 128], [68, 8], [1, 4]]), in_=v3(bs[:, mt * 128 + hh * 32:mt * 128 + hh * 32 + 32]), func=AF.Exp, scale=0.125), [ks], [Px_b])
                    bo_, ko_ = bank()
                    for hh in range(4):
                        for bl in range(8):
                            for mt in range(2):
                                pe(lambda h, hh=hh, bl=bl, mt=mt, Px=Px, Vx=Vx: h.matmul(bo_[0:64, hh * 65:(hh + 1) * 65], lhsT=Px[:, mt, hh, bl, :], rhs=Vx[:, bl, mt, hh, :], start=(bl == 0 and mt == 0), stop=(bl == 7 and mt == 1)), [Px_b, Vx_b], [ko_])
                    if hi_ == 0:
                        dve(lambda h, bo_=bo_: h.tensor_copy(out=tmpS[0:64, 0:260], in_=bo_[0:64, 0:260]), [ko_], ["tmpS"])
                    else:
                        dve(lambda h, bo_=bo_: h.tensor_tensor(out=tmpS[0:64, 0:260], in0=tmpS[0:64, 0:260], in1=bo_[0:64, 0:260], op=ALU.add), [ko_, "tmpS"], ["tmpS"])
                for hi_, half in enumerate((0, 1)):
                    xhalf(hi_, half)
                ov = tmpS[0:64, 0:260].rearrange("p (h e) -> p h e", e=65)
                dve(lambda h: h.reciprocal(out=reca[0:64], in_=ov[:, :, 64]), ["tmpS"], ["reca"])
                dve(lambda h: h.tensor_tensor(out=otok[0:64].rearrange("p (h d) -> p h d", d=64), in0=ov[:, :, 0:64], in1=reca[0:64].unsqueeze(2).to_broadcast([64, 4, 64]), op=ALU.mult), ["tmpS", "reca"], ["otok"])
                if dbg:
                    dpl(dbgo[0:64, 0:256], otok[0:64, 0:256], ["otok"])
                tb, tk = tbank()
                for hc in range(2):
                    pe(lambda h, hc=hc: h.transpose(tb[:, hc * 128:hc * 128 + 64], otok[0:64, hc * 128:(hc + 1) * 128], identb[0:64, 0:64]), ["otok", "identb"], [tk])
                dve(lambda h: h.tensor_copy(out=oxT[:, :, 0:64], in_=tb[:, 0:256].rearrange("p (a b) -> p a b", b=128)[:, :, 0:64]), [tk], ["otok2"])
                bo = [bank(), bank()]
                for nh in range(2):
                    for hc in range(2):
                        pe(lambda h, hc=hc, nh=nh: h.matmul(bo[nh][0][0:64, :], lhsT=oxT[:, hc, 0:64], rhs=W_xo[1][:, hc * 1024 + nh * 512:hc * 1024 + (nh + 1) * 512], start=(hc == 0), stop=(hc == 1)), ["otok2", W_xo[0]], [bo[nh][1]])
                ln_tile(16, 2 + 3 * l, bo)
            if '2' in DBGP:
                p2s()
            load_ln(3 + 3 * l)
            def p3(j):
                r0 = ring[j % 4]
                Wg_ = warena(f"fg{j % 4}", r0, 2048); Wu_ = warena(f"fu{j % 4}", r0 + 2048, 2048); Wd_ = warena(f"fd{j % 4}", r0 + 4096, 2048)
                dpl(Wg_[1].rearrange("p (a b) -> p a b", b=256), wg[l, :, :, j * 256:(j + 1) * 256], w=[Wg_[0]])
                dpl(Wu_[1].rearrange("p (a b) -> p a b", b=256), wu[l, :, :, j * 256:(j + 1) * 256], w=[Wu_[0]])
                dpl(Wd_[1].rearrange("p (a b) -> p a b", b=1024), wd[l, :, 2 * j:2 * j + 2, :], w=[Wd_[0]])
                def p3t(t):
                    n = ntok(t); c0 = t * 128
                    for cc in range(2):
                        bg, kg = bank(); bu, ku = bank()
                        for k in range(8):
                            pe(lambda h, k=k, cc=cc, bg=bg: h.matmul(bg[:, 0:n], lhsT=Wg_[1][:, k * 256 + cc * 128:k * 256 + (cc + 1) * 128], rhs=xT[:, k, c0:c0 + n], start=(k == 0), stop=(k == 7)), [f"xT{t}", Wg_[0]], [kg])
                        for k in range(8):
                            pe(lambda h, k=k, cc=cc, bu=bu: h.matmul(bu[:, 0:n], lhsT=Wu_[1][:, k * 256 + cc * 128:k * 256 + (cc + 1) * 128], rhs=xT[:, k, c0:c0 + n], start=(k == 0), stop=(k == 7)), [f"xT{t}", Wu_[0]], [ku])
                        sg_, sgk = ((hE, "hE"), (hCU, "hCU"), (hEC, "hEC"), (hEN, "hEN"))[(t % 2) * 2 + cc]
                        hT_, hTk = ((hT, "hL1"), (hT2, "hL2"))[t % 2]
                        act(lambda h, bg=bg, sg_=sg_: h.activation(out=sg_[:, 0:n], in_=bg[:, 0:n], func=AF.Silu), [kg], [sgk])
                        dve(lambda h, bu=bu, cc=cc, sg_=sg_, hT_=hT_: h.tensor_tensor(out=hT_[:, cc, 0:n], in0=sg_[:, 0:n], in1=bu[:, 0:n], op=ALU.mult), [sgk, ku], [hTk])
                    def down():
                        bo = [bank(), bank()]
                        for nh in range(2):
                            for cc in range(2):
                                pe(lambda h, cc=cc, nh=nh: h.matmul(bo[nh][0][0:n, :], lhsT=hT_[:, cc, 0:n], rhs=Wd_[1][:, cc * 1024 + nh * 512:cc * 1024 + (nh + 1) * 512], start=(cc == 0), stop=(cc == 1)), [hTk, Wd_[0]], [bo[nh][1]])
                        ln_tile(t, (3 + 3 * l) if j == 10 else None, bo, first=(j == 0))
                    return down
                prevD = None
                for t in range(NT):
                    d_ = p3t(t)
                    if prevD:
                        prevD()
                    prevD = d_
                prevD()
            for j in (range(11) if '3' in DBGP else []):
                p3(j)
        for l_ in range(nl):
            layer(l_)
        for t in range(16):
            dsp(yp[t * 128:(t + 1) * 128, :], x_tok[:, t, :], [f"x{t}"])
        dsp(ys[:, :], x_tok[0:64, 16, :], ["x16"])
        S.final()
        S.emit(nc, sems, dsems, block)
    return nc


def _host_layout(inp, nl=4, cores=range(8)):
    f = lambda a: np.ascontiguousarray(np.asarray(a, dtype=np.float32))
    pk = lambda w: f(np.asarray(w)[:nl].reshape(nl, -1, 128, np.asarray(w).shape[-1]).transpose(0, 2, 1, 3))
    sh = {}
    sh["win"] = pk(np.asarray(inp["w_in"])[:, :, _perm()])
    sh["wo"] = pk(inp["w_o"]); sh["wxq"] = pk(inp["w_xq"]); sh["wxk"] = pk(inp["w_xk"]); sh["wxv"] = pk(inp["w_xv"])
    sh["wxo"] = pk(inp["w_xo"]); sh["wg"] = pk(inp["w_gate"]); sh["wu"] = pk(inp["w_up"]); sh["wd"] = pk(inp["w_down"])
    gs = [inp["emb_ln_g"]]; bs = [inp["emb_ln_b"]]
    for l in range(4):
        for k in (1, 2, 3):
            gs.append(np.asarray(inp[f"ln{k}_g"])[l]); bs.append(np.asarray(inp[f"ln{k}_b"])[l])
    rows = np.stack([np.stack([np.asarray(g), np.asarray(b)]) for g, b in zip(gs, bs)])
    sh["lnrows"] = f(rows)
    sh["lncols"] = f(rows.reshape(13, 2, 8, 128).transpose(3, 0, 1, 2).reshape(128, 13 * 16))
    sh["sink"] = f(np.asarray(inp["attn_sink"]).reshape(1, 16))
    sh["hlb"] = f(np.asarray(inp["hgrn_lb"]).reshape(4, 2, 128).transpose(2, 1, 0).reshape(128, 8))
    sh["hng"] = f(inp["hgrn_norm_g"])
    sh["cw"] = f(np.asarray(inp["conv_w"]).reshape(4, 3, 2, 128).transpose(3, 0, 2, 1).reshape(128, 24))
    sh["pw"] = f(inp["pool_w"])
    sh["psc"] = f(np.asarray(inp["pool_scale"]).reshape(4, 2, 128).transpose(2, 0, 1).reshape(128, 8))
    sh["ctab"] = _CT; sh["stab"] = _STAB
    maps = []
    for c in cores:
        m = dict(sh)
        b = slice(16 * c, 16 * c + 16)
        m["xp"] = f(inp["x_prompt"][c]); m["xs"] = f(np.asarray(inp["x_sample"])[b].reshape(64, D))
        m["ck"] = f(np.asarray(inp["cache_swa_k"])[:nl, b].reshape(nl, 16, 128, 128))
        m["cv"] = f(np.asarray(inp["cache_swa_v"])[:nl, b].reshape(nl, 16, 128, 128))
        m["sh"] = f(np.asarray(inp["state_hgrn"])[:nl, b])
        m["scv"] = f(np.asarray(inp["state_conv"])[:nl, b].reshape(nl, 32, 256))
        m["spl"] = f(np.asarray(inp["state_pool"])[:nl, b])
        m["cmk"] = f(np.asarray(inp["cache_mem_k"])[:nl, b].reshape(nl, 16, 256, 256))
        m["cmv"] = f(np.asarray(inp["cache_mem_v"])[:nl, b].reshape(nl, 16, 256, 256))
        m["memp"] = f(inp["mem_prompt"][c])
        maps.append(m)
    return maps


_NC = None


def kernel(**inputs):
    global _NC
    if _NC is None:
        _NC = build()
    maps = _host_layout(inputs, 4, range(8))
    res = run_bass_kernel_spmd(_NC, maps, core_ids=list(range(8))).results
    st = lambda k: np.stack([r[k] for r in res])
    cat = lambda k: np.concatenate([r[k] for r in res], 1)
    return (st("yp"), np.concatenate([r["ys"].reshape(16, 4, D) for r in res], 0),
            st("okp").transpose(1, 0, 2, 3).reshape(4, 8, 128, 2, 64), st("ovp").transpose(1, 0, 2, 3).reshape(4, 8, 128, 2, 64),
            st("ohp").transpose(1, 0, 2, 3, 4), st("ocp").transpose(1, 0, 2, 3), st("opp").transpose(1, 0, 2, 3),
            st("omk").transpose(1, 0, 2, 3).reshape(4, 8, 256, 4, 64), st("omv").transpose(1, 0, 2, 3).reshape(4, 8, 256, 4, 64),
            cat("oks").reshape(4, 128, 128, 2, 64), cat("ovs").reshape(4, 128, 128, 2, 64), cat("ohs"), cat("ocs"), cat("ops"))
```

```python
import numpy as np
import concourse.bass as bass
import concourse.mybir as mybir

F32 = mybir.dt.float32
BF16 = mybir.dt.bfloat16
ALU = mybir.AluOpType
AF = mybir.ActivationFunctionType
AX = mybir.AxisListType

import os as _os
SERIAL = bool(_os.environ.get('SERIAL'))
NOSAME = bool(_os.environ.get('NOSAME'))
NS = 8
ND = 12


class Buf:
    __slots__ = ("lo", "hi", "name")

    def __init__(self, name, lo, hi):
        self.name, self.lo, self.hi = name, lo, hi


class _Op:
    __slots__ = ("eng", "fn", "idx", "is_dma", "waits", "needs_inc", "seq", "dsem", "dval")

    def __init__(self, eng, fn, is_dma):
        self.eng = eng
        self.fn = fn
        self.is_dma = is_dma
        self.waits = []
        self.needs_inc = False
        self.seq = -1
        self.idx = -1
        self.dsem = -1
        self.dval = 0


class Sched:
    ENGS = ("pe", "act", "dve", "pool", "sp")

    def __init__(self):
        self.ops = {e: [] for e in self.ENGS}
        self.last_w = {}
        self.readers = {}
        self.live = []
        self.known = {e: {f: -1 for f in self.ENGS} for e in self.ENGS}
        self.known_dma = {e: set() for e in self.ENGS}
        self.dma_count = {"sp": 0, "pool": 0}
        self.dma_hist = {"sp": {}, "pool": {}}

    def _overl(self, b):
        return [r for r in self.live if r is not b and r.lo < b.hi and b.lo < r.hi]

    def op(self, eng, fn, reads=(), writes=(), dma=False):
        o = _Op(eng, fn, dma)
        o.idx = len(self.ops[eng])
        deps = {}
        for k in reads:
            w = self.last_w.get(k)
            if w is not None:
                deps[w] = True
            if isinstance(k, str) and k[:2] in ("ps", "pT"):
                for r in self.readers.get(k, ()):
                    if r.eng != eng and r not in deps:
                        deps[r] = False
            if isinstance(k, Buf):
                for r in self._overl(k):
                    w = self.last_w.get(r)
                    if w is not None:
                        deps[w] = True
        for k in writes:
            ks = [k] + (self._overl(k) if isinstance(k, Buf) else [])
            for kk in ks:
                w = self.last_w.get(kk)
                if w is not None:
                    deps[w] = True
                for r in self.readers.get(kk, ()):
                    if r not in deps:
                        deps[r] = False
        if SERIAL:
            for e2 in self.ENGS:
                if self.ops[e2]:
                    deps.setdefault(self.ops[e2][-1], True)
        if dma:
            q = eng
            j = self.dma_count[q]
            self.dma_count[q] += 1
            o.dsem = j % ND
            o.dval = 16 * (j // ND + 1)
            prev = self.dma_hist[q].get(o.dsem)
            if prev is not None:
                deps.setdefault(prev, True)
            self.dma_hist[q][o.dsem] = o
            o.needs_inc = True
        kn = self.known[eng]
        for d, hard in deps.items():
            if d is o:
                continue
            if d.is_dma:
                if d in self.known_dma[eng]:
                    continue
                self.known_dma[eng].add(d)
                o.waits.append(d)
                continue
            if d.eng == eng and (not hard or eng == "pe" or NOSAME):
                continue
            if d.idx <= kn[d.eng]:
                continue
            kn[d.eng] = d.idx
            d.needs_inc = True
            o.waits.append(d)
        for k in reads:
            self.readers.setdefault(k, []).append(o)
            if isinstance(k, Buf) and k not in self.live:
                self.live.append(k)
        for k in writes:
            if isinstance(k, Buf):
                for r in self._overl(k):
                    self.live.remove(r)
                    self.last_w.pop(r, None)
                    self.readers.pop(r, None)
                if k not in self.live:
                    self.live.append(k)
            self.last_w[k] = o
            self.readers[k] = []
        self.ops[eng].append(o)
        return o

    def final(self):
        o = _Op("sp", lambda h: h.nop(), False)
        o.idx = len(self.ops["sp"])
        for q in self.dma_hist:
            for d in self.dma_hist[q].values():
                o.waits.append(d)
        self.ops["sp"].append(o)

    def emit(self, nc, sems, dsems, block):
        for e in self.ENGS:
            m = 0
            for o in self.ops[e]:
                if o.needs_inc and not o.is_dma:
                    o.seq = m
                    m += 1

        def run(e, h):
            for o in self.ops[e]:
                for d in o.waits:
                    if d.is_dma:
                        h.wait_ge(dsems[d.eng][d.dsem], d.dval)
                    else:
                        h.wait_ge(sems[d.eng][d.seq % NS], d.seq // NS + 1)
                ins = o.fn(h)
                if o.needs_inc:
                    if o.is_dma:
                        ins.then_inc(dsems[e][o.dsem], 16)
                    else:
                        ins.then_inc(sems[e][o.seq % NS], 1)

        @block.tensor
        def _(h):
            run("pe", h)

        @block.scalar
        def _(h):
            run("act", h)

        @block.vector
        def _(h):
            run("dve", h)

        @block.gpsimd
        def _(h):
            run("pool", h)

        @block.sync
        def _(h):
            run("sp", h)
from contextlib import ExitStack
import threading
from concourse.bass_utils import run_bass_kernel_spmd

import os
DBG_SKIP = bool(os.environ.get('DBG_SKIP'))
DBGP = os.environ.get('DBGP', '123')
D = 1024; DEPTH = 4; NT = 17; TS = 64; DFF = 2816
ALPHA = (2 * DEPTH) ** 0.25
LN_EPS = 1e-5; RMS_EPS = 1e-6
cA, cB, cBQ, cBF, cCB, cK, cV, cI, cG, cCC, cCH, cDV = 0, 128, 256, 512, 768, 1024, 1152, 1280, 1536, 1792, 2048, 2304


def _perm():
    q = np.arange(256).reshape(2, 2, 64)
    qa = q[:, 0, :].reshape(-1); qb = q[:, 1, :].reshape(-1)
    r = lambda a, b: np.arange(a, b)
    return np.concatenate([qa, qb, r(512, 768), r(768, 1024), r(1536, 1792), r(256, 384), r(384, 512),
                           r(1024, 1280), r(1280, 1536), r(1792, 2048), r(2048, 2304), r(2304, 2560)])


def _consts():
    c = {}
    s = np.arange(128)[:, None]; q = np.arange(128)[None, :]
    slopes = [2.0 ** (-2.0 * (h + 1)) for h in range(4)]
    NEG = -30000.0
    bt = np.zeros((128, 2, 2, 2, 128), np.float32)
    bsc = np.zeros((128, 2, 2, 64), np.float32)
    bsn = np.full((128, 2, 2, 64), NEG, np.float32)
    for g in range(2):
        for kv in range(2):
            sl = slopes[kv * 2 + g]
            bt[:, 0, g, kv, :] = np.where(s <= q, -sl * (q - s), NEG)
            bt[:, 1, g, kv, :] = np.where(s >= q, -sl * (128 + q - s), NEG)
            for b in range(16):
                for t in range(4):
                    col = b * 4 + t
                    bsc[:, g, kv, col] = np.where(np.arange(128) >= t, -sl * (t + 128 - np.arange(128)), NEG)
                    for t2 in range(t + 1):
                        bsn[b * 4 + t2, g, kv, col] = -sl * (t - t2)
    c["bt"] = bt.reshape(128, 1024)
    global _STAB
    _STAB = np.ascontiguousarray(np.concatenate([bsc.reshape(128, 256), bsn.reshape(128, 256)], 1))
    tri = (np.arange(64)[:, None] <= np.arange(64)[None, :]).astype(np.float32)
    c["tri2"] = np.concatenate([tri, tri], 0)
    bdc = np.zeros((128, 64), np.float32); mind = np.zeros((128, 16), np.float32)
    for b in range(16):
        mind[b * 4:(b + 1) * 4, b] = 1.0
        for t in range(4):
            bdc[b * 4:b * 4 + t + 1, b * 4 + t] = 1.0
    c["bdc"] = bdc; c["mind"] = mind
    ic = np.zeros((128, 2, 16), np.float32)
    for hc in range(2):
        for half in range(2):
            w = (2, 4, 8, 16)[hc * 2 + half]
            ic[half * 64:(half + 1) * 64, hc, :] = 1.0 / np.minimum(np.arange(16) + 1, w)
    c["icnt"] = ic.reshape(128, 32)
    c["ident"] = np.eye(128, dtype=np.float32)
    names = list(c.keys()); offs = {}; o = 0
    for n in names:
        offs[n] = (o, c[n].shape[1]); o += c[n].shape[1]
    return np.concatenate([c[n] for n in names], 1), offs


_STAB = None
_CT, _CO = _consts()


def build(nl=DEPTH, dbg=False):
    nc = bass.Bass("TRN2", target_bir_lowering=False)
    I = lambda n, s: nc.dram_tensor(n, list(s), F32, kind="ExternalInput").ap()
    O = lambda n, s: nc.dram_tensor(n, list(s), F32, kind="ExternalOutput").ap()
    xp = I("xp", (2048, D)); xs = I("xs", (64, D))
    ck = I("ck", (nl, 16, 128, 128)); cv = I("cv", (nl, 16, 128, 128))
    sh = I("sh", (nl, 16, 4, 64, 64)); scv = I("scv", (nl, 32, 256)); spl = I("spl", (nl, 16, 15, 256))
    cmk = I("cmk", (nl, 16, 256, 256)); cmv = I("cmv", (nl, 16, 256, 256)); memp = I("memp", (256, D))
    win = I("win", (nl, 128, 8, 2560)); wo = I("wo", (nl, 128, 8, 1024))
    wxq = I("wxq", (nl, 128, 8, 256)); wxk = I("wxk", (nl, 128, 8, 256)); wxv = I("wxv", (nl, 128, 8, 256))
    wxo = I("wxo", (nl, 128, 2, 1024)); wg = I("wg", (nl, 128, 8, DFF)); wu = I("wu", (nl, 128, 8, DFF))
    wd = I("wd", (nl, 128, 22, 1024))
    lncols = I("lncols", (128, 13 * 16)); lnrows = I("lnrows", (13, 2, 1024))
    sink = I("sink", (1, 16)); hlb = I("hlb", (128, 8)); hng = I("hng", (4, 256))
    cw = I("cw", (128, 24)); pw = I("pw", (4, 4, 64, 64)); psc = I("psc", (128, 8))
    ctab = I("ctab", _CT.shape); stab = I("stab", (128, 512))
    yp = O("yp", (2048, D)); ys = O("ys", (64, D))
    okp = O("okp", (nl, 128, 128)); ovp = O("ovp", (nl, 128, 128)); ohp = O("ohp", (nl, 4, 64, 64))
    ocp = O("ocp", (nl, 2, 256)); opp = O("opp", (nl, 15, 256)); omk = O("omk", (nl, 256, 256)); omv = O("omv", (nl, 256, 256))
    oks = O("oks", (nl, 16, 128, 128)); ovs = O("ovs", (nl, 16, 128, 128)); ohs = O("ohs", (nl, 16, 4, 64, 64))
    ocs = O("ocs", (nl, 16, 2, 256)); ops_ = O("ops", (nl, 16, 15, 256))
    dbgo = O("dbgo", (1024, 2112)) if dbg else None
    dbg2 = None
    S = Sched()
    es = ExitStack()
    with es:
        def sb(name, shape, dt=F32):
            return es.enter_context(nc.sbuf_tensor(name, list(shape), dt))
        x_tok = sb("x_tok", (128, NT, D)); xT = sb("xT", (128, 8, 2112), BF16)
        WA = sb("WA", (128, 28672), BF16)
        WK = None
        ct = sb("ct", (128, _CT.shape[1])); identb = sb("identb", (128, 128), BF16)
        lnc = sb("lnc", (128, 13, 2, 8)); Grow = sb("Grow", (128, D)); Brow = sb("Brow", (128, D))
        esink = sb("esink", (128, 16)); lbt = sb("lbt", (128, 2, 4)); oml = sb("oml", (128, 2, 4)); lbe = sb("lbe", (128, 2, 4)); lbs = sb("lbs", (128, 2, 1))
        normg = sb("normg", (128, 256)); cwt = sb("cwt", (128, 4, 2, 3)); psct = sb("psct", (128, 4, 2)); wbd = sb("wbd", (128, 2, 128), BF16)
        ones = sb("ones", (128, 64))
        ps = [es.enter_context(nc.psum_tensor(f"ps{i}", [128, 512], F32)) for i in range(8)]
        sems = {e: [es.enter_context(nc.semaphore(f"s_{e}{i}")) for i in range(NS)] for e in Sched.ENGS}
        dsems = {e: [es.enter_context(nc.semaphore(f"d_{e}{i}")) for i in range(ND)] for e in ("sp", "pool")}
        block = es.enter_context(nc.Block())

        C = lambda n: ct[:, _CO[n][0]:_CO[n][0] + _CO[n][1]]
        ident = C("ident")

        tl = threading.local()
        pools = {"ALL": list(range(8)), "A": [0, 1], "B": [2, 3], "Bt": [4], "C": [5], "D": [6], "E": [7], "P0": [0, 1, 2, 3], "P1": [4, 5, 6, 7]}
        ppos = {kk: 0 for kk in pools}

        def _next(pname):
            lst = pools[pname]; i = lst[ppos[pname] % len(lst)]; ppos[pname] += 1
            return i

        def bank():
            i = _next(getattr(tl, "pool", "ALL"))
            return ps[i], f"ps{i}"

        def tbank():
            i = _next(getattr(tl, "tpool", None) or getattr(tl, "pool", "ALL"))
            return ps[i][:].bitcast(BF16), f"ps{i}"

        _w = os.environ.get("WTS", "A1B2C1D1E1")
        wts = {_w[i]: int(_w[i + 1]) for i in range(0, len(_w), 2)}

        def run_interleaved(funcs, pnames, tpnames=None):
            nf = len(funcs); turn = [0]; alive = [True] * nf; cv = threading.Condition(); errs = []

            def handoff(i):
                j = (i + 1) % nf; c_ = 0
                while not alive[j] and c_ < nf:
                    j = (j + 1) % nf; c_ += 1
                turn[0] = j; cv.notify_all()

            cnt = [0] * nf

            def step(i):
                cnt[i] += 1
                if cnt[i] % wts.get(pnames[i], 1) != 0:
                    return
                with cv:
                    handoff(i)
                    while turn[0] != i:
                        cv.wait()

            def worker(i):
                tl.pool = pnames[i]; tl.tpool = tpnames[i] if tpnames else None
                tl.step = lambda: step(i)
                with cv:
                    while turn[0] != i:
                        cv.wait()
                try:
                    funcs[i]()
                except BaseException as e:
                    errs.append(e)
                finally:
                    with cv:
                        alive[i] = False
                        handoff(i)
            ths = [threading.Thread(target=worker, args=(i,)) for i in range(nf)]
            for th in ths:
                th.start()
            for th in ths:
                th.join()
            if errs:
                raise errs[0]

        def _y():
            st_ = getattr(tl, "step", None)
            if st_:
                st_()

        def warena(name, lo, n):
            return Buf(name, 2 * lo, 2 * (lo + n)), WA[:, lo:lo + n]

        def work(name, lo, n, dt=F32):
            v = WK[:, lo:lo + n]
            if dt == BF16:
                v = v.bitcast(BF16)
            return Buf(name, 1000000 + 4 * lo, 1000000 + 4 * (lo + n)), v

        def _mk(eng):
            def f(fn, r=(), w=()):
                S.op(eng, fn, r, w); _y()
            return f
        dve = _mk("dve"); act = _mk("act"); pe = _mk("pe"); pool = _mk("pool")

        def dsp(o, i, r=(), w=()):
            S.op("sp", lambda h: h.dma_start(out=o, in_=i), r, w, dma=True); _y()

        def dpl(o, i, r=(), w=()):
            S.op("pool", lambda h: h.dma_start(out=o, in_=i), r, w, dma=True); _y()

        dsp(ct[:], ctab[:], w=["ct"])
        dsp(lnc[:].rearrange("p a b c -> p (a b c)"), lncols[:], w=["lnc"])
        dsp(cwt[:].rearrange("p a b c -> p (a b c)"), cw[:], w=["cwt"])
        dsp(psct[:].rearrange("p a b -> p (a b)"), psc[:], w=["psct"])
        dsp(lbt[:].rearrange("p a b -> p (a b)"), hlb[:], w=["lbt"])
        dsp(esink[:], sink[:].partition_broadcast(128), w=["esink"])
        dve(lambda h: h.tensor_copy(out=identb[:], in_=ident), ["ct"], ["identb"])
        dve(lambda h: h.memset(ones[:], 1.0), w=["ones"])
        act(lambda h: h.activation(out=esink[:], in_=esink[:], func=AF.Exp), ["esink"], ["esink"])
        act(lambda h: h.activation(out=lbe[:], in_=lbt[:], func=AF.Exp), ["lbt"], ["lbe"])
        dve(lambda h: h.tensor_reduce(out=lbs[:], in_=lbe[:], axis=AX.X, op=ALU.add), ["lbe"], ["lbs"])
        dve(lambda h: h.reciprocal(out=lbs[:], in_=lbs[:]), ["lbs"], ["lbs"])
        dve(lambda h: h.tensor_tensor(out=lbe[:], in0=lbe[:], in1=lbs[:].to_broadcast([128, 2, 4]), op=ALU.mult), ["lbe", "lbs"], ["lbe"])
        dve(lambda h: h.memset(lbt[:, :, 0:1], 0.0), ["lbe"], ["lbt"])
        for l in range(1, 4):
            dve(lambda h, l=l: h.tensor_tensor(out=lbt[:, :, l:l + 1], in0=lbt[:, :, l - 1:l], in1=lbe[:, :, l:l + 1], op=ALU.add), ["lbt", "lbe"], ["lbt"])
        dve(lambda h: h.tensor_scalar(out=oml[:], in0=lbt[:], scalar1=-1.0, scalar2=1.0, op0=ALU.mult, op1=ALU.add), ["lbt"], ["oml"])

        for t in range(16):
            dsp(x_tok[:, t, :], xp[t * 128:(t + 1) * 128, :], w=[f"x{t}"])
        dsp(x_tok[0:64, 16, :], xs[:, :], w=["x16"])

        def ntok(t):
            return 64 if t == 16 else 128

        lnst = sb("lnst", (128, 2, 6)); lnmv = sb("lnmv", (128, 2)); lnr = sb("lnr", (128, 2)); zbf = sb("zbf", (128, D), BF16)

        def load_ln(i):
            dsp(Grow[:], lnrows[i, 0:1, :].partition_broadcast(128), w=["Grow"])
            dsp(Brow[:], lnrows[i, 1:2, :].partition_broadcast(128), w=["Brow"])

        def ln_tile(t, i, banks, first=True):
            n = ntok(t); xk = f"x{t}"
            xt = x_tok[0:n, t, :]
            if banks is not None:
                for nh, (b, bk) in enumerate(banks):
                    if first:
                        dve(lambda h, b=b, nh=nh: h.scalar_tensor_tensor(out=xt[:, nh * 512:(nh + 1) * 512], in0=xt[:, nh * 512:(nh + 1) * 512], scalar=ALPHA, in1=b[0:n, :], op0=ALU.mult, op1=ALU.add), [xk, bk], [xk])
                    else:
                        dve(lambda h, b=b, nh=nh: h.tensor_tensor(out=xt[:, nh * 512:(nh + 1) * 512], in0=xt[:, nh * 512:(nh + 1) * 512], in1=b[0:n, :], op=ALU.add), [xk, bk], [xk])
            if i is None:
                return
            for a in range(2):
                dve(lambda h, a=a: h.bn_stats(out=lnst[0:n, a, :], in_=xt[:, a * 512:(a + 1) * 512]), [xk], ["lnst"])
            dve(lambda h: h.bn_aggr(out=lnmv[0:n], in_=lnst[0:n].rearrange("p a b -> p (a b)")), ["lnst"], ["lnmv"])
            act(lambda h: h.activation(out=lnr[0:n, 0:1], in_=lnmv[0:n, 1:2], func=AF.Ln, bias=LN_EPS, scale=1.0), ["lnmv"], ["lnr"])
            act(lambda h: h.activation(out=lnr[0:n, 0:1], in_=lnr[0:n, 0:1], func=AF.Exp, scale=-0.5), ["lnr"], ["lnr"])
            dve(lambda h: h.scalar_tensor_tensor(out=lnr[0:n, 1:2], in0=lnmv[0:n, 0:1], scalar=-1.0, in1=lnr[0:n, 0:1], op0=ALU.mult, op1=ALU.mult), ["lnr", "lnmv"], ["lnr2"])
            act(lambda h: h.activation(out=zbf[0:n], in_=xt, func=AF.Identity, scale=lnr[0:n, 0:1], bias=lnr[0:n, 1:2]), [xk, "lnr", "lnr2"], ["zbf"])
            dve(lambda h: h.tensor_scalar(out=xt, in0=xt, scalar1=lnr[0:n, 0:1], scalar2=lnr[0:n, 1:2], op0=ALU.mult, op1=ALU.add), [xk, "lnr", "lnr2"], [xk])
            dve(lambda h: h.tensor_tensor(out=xt, in0=xt, in1=Grow[0:n], op=ALU.mult), [xk, "Grow"], [xk])
            pool(lambda h: h.tensor_tensor(out=xt, in0=xt, in1=Brow[0:n], op=ALU.add), [xk, "Brow"], [xk])
            tb, tk = tbank()
            for c in range(8):
                pe(lambda h, c=c: h.transpose(tb[:, c * 128:c * 128 + n], zbf[0:n, c * 128:(c + 1) * 128], identb[0:n, 0:n]), ["zbf", "identb"], [tk])
            for c in range(8):
                if c % 2 == 0:
                    act(lambda h, c=c: h.activation(out=xT[:, c, t * 128:t * 128 + n], in_=tb[:, c * 128:c * 128 + n], func=AF.Identity, scale=lnc[:, i, 0, c:c + 1], bias=lnc[:, i, 1, c:c + 1]), [tk, "lnc"], [f"xT{t}"])
                else:
                    dve(lambda h, c=c: h.tensor_scalar(out=xT[:, c, t * 128:t * 128 + n], in0=tb[:, c * 128:c * 128 + n], scalar1=lnc[:, i, 0, c:c + 1], scalar2=lnc[:, i, 1, c:c + 1], op0=ALU.mult, op1=ALU.add), [tk, "lnc"], [f"xT{t}"])

        W_in = [warena(f"win{k}", k * 2560, 2560) for k in range(8)]
        W_o = warena("wo", 20480, 8192)
        for k in range(8):
            dpl(W_in[k][1], win[0, :, k, :], w=[W_in[k][0]])
        dpl(W_o[1].rearrange("p (a b) -> p a b", b=1024), wo[0], w=[W_o[0]])
        load_ln(0)
        for t in range(NT):
            ln_tile(t, 0, None)

        W_xq = warena("wxq", 0, 2048); W_xo = warena("wxo", 2048, 2048); W_xk = warena("wxk", 4096, 2048); W_xv = warena("wxv", 6144, 2048)
        ring = [8192, 14336, 20480, 0]

        mixTs = [sb("mixT0", (128, 8, 128), BF16), sb("mixT1", (128, 8, 128), BF16)]
        qZ = sb("qZ", (128, 2, 2, 128), BF16); kTr = sb("kTr", (128, 2, 128), BF16); Vaug = sb("Vaug", (128, 2, 2, 65), BF16)
        tmpS = sb("tmpS", (128, 512)); PT = sb("PT", (128, 2, 512), BF16); otok = sb("otok", (128, 256), BF16)
        den = sb("den", (128, 4)); reca = sb("reca", (128, 4))
        uext = sb("uext", (128, 2, 130)); chs = sb("chs", (128, 128)); ycv = sb("ycv", (128, 128))
        ext = sb("ext", (128, 2, 144)); s2 = sb("s2", (128, 144)); s4 = sb("s4", (128, 144)); s8 = sb("s8", (128, 144)); s16 = sb("s16", (128, 144))
        pooled = sb("pooled", (128, 2, 128), BF16)
        hE = sb("hE", (128, 128)); hL1 = sb("hL1", (128, 128)); hL2 = sb("hL2", (128, 128)); hCU = sb("hCU", (128, 128))
        hEC = sb("hEC", (128, 128)); hEN = sb("hEN", (128, 128)); hEQ = sb("hEQ", (128, 128))
        qSZ = sb("qSZ", (128, 2, 2, 2, 64), BF16); kAb = sb("kAb", (128, 2, 128), BF16); kAtokZ = sb("kAtokZ", (128, 2, 256), BF16)
        Vh = sb("Vh", (128, 256), BF16); Sst = sb("Sst", (128, 2, 64)); Sbf = sb("Sbf", (128, 2, 2, 64), BF16); ECL = sb("ECL", (128, 2, 2))
        AtZ = sb("AtZ", (128, 2, 4, 64), BF16); sqb = sb("sqb", (128, 256)); sgb = sb("sgb", (128, 256)); otok2 = sb("otok2", (128, 256), BF16)
        ssq = sb("ssq", (128, 4)); rsq = sb("rsq", (128, 4))
        pool(lambda h: h.memset(qSZ[:], 0.0), w=["qSZ"])
        pool(lambda h: h.memset(kAtokZ[:], 0.0), w=["kAtokZ"])
        pool(lambda h: h.memset(AtZ[:], 0.0), w=["AtZ"])
        ptmp = sb("ptmp", (128, 16)); uxs = sb("uxs", (128, 2, 16, 6)); exs = sb("exs", (128, 2, 16, 20)); ECLs = sb("ECLs", (128, 2, 16))
        dbgb = None
        print("SBUF bytes remaining:", nc.sbuf_bytes_remaining)
        pool(lambda h: h.memset(qZ[:], 0.0), w=["qZ"])
        pool(lambda h: h.memset(Vaug[:], 1.0), w=["Va0", "Va1"])
        mva = sb("mva", (128, 2, 4, 65), BF16)
        sgt = hE; hT = hL1[:].bitcast(BF16).rearrange("p (a b) -> p a b", b=128); hT2 = hL2[:].bitcast(BF16).rearrange("p (a b) -> p a b", b=128)
        mkT = kAtokZ; qxT = kAb; oxt = otok; oxT = otok2[:].rearrange("p (a b) -> p a b", b=128); rec = reca
        PX = PT[:].rearrange("p a (b c) -> p a b c", c=128)
        memTb, memT_ = warena("memT", 8192, 2048)
        memT = memT_.rearrange("p (c m) -> p c m", m=256)
        dve(lambda h: h.memset(mva[:], 1.0), w=["mva"])
        MEMT = [memTb]

        def layer(l):
            if l > 0:
                for k in range(8):
                    dpl(W_in[k][1], win[l, :, k, :], w=[W_in[k][0]])
                dpl(W_o[1].rearrange("p (a b) -> p a b", b=1024), wo[l], w=[W_o[0]])
            load_ln(1 + 3 * l)
            WIN = [w[0] for w in W_in]
            pool(lambda h: h.memset(Sst[:], 0.0), w=["Sst"])
            pool(lambda h: h.memset(exs[:], 0.0), w=["exs0", "exs1"])
            pool(lambda h: h.memset(kAtokZ[:], 0.0), w=["kAtokZ"])
            pool(lambda h: h.memset(uext[:, :, 0:2], 0.0), w=["uext0", "uext1"])
            pool(lambda h: h.memset(ext[:, :, 0:16], 0.0), w=["ext0", "ext1"])
            pool(lambda h: h.memset(wbd[:], 0.0), w=["wbd"])
            for g_ in range(4):
                dpl(wbd[64 * (g_ % 2):64 * (g_ % 2) + 64, g_ // 2, 64 * (g_ % 2):64 * (g_ % 2) + 64], pw[l, g_], w=["wbd"])
            dsp(normg[:], hng[l:l + 1, :].partition_broadcast(128), w=["normg"])
            def p1(t, prev_post):
                n = ntok(t); c0 = t * 128
                mixT = mixTs[t % 2]; mk = f"mixT{t % 2}"
                b1, k1 = bank()
                for k in range(8):
                    pe(lambda h, k=k, b1=b1: h.matmul(b1[0:n, :], lhsT=xT[:, k, c0:c0 + n], rhs=W_in[k][1][:, cK:cK + 512], start=(k == 0), stop=(k == 7)), [f"xT{t}"] + WIN, [k1])
                act(lambda h: h.copy(out=Vh[0:n], in_=b1[0:n, 256:512]), [k1], ["Vh"])
                act(lambda h: h.copy(out=Vaug[0:n, t % 2, :, 0:64], in_=b1[0:n, 128:256].rearrange("p (kv d) -> p kv d", d=64)), [k1], [f"Va{t % 2}"])
                if t >= 15:
                    dve(lambda h, b1=b1: h.tensor_copy(out=sqb[0:n, 0:256], in_=b1[0:n, 0:256]), [k1], ["sqb"])
                    if t == 15:
                        dsp(okp[l], sqb[:, 0:128], ["sqb"]); dsp(ovp[l], sqb[:, 128:256], ["sqb"])
                    elif not DBG_SKIP:
                        dsp(oks[l, :, 0:124, :], ck[l, :, 4:128, :]); dsp(ovs[l, :, 0:124, :], cv[l, :, 4:128, :])
                        for tt in range(4):
                            dsp(oks[l, :, 124 + tt, :], sqb[tt:64:4, 0:128], ["sqb"]); dsp(ovs[l, :, 124 + tt, :], sqb[tt:64:4, 128:256], ["sqb"])
                    b2, k2 = bank(); b3, k3 = bank()
                    for k in range(8):
                        pe(lambda h, k=k, b2=b2: h.matmul(b2[0:n, :], lhsT=xT[:, k, c0:c0 + n], rhs=W_in[k][1][:, cCC:cCC + 512], start=(k == 0), stop=(k == 7)), [f"xT{t}"] + WIN, [k2])
                    for k in range(8):
                        pe(lambda h, k=k, b3=b3: h.matmul(b3[0:n, 0:256], lhsT=xT[:, k, c0:c0 + n], rhs=W_in[k][1][:, cDV:cDV + 256], start=(k == 0), stop=(k == 7)), [f"xT{t}"] + WIN, [k3])
                    act(lambda h, b2=b2: h.copy(out=sgb[0:n, :], in_=b2[0:n, 256:512]), [k2], ["sgb"])
                    dve(lambda h, b2=b2: h.tensor_tensor(out=sgb[0:n, :], in0=b2[0:n, 0:256], in1=sgb[0:n, :], op=ALU.mult), [k2, "sgb"], ["sgb"])
                    act(lambda h, b3=b3: h.copy(out=tmpS[0:n, 0:256], in_=b3[0:n, 0:256]), [k3], ["tmpS"])
                    if t == 15:
                        dsp(ocp[l], sgb[126:128, :], ["sgb"]); dsp(opp[l], tmpS[113:128, 0:256], ["tmpS"])
                    elif not DBG_SKIP:
                        dsp(ops_[l, :, 0:11, :], spl[l, :, 4:15, :])
                        for tt in range(4):
                            dsp(ops_[l, :, 11 + tt, :], tmpS[tt:64:4, 0:256], ["tmpS"])
                            if tt >= 2:
                                dsp(ocs[l, :, tt - 2, :], sgb[tt:64:4, :], ["sgb"])
                def fm(col):
                    b, bk = bank()
                    for k in range(8):
                        pe(lambda h, k=k: h.matmul(b[:, 0:n], lhsT=W_in[k][1][:, col:col + 128], rhs=xT[:, k, c0:c0 + n], start=(k == 0), stop=(k == 7)), [f"xT{t}"] + WIN, [bk])
                    return b, bk
                sl = t % 2
                v3 = lambda ap: ap.rearrange("p (b t) -> p b t", t=4)

                def attn_out(bo_, ko_):
                    ov = bo_[0:n, 0:260].rearrange("p (h e) -> p h e", e=65)
                    dve(lambda h: h.tensor_tensor(out=den[0:n], in0=ov[:, :, 64], in1=esink[0:n, l * 4:(l + 1) * 4], op=ALU.add), [ko_, "esink"], ["den"])
                    dve(lambda h: h.reciprocal(out=reca[0:n], in_=den[0:n]), ["den"], ["reca"])
                    dve(lambda h: h.tensor_tensor(out=otok[0:n].rearrange("p (h d) -> p h d", d=64), in0=ov[:, :, 0:64], in1=reca[0:n].unsqueeze(2).to_broadcast([n, 4, 64]), op=ALU.mult), [ko_, "reca"], ["otok"])

                def hgrn_gate():
                    bg, kg = bank()
                    for k_ in range(8):
                        pe(lambda h, k_=k_: h.matmul(bg[0:n, 0:256], lhsT=xT[:, k_, c0:c0 + n], rhs=W_in[k_][1][:, cG:cG + 256], start=(k_ == 0), stop=(k_ == 7)), [f"xT{t}"] + WIN, [kg])
                    act(lambda h: h.activation(out=sgb[0:n], in_=bg[0:n, 0:256], func=AF.Exp, scale=-1.0), [kg], ["sgb"])
                    act(lambda h: h.activation(out=sgb[0:n], in_=sgb[0:n], func=AF.Ln, scale=1.0, bias=1.0), ["sgb"], ["sgb"])
                    act(lambda h: h.activation(out=sgb[0:n], in_=sgb[0:n], func=AF.Exp, scale=-1.0), ["sgb"], ["sgb"])
                    dve(lambda h: h.tensor_tensor(out=sgb[0:n], in0=bg[0:n, 0:256], in1=sgb[0:n], op=ALU.mult), [kg, "sgb"], ["sgb"])

                def hgrn_out(bO, kO):
                    act(lambda h: h.activation(out=sqb[0:n], in_=bO[0:n, 0:256], func=AF.Square), [kO], ["sqb"])
                    dve(lambda h: h.tensor_reduce(out=ssq[0:n], in_=sqb[0:n].rearrange("p (a v) -> p a v", v=64), axis=AX.X, op=ALU.add), ["sqb"], ["ssq"])
                    act(lambda h: h.activation(out=rsq[0:n], in_=ssq[0:n], func=AF.Ln, scale=1.0 / 64, bias=RMS_EPS), ["ssq"], ["rsq"])
                    act(lambda h: h.activation(out=rsq[0:n], in_=rsq[0:n], func=AF.Exp, scale=-0.5), ["rsq"], ["rsq"])
                    dve(lambda h: h.tensor_tensor(out=sqb[0:n].rearrange("p (a v) -> p a v", v=64), in0=bO[0:n, 0:256].rearrange("p (a v) -> p a v", v=64), in1=rsq[0:n].unsqueeze(2).to_broadcast([n, 4, 64]), op=ALU.mult), [kO, "rsq", "sqb"], ["sqb"])
                    dve(lambda h: h.tensor_tensor(out=sqb[0:n], in0=sqb[0:n], in1=normg[0:n], op=ALU.mult), ["sqb", "normg"], ["sqb"])
                    dve(lambda h: h.tensor_tensor(out=otok2[0:n], in0=sqb[0:n], in1=sgb[0:n], op=ALU.mult), ["sqb", "sgb"], ["otok2"])
                if t == 16:
                    dsp(sqb[0:32, :], scv[l], w=["sqb"])
                    for hc in range(2):
                        bT, kT_ = bank()
                        pe(lambda h, hc=hc, bT=bT: h.transpose(bT[:, 0:32], sqb[0:32, hc * 128:(hc + 1) * 128], ident[0:32, 0:32]), ["sqb", "ct"], [kT_])
                        act(lambda h, hc=hc, bT=bT: h.copy(out=uxs[:, hc, :, 0:2], in_=bT[:, 0:32].rearrange("p (b j) -> p b j", j=2)), [kT_], [f"uxs{hc}"])
                    for half in range(2):
                        dsp(sqb[0:120, :], spl[l, 8 * half:8 * half + 8].rearrange("b j c -> (b j) c"), w=["sqb"])
                        for hc in range(2):
                            bT, kT_ = bank()
                            pe(lambda h, hc=hc, bT=bT: h.transpose(bT[:, 0:120], sqb[0:120, hc * 128:(hc + 1) * 128], ident[0:120, 0:120]), ["sqb", "ct"], [kT_])
                            act(lambda h, hc=hc, bT=bT, half=half: h.copy(out=exs[:, hc, 8 * half:8 * half + 8, 1:16], in_=bT[:, 0:120].rearrange("p (b j) -> p b j", j=15)), [kT_], [f"exs{hc}"])
                def secA():
                    for g, col in enumerate((cA, cB)):
                        b, bk = fm(col)
                        for kv in range(2):
                            act(lambda h, b=b, g=g, kv=kv: h.copy(out=qZ[64 * kv:64 * kv + 64, g, kv, 0:n], in_=b[64 * kv:64 * kv + 64, 0:n]), [bk], ["qZ"])
                    bK, kK_ = fm(cK)
                    act(lambda h: h.copy(out=kTr[:, sl, 0:n], in_=bK[:, 0:n]), [kK_], [f"kT{sl}"])
                    if t < 16:
                        blks = [0] if t == 0 else [1, 0]
                        for blk in blks:
                            ksl = sl if blk == 0 else 1 - sl
                            bs, ks = bank()
                            for g in range(2):
                                pe(lambda h, g=g, bs=bs, ksl=ksl: h.matmul(bs[:, g * 256:(g + 1) * 256], lhsT=kTr[:, ksl, :], rhs=qZ[:, g, :, :].rearrange("p a q -> p (a q)"), start=True, stop=True), [f"kT{ksl}", "qZ"], [ks])
                            dve(lambda h, bs=bs, blk=blk: h.scalar_tensor_tensor(out=tmpS[:], in0=bs[:, :], scalar=0.125, in1=C("bt")[:, blk * 512:(blk + 1) * 512], op0=ALU.mult, op1=ALU.add), [ks, "ct"], ["tmpS"])
                            act(lambda h, blk=blk: h.activation(out=PT[:, blk, :], in_=tmpS[:], func=AF.Exp), ["tmpS"], [f"PT{blk}"])
                        bo_, ko_ = bank()
                        for kv in range(2):
                            for g in range(2):
                                hh = kv * 2 + g; pc = (g * 2 + kv) * 128
                                if t > 0:
                                    pe(lambda h, hh=hh, pc=pc, kv=kv: h.matmul(bo_[:, hh * 65:(hh + 1) * 65], lhsT=PT[:, 1, pc:pc + 128], rhs=Vaug[:, 1 - sl, kv, :], start=True, stop=False), ["PT1", f"Va{1 - sl}"], [ko_])
                                pe(lambda h, hh=hh, pc=pc, kv=kv: h.matmul(bo_[:, hh * 65:(hh + 1) * 65], lhsT=PT[:, 0, pc:pc + 128], rhs=Vaug[:, sl, kv, :], start=(t == 0), stop=True), ["PT0", f"Va{sl}"], [ko_])
                        attn_out(bo_, ko_)
                def secB():
                    if t < 16:
                        hgrn_gate()
                        tbH, tkH = tbank()
                        for hc in range(2):
                            bq, kq = fm(cBQ + hc * 128); bf, kf = fm(cBF + hc * 128)
                            act(lambda h, bf=bf: h.activation(out=hE[:, 0:n], in_=bf[:, 0:n], func=AF.Exp, scale=-1.0), [kf], ["hE"])
                            act(lambda h, hc=hc: h.activation(out=hL1[:, 0:n], in_=hE[:, 0:n], func=AF.Ln, scale=lbt[:, hc, l:l + 1], bias=1.0), ["hE", "lbt"], ["hL1"])
                            act(lambda h: h.activation(out=hL2[:, 0:n], in_=hE[:, 0:n], func=AF.Ln, scale=1.0, bias=1.0), ["hE"], ["hL2"])
                            dve(lambda h: h.tensor_tensor(out=hL1[:, 0:n], in0=hL1[:, 0:n], in1=hL2[:, 0:n], op=ALU.subtract), ["hL1", "hL2"], ["hL1"])
                            act(lambda h: h.activation(out=hL2[:, 0:n], in_=hL2[:, 0:n], func=AF.Exp, scale=-1.0), ["hL2"], ["hL2"])
                            dve(lambda h, hc=hc: h.scalar_tensor_tensor(out=hE[:, 0:n], in0=hE[:, 0:n], scalar=oml[:, hc, l:l + 1], in1=hL2[:, 0:n], op0=ALU.mult, op1=ALU.mult), ["hE", "hL2", "oml"], ["hE"])
                            for c2 in range(2):
                                dve(lambda h, c2=c2: h.tensor_tensor_scan(out=hCU[:, c2 * 64:(c2 + 1) * 64], data0=ones[:, 0:64], data1=hL1[:, c2 * 64:(c2 + 1) * 64], initial=0.0, op0=ALU.mult, op1=ALU.add), ["hL1", "ones"], ["hCU"])
                            act(lambda h: h.activation(out=hEC[:, 0:n], in_=hCU[:, 0:n], func=AF.Exp), ["hCU"], ["hEC"])
                            act(lambda h: h.activation(out=hEN[:, 0:n], in_=hCU[:, 0:n], func=AF.Exp, scale=-1.0), ["hCU"], ["hEN"])
                            act(lambda h, bq=bq: h.activation(out=hEQ[:, 0:n], in_=bq[:, 0:n], func=AF.Exp, scale=-1.0), [kq], ["hEQ"])
                            act(lambda h: h.activation(out=hEQ[:, 0:n], in_=hEQ[:, 0:n], func=AF.Ln, scale=1.0, bias=1.0), ["hEQ"], ["hEQ"])
                            act(lambda h: h.activation(out=hEQ[:, 0:n], in_=hEQ[:, 0:n], func=AF.Exp, scale=-1.0), ["hEQ"], ["hEQ"])
                            dve(lambda h, bq=bq: h.tensor_tensor(out=hEQ[:, 0:n], in0=bq[:, 0:n], in1=hEQ[:, 0:n], op=ALU.mult), [kq, "hEQ"], ["hEQ"])
                            for hp in range(2):
                                rw = slice(64 * hp, 64 * hp + 64)
                                dve(lambda h, hc=hc, hp=hp, rw=rw: h.tensor_tensor(out=qSZ[rw, hc, :, hp, :], in0=hEQ[rw, :].rearrange("p (c t) -> p c t", t=64), in1=hEC[rw, :].rearrange("p (c t) -> p c t", t=64), op=ALU.mult), ["hEQ", "hEC"], ["qSZ"])
                            dve(lambda h, hc=hc: h.tensor_tensor(out=kAb[:, hc, :], in0=hE[:, 0:n], in1=hEN[:, 0:n], op=ALU.mult), ["hE", "hEN"], [f"kAb{hc}"])
                            act(lambda h, hc=hc: h.copy(out=ECL[:, hc, :], in_=hEC[:, 63:128:64]), ["hEC"], ["ECL"])
                            pe(lambda h, hc=hc: h.transpose(tbH[:, hc * 128:(hc + 1) * 128], kAb[:, hc, :], identb[:]), [f"kAb{hc}", "identb"], [tkH])
                        for c2 in range(2):
                            rw = slice(64 * c2, 64 * c2 + 64)
                            dve(lambda h, c2=c2, rw=rw: h.tensor_copy(out=kAtokZ[rw, c2, :], in_=tbH[rw, 0:256]), [tkH], ["kAtokZ"])
                        bU, kU = bank()
                        for c2 in range(2):
                            for hc in range(2):
                                pe(lambda h, c2=c2, hc=hc: h.matmul(bU[:, (c2 * 2 + hc) * 128:(c2 * 2 + hc + 1) * 128], lhsT=kAtokZ[:, c2, hc * 128:(hc + 1) * 128], rhs=Vh[:, hc * 128:(hc + 1) * 128], start=True, stop=True), ["kAtokZ", "Vh"], [kU])
                        for c2 in range(2):
                            act(lambda h, c2=c2: h.copy(out=Sbf[:, :, c2, :], in_=Sst[:]), ["Sst"], ["Sbf"])
                            for hp in range(2):
                                rw = slice(64 * hp, 64 * hp + 64)
                                dve(lambda h, c2=c2, hp=hp, rw=rw: h.tensor_tensor(out=Sst[rw], in0=Sst[rw], in1=bU[rw, c2 * 256:(c2 + 1) * 256].rearrange("p (hc hh v) -> p hc hh v", hc=2, hh=2)[:, :, hp, :], op=ALU.add), ["Sst", kU], ["Sst"])
                            dve(lambda h, c2=c2: h.tensor_tensor(out=Sst[:], in0=Sst[:], in1=ECL[:, :, c2:c2 + 1].to_broadcast([128, 2, 64]), op=ALU.mult), ["Sst", "ECL"], ["Sst"])
                        bA, kA_ = bank()
                        for c2 in range(2):
                            for hc in range(2):
                                pe(lambda h, c2=c2, hc=hc: h.matmul(bA[64 * c2:64 * c2 + 64, hc * 128:(hc + 1) * 128], lhsT=kAb[:, hc, c2 * 64:(c2 + 1) * 64], rhs=qSZ[:, hc, c2, :, :].rearrange("p a t -> p (a t)"), start=True, stop=True), [f"kAb{hc}", "qSZ"], [kA_])
                        for c2 in range(2):
                            rw = slice(64 * c2, 64 * c2 + 64)
                            dve(lambda h, c2=c2, rw=rw: h.tensor_tensor(out=AtZ[rw, c2, :, :], in0=bA[rw, 0:256].rearrange("p (a t) -> p a t", t=64), in1=C("tri2")[rw, :].unsqueeze(1).to_broadcast([64, 4, 64]), op=ALU.mult), [kA_, "ct"], ["AtZ"])
                        bO, kO = bank()
                        for c2 in range(2):
                            for hh in range(4):
                                hc, hp = hh // 2, hh % 2
                                pe(lambda h, c2=c2, hh=hh, hc=hc, hp=hp: h.matmul(bO[64 * c2:64 * c2 + 64, hh * 64:(hh + 1) * 64], lhsT=qSZ[:, hc, c2, hp, :], rhs=Sbf[:, hc, c2, :], start=True, stop=False), ["qSZ", "Sbf"], [kO])
                                pe(lambda h, c2=c2, hh=hh: h.matmul(bO[64 * c2:64 * c2 + 64, hh * 64:(hh + 1) * 64], lhsT=AtZ[:, c2, hh, :], rhs=Vh[:, hh * 64:(hh + 1) * 64], start=False, stop=True), ["AtZ", "Vh"], [kO])
                        hgrn_out(bO, kO)
                        if t == int(os.environ.get('DBG_ST', '15')):
                            dsp(ohp[l].rearrange("(hc hp) k v -> (hp k) hc v", hc=2), Sst[:], ["Sst"])
                    else:
                        tbH, tkH = tbank()
                        for hc in range(2):
                            bq, kq = fm(cBQ + hc * 128); bf, kf = fm(cBF + hc * 128)
                            act(lambda h, bf=bf: h.activation(out=hE[:, 0:n], in_=bf[:, 0:n], func=AF.Exp, scale=-1.0), [kf], ["hE"])
                            act(lambda h, hc=hc: h.activation(out=hL1[:, 0:n], in_=hE[:, 0:n], func=AF.Ln, scale=lbt[:, hc, l:l + 1], bias=1.0), ["hE", "lbt"], ["hL1"])
                            act(lambda h: h.activation(out=hL2[:, 0:n], in_=hE[:, 0:n], func=AF.Ln, scale=1.0, bias=1.0), ["hE"], ["hL2"])
                            dve(lambda h: h.tensor_tensor(out=hL1[:, 0:n], in0=hL1[:, 0:n], in1=hL2[:, 0:n], op=ALU.subtract), ["hL1", "hL2"], ["hL1"])
                            act(lambda h: h.activation(out=hL2[:, 0:n], in_=hL2[:, 0:n], func=AF.Exp, scale=-1.0), ["hL2"], ["hL2"])
                            dve(lambda h, hc=hc: h.scalar_tensor_tensor(out=hE[:, 0:n], in0=hE[:, 0:n], scalar=oml[:, hc, l:l + 1], in1=hL2[:, 0:n], op0=ALU.mult, op1=ALU.mult), ["hE", "hL2", "oml"], ["hE"])
                            dve(lambda h: h.tensor_copy(out=hCU[:, 0:n], in_=hL1[:, 0:n]), ["hL1"], ["hCU"])
                            for tt in range(1, 4):
                                dve(lambda h, tt=tt: h.tensor_tensor(out=v3(hCU[:, 0:n])[:, :, tt:tt + 1], in0=v3(hCU[:, 0:n])[:, :, tt - 1:tt], in1=v3(hL1[:, 0:n])[:, :, tt:tt + 1], op=ALU.add), ["hCU", "hL1"], ["hCU"])
                            act(lambda h: h.activation(out=hEC[:, 0:n], in_=hCU[:, 0:n], func=AF.Exp), ["hCU"], ["hEC"])
                            act(lambda h: h.activation(out=hEN[:, 0:n], in_=hCU[:, 0:n], func=AF.Exp, scale=-1.0), ["hCU"], ["hEN"])
                            act(lambda h, bq=bq: h.activation(out=hEQ[:, 0:n], in_=bq[:, 0:n], func=AF.Exp, scale=-1.0), [kq], ["hEQ"])
                            act(lambda h: h.activation(out=hEQ[:, 0:n], in_=hEQ[:, 0:n], func=AF.Ln, scale=1.0, bias=1.0), ["hEQ"], ["hEQ"])
                            act(lambda h: h.activation(out=hEQ[:, 0:n], in_=hEQ[:, 0:n], func=AF.Exp, scale=-1.0), ["hEQ"], ["hEQ"])
                            dve(lambda h, bq=bq: h.tensor_tensor(out=hEQ[:, 0:n], in0=bq[:, 0:n], in1=hEQ[:, 0:n], op=ALU.mult), [kq, "hEQ"], ["hEQ"])
                            for hp in range(2):
                                rw = slice(64 * hp, 64 * hp + 64)
                                dve(lambda h, hc=hc, hp=hp, rw=rw: h.tensor_tensor(out=qSZ[rw, hc, 0, hp, :], in0=hEQ[rw, 0:n], in1=hEC[rw, 0:n], op=ALU.mult), ["hEQ", "hEC"], ["qSZ"])
                            dve(lambda h, hc=hc: h.tensor_tensor(out=kAb[:, hc, 0:n], in0=hE[:, 0:n], in1=hEN[:, 0:n], op=ALU.mult), ["hE", "hEN"], [f"kAb{hc}"])
                            act(lambda h, hc=hc: h.copy(out=ECLs[:, hc, :], in_=v3(hEC[:, 0:n])[:, :, 3]), ["hEC"], ["ECLs"])
                            dve(lambda h, hc=hc: h.tensor_tensor(out=v3(kAb[:, hc, 64:128]), in0=v3(kAb[:, hc, 0:64]), in1=ECLs[:, hc, :].unsqueeze(2).to_broadcast([128, 16, 4]), op=ALU.mult), [f"kAb{hc}", "ECLs"], [f"kAb{hc}"])
                            pe(lambda h, hc=hc: h.transpose(tbH[0:64, hc * 128:(hc + 1) * 128], kAb[:, hc, 64:128], identb[:]), [f"kAb{hc}", "identb"], [tkH])
                        dve(lambda h: h.tensor_copy(out=kAtokZ[0:64, 0, :], in_=tbH[0:64, 0:256]), [tkH], ["kAtokZ"])
                        bA, kA_ = bank()
                        for hc in range(2):
                            pe(lambda h, hc=hc: h.matmul(bA[0:64, hc * 128:(hc + 1) * 128], lhsT=kAb[:, hc, 0:64], rhs=qSZ[:, hc, 0, :, :].rearrange("p a t -> p (a t)"), start=True, stop=True), [f"kAb{hc}", "qSZ"], [kA_])
                        dve(lambda h: h.tensor_tensor(out=AtZ[0:64, 0, :, :], in0=bA[0:64, 0:256].rearrange("p (a t) -> p a t", t=64), in1=C("bdc")[0:64, :].unsqueeze(1).to_broadcast([64, 4, 64]), op=ALU.mult), [kA_, "ct"], ["AtZ"])
                        hgrn_gate()
                def secC():
                    for hc in range(2):
                        if t < 16:
                            bch, kch = fm(cCH + hc * 128)
                            act(lambda h, bch=bch: h.copy(out=chs[:, 0:n], in_=bch[:, 0:n]), [kch], ["chs"])
                            bcc, kcc = fm(cCC + hc * 128)
                            dve(lambda h, bcc=bcc, hc=hc: h.tensor_tensor(out=uext[:, hc, 2:2 + n], in0=bcc[:, 0:n], in1=chs[:, 0:n], op=ALU.mult), [kcc, "chs"], [f"uext{hc}"])
                            dve(lambda h, hc=hc: h.tensor_scalar(out=ycv[:, 0:n], in0=uext[:, hc, 2:2 + n], scalar1=cwt[:, l, hc, 2:3], scalar2=None, op0=ALU.mult), [f"uext{hc}", "cwt"], ["ycv"])
                            for j in (1, 0):
                                dve(lambda h, hc=hc, j=j: h.scalar_tensor_tensor(out=ycv[:, 0:n], in0=uext[:, hc, j:j + n], scalar=cwt[:, l, hc, j:j + 1], in1=ycv[:, 0:n], op0=ALU.mult, op1=ALU.add), [f"uext{hc}", "cwt", "ycv"], ["ycv"])
                            bcb, kcb = fm(cCB + hc * 128)
                            dve(lambda h, hc=hc, bcb=bcb: h.tensor_tensor(out=mixT[:, 4 + hc, 0:n], in0=ycv[:, 0:n], in1=bcb[:, 0:n], op=ALU.mult), ["ycv", kcb], [mk])
                            act(lambda h, hc=hc: h.copy(out=uext[:, hc, 0:2], in_=uext[:, hc, n:n + 2]), [f"uext{hc}"], [f"uext{hc}"])
                        else:
                            v3 = lambda ap: ap.rearrange("p (b t) -> p b t", t=4)
                            bch, kch = fm(cCH + hc * 128)
                            act(lambda h, bch=bch: h.copy(out=chs[:, 0:n], in_=bch[:, 0:n]), [kch], ["chs"])
                            bcc, kcc = fm(cCC + hc * 128)
                            dve(lambda h, bcc=bcc, hc=hc: h.tensor_tensor(out=uxs[:, hc, :, 2:6], in0=v3(bcc[:, 0:n]), in1=v3(chs[:, 0:n]), op=ALU.mult), [kcc, "chs"], [f"uxs{hc}"])
                            dve(lambda h, hc=hc: h.tensor_scalar(out=v3(ycv[:, 0:n]), in0=uxs[:, hc, :, 2:6], scalar1=cwt[:, l, hc, 2:3], scalar2=None, op0=ALU.mult), [f"uxs{hc}", "cwt"], ["ycv"])
                            for j in (1, 0):
                                dve(lambda h, hc=hc, j=j: h.scalar_tensor_tensor(out=v3(ycv[:, 0:n]), in0=uxs[:, hc, :, j:j + 4], scalar=cwt[:, l, hc, j:j + 1], in1=v3(ycv[:, 0:n]), op0=ALU.mult, op1=ALU.add), [f"uxs{hc}", "cwt", "ycv"], ["ycv"])
                            bcb, kcb = fm(cCB + hc * 128)
                            dve(lambda h, hc=hc, bcb=bcb: h.tensor_tensor(out=mixT[:, 4 + hc, 0:n], in0=ycv[:, 0:n], in1=bcb[:, 0:n], op=ALU.mult), ["ycv", kcb], [mk])
                def secD():
                    for hc in range(2):
                        bdv, kdv = fm(cDV + hc * 128)
                        if t < 16:
                            W_ = 16 + n
                            act(lambda h, bdv=bdv, hc=hc: h.copy(out=ext[:, hc, 16:16 + n], in_=bdv[:, 0:n]), [kdv], [f"ext{hc}"])
                            pool(lambda h, hc=hc: h.tensor_tensor(out=s2[:, 1:W_], in0=ext[:, hc, 1:W_], in1=ext[:, hc, 0:W_ - 1], op=ALU.add), [f"ext{hc}"], ["s2"])
                            pool(lambda h: h.tensor_tensor(out=s4[:, 3:W_], in0=s2[:, 3:W_], in1=s2[:, 1:W_ - 2], op=ALU.add), ["s2"], ["s4"])
                            if hc == 1:
                                pool(lambda h: h.tensor_tensor(out=s8[:, 7:W_], in0=s4[:, 7:W_], in1=s4[:, 3:W_ - 4], op=ALU.add), ["s4"], ["s8"])
                                pool(lambda h: h.tensor_tensor(out=s16[:, 15:W_], in0=s8[:, 15:W_], in1=s8[:, 7:W_ - 8], op=ALU.add), ["s8"], ["s16"])
                            sel = [(s2, "s2", 2.0), (s4, "s4", 4.0)] if hc == 0 else [(s8, "s8", 8.0), (s16, "s16", 16.0)]
                            for half, (sv, sk, w_) in enumerate(sel):
                                rows = slice(64 * half, 64 * half + 64)
                                dve(lambda h, sv=sv, rows=rows, w_=w_, hc=hc: h.scalar_tensor_tensor(out=pooled[rows, hc, 0:n], in0=sv[rows, 16:W_], scalar=1.0 / w_, in1=ext[rows, hc, 16:W_], op0=ALU.mult, op1=ALU.subtract), [sk, f"ext{hc}"], [f"pooled{hc}"])
                                if t == 0:
                                    dve(lambda h, sv=sv, rows=rows, hc=hc: h.tensor_tensor(out=ptmp[rows, 0:16], in0=sv[rows, 16:32], in1=C("icnt")[rows, hc * 16:(hc + 1) * 16], op=ALU.mult), [sk, "ct"], ["ptmp"])
                                    dve(lambda h, rows=rows, hc=hc: h.tensor_tensor(out=pooled[rows, hc, 0:16], in0=ptmp[rows, 0:16], in1=ext[rows, hc, 16:32], op=ALU.subtract), ["ptmp", f"ext{hc}", f"pooled{hc}"], [f"pooled{hc}"])
                            by, ky = bank()
                            pe(lambda h, hc=hc, by=by: h.matmul(by[:, 0:n], lhsT=wbd[:, hc, :], rhs=pooled[:, hc, 0:n], start=True, stop=True), ["wbd", f"pooled{hc}"], [ky])
                            act(lambda h, hc=hc, by=by: h.activation(out=mixT[:, 6 + hc, 0:n], in_=by[:, 0:n], func=AF.Copy, scale=psct[:, l, hc:hc + 1]), [ky, "psct"], [mk])
                            act(lambda h, hc=hc: h.copy(out=ext[:, hc, 0:16], in_=ext[:, hc, n:n + 16]), [f"ext{hc}"], [f"ext{hc}"])
                        else:
                            v3 = lambda ap: ap.rearrange("p (b t) -> p b t", t=4)
                            act(lambda h, bdv=bdv, hc=hc: h.copy(out=exs[:, hc, :, 16:20], in_=v3(bdv[:, 0:n])), [kdv], [f"exs{hc}"])
                            for half in range(2):
                                w_ = (2, 4, 8, 16)[hc * 2 + half]; rows = slice(64 * half, 64 * half + 64)
                                dve(lambda h, rows=rows, hc=hc: h.tensor_copy(out=v3(s2[rows, 0:n]), in_=exs[rows, hc, :, 16:20]), [f"exs{hc}"], ["s2"])
                                for j in range(1, w_):
                                    dve(lambda h, rows=rows, hc=hc, j=j: h.tensor_tensor(out=v3(s2[rows, 0:n]), in0=v3(s2[rows, 0:n]), in1=exs[rows, hc, :, 16 - j:20 - j], op=ALU.add), ["s2", f"exs{hc}"], ["s2"])
                                dve(lambda h, rows=rows, hc=hc, w_=w_: h.scalar_tensor_tensor(out=v3(pooled[rows, hc, 0:n]), in0=v3(s2[rows, 0:n]), scalar=1.0 / w_, in1=exs[rows, hc, :, 16:20], op0=ALU.mult, op1=ALU.subtract), ["s2", f"exs{hc}"], [f"pooled{hc}"])
                            by, ky = bank()
                            pe(lambda h, hc=hc, by=by: h.matmul(by[:, 0:n], lhsT=wbd[:, hc, :], rhs=pooled[:, hc, 0:n], start=True, stop=True), ["wbd", f"pooled{hc}"], [ky])
                            act(lambda h, hc=hc, by=by: h.activation(out=mixT[:, 6 + hc, 0:n], in_=by[:, 0:n], func=AF.Copy, scale=psct[:, l, hc:hc + 1]), [ky, "psct"], [mk])
                if os.environ.get('NO_ILV'):
                    secA(); secB(); secC(); secD()
                    if prev_post:
                        prev_post()
                else:
                    order = os.environ.get("ORD", "EBDCA")
                    table = {"A": (secA, "A", None), "B": (secB, "B", "Bt"), "C": (secC, "C", None), "D": (secD, "D", None), "E": (prev_post, "E", None)}
                    sel = [table[c_] for c_ in order if table[c_][0] is not None]
                    run_interleaved([s_[0] for s_ in sel], [s_[1] for s_ in sel], [s_[2] for s_ in sel])
                if t == 16:
                    PS_ = 28672
                    kc_b, kc_ = warena("kc", 0, 2048); kc = kc_.rearrange("p (b c) -> p b c", c=128)
                    kcT_b, kcT_ = warena("kcT", 2048, 2048); kcT = kcT_.rearrange("p (b c) -> p b c", c=128)
                    Vc_b, Vc_ = warena("Vc", 4096, 2080); Vc = Vc_.rearrange("p (b kv e) -> p b kv e", kv=2, e=65)
                    Pb_b, Pb_ = warena("Pb", 6400, 4096); Pb = Pb_.rearrange("p (g kv b q) -> p g kv b q", g=2, kv=2, b=16)
                    tabs_b, tabs_ = warena("stab", 18944, 1024); tabs = tabs_.bitcast(F32)
                    dsp(tabs, stab[:, :], w=[tabs_b])
                    dpl(kc, ck[l].rearrange("b s c -> s b c"), w=[kc_b])
                    pool(lambda h: h.memset(Vc_, 1.0), w=[Vc_b])
                    for kv in range(2):
                        dpl(Vc[:, :, kv, 0:64], cv[l, :, :, kv * 64:(kv + 1) * 64].rearrange("b s d -> s b d"), w=[Vc_b])
                    pool(lambda h: h.memset(Pb_, 0.0), w=[Pb_b])
                    for half in range(2):
                        tbk, tkk = tbank()
                        for j in range(8):
                            pe(lambda h, j=j, half=half, tbk=tbk: h.transpose(tbk[:, j * 128:(j + 1) * 128], kc[:, half * 8 + j, :], identb[:]), [kc_b, "identb"], [tkk])
                        dve(lambda h, half=half, tbk=tbk: h.tensor_copy(out=kcT[:, half * 8:(half + 1) * 8, :], in_=tbk[:, :].rearrange("p (b c) -> p b c", c=128)), [tkk], [kcT_b])
                    bs, ks = bank()
                    for b_ in range(16):
                        for g in range(2):
                            pe(lambda h, b_=b_, g=g: h.matmul(bs[:, g * 128:(g + 1) * 128].rearrange("p (kv q) -> p kv q", kv=2)[:, :, b_ * 4:b_ * 4 + 4], lhsT=kcT[:, b_, :], rhs=qZ[:, g, :, b_ * 4:b_ * 4 + 4], start=True, stop=True), [kcT_b, "qZ"], [ks])
                    bn, kn = bank()
                    for g in range(2):
                        pe(lambda h, g=g: h.matmul(bn[0:64, g * 128:(g + 1) * 128].rearrange("p (kv q) -> p kv q", kv=2), lhsT=kTr[:, 0, 0:64], rhs=qZ[:, g, :, 0:64], start=True, stop=True), ["kT0", "qZ"], [kn])
                    dve(lambda h: h.scalar_tensor_tensor(out=tmpS[:, 0:256], in0=bs[:, 0:256], scalar=0.125, in1=tabs[:, 0:256], op0=ALU.mult, op1=ALU.add), [ks, tabs_b], ["tmpS"])
                    dve(lambda h: h.scalar_tensor_tensor(out=tmpS[0:64, 256:512], in0=bn[0:64, 0:256], scalar=0.125, in1=tabs[0:64, 256:512], op0=ALU.mult, op1=ALU.add), [kn, tabs_b, "tmpS"], ["tmpS"])
                    for gk in range(4):
                        act(lambda h, gk=gk: h.activation(out=bass.AP(WA, 6400 + gk * 1024, [[PS_, 128], [68, 16], [1, 4]]), in_=v3(tmpS[:, gk * 64:(gk + 1) * 64]), func=AF.Exp), ["tmpS"], [Pb_b])
                    act(lambda h: h.activation(out=PT[0:64, 0, 0:256], in_=tmpS[0:64, 256:512], func=AF.Exp), ["tmpS"], ["PT0"])
                    bo_, ko_ = bank()
                    for kv in range(2):
                        for g in range(2):
                            hh = kv * 2 + g; pc = (g * 2 + kv) * 64
                            for b_ in range(16):
                                pe(lambda h, hh=hh, g=g, kv=kv, b_=b_: h.matmul(bo_[0:64, hh * 65:(hh + 1) * 65], lhsT=Pb[:, g, kv, b_, :], rhs=Vc[:, b_, kv, :], start=(b_ == 0), stop=False), [Pb_b, Vc_b], [ko_])
                            pe(lambda h, hh=hh, pc=pc, kv=kv: h.matmul(bo_[0:64, hh * 65:(hh + 1) * 65], lhsT=PT[0:64, 0, pc:pc + 64], rhs=Vaug[0:64, 0, kv, :], start=False, stop=True), ["PT0", "Va0"], [ko_])
                    attn_out(bo_, ko_)
                    S0f_b, S0f_ = warena("S0f", 10496, 4096); S0f = S0f_.bitcast(F32).rearrange("p (hc b v) -> p hc b v", hc=2, b=16)
                    S0b_b, S0b_ = warena("S0b", 14592, 2048); S0b = S0b_.rearrange("p (hc b v) -> p hc b v", hc=2, b=16)
                    qSb_b, qSb_ = warena("qSb", 16640, 2048); qSbig = qSb_.rearrange("p (hp b q) -> p hp b q", hp=2, b=16)
                    Vbg_b, Vbg_ = warena("Vbg", 0, 4096); Vbig = Vbg_.rearrange("p (b c) -> p b c", c=256)
                    for hc in range(2):
                        for hp in range(2):
                            srcS = sh[l, :, hc * 2 + hp].rearrange("b k v -> k b v")
                            dsp(S0f[64 * hp:64 * hp + 64, hc], srcS, w=[S0f_b]); dpl(S0b[64 * hp:64 * hp + 64, hc], srcS, w=[S0b_b])
                    pool(lambda h: h.memset(qSb_, 0.0), w=[qSb_b])
                    dve(lambda h: h.tensor_tensor(out=Vbig[0:64], in0=Vh[0:64, :].unsqueeze(1).to_broadcast([64, 16, 256]), in1=C("mind")[0:64, :].unsqueeze(2).to_broadcast([64, 16, 256]), op=ALU.mult), ["Vh", "ct"], [Vbg_b])
                    bO, kO = bank()
                    for hc in range(2):
                        for hp in range(2):
                            rw = slice(64 * hp, 64 * hp + 64)
                            dve(lambda h, hc=hc, hp=hp, rw=rw: h.tensor_copy(out=bass.AP(WA, 64 * hp * PS_ + 16640 + hp * 1024, [[PS_, 64], [68, 16], [1, 4]]), in_=v3(qSZ[rw, hc, 0, hp, :])), ["qSZ"], [qSb_b])
                        for hp in range(2):
                            hh = hc * 2 + hp
                            for b_ in range(16):
                                pe(lambda h, hh=hh, hc=hc, hp=hp, b_=b_: h.matmul(bO[0:64, hh * 64:(hh + 1) * 64], lhsT=qSbig[:, hp, b_, :], rhs=S0b[:, hc, b_, :], start=(b_ == 0), stop=False), [qSb_b, S0b_b], [kO])
                            pe(lambda h, hh=hh: h.matmul(bO[0:64, hh * 64:(hh + 1) * 64], lhsT=AtZ[:, 0, hh, :], rhs=Vh[:, hh * 64:(hh + 1) * 64], start=False, stop=True), ["AtZ", "Vh"], [kO])
                    hgrn_out(bO, kO)
                    for hc in range(2):
                        for bg_ in range(4):
                            bU, kU = bank()
                            pe(lambda h, hc=hc, bg_=bg_, bU=bU: h.matmul(bU[:, 0:512].rearrange("p (b c) -> p b c", c=128), lhsT=kAtokZ[0:64, 0, hc * 128:(hc + 1) * 128], rhs=Vbig[0:64, bg_ * 4:(bg_ + 1) * 4, hc * 128:(hc + 1) * 128], start=True, stop=True), ["kAtokZ", Vbg_b], [kU])
                            for hp in range(2):
                                rw = slice(64 * hp, 64 * hp + 64); b4 = slice(bg_ * 4, bg_ * 4 + 4)
                                dve(lambda h, hc=hc, rw=rw, b4=b4: h.tensor_tensor(out=S0f[rw, hc, b4, :], in0=S0f[rw, hc, b4, :], in1=ECLs[rw, hc, b4].unsqueeze(2).to_broadcast([64, 4, 64]), op=ALU.mult), [S0f_b, "ECLs"], [S0f_b])
                                dve(lambda h, hc=hc, hp=hp, rw=rw, b4=b4, bU=bU: h.tensor_tensor(out=S0f[rw, hc, b4, :], in0=S0f[rw, hc, b4, :], in1=bU[rw, 0:512].rearrange("p (b c) -> p b c", c=128)[:, :, hp * 64:(hp + 1) * 64], op=ALU.add), [S0f_b, kU], [S0f_b])
                    for hc in range(2):
                        for hp in range(2):
                            dsp(ohs[l, :, hc * 2 + hp].rearrange("b k v -> k b v"), S0f[64 * hp:64 * hp + 64, hc], [S0f_b])
                def post():
                    tb, tk = tbank()
                    for c in range(2):
                        pe(lambda h, c=c: h.transpose(tb[:, c * 128:c * 128 + n], otok[0:n, c * 128:(c + 1) * 128], identb[0:n, 0:n]), ["otok", "identb"], [tk])
                    dve(lambda h: h.tensor_copy(out=mixT[:, 0:2, 0:n], in_=tb[:, 0:256].rearrange("p (a b) -> p a b", b=128)[:, :, 0:n]), [tk], [mk])
                    tb2, tk2 = tbank()
                    for c in range(2):
                        pe(lambda h, c=c: h.transpose(tb2[:, c * 128:c * 128 + n], otok2[0:n, c * 128:(c + 1) * 128], identb[0:n, 0:n]), ["otok2", "identb"], [tk2])
                    dve(lambda h: h.tensor_copy(out=mixT[:, 2:4, 0:n], in_=tb2[:, 0:256].rearrange("p (a b) -> p a b", b=128)[:, :, 0:n]), [tk2], [mk])
                    if dbg:
                        for c in range(8):
                            dpl(dbgo[c * 128:(c + 1) * 128, c0:c0 + n], mixT[:, c, 0:n], [mk])
                    for nh in range(2):
                        bo_h, bok_h = bank()
                        for k in range(8):
                            pe(lambda h, k=k, nh=nh, bo_h=bo_h: h.matmul(bo_h[0:n, :], lhsT=mixT[:, k, 0:n], rhs=W_o[1][:, k * 1024 + nh * 512:k * 1024 + (nh + 1) * 512], start=(k == 0), stop=(k == 7)), [mk, W_o[0]], [bok_h])
                        dve(lambda h, nh=nh, bo_h=bo_h: h.scalar_tensor_tensor(out=x_tok[0:n, t, nh * 512:(nh + 1) * 512], in0=x_tok[0:n, t, nh * 512:(nh + 1) * 512], scalar=ALPHA, in1=bo_h[0:n, :], op0=ALU.mult, op1=ALU.add), [f"x{t}", bok_h], [f"x{t}"])
                    ln_tile(t, 1 + 3 * l, None)
                return post
            prev = None
            for t in (range(NT) if '1' in DBGP else []):
                prev = p1(t, prev)
            if prev:
                prev()
            dpl(W_xq[1].rearrange("p (a b) -> p a b", b=256), wxq[l], w=[W_xq[0]])
            dpl(W_xo[1].rearrange("p (a b) -> p a b", b=1024), wxo[l], w=[W_xo[0]])
            dpl(W_xk[1].rearrange("p (a b) -> p a b", b=256), wxk[l], w=[W_xk[0]])
            dpl(W_xv[1].rearrange("p (a b) -> p a b", b=256), wxv[l], w=[W_xv[0]])
            load_ln(2 + 3 * l)
            for mt in range(2):
                for hf in range(2):
                    dsp(tmpS[:], memp[mt * 128:(mt + 1) * 128, hf * 512:(hf + 1) * 512], w=["tmpS"])
                    act(lambda h, hf=hf: h.copy(out=zbf[:, hf * 512:(hf + 1) * 512], in_=tmpS[:]), ["tmpS"], ["zbf"])
                tbm, tkm = tbank()
                for c in range(8):
                    pe(lambda h, c=c, tbm=tbm: h.transpose(tbm[:, c * 128:(c + 1) * 128], zbf[:, c * 128:(c + 1) * 128], identb[:]), ["zbf", "identb"], [tkm])
                dve(lambda h, mt=mt, tbm=tbm: h.tensor_copy(out=memT[:, :, mt * 128:(mt + 1) * 128], in_=tbm[:].rearrange("p (c m) -> p c m", m=128)), [tkm], [memTb])
            def p2m(mt):
                bk_, kk_ = bank()
                for k in range(8):
                    pe(lambda h, k=k, mt=mt, b=bk_: h.matmul(b[:, 0:256], lhsT=memT[:, k, mt * 128:(mt + 1) * 128], rhs=W_xk[1][:, k * 256:(k + 1) * 256], start=(k == 0), stop=(k == 7)), MEMT + [W_xk[0]], [kk_])
                for k in range(8):
                    pe(lambda h, k=k, mt=mt, b=bk_: h.matmul(b[:, 256:512], lhsT=memT[:, k, mt * 128:(mt + 1) * 128], rhs=W_xv[1][:, k * 256:(k + 1) * 256], start=(k == 0), stop=(k == 7)), MEMT + [W_xv[0]], [kk_])
                dve(lambda h, b=bk_: h.tensor_copy(out=tmpS[:, 0:512], in_=b[:, :]), [kk_], ["tmpS"])
                act(lambda h, b=bk_, mt=mt: h.copy(out=mva[:, mt, :, 0:64], in_=b[:, 256:512].rearrange("p (h d) -> p h d", d=64)), [kk_], ["mva"])
                dsp(omk[l, mt * 128:(mt + 1) * 128, :], tmpS[:, 0:256], ["tmpS"]); dsp(omv[l, mt * 128:(mt + 1) * 128, :], tmpS[:, 256:512], ["tmpS"])
            for mt in (range(2) if '2' in DBGP else []):
                p2m(mt)
            for hc in range(2):
                bk_, kk_ = bank()
                for k in range(8):
                    pe(lambda h, k=k, hc=hc, b=bk_: h.matmul(b[:, 0:256], lhsT=W_xk[1][:, k * 256 + hc * 128:k * 256 + (hc + 1) * 128], rhs=memT[:, k, :], start=(k == 0), stop=(k == 7)), MEMT + [W_xk[0]], [kk_])
                act(lambda h, hc=hc, b=bk_: h.copy(out=mkT[:, hc, :], in_=b[:, 0:256]), [kk_], ["kAtokZ"])
            qxT2 = hEQ[:].bitcast(BF16).rearrange("p (a b) -> p a b", b=128)
            PX2 = tmpS[:].bitcast(BF16).rearrange("p (a b c) -> p a b c", a=2, c=128)

            def p2(t):
                n = 128; c0 = t * 128
                par = t % 2
                qx = (qxT, qxT2)[par]; qk = (["kAb0", "kAb1"], ["hEQ", "hEQ"])[par]
                px = (PX, PX2)[par]; pk = (["PT0", "PT1"], ["tmpS", "tmpS"])[par]
                for hc in range(2):
                    bq, kq = bank()
                    for k in range(8):
                        pe(lambda h, k=k, hc=hc, bq=bq: h.matmul(bq[:, 0:n], lhsT=W_xq[1][:, k * 256 + hc * 128:k * 256 + (hc + 1) * 128], rhs=xT[:, k, c0:c0 + n], start=(k == 0), stop=(k == 7)), [f"xT{t}", W_xq[0]], [kq])
                    act(lambda h, hc=hc, bq=bq: h.copy(out=qx[:, hc, :], in_=bq[:, 0:n]), [kq], [qk[hc]])
                bsc = [bank(), bank()]
                for mt in range(2):
                    for hc in range(2):
                        for hp in range(2):
                            pe(lambda h, hc=hc, hp=hp, mt=mt: h.matmul(bsc[hp][0][:, (mt * 2 + hc) * 128:(mt * 2 + hc + 1) * 128], lhsT=mkT[64 * hp:64 * hp + 64, hc, mt * 128:(mt + 1) * 128], rhs=qx[64 * hp:64 * hp + 64, hc, :], start=True, stop=True), ["kAtokZ"] + qk, [bsc[hp][1]])
                for hp in range(2):
                    act(lambda h, hp=hp: h.activation(out=px[:, hp, :, :].rearrange("p a q -> p (a q)"), in_=bsc[hp][0][:, :], func=AF.Exp, scale=0.125), [bsc[hp][1]], [pk[hp]])

                def stage_b():
                    bo_, ko_ = bank()
                    for hh in range(4):
                        hc, hp = hh // 2, hh % 2
                        for mt in range(2):
                            pe(lambda h, hh=hh, hc=hc, hp=hp, mt=mt: h.matmul(bo_[:, hh * 65:(hh + 1) * 65], lhsT=px[:, hp, mt * 2 + hc, :], rhs=mva[:, mt, hh, :], start=(mt == 0), stop=(mt == 1)), [pk[hp], "mva"], [ko_])
                    ov = bo_[:, 0:260].rearrange("p (h e) -> p h e", e=65)
                    dve(lambda h: h.reciprocal(out=rec[:], in_=ov[:, :, 64]), [ko_], ["reca"])
                    dve(lambda h: h.tensor_tensor(out=oxt[:].rearrange("p (h d) -> p h d", d=64), in0=ov[:, :, 0:64], in1=rec[:].unsqueeze(2).to_broadcast([128, 4, 64]), op=ALU.mult), [ko_, "reca"], ["otok"])
                    tb, tk = tbank()
                    for hc in range(2):
                        pe(lambda h, hc=hc: h.transpose(tb[:, hc * 128:(hc + 1) * 128], oxt[:, hc * 128:(hc + 1) * 128], identb[:]), ["otok", "identb"], [tk])
                    dve(lambda h: h.tensor_copy(out=oxT[:].rearrange("p a b -> p (a b)"), in_=tb[:, 0:256]), [tk], ["otok2"])
                    bo = [bank(), bank()]
                    for nh in range(2):
                        for hc in range(2):
                            pe(lambda h, hc=hc, nh=nh: h.matmul(bo[nh][0][:, :], lhsT=oxT[:, hc, :], rhs=W_xo[1][:, hc * 1024 + nh * 512:hc * 1024 + (nh + 1) * 512], start=(hc == 0), stop=(hc == 1)), ["otok2", W_xo[0]], [bo[nh][1]])
                    ln_tile(t, 2 + 3 * l, bo)
                return stage_b
            prevB = None
            for t in (range(16) if '2' in DBGP else []):
                sb_ = p2(t)
                if prevB:
                    prevB()
                prevB = sb_
            if prevB:
                prevB()
            def p2s():
                t = 16; n = 64; c0 = 2048; PS_ = 28672
                v3 = lambda ap: ap.rearrange("p (b t) -> p b t", t=4)
                for hc in range(2):
                    bq, kq = bank()
                    for k_ in range(8):
                        pe(lambda h, k_=k_, hc=hc, bq=bq: h.matmul(bq[:, 0:n], lhsT=W_xq[1][:, k_ * 256 + hc * 128:k_ * 256 + (hc + 1) * 128], rhs=xT[:, k_, c0:c0 + n], start=(k_ == 0), stop=(k_ == 7)), ["xT16", W_xq[0]], [kq])
                    for hp in range(2):
                        act(lambda h, hc=hc, hp=hp, bq=bq: h.copy(out=qZ[64 * hp:64 * hp + 64, hc, hp, 0:n], in_=bq[64 * hp:64 * hp + 64, 0:n]), [kq], ["qZ"])
                def xhalf(hi_, half):
                    b0 = 8 * half
                    Kc_b, Kc_ = warena("xKc", 10240, 4096); Kc = Kc_.rearrange("p (b mt c) -> p b mt c", b=8, mt=2)
                    KcT_b, KcT_ = warena("xKcT", 14336, 4096); KcT = KcT_.rearrange("p (b hc m) -> p b hc m", b=8, hc=2)
                    Vx_b, Vx_ = warena("xVc", 18432, 4160); Vx = Vx_.rearrange("p (b mt hh e) -> p b mt hh e", b=8, mt=2, hh=4)
                    Px_b, Px_ = warena("xPb", 22592, 4096); Px = Px_.rearrange("p (mt hh b q) -> p mt hh b q", mt=2, hh=4, b=8)
                    pool(lambda h, Vx_=Vx_: h.memset(Vx_, 1.0), w=[Vx_b])
                    pool(lambda h, Px_=Px_: h.memset(Px_, 0.0), w=[Px_b])
                    for mt in range(2):
                        dpl(Kc[:, :, mt, :], cmk[l, b0:b0 + 8, mt * 128:(mt + 1) * 128, :].rearrange("b m c -> m b c"), w=[Kc_b])
                        for hh in range(4):
                            dpl(Vx[:, :, mt, hh, 0:64], cmv[l, b0:b0 + 8, mt * 128:(mt + 1) * 128, hh * 64:(hh + 1) * 64].rearrange("b m d -> m b d"), w=[Vx_b])
                    for bp in range(4):
                        tbk, tkk = tbank()
                        for j in range(8):
                            bl, hc, mt = bp * 2 + j // 4, (j // 2) % 2, j % 2
                            pe(lambda h, j=j, bl=bl, hc=hc, mt=mt, tbk=tbk, Kc=Kc: h.transpose(tbk[:, j * 128:(j + 1) * 128], Kc[:, bl, mt, hc * 128:(hc + 1) * 128], identb[:]), [Kc_b, "identb"], [tkk])
                        dve(lambda h, bp=bp, tbk=tbk, KcT_=KcT_: h.tensor_copy(out=KcT_[:, bp * 1024:(bp + 1) * 1024], in_=tbk[:, :]), [tkk], [KcT_b])
                    bs, ks = bank()
                    for bl in range(8):
                        for mt in range(2):
                            for hc in range(2):
                                pe(lambda h, bl=bl, mt=mt, hc=hc, KcT=KcT: h.matmul(bs[:, (mt * 2 + hc) * 64:(mt * 2 + hc + 1) * 64].rearrange("p (hp q) -> p hp q", hp=2)[:, :, bl * 4:bl * 4 + 4], lhsT=KcT[:, bl, hc, mt * 128:(mt + 1) * 128], rhs=qZ[:, hc, :, (b0 + bl) * 4:(b0 + bl) * 4 + 4], start=True, stop=True), [KcT_b, "qZ"], [ks])
                    for mt in range(2):
                        for hh in range(4):
                            act(lambda h, mt=mt, hh=hh, half=half: h.activation(out=bass.AP(WA, 22592 + (mt * 4 + hh) * 512 + half * 32, [[PS_, 128], [68, 8], [1, 4]]), in_=v3(bs[:, mt * 128 + hh * 32:mt * 128 + hh * 32 + 32]), func=AF.Exp, scale=0.125), [ks], [Px_b])
                    bo_, ko_ = bank()
                    for hh in range(4):
                        for bl in range(8):
                            for mt in range(2):
                                pe(lambda h, hh=hh, bl=bl, mt=mt, Px=Px, Vx=Vx: h.matmul(bo_[0:64, hh * 65:(hh + 1) * 65], lhsT=Px[:, mt, hh, bl, :], rhs=Vx[:, bl, mt, hh, :], start=(bl == 0 and mt == 0), stop=(bl == 7 and mt == 1)), [Px_b, Vx_b], [ko_])
                    if hi_ == 0:
                        dve(lambda h, bo_=bo_: h.tensor_copy(out=tmpS[0:64, 0:260], in_=bo_[0:64, 0:260]), [ko_], ["tmpS"])
                    else:
                        dve(lambda h, bo_=bo_: h.tensor_tensor(out=tmpS[0:64, 0:260], in0=tmpS[0:64, 0:260], in1=bo_[0:64, 0:260], op=ALU.add), [ko_, "tmpS"], ["tmpS"])
                for hi_, half in enumerate((0, 1)):
                    xhalf(hi_, half)
                ov = tmpS[0:64, 0:260].rearrange("p (h e) -> p h e", e=65)
                dve(lambda h: h.reciprocal(out=reca[0:64], in_=ov[:, :, 64]), ["tmpS"], ["reca"])
                dve(lambda h: h.tensor_tensor(out=otok[0:64].rearrange("p (h d) -> p h d", d=64), in0=ov[:, :, 0:64], in1=reca[0:64].unsqueeze(2).to_broadcast([64, 4, 64]), op=ALU.mult), ["tmpS", "reca"], ["otok"])
                if dbg:
                    dpl(dbgo[0:64, 0:256], otok[0:64, 0:256], ["otok"])
                tb, tk = tbank()
                for hc in range(2):
                    pe(lambda h, hc=hc: h.transpose(tb[:, hc * 128:hc * 128 + 64], otok[0:64, hc * 128:(hc + 1) * 128], identb[0:64, 0:64]), ["otok", "identb"], [tk])
                dve(lambda h: h.tensor_copy(out=oxT[:, :, 0:64], in_=tb[:, 0:256].rearrange("p (a b) -> p a b", b=128)[:, :, 0:64]), [tk], ["otok2"])
                bo = [bank(), bank()]
                for nh in range(2):
                    for hc in range(2):
                        pe(lambda h, hc=hc, nh=nh: h.matmul(bo[nh][0][0:64, :], lhsT=oxT[:, hc, 0:64], rhs=W_xo[1][:, hc * 1024 + nh * 512:hc * 1024 + (nh + 1) * 512], start=(hc == 0), stop=(hc == 1)), ["otok2", W_xo[0]], [bo[nh][1]])
                ln_tile(16, 2 + 3 * l, bo)
            if '2' in DBGP:
                p2s()
            load_ln(3 + 3 * l)
            def p3(j):
                r0 = ring[j % 4]
                Wg_ = warena(f"fg{j % 4}", r0, 2048); Wu_ = warena(f"fu{j % 4}", r0 + 2048, 2048); Wd_ = warena(f"fd{j % 4}", r0 + 4096, 2048)
                dpl(Wg_[1].rearrange("p (a b) -> p a b", b=256), wg[l, :, :, j * 256:(j + 1) * 256], w=[Wg_[0]])
                dpl(Wu_[1].rearrange("p (a b) -> p a b", b=256), wu[l, :, :, j * 256:(j + 1) * 256], w=[Wu_[0]])
                dpl(Wd_[1].rearrange("p (a b) -> p a b", b=1024), wd[l, :, 2 * j:2 * j + 2, :], w=[Wd_[0]])
                def p3t(t):
                    n = ntok(t); c0 = t * 128
                    for cc in range(2):
                        bg, kg = bank(); bu, ku = bank()
                        for k in range(8):
                            pe(lambda h, k=k, cc=cc, bg=bg: h.matmul(bg[:, 0:n], lhsT=Wg_[1][:, k * 256 + cc * 128:k * 256 + (cc + 1) * 128], rhs=xT[:, k, c0:c0 + n], start=(k == 0), stop=(k == 7)), [f"xT{t}", Wg_[0]], [kg])
                        for k in range(8):
                            pe(lambda h, k=k, cc=cc, bu=bu: h.matmul(bu[:, 0:n], lhsT=Wu_[1][:, k * 256 + cc * 128:k * 256 + (cc + 1) * 128], rhs=xT[:, k, c0:c0 + n], start=(k == 0), stop=(k == 7)), [f"xT{t}", Wu_[0]], [ku])
                        sg_, sgk = ((hE, "hE"), (hCU, "hCU"), (hEC, "hEC"), (hEN, "hEN"))[(t % 2) * 2 + cc]
                        hT_, hTk = ((hT, "hL1"), (hT2, "hL2"))[t % 2]
                        act(lambda h, bg=bg, sg_=sg_: h.activation(out=sg_[:, 0:n], in_=bg[:, 0:n], func=AF.Silu), [kg], [sgk])
                        dve(lambda h, bu=bu, cc=cc, sg_=sg_, hT_=hT_: h.tensor_tensor(out=hT_[:, cc, 0:n], in0=sg_[:, 0:n], in1=bu[:, 0:n], op=ALU.mult), [sgk, ku], [hTk])
                    def down():
                        bo = [bank(), bank()]
                        for nh in range(2):
                            for cc in range(2):
                                pe(lambda h, cc=cc, nh=nh: h.matmul(bo[nh][0][0:n, :], lhsT=hT_[:, cc, 0:n], rhs=Wd_[1][:, cc * 1024 + nh * 512:cc * 1024 + (nh + 1) * 512], start=(cc == 0), stop=(cc == 1)), [hTk, Wd_[0]], [bo[nh][1]])
                        ln_tile(t, (3 + 3 * l) if j == 10 else None, bo, first=(j == 0))
                    return down
                prevD = None
                for t in range(NT):
                    d_ = p3t(t)
                    if prevD:
                        prevD()
                    prevD = d_
                prevD()
            for j in (range(11) if '3' in DBGP else []):
                p3(j)
        for l_ in range(nl):
            layer(l_)
        for t in range(16):
            dsp(yp[t * 128:(t + 1) * 128, :], x_tok[:, t, :], [f"x{t}"])
        dsp(ys[:, :], x_tok[0:64, 16, :], ["x16"])
        S.final()
        S.emit(nc, sems, dsems, block)
    return nc


def _host_layout(inp, nl=4, cores=range(8)):
    f = lambda a: np.ascontiguousarray(np.asarray(a, dtype=np.float32))
    pk = lambda w: f(np.asarray(w)[:nl].reshape(nl, -1, 128, np.asarray(w).shape[-1]).transpose(0, 2, 1, 3))
    sh = {}
    sh["win"] = pk(np.asarray(inp["w_in"])[:, :, _perm()])
    sh["wo"] = pk(inp["w_o"]); sh["wxq"] = pk(inp["w_xq"]); sh["wxk"] = pk(inp["w_xk"]); sh["wxv"] = pk(inp["w_xv"])
    sh["wxo"] = pk(inp["w_xo"]); sh["wg"] = pk(inp["w_gate"]); sh["wu"] = pk(inp["w_up"]); sh["wd"] = pk(inp["w_down"])
    gs = [inp["emb_ln_g"]]; bs = [inp["emb_ln_b"]]
    for l in range(4):
        for k in (1, 2, 3):
            gs.append(np.asarray(inp[f"ln{k}_g"])[l]); bs.append(np.asarray(inp[f"ln{k}_b"])[l])
    rows = np.stack([np.stack([np.asarray(g), np.asarray(b)]) for g, b in zip(gs, bs)])
    sh["lnrows"] = f(rows)
    sh["lncols"] = f(rows.reshape(13, 2, 8, 128).transpose(3, 0, 1, 2).reshape(128, 13 * 16))
    sh["sink"] = f(np.asarray(inp["attn_sink"]).reshape(1, 16))
    sh["hlb"] = f(np.asarray(inp["hgrn_lb"]).reshape(4, 2, 128).transpose(2, 1, 0).reshape(128, 8))
    sh["hng"] = f(inp["hgrn_norm_g"])
    sh["cw"] = f(np.asarray(inp["conv_w"]).reshape(4, 3, 2, 128).transpose(3, 0, 2, 1).reshape(128, 24))
    sh["pw"] = f(inp["pool_w"])
    sh["psc"] = f(np.asarray(inp["pool_scale"]).reshape(4, 2, 128).transpose(2, 0, 1).reshape(128, 8))
    sh["ctab"] = _CT; sh["stab"] = _STAB
    maps = []
    for c in cores:
        m = dict(sh)
        b = slice(16 * c, 16 * c + 16)
        m["xp"] = f(inp["x_prompt"][c]); m["xs"] = f(np.asarray(inp["x_sample"])[b].reshape(64, D))
        m["ck"] = f(np.asarray(inp["cache_swa_k"])[:nl, b].reshape(nl, 16, 128, 128))
        m["cv"] = f(np.asarray(inp["cache_swa_v"])[:nl, b].reshape(nl, 16, 128, 128))
        m["sh"] = f(np.asarray(inp["state_hgrn"])[:nl, b])
        m["scv"] = f(np.asarray(inp["state_conv"])[:nl, b].reshape(nl, 32, 256))
        m["spl"] = f(np.asarray(inp["state_pool"])[:nl, b])
        m["cmk"] = f(np.asarray(inp["cache_mem_k"])[:nl, b].reshape(nl, 16, 256, 256))
        m["cmv"] = f(np.asarray(inp["cache_mem_v"])[:nl, b].reshape(nl, 16, 256, 256))
        m["memp"] = f(inp["mem_prompt"][c])
        maps.append(m)
    return maps


_NC = None


def kernel(**inputs):
    global _NC
    if _NC is None:
        _NC = build()
    maps = _host_layout(inputs, 4, range(8))
    res = run_bass_kernel_spmd(_NC, maps, core_ids=list(range(8))).results
    st = lambda k: np.stack([r[k] for r in res])
    cat = lambda k: np.concatenate([r[k] for r in res], 1)
    return (st("yp"), np.concatenate([r["ys"].reshape(16, 4, D) for r in res], 0),
            st("okp").transpose(1, 0, 2, 3).reshape(4, 8, 128, 2, 64), st("ovp").transpose(1, 0, 2, 3).reshape(4, 8, 128, 2, 64),
            st("ohp").transpose(1, 0, 2, 3, 4), st("ocp").transpose(1, 0, 2, 3), st("opp").transpose(1, 0, 2, 3),
            st("omk").transpose(1, 0, 2, 3).reshape(4, 8, 256, 4, 64), st("omv").transpose(1, 0, 2, 3).reshape(4, 8, 256, 4, 64),
            cat("oks").reshape(4, 128, 128, 2, 64), cat("ovs").reshape(4, 128, 128, 2, 64), cat("ohs"), cat("ocs"), cat("ops"))
```

```python
import numpy as np
import concourse.bass as bass
import concourse.mybir as mybir

F32 = mybir.dt.float32
BF16 = mybir.dt.bfloat16
ALU = mybir.AluOpType
AF = mybir.ActivationFunctionType
AX = mybir.AxisListType

import os as _os
SERIAL = bool(_os.environ.get('SERIAL'))
NOSAME = bool(_os.environ.get('NOSAME'))
NS = 1
ND = 12


class Buf:
    __slots__ = ("lo", "hi", "name")

    def __init__(self, name, lo, hi):
        self.name, self.lo, self.hi = name, lo, hi


class _Op:
    __slots__ = ("eng", "fn", "idx", "is_dma", "waits", "needs_inc", "seq", "dsem", "dval")

    def __init__(self, eng, fn, is_dma):
        self.eng = eng
        self.fn = fn
        self.is_dma = is_dma
        self.waits = []
        self.needs_inc = False
        self.seq = -1
        self.idx = -1
        self.dsem = -1
        self.dval = 0


class Sched:
    ENGS = ("pe", "act", "dve", "pool", "sp")

    def __init__(self):
        self.ops = {e: [] for e in self.ENGS}
        self.last_w = {}
        self.readers = {}
        self.live = []
        self.known = {e: {f: -1 for f in self.ENGS} for e in self.ENGS}
        self.known_dma = {e: set() for e in self.ENGS}
        self.dma_count = {"sp": 0, "pool": 0}
        self.dma_hist = {"sp": {}, "pool": {}}

    def _overl(self, b):
        return [r for r in self.live if r is not b and r.lo < b.hi and b.lo < r.hi]

    def op(self, eng, fn, reads=(), writes=(), dma=False):
        o = _Op(eng, fn, dma)
        o.idx = len(self.ops[eng])
        deps = {}
        for k in reads:
            w = self.last_w.get(k)
            if w is not None:
                deps[w] = True
            if isinstance(k, str) and k[:2] in ("ps", "pT"):
                for r in self.readers.get(k, ()):
                    if r.eng != eng and r not in deps:
                        deps[r] = False
            if isinstance(k, Buf):
                for r in self._overl(k):
                    w = self.last_w.get(r)
                    if w is not None:
                        deps[w] = True
        for k in writes:
            ks = [k] + (self._overl(k) if isinstance(k, Buf) else [])
            for kk in ks:
                w = self.last_w.get(kk)
                if w is not None:
                    deps[w] = True
                for r in self.readers.get(kk, ()):
                    if r not in deps:
                        deps[r] = False
        if SERIAL:
            for e2 in self.ENGS:
                if self.ops[e2]:
                    deps.setdefault(self.ops[e2][-1], True)
        if dma:
            q = eng
            j = self.dma_count[q]
            self.dma_count[q] += 1
            o.dsem = j % ND
            o.dval = 16 * (j // ND + 1)
            prev = self.dma_hist[q].get(o.dsem)
            if prev is not None:
                deps.setdefault(prev, True)
            self.dma_hist[q][o.dsem] = o
            o.needs_inc = True
        kn = self.known[eng]
        latest = {}
        for d, hard in deps.items():
            if d is o:
                continue
            if d.is_dma:
                if d in self.known_dma[eng]:
                    continue
                self.known_dma[eng].add(d)
                o.waits.append(d)
                continue
            if d.eng == eng and (not hard or eng == "pe" or NOSAME):
                continue
            if d.idx <= kn[d.eng]:
                continue
            if d.eng not in latest or d.idx > latest[d.eng].idx:
                latest[d.eng] = d
        for e_, d in latest.items():
            kn[e_] = d.idx
            d.needs_inc = True
            o.waits.append(d)
        for k in reads:
            self.readers.setdefault(k, []).append(o)
            if isinstance(k, Buf) and k not in self.live:
                self.live.append(k)
        for k in writes:
            if isinstance(k, Buf):
                for r in self._overl(k):
                    self.live.remove(r)
                    self.last_w.pop(r, None)
                    self.readers.pop(r, None)
                if k not in self.live:
                    self.live.append(k)
            self.last_w[k] = o
            self.readers[k] = []
        self.ops[eng].append(o)
        return o

    def final(self):
        o = _Op("sp", lambda h: h.nop(), False)
        o.idx = len(self.ops["sp"])
        for q in self.dma_hist:
            for d in self.dma_hist[q].values():
                o.waits.append(d)
        self.ops["sp"].append(o)

    def emit(self, nc, sems, dsems, block):
        for e in self.ENGS:
            m = 0
            for o in self.ops[e]:
                if o.needs_inc and not o.is_dma:
                    o.seq = m
                    m += 1

        def run(e, h):
            for o in self.ops[e]:
                for d in o.waits:
                    if d.is_dma:
                        h.wait_ge(dsems[d.eng][d.dsem], d.dval)
                    else:
                        h.wait_ge(sems[d.eng][d.seq % NS], d.seq // NS + 1)
                ins = o.fn(h)
                if o.needs_inc:
                    if o.is_dma:
                        ins.then_inc(dsems[e][o.dsem], 16)
                    else:
                        ins.then_inc(sems[e][o.seq % NS], 1)

        @block.tensor
        def _(h):
            run("pe", h)

        @block.scalar
        def _(h):
            run("act", h)

        @block.vector
        def _(h):
            run("dve", h)

        @block.gpsimd
        def _(h):
            run("pool", h)

        @block.sync
        def _(h):
            run("sp", h)
from contextlib import ExitStack
import threading
from concourse.bass_utils import run_bass_kernel_spmd

import os
DBG_SKIP = bool(os.environ.get('DBG_SKIP'))
DBGP = os.environ.get('DBGP', '123')
D = 1024; DEPTH = 4; NT = 17; TS = 64; DFF = 2816
ALPHA = (2 * DEPTH) ** 0.25
LN_EPS = 1e-5; RMS_EPS = 1e-6
cA, cB, cBQ, cBF, cCB, cK, cV, cI, cG, cCC, cCH, cDV = 0, 128, 256, 512, 768, 1024, 1152, 1280, 1536, 1792, 2048, 2304


def _perm():
    q = np.arange(256).reshape(2, 2, 64)
    qa = q[:, 0, :].reshape(-1); qb = q[:, 1, :].reshape(-1)
    r = lambda a, b: np.arange(a, b)
    return np.concatenate([qa, qb, r(512, 768), r(768, 1024), r(1536, 1792), r(256, 384), r(384, 512),
                           r(1024, 1280), r(1280, 1536), r(1792, 2048), r(2048, 2304), r(2304, 2560)])


def _consts():
    c = {}
    s = np.arange(128)[:, None]; q = np.arange(128)[None, :]
    slopes = [2.0 ** (-2.0 * (h + 1)) for h in range(4)]
    NEG = -30000.0
    bt = np.zeros((128, 2, 2, 2, 128), np.float32)
    bsc = np.zeros((128, 2, 2, 64), np.float32)
    bsn = np.full((128, 2, 2, 64), NEG, np.float32)
    for g in range(2):
        for kv in range(2):
            sl = slopes[kv * 2 + g]
            bt[:, 0, g, kv, :] = np.where(s <= q, -sl * (q - s), NEG)
            bt[:, 1, g, kv, :] = np.where(s >= q, -sl * (128 + q - s), NEG)
            for b in range(16):
                for t in range(4):
                    col = b * 4 + t
                    bsc[:, g, kv, col] = np.where(np.arange(128) >= t, -sl * (t + 128 - np.arange(128)), NEG)
                    for t2 in range(t + 1):
                        bsn[b * 4 + t2, g, kv, col] = -sl * (t - t2)
    c["bt"] = bt.reshape(128, 1024)
    global _STAB
    _STAB = np.ascontiguousarray(np.concatenate([bsc.reshape(128, 256), bsn.reshape(128, 256)], 1))
    tri = (np.arange(64)[:, None] <= np.arange(64)[None, :]).astype(np.float32)
    c["tri2"] = np.concatenate([tri, tri], 0)
    bdc = np.zeros((128, 64), np.float32); mind = np.zeros((128, 16), np.float32)
    for b in range(16):
        mind[b * 4:(b + 1) * 4, b] = 1.0
        for t in range(4):
            bdc[b * 4:b * 4 + t + 1, b * 4 + t] = 1.0
    c["bdc"] = bdc; c["mind"] = mind
    ic = np.zeros((128, 2, 16), np.float32)
    for hc in range(2):
        for half in range(2):
            w = (2, 4, 8, 16)[hc * 2 + half]
            ic[half * 64:(half + 1) * 64, hc, :] = 1.0 / np.minimum(np.arange(16) + 1, w)
    c["icnt"] = ic.reshape(128, 32)
    c["ident"] = np.eye(128, dtype=np.float32)
    names = list(c.keys()); offs = {}; o = 0
    for n in names:
        offs[n] = (o, c[n].shape[1]); o += c[n].shape[1]
    return np.concatenate([c[n] for n in names], 1), offs


_STAB = None
_CT, _CO = _consts()


def build(nl=DEPTH, dbg=False):
    nc = bass.Bass("TRN2", target_bir_lowering=False)
    I = lambda n, s: nc.dram_tensor(n, list(s), F32, kind="ExternalInput").ap()
    O = lambda n, s: nc.dram_tensor(n, list(s), F32, kind="ExternalOutput").ap()
    xp = I("xp", (2048, D)); xs = I("xs", (64, D))
    ck = I("ck", (nl, 16, 128, 128)); cv = I("cv", (nl, 16, 128, 128))
    sh = I("sh", (nl, 16, 4, 64, 64)); scv = I("scv", (nl, 32, 256)); spl = I("spl", (nl, 16, 15, 256))
    cmk = I("cmk", (nl, 16, 256, 256)); cmv = I("cmv", (nl, 16, 256, 256)); memp = I("memp", (256, D))
    win = I("win", (nl, 128, 8, 2560)); wo = I("wo", (nl, 128, 8, 1024))
    wxq = I("wxq", (nl, 128, 8, 256)); wxk = I("wxk", (nl, 128, 8, 256)); wxv = I("wxv", (nl, 128, 8, 256))
    wxo = I("wxo", (nl, 128, 2, 1024)); wg = I("wg", (nl, 128, 8, DFF)); wu = I("wu", (nl, 128, 8, DFF))
    wd = I("wd", (nl, 128, 22, 1024))
    lncols = I("lncols", (128, 13 * 16)); lnrows = I("lnrows", (13, 2, 1024))
    sink = I("sink", (1, 16)); hlb = I("hlb", (128, 8)); hng = I("hng", (4, 256))
    cw = I("cw", (128, 24)); pw = I("pw", (4, 4, 64, 64)); psc = I("psc", (128, 8))
    ctab = I("ctab", _CT.shape); stab = I("stab", (128, 512))
    yp = O("yp", (2048, D)); ys = O("ys", (64, D))
    okp = O("okp", (nl, 128, 128)); ovp = O("ovp", (nl, 128, 128)); ohp = O("ohp", (nl, 4, 64, 64))
    ocp = O("ocp", (nl, 2, 256)); opp = O("opp", (nl, 15, 256)); omk = O("omk", (nl, 256, 256)); omv = O("omv", (nl, 256, 256))
    oks = O("oks", (nl, 16, 128, 128)); ovs = O("ovs", (nl, 16, 128, 128)); ohs = O("ohs", (nl, 16, 4, 64, 64))
    ocs = O("ocs", (nl, 16, 2, 256)); ops_ = O("ops", (nl, 16, 15, 256))
    dbgo = O("dbgo", (1024, 2112)) if dbg else None
    dbg2 = None
    S = Sched()
    es = ExitStack()
    with es:
        def sb(name, shape, dt=F32):
            return es.enter_context(nc.sbuf_tensor(name, list(shape), dt))
        x_tok = sb("x_tok", (128, NT, D)); xT = sb("xT", (128, 8, 2112), BF16)
        WA = sb("WA", (128, 28672), BF16)
        WK = None
        ct = sb("ct", (128, _CT.shape[1])); identb = sb("identb", (128, 128), BF16)
        lnc = sb("lnc", (128, 13, 2, 8)); Grow = sb("Grow", (128, D)); Brow = sb("Brow", (128, D))
        esink = sb("esink", (128, 16)); lbt = sb("lbt", (128, 2, 4)); oml = sb("oml", (128, 2, 4)); lbe = sb("lbe", (128, 2, 4)); lbs = sb("lbs", (128, 2, 1))
        normg = sb("normg", (128, 256)); cwt = sb("cwt", (128, 4, 2, 3)); psct = sb("psct", (128, 4, 2)); wbd = sb("wbd", (128, 2, 128), BF16)
        ones = sb("ones", (128, 64))
        ps = [es.enter_context(nc.psum_tensor(f"ps{i}", [128, 512], F32)) for i in range(8)]
        sems = {e: [es.enter_context(nc.semaphore(f"s_{e}{i}")) for i in range(NS)] for e in Sched.ENGS}
        dsems = {e: [es.enter_context(nc.semaphore(f"d_{e}{i}")) for i in range(ND)] for e in ("sp", "pool")}
        block = es.enter_context(nc.Block())

        C = lambda n: ct[:, _CO[n][0]:_CO[n][0] + _CO[n][1]]
        ident = C("ident")

        tl = threading.local()
        pools = {"ALL": list(range(8)), "A": [0, 1], "B": [2, 3], "Bt": [4], "C": [5], "D": [6], "E": [7], "P0": [0, 1, 2, 3], "P1": [4, 5, 6, 7]}
        ppos = {kk: 0 for kk in pools}

        def _next(pname):
            lst = pools[pname]; i = lst[ppos[pname] % len(lst)]; ppos[pname] += 1
            return i

        def bank():
            i = _next(getattr(tl, "pool", "ALL"))
            return ps[i], f"ps{i}"

        def tbank():
            i = _next(getattr(tl, "tpool", None) or getattr(tl, "pool", "ALL"))
            return ps[i][:].bitcast(BF16), f"ps{i}"

        _w = os.environ.get("WTS", "A1B2C1D1E1")
        wts = {_w[i]: int(_w[i + 1]) for i in range(0, len(_w), 2)}

        def run_interleaved(funcs, pnames, tpnames=None):
            nf = len(funcs); turn = [0]; alive = [True] * nf; cv = threading.Condition(); errs = []

            def handoff(i):
                j = (i + 1) % nf; c_ = 0
                while not alive[j] and c_ < nf:
                    j = (j + 1) % nf; c_ += 1
                turn[0] = j; cv.notify_all()

            cnt = [0] * nf

            def step(i):
                cnt[i] += 1
                if cnt[i] % wts.get(pnames[i], 1) != 0:
                    return
                with cv:
                    handoff(i)
                    while turn[0] != i:
                        cv.wait()

            def worker(i):
                tl.pool = pnames[i]; tl.tpool = tpnames[i] if tpnames else None
                tl.step = lambda: step(i)
                with cv:
                    while turn[0] != i:
                        cv.wait()
                try:
                    funcs[i]()
                except BaseException as e:
                    errs.append(e)
                finally:
                    with cv:
                        alive[i] = False
                        handoff(i)
            ths = [threading.Thread(target=worker, args=(i,)) for i in range(nf)]
            for th in ths:
                th.start()
            for th in ths:
                th.join()
            if errs:
                raise errs[0]

        def _y():
            st_ = getattr(tl, "step", None)
            if st_:
                st_()

        def warena(name, lo, n):
            return Buf(name, 2 * lo, 2 * (lo + n)), WA[:, lo:lo + n]

        def work(name, lo, n, dt=F32):
            v = WK[:, lo:lo + n]
            if dt == BF16:
                v = v.bitcast(BF16)
            return Buf(name, 1000000 + 4 * lo, 1000000 + 4 * (lo + n)), v

        def _mk(eng):
            def f(fn, r=(), w=()):
                S.op(eng, fn, r, w); _y()
            return f
        dve = _mk("dve"); act = _mk("act"); pe = _mk("pe"); pool = _mk("pool")

        def dsp(o, i, r=(), w=()):
            S.op("sp", lambda h: h.dma_start(out=o, in_=i), r, w, dma=True); _y()

        def dpl(o, i, r=(), w=()):
            S.op("pool", lambda h: h.dma_start(out=o, in_=i), r, w, dma=True); _y()

        dsp(ct[:], ctab[:], w=["ct"])
        dsp(lnc[:].rearrange("p a b c -> p (a b c)"), lncols[:], w=["lnc"])
        dsp(cwt[:].rearrange("p a b c -> p (a b c)"), cw[:], w=["cwt"])
        dsp(psct[:].rearrange("p a b -> p (a b)"), psc[:], w=["psct"])
        dsp(lbt[:].rearrange("p a b -> p (a b)"), hlb[:], w=["lbt"])
        dsp(esink[:], sink[:].partition_broadcast(128), w=["esink"])
        dve(lambda h: h.tensor_copy(out=identb[:], in_=ident), ["ct"], ["identb"])
        dve(lambda h: h.memset(ones[:], 1.0), w=["ones"])
        act(lambda h: h.activation(out=esink[:], in_=esink[:], func=AF.Exp), ["esink"], ["esink"])
        act(lambda h: h.activation(out=lbe[:], in_=lbt[:], func=AF.Exp), ["lbt"], ["lbe"])
        dve(lambda h: h.tensor_reduce(out=lbs[:], in_=lbe[:], axis=AX.X, op=ALU.add), ["lbe"], ["lbs"])
        dve(lambda h: h.reciprocal(out=lbs[:], in_=lbs[:]), ["lbs"], ["lbs"])
        dve(lambda h: h.tensor_tensor(out=lbe[:], in0=lbe[:], in1=lbs[:].to_broadcast([128, 2, 4]), op=ALU.mult), ["lbe", "lbs"], ["lbe"])
        dve(lambda h: h.memset(lbt[:, :, 0:1], 0.0), ["lbe"], ["lbt"])
        for l in range(1, 4):
            dve(lambda h, l=l: h.tensor_tensor(out=lbt[:, :, l:l + 1], in0=lbt[:, :, l - 1:l], in1=lbe[:, :, l:l + 1], op=ALU.add), ["lbt", "lbe"], ["lbt"])
        dve(lambda h: h.tensor_scalar(out=oml[:], in0=lbt[:], scalar1=-1.0, scalar2=1.0, op0=ALU.mult, op1=ALU.add), ["lbt"], ["oml"])

        for t in range(16):
            dsp(x_tok[:, t, :], xp[t * 128:(t + 1) * 128, :], w=[f"x{t}"])
        dsp(x_tok[0:64, 16, :], xs[:, :], w=["x16"])

        def ntok(t):
            return 64 if t == 16 else 128

        lnst = sb("lnst", (128, 2, 6)); lnmv = sb("lnmv", (128, 2)); lnr = sb("lnr", (128, 2)); zbf = sb("zbf", (128, D), BF16)

        def load_ln(i):
            dsp(Grow[:], lnrows[i, 0:1, :].partition_broadcast(128), w=["Grow"])
            dsp(Brow[:], lnrows[i, 1:2, :].partition_broadcast(128), w=["Brow"])

        def ln_tile(t, i, banks, first=True):
            n = ntok(t); xk = f"x{t}"
            xt = x_tok[0:n, t, :]
            if banks is not None:
                for nh, (b, bk) in enumerate(banks):
                    if first:
                        dve(lambda h, b=b, nh=nh: h.scalar_tensor_tensor(out=xt[:, nh * 512:(nh + 1) * 512], in0=xt[:, nh * 512:(nh + 1) * 512], scalar=ALPHA, in1=b[0:n, :], op0=ALU.mult, op1=ALU.add), [xk, bk], [xk])
                    else:
                        dve(lambda h, b=b, nh=nh: h.tensor_tensor(out=xt[:, nh * 512:(nh + 1) * 512], in0=xt[:, nh * 512:(nh + 1) * 512], in1=b[0:n, :], op=ALU.add), [xk, bk], [xk])
            if i is None:
                return
            for a in range(2):
                dve(lambda h, a=a: h.bn_stats(out=lnst[0:n, a, :], in_=xt[:, a * 512:(a + 1) * 512]), [xk], ["lnst"])
            dve(lambda h: h.bn_aggr(out=lnmv[0:n], in_=lnst[0:n].rearrange("p a b -> p (a b)")), ["lnst"], ["lnmv"])
            act(lambda h: h.activation(out=lnr[0:n, 0:1], in_=lnmv[0:n, 1:2], func=AF.Ln, bias=LN_EPS, scale=1.0), ["lnmv"], ["lnr"])
            act(lambda h: h.activation(out=lnr[0:n, 0:1], in_=lnr[0:n, 0:1], func=AF.Exp, scale=-0.5), ["lnr"], ["lnr"])
            dve(lambda h: h.scalar_tensor_tensor(out=lnr[0:n, 1:2], in0=lnmv[0:n, 0:1], scalar=-1.0, in1=lnr[0:n, 0:1], op0=ALU.mult, op1=ALU.mult), ["lnr", "lnmv"], ["lnr2"])
            act(lambda h: h.activation(out=zbf[0:n], in_=xt, func=AF.Identity, scale=lnr[0:n, 0:1], bias=lnr[0:n, 1:2]), [xk, "lnr", "lnr2"], ["zbf"])
            dve(lambda h: h.tensor_scalar(out=xt, in0=xt, scalar1=lnr[0:n, 0:1], scalar2=lnr[0:n, 1:2], op0=ALU.mult, op1=ALU.add), [xk, "lnr", "lnr2"], [xk])
            dve(lambda h: h.tensor_tensor(out=xt, in0=xt, in1=Grow[0:n], op=ALU.mult), [xk, "Grow"], [xk])
            pool(lambda h: h.tensor_tensor(out=xt, in0=xt, in1=Brow[0:n], op=ALU.add), [xk, "Brow"], [xk])
            tb, tk = tbank()
            for c in range(8):
                pe(lambda h, c=c: h.transpose(tb[:, c * 128:c * 128 + n], zbf[0:n, c * 128:(c + 1) * 128], identb[0:n, 0:n]), ["zbf", "identb"], [tk])
            for c in range(8):
                if c % 2 == 0:
                    act(lambda h, c=c: h.activation(out=xT[:, c, t * 128:t * 128 + n], in_=tb[:, c * 128:c * 128 + n], func=AF.Identity, scale=lnc[:, i, 0, c:c + 1], bias=lnc[:, i, 1, c:c + 1]), [tk, "lnc"], [f"xT{t}"])
                else:
                    dve(lambda h, c=c: h.tensor_scalar(out=xT[:, c, t * 128:t * 128 + n], in0=tb[:, c * 128:c * 128 + n], scalar1=lnc[:, i, 0, c:c + 1], scalar2=lnc[:, i, 1, c:c + 1], op0=ALU.mult, op1=ALU.add), [tk, "lnc"], [f"xT{t}"])

        W_in = [warena(f"win{k}", k * 2560, 2560) for k in range(8)]
        W_o = warena("wo", 20480, 8192)
        for k in range(8):
            dpl(W_in[k][1], win[0, :, k, :], w=[W_in[k][0]])
        dpl(W_o[1].rearrange("p (a b) -> p a b", b=1024), wo[0], w=[W_o[0]])
        load_ln(0)
        for t in range(NT):
            ln_tile(t, 0, None)

        W_xq = warena("wxq", 0, 2048); W_xo = warena("wxo", 2048, 2048); W_xk = warena("wxk", 4096, 2048); W_xv = warena("wxv", 6144, 2048)
        ring = [8192, 14336, 20480, 0]

        mixTs = [sb("mixT0", (128, 8, 128), BF16), sb("mixT1", (128, 8, 128), BF16)]
        qZ = sb("qZ", (128, 2, 2, 128), BF16); kTr = sb("kTr", (128, 2, 128), BF16); Vaug = sb("Vaug", (128, 2, 2, 65), BF16)
        tmpS = sb("tmpS", (128, 512)); PT = sb("PT", (128, 2, 512), BF16); otok = sb("otok", (128, 256), BF16)
        den = sb("den", (128, 4)); reca = sb("reca", (128, 4))
        uext = sb("uext", (128, 2, 130)); chs = sb("chs", (128, 128)); ycv = sb("ycv", (128, 128))
        ext = sb("ext", (128, 2, 144)); s2 = sb("s2", (128, 144)); s4 = sb("s4", (128, 144)); s8 = sb("s8", (128, 144)); s16 = sb("s16", (128, 144))
        pooled = sb("pooled", (128, 2, 128), BF16)
        hE = sb("hE", (128, 128)); hL1 = sb("hL1", (128, 128)); hL2 = sb("hL2", (128, 128)); hCU = sb("hCU", (128, 128))
        hEC = sb("hEC", (128, 128)); hEN = sb("hEN", (128, 128)); hEQ = sb("hEQ", (128, 128))
        qSZ = sb("qSZ", (128, 2, 2, 2, 64), BF16); kAb = sb("kAb", (128, 2, 128), BF16); kAtokZ = sb("kAtokZ", (128, 2, 256), BF16)
        Vh = sb("Vh", (128, 256), BF16); Sst = sb("Sst", (128, 2, 64)); Sbf = sb("Sbf", (128, 2, 2, 64), BF16); ECL = sb("ECL", (128, 2, 2))
        AtZ = sb("AtZ", (128, 2, 4, 64), BF16); sqb = sb("sqb", (128, 256)); sgb = sb("sgb", (128, 256)); otok2 = sb("otok2", (128, 256), BF16)
        ssq = sb("ssq", (128, 4)); rsq = sb("rsq", (128, 4))
        pool(lambda h: h.memset(qSZ[:], 0.0), w=["qSZ"])
        pool(lambda h: h.memset(kAtokZ[:], 0.0), w=["kAtokZ"])
        pool(lambda h: h.memset(AtZ[:], 0.0), w=["AtZ"])
        ptmp = sb("ptmp", (128, 16)); uxs = sb("uxs", (128, 2, 16, 6)); exs = sb("exs", (128, 2, 16, 20)); ECLs = sb("ECLs", (128, 2, 16))
        dbgb = None
        print("SBUF bytes remaining:", nc.sbuf_bytes_remaining)
        pool(lambda h: h.memset(qZ[:], 0.0), w=["qZ"])
        pool(lambda h: h.memset(Vaug[:], 1.0), w=["Va0", "Va1"])
        mva = sb("mva", (128, 2, 4, 65), BF16)
        sgt = hE; hT = hL1[:].bitcast(BF16).rearrange("p (a b) -> p a b", b=128); hT2 = hL2[:].bitcast(BF16).rearrange("p (a b) -> p a b", b=128)
        mkT = kAtokZ; qxT = kAb; oxt = otok; oxT = otok2[:].rearrange("p (a b) -> p a b", b=128); rec = reca
        PX = PT[:].rearrange("p a (b c) -> p a b c", c=128)
        memTb, memT_ = warena("memT", 8192, 2048)
        memT = memT_.rearrange("p (c m) -> p c m", m=256)
        dve(lambda h: h.memset(mva[:], 1.0), w=["mva"])
        MEMT = [memTb]

        def layer(l):
            if l > 0:
                for k in range(8):
                    dpl(W_in[k][1], win[l, :, k, :], w=[W_in[k][0]])
                dpl(W_o[1].rearrange("p (a b) -> p a b", b=1024), wo[l], w=[W_o[0]])
            load_ln(1 + 3 * l)
            WIN = [w[0] for w in W_in]
            pool(lambda h: h.memset(Sst[:], 0.0), w=["Sst"])
            pool(lambda h: h.memset(exs[:], 0.0), w=["exs0", "exs1"])
            pool(lambda h: h.memset(kAtokZ[:], 0.0), w=["kAtokZ"])
            pool(lambda h: h.memset(uext[:, :, 0:2], 0.0), w=["uext0", "uext1"])
            pool(lambda h: h.memset(ext[:, :, 0:16], 0.0), w=["ext0", "ext1"])
            pool(lambda h: h.memset(wbd[:], 0.0), w=["wbd"])
            for g_ in range(4):
                dpl(wbd[64 * (g_ % 2):64 * (g_ % 2) + 64, g_ // 2, 64 * (g_ % 2):64 * (g_ % 2) + 64], pw[l, g_], w=["wbd"])
            dsp(normg[:], hng[l:l + 1, :].partition_broadcast(128), w=["normg"])
            def p1(t, prev_post):
                n = ntok(t); c0 = t * 128
                mixT = mixTs[t % 2]; mk = f"mixT{t % 2}"
                b1, k1 = bank()
                for k in range(8):
                    pe(lambda h, k=k, b1=b1: h.matmul(b1[0:n, :], lhsT=xT[:, k, c0:c0 + n], rhs=W_in[k][1][:, cK:cK + 512], start=(k == 0), stop=(k == 7)), [f"xT{t}"] + WIN, [k1])
                act(lambda h: h.copy(out=Vh[0:n], in_=b1[0:n, 256:512]), [k1], ["Vh"])
                act(lambda h: h.copy(out=Vaug[0:n, t % 2, :, 0:64], in_=b1[0:n, 128:256].rearrange("p (kv d) -> p kv d", d=64)), [k1], [f"Va{t % 2}"])
                if t >= 15:
                    dve(lambda h, b1=b1: h.tensor_copy(out=sqb[0:n, 0:256], in_=b1[0:n, 0:256]), [k1], ["sqb"])
                    if t == 15:
                        dsp(okp[l], sqb[:, 0:128], ["sqb"]); dsp(ovp[l], sqb[:, 128:256], ["sqb"])
                    elif not DBG_SKIP:
                        dsp(oks[l, :, 0:124, :], ck[l, :, 4:128, :]); dsp(ovs[l, :, 0:124, :], cv[l, :, 4:128, :])
                        for tt in range(4):
                            dsp(oks[l, :, 124 + tt, :], sqb[tt:64:4, 0:128], ["sqb"]); dsp(ovs[l, :, 124 + tt, :], sqb[tt:64:4, 128:256], ["sqb"])
                    b2, k2 = bank(); b3, k3 = bank()
                    for k in range(8):
                        pe(lambda h, k=k, b2=b2: h.matmul(b2[0:n, :], lhsT=xT[:, k, c0:c0 + n], rhs=W_in[k][1][:, cCC:cCC + 512], start=(k == 0), stop=(k == 7)), [f"xT{t}"] + WIN, [k2])
                    for k in range(8):
                        pe(lambda h, k=k, b3=b3: h.matmul(b3[0:n, 0:256], lhsT=xT[:, k, c0:c0 + n], rhs=W_in[k][1][:, cDV:cDV + 256], start=(k == 0), stop=(k == 7)), [f"xT{t}"] + WIN, [k3])
                    act(lambda h, b2=b2: h.copy(out=sgb[0:n, :], in_=b2[0:n, 256:512]), [k2], ["sgb"])
                    dve(lambda h, b2=b2: h.tensor_tensor(out=sgb[0:n, :], in0=b2[0:n, 0:256], in1=sgb[0:n, :], op=ALU.mult), [k2, "sgb"], ["sgb"])
                    act(lambda h, b3=b3: h.copy(out=tmpS[0:n, 0:256], in_=b3[0:n, 0:256]), [k3], ["tmpS"])
                    if t == 15:
                        dsp(ocp[l], sgb[126:128, :], ["sgb"]); dsp(opp[l], tmpS[113:128, 0:256], ["tmpS"])
                    elif not DBG_SKIP:
                        dsp(ops_[l, :, 0:11, :], spl[l, :, 4:15, :])
                        for tt in range(4):
                            dsp(ops_[l, :, 11 + tt, :], tmpS[tt:64:4, 0:256], ["tmpS"])
                            if tt >= 2:
                                dsp(ocs[l, :, tt - 2, :], sgb[tt:64:4, :], ["sgb"])
                def fm(col):
                    b, bk = bank()
                    for k in range(8):
                        pe(lambda h, k=k: h.matmul(b[:, 0:n], lhsT=W_in[k][1][:, col:col + 128], rhs=xT[:, k, c0:c0 + n], start=(k == 0), stop=(k == 7)), [f"xT{t}"] + WIN, [bk])
                    return b, bk
                sl = t % 2
                v3 = lambda ap: ap.rearrange("p (b t) -> p b t", t=4)

                def attn_out(bo_, ko_):
                    ov = bo_[0:n, 0:260].rearrange("p (h e) -> p h e", e=65)
                    dve(lambda h: h.tensor_tensor(out=den[0:n], in0=ov[:, :, 64], in1=esink[0:n, l * 4:(l + 1) * 4], op=ALU.add), [ko_, "esink"], ["den"])
                    dve(lambda h: h.reciprocal(out=reca[0:n], in_=den[0:n]), ["den"], ["reca"])
                    dve(lambda h: h.tensor_tensor(out=otok[0:n].rearrange("p (h d) -> p h d", d=64), in0=ov[:, :, 0:64], in1=reca[0:n].unsqueeze(2).to_broadcast([n, 4, 64]), op=ALU.mult), [ko_, "reca"], ["otok"])

                def hgrn_gate():
                    bg, kg = bank()
                    for k_ in range(8):
                        pe(lambda h, k_=k_: h.matmul(bg[0:n, 0:256], lhsT=xT[:, k_, c0:c0 + n], rhs=W_in[k_][1][:, cG:cG + 256], start=(k_ == 0), stop=(k_ == 7)), [f"xT{t}"] + WIN, [kg])
                    act(lambda h: h.activation(out=sgb[0:n], in_=bg[0:n, 0:256], func=AF.Exp, scale=-1.0), [kg], ["sgb"])
                    act(lambda h: h.activation(out=sgb[0:n], in_=sgb[0:n], func=AF.Ln, scale=1.0, bias=1.0), ["sgb"], ["sgb"])
                    act(lambda h: h.activation(out=sgb[0:n], in_=sgb[0:n], func=AF.Exp, scale=-1.0), ["sgb"], ["sgb"])
                    dve(lambda h: h.tensor_tensor(out=sgb[0:n], in0=bg[0:n, 0:256], in1=sgb[0:n], op=ALU.mult), [kg, "sgb"], ["sgb"])

                def hgrn_out(bO, kO):
                    act(lambda h: h.activation(out=sqb[0:n], in_=bO[0:n, 0:256], func=AF.Square), [kO], ["sqb"])
                    dve(lambda h: h.tensor_reduce(out=ssq[0:n], in_=sqb[0:n].rearrange("p (a v) -> p a v", v=64), axis=AX.X, op=ALU.add), ["sqb"], ["ssq"])
                    act(lambda h: h.activation(out=rsq[0:n], in_=ssq[0:n], func=AF.Ln, scale=1.0 / 64, bias=RMS_EPS), ["ssq"], ["rsq"])
                    act(lambda h: h.activation(out=rsq[0:n], in_=rsq[0:n], func=AF.Exp, scale=-0.5), ["rsq"], ["rsq"])
                    dve(lambda h: h.tensor_tensor(out=sqb[0:n].rearrange("p (a v) -> p a v", v=64), in0=bO[0:n, 0:256].rearrange("p (a v) -> p a v", v=64), in1=rsq[0:n].unsqueeze(2).to_broadcast([n, 4, 64]), op=ALU.mult), [kO, "rsq", "sqb"], ["sqb"])
                    dve(lambda h: h.tensor_tensor(out=sqb[0:n], in0=sqb[0:n], in1=normg[0:n], op=ALU.mult), ["sqb", "normg"], ["sqb"])
                    dve(lambda h: h.tensor_tensor(out=otok2[0:n], in0=sqb[0:n], in1=sgb[0:n], op=ALU.mult), ["sqb", "sgb"], ["otok2"])
                if t == 16:
                    dsp(sqb[0:32, :], scv[l], w=["sqb"])
                    for hc in range(2):
                        bT, kT_ = bank()
                        pe(lambda h, hc=hc, bT=bT: h.transpose(bT[:, 0:32], sqb[0:32, hc * 128:(hc + 1) * 128], ident[0:32, 0:32]), ["sqb", "ct"], [kT_])
                        act(lambda h, hc=hc, bT=bT: h.copy(out=uxs[:, hc, :, 0:2], in_=bT[:, 0:32].rearrange("p (b j) -> p b j", j=2)), [kT_], [f"uxs{hc}"])
                    for half in range(2):
                        dsp(sqb[0:120, :], spl[l, 8 * half:8 * half + 8].rearrange("b j c -> (b j) c"), w=["sqb"])
                        for hc in range(2):
                            bT, kT_ = bank()
                            pe(lambda h, hc=hc, bT=bT: h.transpose(bT[:, 0:120], sqb[0:120, hc * 128:(hc + 1) * 128], ident[0:120, 0:120]), ["sqb", "ct"], [kT_])
                            act(lambda h, hc=hc, bT=bT, half=half: h.copy(out=exs[:, hc, 8 * half:8 * half + 8, 1:16], in_=bT[:, 0:120].rearrange("p (b j) -> p b j", j=15)), [kT_], [f"exs{hc}"])
                def secA():
                    for g, col in enumerate((cA, cB)):
                        b, bk = fm(col)
                        for kv in range(2):
                            act(lambda h, b=b, g=g, kv=kv: h.copy(out=qZ[64 * kv:64 * kv + 64, g, kv, 0:n], in_=b[64 * kv:64 * kv + 64, 0:n]), [bk], ["qZ"])
                    bK, kK_ = fm(cK)
                    act(lambda h: h.copy(out=kTr[:, sl, 0:n], in_=bK[:, 0:n]), [kK_], [f"kT{sl}"])
                    if t < 16:
                        blks = [0] if t == 0 else [1, 0]
                        for blk in blks:
                            ksl = sl if blk == 0 else 1 - sl
                            bs, ks = bank()
                            for g in range(2):
                                pe(lambda h, g=g, bs=bs, ksl=ksl: h.matmul(bs[:, g * 256:(g + 1) * 256], lhsT=kTr[:, ksl, :], rhs=qZ[:, g, :, :].rearrange("p a q -> p (a q)"), start=True, stop=True), [f"kT{ksl}", "qZ"], [ks])
                            dve(lambda h, bs=bs, blk=blk: h.scalar_tensor_tensor(out=tmpS[:], in0=bs[:, :], scalar=0.125, in1=C("bt")[:, blk * 512:(blk + 1) * 512], op0=ALU.mult, op1=ALU.add), [ks, "ct"], ["tmpS"])
                            act(lambda h, blk=blk: h.activation(out=PT[:, blk, :], in_=tmpS[:], func=AF.Exp), ["tmpS"], [f"PT{blk}"])
                        bo_, ko_ = bank()
                        for kv in range(2):
                            for g in range(2):
                                hh = kv * 2 + g; pc = (g * 2 + kv) * 128
                                if t > 0:
                                    pe(lambda h, hh=hh, pc=pc, kv=kv: h.matmul(bo_[:, hh * 65:(hh + 1) * 65], lhsT=PT[:, 1, pc:pc + 128], rhs=Vaug[:, 1 - sl, kv, :], start=True, stop=False), ["PT1", f"Va{1 - sl}"], [ko_])
                                pe(lambda h, hh=hh, pc=pc, kv=kv: h.matmul(bo_[:, hh * 65:(hh + 1) * 65], lhsT=PT[:, 0, pc:pc + 128], rhs=Vaug[:, sl, kv, :], start=(t == 0), stop=True), ["PT0", f"Va{sl}"], [ko_])
                        attn_out(bo_, ko_)
                def secB():
                    if t < 16:
                        hgrn_gate()
                        tbH, tkH = tbank()
                        for hc in range(2):
                            bq, kq = fm(cBQ + hc * 128); bf, kf = fm(cBF + hc * 128)
                            act(lambda h, bf=bf: h.activation(out=hE[:, 0:n], in_=bf[:, 0:n], func=AF.Exp, scale=-1.0), [kf], ["hE"])
                            act(lambda h, hc=hc: h.activation(out=hL1[:, 0:n], in_=hE[:, 0:n], func=AF.Ln, scale=lbt[:, hc, l:l + 1], bias=1.0), ["hE", "lbt"], ["hL1"])
                            act(lambda h: h.activation(out=hL2[:, 0:n], in_=hE[:, 0:n], func=AF.Ln, scale=1.0, bias=1.0), ["hE"], ["hL2"])
                            dve(lambda h: h.tensor_tensor(out=hL1[:, 0:n], in0=hL1[:, 0:n], in1=hL2[:, 0:n], op=ALU.subtract), ["hL1", "hL2"], ["hL1"])
                            act(lambda h: h.activation(out=hL2[:, 0:n], in_=hL2[:, 0:n], func=AF.Exp, scale=-1.0), ["hL2"], ["hL2"])
                            dve(lambda h, hc=hc: h.scalar_tensor_tensor(out=hE[:, 0:n], in0=hE[:, 0:n], scalar=oml[:, hc, l:l + 1], in1=hL2[:, 0:n], op0=ALU.mult, op1=ALU.mult), ["hE", "hL2", "oml"], ["hE"])
                            for c2 in range(2):
                                dve(lambda h, c2=c2: h.tensor_tensor_scan(out=hCU[:, c2 * 64:(c2 + 1) * 64], data0=ones[:, 0:64], data1=hL1[:, c2 * 64:(c2 + 1) * 64], initial=0.0, op0=ALU.mult, op1=ALU.add), ["hL1", "ones"], ["hCU"])
                            act(lambda h: h.activation(out=hEC[:, 0:n], in_=hCU[:, 0:n], func=AF.Exp), ["hCU"], ["hEC"])
                            act(lambda h: h.activation(out=hEN[:, 0:n], in_=hCU[:, 0:n], func=AF.Exp, scale=-1.0), ["hCU"], ["hEN"])
                            act(lambda h, bq=bq: h.activation(out=hEQ[:, 0:n], in_=bq[:, 0:n], func=AF.Exp, scale=-1.0), [kq], ["hEQ"])
                            act(lambda h: h.activation(out=hEQ[:, 0:n], in_=hEQ[:, 0:n], func=AF.Ln, scale=1.0, bias=1.0), ["hEQ"], ["hEQ"])
                            act(lambda h: h.activation(out=hEQ[:, 0:n], in_=hEQ[:, 0:n], func=AF.Exp, scale=-1.0), ["hEQ"], ["hEQ"])
                            dve(lambda h, bq=bq: h.tensor_tensor(out=hEQ[:, 0:n], in0=bq[:, 0:n], in1=hEQ[:, 0:n], op=ALU.mult), [kq, "hEQ"], ["hEQ"])
                            for hp in range(2):
                                rw = slice(64 * hp, 64 * hp + 64)
                                dve(lambda h, hc=hc, hp=hp, rw=rw: h.tensor_tensor(out=qSZ[rw, hc, :, hp, :], in0=hEQ[rw, :].rearrange("p (c t) -> p c t", t=64), in1=hEC[rw, :].rearrange("p (c t) -> p c t", t=64), op=ALU.mult), ["hEQ", "hEC"], ["qSZ"])
                            dve(lambda h, hc=hc: h.tensor_tensor(out=kAb[:, hc, :], in0=hE[:, 0:n], in1=hEN[:, 0:n], op=ALU.mult), ["hE", "hEN"], [f"kAb{hc}"])
                            act(lambda h, hc=hc: h.copy(out=ECL[:, hc, :], in_=hEC[:, 63:128:64]), ["hEC"], ["ECL"])
                            pe(lambda h, hc=hc: h.transpose(tbH[:, hc * 128:(hc + 1) * 128], kAb[:, hc, :], identb[:]), [f"kAb{hc}", "identb"], [tkH])
                        for c2 in range(2):
                            rw = slice(64 * c2, 64 * c2 + 64)
                            dve(lambda h, c2=c2, rw=rw: h.tensor_copy(out=kAtokZ[rw, c2, :], in_=tbH[rw, 0:256]), [tkH], ["kAtokZ"])
                        bU, kU = bank()
                        for c2 in range(2):
                            for hc in range(2):
                                pe(lambda h, c2=c2, hc=hc: h.matmul(bU[:, (c2 * 2 + hc) * 128:(c2 * 2 + hc + 1) * 128], lhsT=kAtokZ[:, c2, hc * 128:(hc + 1) * 128], rhs=Vh[:, hc * 128:(hc + 1) * 128], start=True, stop=True), ["kAtokZ", "Vh"], [kU])
                        for c2 in range(2):
                            act(lambda h, c2=c2: h.copy(out=Sbf[:, :, c2, :], in_=Sst[:]), ["Sst"], ["Sbf"])
                            for hp in range(2):
                                rw = slice(64 * hp, 64 * hp + 64)
                                dve(lambda h, c2=c2, hp=hp, rw=rw: h.tensor_tensor(out=Sst[rw], in0=Sst[rw], in1=bU[rw, c2 * 256:(c2 + 1) * 256].rearrange("p (hc hh v) -> p hc hh v", hc=2, hh=2)[:, :, hp, :], op=ALU.add), ["Sst", kU], ["Sst"])
                            dve(lambda h, c2=c2: h.tensor_tensor(out=Sst[:], in0=Sst[:], in1=ECL[:, :, c2:c2 + 1].to_broadcast([128, 2, 64]), op=ALU.mult), ["Sst", "ECL"], ["Sst"])
                        bA, kA_ = bank()
                        for c2 in range(2):
                            for hc in range(2):
                                pe(lambda h, c2=c2, hc=hc: h.matmul(bA[64 * c2:64 * c2 + 64, hc * 128:(hc + 1) * 128], lhsT=kAb[:, hc, c2 * 64:(c2 + 1) * 64], rhs=qSZ[:, hc, c2, :, :].rearrange("p a t -> p (a t)"), start=True, stop=True), [f"kAb{hc}", "qSZ"], [kA_])
                        for c2 in range(2):
                            rw = slice(64 * c2, 64 * c2 + 64)
                            dve(lambda h, c2=c2, rw=rw: h.tensor_tensor(out=AtZ[rw, c2, :, :], in0=bA[rw, 0:256].rearrange("p (a t) -> p a t", t=64), in1=C("tri2")[rw, :].unsqueeze(1).to_broadcast([64, 4, 64]), op=ALU.mult), [kA_, "ct"], ["AtZ"])
                        bO, kO = bank()
                        for c2 in range(2):
                            for hh in range(4):
                                hc, hp = hh // 2, hh % 2
                                pe(lambda h, c2=c2, hh=hh, hc=hc, hp=hp: h.matmul(bO[64 * c2:64 * c2 + 64, hh * 64:(hh + 1) * 64], lhsT=qSZ[:, hc, c2, hp, :], rhs=Sbf[:, hc, c2, :], start=True, stop=False), ["qSZ", "Sbf"], [kO])
                                pe(lambda h, c2=c2, hh=hh: h.matmul(bO[64 * c2:64 * c2 + 64, hh * 64:(hh + 1) * 64], lhsT=AtZ[:, c2, hh, :], rhs=Vh[:, hh * 64:(hh + 1) * 64], start=False, stop=True), ["AtZ", "Vh"], [kO])
                        hgrn_out(bO, kO)
                        if t == int(os.environ.get('DBG_ST', '15')):
                            dsp(ohp[l].rearrange("(hc hp) k v -> (hp k) hc v", hc=2), Sst[:], ["Sst"])
                    else:
                        tbH, tkH = tbank()
                        for hc in range(2):
                            bq, kq = fm(cBQ + hc * 128); bf, kf = fm(cBF + hc * 128)
                            act(lambda h, bf=bf: h.activation(out=hE[:, 0:n], in_=bf[:, 0:n], func=AF.Exp, scale=-1.0), [kf], ["hE"])
                            act(lambda h, hc=hc: h.activation(out=hL1[:, 0:n], in_=hE[:, 0:n], func=AF.Ln, scale=lbt[:, hc, l:l + 1], bias=1.0), ["hE", "lbt"], ["hL1"])
                            act(lambda h: h.activation(out=hL2[:, 0:n], in_=hE[:, 0:n], func=AF.Ln, scale=1.0, bias=1.0), ["hE"], ["hL2"])
                            dve(lambda h: h.tensor_tensor(out=hL1[:, 0:n], in0=hL1[:, 0:n], in1=hL2[:, 0:n], op=ALU.subtract), ["hL1", "hL2"], ["hL1"])
                            act(lambda h: h.activation(out=hL2[:, 0:n], in_=hL2[:, 0:n], func=AF.Exp, scale=-1.0), ["hL2"], ["hL2"])
                            dve(lambda h, hc=hc: h.scalar_tensor_tensor(out=hE[:, 0:n], in0=hE[:, 0:n], scalar=oml[:, hc, l:l + 1], in1=hL2[:, 0:n], op0=ALU.mult, op1=ALU.mult), ["hE", "hL2", "oml"], ["hE"])
                            dve(lambda h: h.tensor_copy(out=hCU[:, 0:n], in_=hL1[:, 0:n]), ["hL1"], ["hCU"])
                            for tt in range(1, 4):
                                dve(lambda h, tt=tt: h.tensor_tensor(out=v3(hCU[:, 0:n])[:, :, tt:tt + 1], in0=v3(hCU[:, 0:n])[:, :, tt - 1:tt], in1=v3(hL1[:, 0:n])[:, :, tt:tt + 1], op=ALU.add), ["hCU", "hL1"], ["hCU"])
                            act(lambda h: h.activation(out=hEC[:, 0:n], in_=hCU[:, 0:n], func=AF.Exp), ["hCU"], ["hEC"])
                            act(lambda h: h.activation(out=hEN[:, 0:n], in_=hCU[:, 0:n], func=AF.Exp, scale=-1.0), ["hCU"], ["hEN"])
                            act(lambda h, bq=bq: h.activation(out=hEQ[:, 0:n], in_=bq[:, 0:n], func=AF.Exp, scale=-1.0), [kq], ["hEQ"])
                            act(lambda h: h.activation(out=hEQ[:, 0:n], in_=hEQ[:, 0:n], func=AF.Ln, scale=1.0, bias=1.0), ["hEQ"], ["hEQ"])
                            act(lambda h: h.activation(out=hEQ[:, 0:n], in_=hEQ[:, 0:n], func=AF.Exp, scale=-1.0), ["hEQ"], ["hEQ"])
                            dve(lambda h, bq=bq: h.tensor_tensor(out=hEQ[:, 0:n], in0=bq[:, 0:n], in1=hEQ[:, 0:n], op=ALU.mult), [kq, "hEQ"], ["hEQ"])
                            for hp in range(2):
                                rw = slice(64 * hp, 64 * hp + 64)
                                dve(lambda h, hc=hc, hp=hp, rw=rw: h.tensor_tensor(out=qSZ[rw, hc, 0, hp, :], in0=hEQ[rw, 0:n], in1=hEC[rw, 0:n], op=ALU.mult), ["hEQ", "hEC"], ["qSZ"])
                            dve(lambda h, hc=hc: h.tensor_tensor(out=kAb[:, hc, 0:n], in0=hE[:, 0:n], in1=hEN[:, 0:n], op=ALU.mult), ["hE", "hEN"], [f"kAb{hc}"])
                            act(lambda h, hc=hc: h.copy(out=ECLs[:, hc, :], in_=v3(hEC[:, 0:n])[:, :, 3]), ["hEC"], ["ECLs"])
                            dve(lambda h, hc=hc: h.tensor_tensor(out=v3(kAb[:, hc, 64:128]), in0=v3(kAb[:, hc, 0:64]), in1=ECLs[:, hc, :].unsqueeze(2).to_broadcast([128, 16, 4]), op=ALU.mult), [f"kAb{hc}", "ECLs"], [f"kAb{hc}"])
                            pe(lambda h, hc=hc: h.transpose(tbH[0:64, hc * 128:(hc + 1) * 128], kAb[:, hc, 64:128], identb[:]), [f"kAb{hc}", "identb"], [tkH])
                        dve(lambda h: h.tensor_copy(out=kAtokZ[0:64, 0, :], in_=tbH[0:64, 0:256]), [tkH], ["kAtokZ"])
                        bA, kA_ = bank()
                        for hc in range(2):
                            pe(lambda h, hc=hc: h.matmul(bA[0:64, hc * 128:(hc + 1) * 128], lhsT=kAb[:, hc, 0:64], rhs=qSZ[:, hc, 0, :, :].rearrange("p a t -> p (a t)"), start=True, stop=True), [f"kAb{hc}", "qSZ"], [kA_])
                        dve(lambda h: h.tensor_tensor(out=AtZ[0:64, 0, :, :], in0=bA[0:64, 0:256].rearrange("p (a t) -> p a t", t=64), in1=C("bdc")[0:64, :].unsqueeze(1).to_broadcast([64, 4, 64]), op=ALU.mult), [kA_, "ct"], ["AtZ"])
                        hgrn_gate()
                def secC():
                    for hc in range(2):
                        if t < 16:
                            bch, kch = fm(cCH + hc * 128)
                            act(lambda h, bch=bch: h.copy(out=chs[:, 0:n], in_=bch[:, 0:n]), [kch], ["chs"])
                            bcc, kcc = fm(cCC + hc * 128)
                            dve(lambda h, bcc=bcc, hc=hc: h.tensor_tensor(out=uext[:, hc, 2:2 + n], in0=bcc[:, 0:n], in1=chs[:, 0:n], op=ALU.mult), [kcc, "chs"], [f"uext{hc}"])
                            dve(lambda h, hc=hc: h.tensor_scalar(out=ycv[:, 0:n], in0=uext[:, hc, 2:2 + n], scalar1=cwt[:, l, hc, 2:3], scalar2=None, op0=ALU.mult), [f"uext{hc}", "cwt"], ["ycv"])
                            for j in (1, 0):
                                dve(lambda h, hc=hc, j=j: h.scalar_tensor_tensor(out=ycv[:, 0:n], in0=uext[:, hc, j:j + n], scalar=cwt[:, l, hc, j:j + 1], in1=ycv[:, 0:n], op0=ALU.mult, op1=ALU.add), [f"uext{hc}", "cwt", "ycv"], ["ycv"])
                            bcb, kcb = fm(cCB + hc * 128)
                            dve(lambda h, hc=hc, bcb=bcb: h.tensor_tensor(out=mixT[:, 4 + hc, 0:n], in0=ycv[:, 0:n], in1=bcb[:, 0:n], op=ALU.mult), ["ycv", kcb], [mk])
                            act(lambda h, hc=hc: h.copy(out=uext[:, hc, 0:2], in_=uext[:, hc, n:n + 2]), [f"uext{hc}"], [f"uext{hc}"])
                        else:
                            v3 = lambda ap: ap.rearrange("p (b t) -> p b t", t=4)
                            bch, kch = fm(cCH + hc * 128)
                            act(lambda h, bch=bch: h.copy(out=chs[:, 0:n], in_=bch[:, 0:n]), [kch], ["chs"])
                            bcc, kcc = fm(cCC + hc * 128)
                            dve(lambda h, bcc=bcc, hc=hc: h.tensor_tensor(out=uxs[:, hc, :, 2:6], in0=v3(bcc[:, 0:n]), in1=v3(chs[:, 0:n]), op=ALU.mult), [kcc, "chs"], [f"uxs{hc}"])
                            dve(lambda h, hc=hc: h.tensor_scalar(out=v3(ycv[:, 0:n]), in0=uxs[:, hc, :, 2:6], scalar1=cwt[:, l, hc, 2:3], scalar2=None, op0=ALU.mult), [f"uxs{hc}", "cwt"], ["ycv"])
                            for j in (1, 0):
                                dve(lambda h, hc=hc, j=j: h.scalar_tensor_tensor(out=v3(ycv[:, 0:n]), in0=uxs[:, hc, :, j:j + 4], scalar=cwt[:, l, hc, j:j + 1], in1=v3(ycv[:, 0:n]), op0=ALU.mult, op1=ALU.add), [f"uxs{hc}", "cwt", "ycv"], ["ycv"])
                            bcb, kcb = fm(cCB + hc * 128)
                            dve(lambda h, hc=hc, bcb=bcb: h.tensor_tensor(out=mixT[:, 4 + hc, 0:n], in0=ycv[:, 0:n], in1=bcb[:, 0:n], op=ALU.mult), ["ycv", kcb], [mk])
                def secD():
                    for hc in range(2):
                        bdv, kdv = fm(cDV + hc * 128)
                        if t < 16:
                            W_ = 16 + n
                            act(lambda h, bdv=bdv, hc=hc: h.copy(out=ext[:, hc, 16:16 + n], in_=bdv[:, 0:n]), [kdv], [f"ext{hc}"])
                            pool(lambda h, hc=hc: h.tensor_tensor(out=s2[:, 1:W_], in0=ext[:, hc, 1:W_], in1=ext[:, hc, 0:W_ - 1], op=ALU.add), [f"ext{hc}"], ["s2"])
                            pool(lambda h: h.tensor_tensor(out=s4[:, 3:W_], in0=s2[:, 3:W_], in1=s2[:, 1:W_ - 2], op=ALU.add), ["s2"], ["s4"])
                            if hc == 1:
                                pool(lambda h: h.tensor_tensor(out=s8[:, 7:W_], in0=s4[:, 7:W_], in1=s4[:, 3:W_ - 4], op=ALU.add), ["s4"], ["s8"])
                                pool(lambda h: h.tensor_tensor(out=s16[:, 15:W_], in0=s8[:, 15:W_], in1=s8[:, 7:W_ - 8], op=ALU.add), ["s8"], ["s16"])
                            sel = [(s2, "s2", 2.0), (s4, "s4", 4.0)] if hc == 0 else [(s8, "s8", 8.0), (s16, "s16", 16.0)]
                            for half, (sv, sk, w_) in enumerate(sel):
                                rows = slice(64 * half, 64 * half + 64)
                                dve(lambda h, sv=sv, rows=rows, w_=w_, hc=hc: h.scalar_tensor_tensor(out=pooled[rows, hc, 0:n], in0=sv[rows, 16:W_], scalar=1.0 / w_, in1=ext[rows, hc, 16:W_], op0=ALU.mult, op1=ALU.subtract), [sk, f"ext{hc}"], [f"pooled{hc}"])
                                if t == 0:
                                    dve(lambda h, sv=sv, rows=rows, hc=hc: h.tensor_tensor(out=ptmp[rows, 0:16], in0=sv[rows, 16:32], in1=C("icnt")[rows, hc * 16:(hc + 1) * 16], op=ALU.mult), [sk, "ct"], ["ptmp"])
                                    dve(lambda h, rows=rows, hc=hc: h.tensor_tensor(out=pooled[rows, hc, 0:16], in0=ptmp[rows, 0:16], in1=ext[rows, hc, 16:32], op=ALU.subtract), ["ptmp", f"ext{hc}", f"pooled{hc}"], [f"pooled{hc}"])
                            by, ky = bank()
                            pe(lambda h, hc=hc, by=by: h.matmul(by[:, 0:n], lhsT=wbd[:, hc, :], rhs=pooled[:, hc, 0:n], start=True, stop=True), ["wbd", f"pooled{hc}"], [ky])
                            act(lambda h, hc=hc, by=by: h.activation(out=mixT[:, 6 + hc, 0:n], in_=by[:, 0:n], func=AF.Copy, scale=psct[:, l, hc:hc + 1]), [ky, "psct"], [mk])
                            act(lambda h, hc=hc: h.copy(out=ext[:, hc, 0:16], in_=ext[:, hc, n:n + 16]), [f"ext{hc}"], [f"ext{hc}"])
                        else:
                            v3 = lambda ap: ap.rearrange("p (b t) -> p b t", t=4)
                            act(lambda h, bdv=bdv, hc=hc: h.copy(out=exs[:, hc, :, 16:20], in_=v3(bdv[:, 0:n])), [kdv], [f"exs{hc}"])
                            for half in range(2):
                                w_ = (2, 4, 8, 16)[hc * 2 + half]; rows = slice(64 * half, 64 * half + 64)
                                dve(lambda h, rows=rows, hc=hc: h.tensor_copy(out=v3(s2[rows, 0:n]), in_=exs[rows, hc, :, 16:20]), [f"exs{hc}"], ["s2"])
                                for j in range(1, w_):
                                    dve(lambda h, rows=rows, hc=hc, j=j: h.tensor_tensor(out=v3(s2[rows, 0:n]), in0=v3(s2[rows, 0:n]), in1=exs[rows, hc, :, 16 - j:20 - j], op=ALU.add), ["s2", f"exs{hc}"], ["s2"])
                                dve(lambda h, rows=rows, hc=hc, w_=w_: h.scalar_tensor_tensor(out=v3(pooled[rows, hc, 0:n]), in0=v3(s2[rows, 0:n]), scalar=1.0 / w_, in1=exs[rows, hc, :, 16:20], op0=ALU.mult, op1=ALU.subtract), ["s2", f"exs{hc}"], [f"pooled{hc}"])
                            by, ky = bank()
                            pe(lambda h, hc=hc, by=by: h.matmul(by[:, 0:n], lhsT=wbd[:, hc, :], rhs=pooled[:, hc, 0:n], start=True, stop=True), ["wbd", f"pooled{hc}"], [ky])
                            act(lambda h, hc=hc, by=by: h.activation(out=mixT[:, 6 + hc, 0:n], in_=by[:, 0:n], func=AF.Copy, scale=psct[:, l, hc:hc + 1]), [ky, "psct"], [mk])
                if os.environ.get('NO_ILV'):
                    secA(); secB(); secC(); secD()
                    if prev_post:
                        prev_post()
                else:
                    order = os.environ.get("ORD", "EBDCA")
                    table = {"A": (secA, "A", None), "B": (secB, "B", "Bt"), "C": (secC, "C", None), "D": (secD, "D", None), "E": (prev_post, "E", None)}
                    sel = [table[c_] for c_ in order if table[c_][0] is not None]
                    run_interleaved([s_[0] for s_ in sel], [s_[1] for s_ in sel], [s_[2] for s_ in sel])
                if t == 16:
                    PS_ = 28672
                    kc_b, kc_ = warena("kc", 0, 2048); kc = kc_.rearrange("p (b c) -> p b c", c=128)
                    kcT_b, kcT_ = warena("kcT", 2048, 2048); kcT = kcT_.rearrange("p (b c) -> p b c", c=128)
                    Vc_b, Vc_ = warena("Vc", 4096, 2080); Vc = Vc_.rearrange("p (b kv e) -> p b kv e", kv=2, e=65)
                    Pb_b, Pb_ = warena("Pb", 6400, 4096); Pb = Pb_.rearrange("p (g kv b q) -> p g kv b q", g=2, kv=2, b=16)
                    tabs_b, tabs_ = warena("stab", 18944, 1024); tabs = tabs_.bitcast(F32)
                    dsp(tabs, stab[:, :], w=[tabs_b])
                    dpl(kc, ck[l].rearrange("b s c -> s b c"), w=[kc_b])
                    pool(lambda h: h.memset(Vc_, 1.0), w=[Vc_b])
                    for kv in range(2):
                        dpl(Vc[:, :, kv, 0:64], cv[l, :, :, kv * 64:(kv + 1) * 64].rearrange("b s d -> s b d"), w=[Vc_b])
                    pool(lambda h: h.memset(Pb_, 0.0), w=[Pb_b])
                    for half in range(2):
                        tbk, tkk = tbank()
                        for j in range(8):
                            pe(lambda h, j=j, half=half, tbk=tbk: h.transpose(tbk[:, j * 128:(j + 1) * 128], kc[:, half * 8 + j, :], identb[:]), [kc_b, "identb"], [tkk])
                        dve(lambda h, half=half, tbk=tbk: h.tensor_copy(out=kcT[:, half * 8:(half + 1) * 8, :], in_=tbk[:, :].rearrange("p (b c) -> p b c", c=128)), [tkk], [kcT_b])
                    bs, ks = bank()
                    for b_ in range(16):
                        for g in range(2):
                            pe(lambda h, b_=b_, g=g: h.matmul(bs[:, g * 128:(g + 1) * 128].rearrange("p (kv q) -> p kv q", kv=2)[:, :, b_ * 4:b_ * 4 + 4], lhsT=kcT[:, b_, :], rhs=qZ[:, g, :, b_ * 4:b_ * 4 + 4], start=True, stop=True), [kcT_b, "qZ"], [ks])
                    bn, kn = bank()
                    for g in range(2):
                        pe(lambda h, g=g: h.matmul(bn[0:64, g * 128:(g + 1) * 128].rearrange("p (kv q) -> p kv q", kv=2), lhsT=kTr[:, 0, 0:64], rhs=qZ[:, g, :, 0:64], start=True, stop=True), ["kT0", "qZ"], [kn])
                    dve(lambda h: h.scalar_tensor_tensor(out=tmpS[:, 0:256], in0=bs[:, 0:256], scalar=0.125, in1=tabs[:, 0:256], op0=ALU.mult, op1=ALU.add), [ks, tabs_b], ["tmpS"])
                    dve(lambda h: h.scalar_tensor_tensor(out=tmpS[0:64, 256:512], in0=bn[0:64, 0:256], scalar=0.125, in1=tabs[0:64, 256:512], op0=ALU.mult, op1=ALU.add), [kn, tabs_b, "tmpS"], ["tmpS"])
                    for gk in range(4):
                        act(lambda h, gk=gk: h.activation(out=bass.AP(WA, 6400 + gk * 1024, [[PS_, 128], [68, 16], [1, 4]]), in_=v3(tmpS[:, gk * 64:(gk + 1) * 64]), func=AF.Exp), ["tmpS"], [Pb_b])
                    act(lambda h: h.activation(out=PT[0:64, 0, 0:256], in_=tmpS[0:64, 256:512], func=AF.Exp), ["tmpS"], ["PT0"])
                    bo_, ko_ = bank()
                    for kv in range(2):
                        for g in range(2):
                            hh = kv * 2 + g; pc = (g * 2 + kv) * 64
                            for b_ in range(16):
                                pe(lambda h, hh=hh, g=g, kv=kv, b_=b_: h.matmul(bo_[0:64, hh * 65:(hh + 1) * 65], lhsT=Pb[:, g, kv, b_, :], rhs=Vc[:, b_, kv, :], start=(b_ == 0), stop=False), [Pb_b, Vc_b], [ko_])
                            pe(lambda h, hh=hh, pc=pc, kv=kv: h.matmul(bo_[0:64, hh * 65:(hh + 1) * 65], lhsT=PT[0:64, 0, pc:pc + 64], rhs=Vaug[0:64, 0, kv, :], start=False, stop=True), ["PT0", "Va0"], [ko_])
                    attn_out(bo_, ko_)
                    S0f_b, S0f_ = warena("S0f", 10496, 4096); S0f = S0f_.bitcast(F32).rearrange("p (hc b v) -> p hc b v", hc=2, b=16)
                    S0b_b, S0b_ = warena("S0b", 14592, 2048); S0b = S0b_.rearrange("p (hc b v) -> p hc b v", hc=2, b=16)
                    qSb_b, qSb_ = warena("qSb", 16640, 2048); qSbig = qSb_.rearrange("p (hp b q) -> p hp b q", hp=2, b=16)
                    Vbg_b, Vbg_ = warena("Vbg", 0, 4096); Vbig = Vbg_.rearrange("p (b c) -> p b c", c=256)
                    for hc in range(2):
                        for hp in range(2):
                            srcS = sh[l, :, hc * 2 + hp].rearrange("b k v -> k b v")
                            dsp(S0f[64 * hp:64 * hp + 64, hc], srcS, w=[S0f_b]); dpl(S0b[64 * hp:64 * hp + 64, hc], srcS, w=[S0b_b])
                    pool(lambda h: h.memset(qSb_, 0.0), w=[qSb_b])
                    dve(lambda h: h.tensor_tensor(out=Vbig[0:64], in0=Vh[0:64, :].unsqueeze(1).to_broadcast([64, 16, 256]), in1=C("mind")[0:64, :].unsqueeze(2).to_broadcast([64, 16, 256]), op=ALU.mult), ["Vh", "ct"], [Vbg_b])
                    bO, kO = bank()
                    for hc in range(2):
                        for hp in range(2):
                            rw = slice(64 * hp, 64 * hp + 64)
                            dve(lambda h, hc=hc, hp=hp, rw=rw: h.tensor_copy(out=bass.AP(WA, 64 * hp * PS_ + 16640 + hp * 1024, [[PS_, 64], [68, 16], [1, 4]]), in_=v3(qSZ[rw, hc, 0, hp, :])), ["qSZ"], [qSb_b])
                        for hp in range(2):
                            hh = hc * 2 + hp
                            for b_ in range(16):
                                pe(lambda h, hh=hh, hc=hc, hp=hp, b_=b_: h.matmul(bO[0:64, hh * 64:(hh + 1) * 64], lhsT=qSbig[:, hp, b_, :], rhs=S0b[:, hc, b_, :], start=(b_ == 0), stop=False), [qSb_b, S0b_b], [kO])
                            pe(lambda h, hh=hh: h.matmul(bO[0:64, hh * 64:(hh + 1) * 64], lhsT=AtZ[:, 0, hh, :], rhs=Vh[:, hh * 64:(hh + 1) * 64], start=False, stop=True), ["AtZ", "Vh"], [kO])
                    hgrn_out(bO, kO)
                    for hc in range(2):
                        for bg_ in range(4):
                            bU, kU = bank()
                            pe(lambda h, hc=hc, bg_=bg_, bU=bU: h.matmul(bU[:, 0:512].rearrange("p (b c) -> p b c", c=128), lhsT=kAtokZ[0:64, 0, hc * 128:(hc + 1) * 128], rhs=Vbig[0:64, bg_ * 4:(bg_ + 1) * 4, hc * 128:(hc + 1) * 128], start=True, stop=True), ["kAtokZ", Vbg_b], [kU])
                            for hp in range(2):
                                rw = slice(64 * hp, 64 * hp + 64); b4 = slice(bg_ * 4, bg_ * 4 + 4)
                                dve(lambda h, hc=hc, rw=rw, b4=b4: h.tensor_tensor(out=S0f[rw, hc, b4, :], in0=S0f[rw, hc, b4, :], in1=ECLs[rw, hc, b4].unsqueeze(2).to_broadcast([64, 4, 64]), op=ALU.mult), [S0f_b, "ECLs"], [S0f_b])
                                dve(lambda h, hc=hc, hp=hp, rw=rw, b4=b4, bU=bU: h.tensor_tensor(out=S0f[rw, hc, b4, :], in0=S0f[rw, hc, b4, :], in1=bU[rw, 0:512].rearrange("p (b c) -> p b c", c=128)[:, :, hp * 64:(hp + 1) * 64], op=ALU.add), [S0f_b, kU], [S0f_b])
                    for hc in range(2):
                        for hp in range(2):
                            dsp(ohs[l, :, hc * 2 + hp].rearrange("b k v -> k b v"), S0f[64 * hp:64 * hp + 64, hc], [S0f_b])
                def post():
                    tb, tk = tbank()
                    for c in range(2):
                        pe(lambda h, c=c: h.transpose(tb[:, c * 128:c * 128 + n], otok[0:n, c * 128:(c + 1) * 128], identb[0:n, 0:n]), ["otok", "identb"], [tk])
                    dve(lambda h: h.tensor_copy(out=mixT[:, 0:2, 0:n], in_=tb[:, 0:256].rearrange("p (a b) -> p a b", b=128)[:, :, 0:n]), [tk], [mk])
                    tb2, tk2 = tbank()
                    for c in range(2):
                        pe(lambda h, c=c: h.transpose(tb2[:, c * 128:c * 128 + n], otok2[0:n, c * 128:(c + 1) * 128], identb[0:n, 0:n]), ["otok2", "identb"], [tk2])
                    dve(lambda h: h.tensor_copy(out=mixT[:, 2:4, 0:n], in_=tb2[:, 0:256].rearrange("p (a b) -> p a b", b=128)[:, :, 0:n]), [tk2], [mk])
                    if dbg:
                        for c in range(8):
                            dpl(dbgo[c * 128:(c + 1) * 128, c0:c0 + n], mixT[:, c, 0:n], [mk])
                    for nh in range(2):
                        bo_h, bok_h = bank()
                        for k in range(8):
                            pe(lambda h, k=k, nh=nh, bo_h=bo_h: h.matmul(bo_h[0:n, :], lhsT=mixT[:, k, 0:n], rhs=W_o[1][:, k * 1024 + nh * 512:k * 1024 + (nh + 1) * 512], start=(k == 0), stop=(k == 7)), [mk, W_o[0]], [bok_h])
                        dve(lambda h, nh=nh, bo_h=bo_h: h.scalar_tensor_tensor(out=x_tok[0:n, t, nh * 512:(nh + 1) * 512], in0=x_tok[0:n, t, nh * 512:(nh + 1) * 512], scalar=ALPHA, in1=bo_h[0:n, :], op0=ALU.mult, op1=ALU.add), [f"x{t}", bok_h], [f"x{t}"])
                    ln_tile(t, 1 + 3 * l, None)
                return post
            prev = None
            for t in (range(NT) if '1' in DBGP else []):
                prev = p1(t, prev)
            if prev:
                prev()
            dpl(W_xq[1].rearrange("p (a b) -> p a b", b=256), wxq[l], w=[W_xq[0]])
            dpl(W_xo[1].rearrange("p (a b) -> p a b", b=1024), wxo[l], w=[W_xo[0]])
            dpl(W_xk[1].rearrange("p (a b) -> p a b", b=256), wxk[l], w=[W_xk[0]])
            dpl(W_xv[1].rearrange("p (a b) -> p a b", b=256), wxv[l], w=[W_xv[0]])
            load_ln(2 + 3 * l)
            for mt in range(2):
                for hf in range(2):
                    dsp(tmpS[:], memp[mt * 128:(mt + 1) * 128, hf * 512:(hf + 1) * 512], w=["tmpS"])
                    act(lambda h, hf=hf: h.copy(out=zbf[:, hf * 512:(hf + 1) * 512], in_=tmpS[:]), ["tmpS"], ["zbf"])
                tbm, tkm = tbank()
                for c in range(8):
                    pe(lambda h, c=c, tbm=tbm: h.transpose(tbm[:, c * 128:(c + 1) * 128], zbf[:, c * 128:(c + 1) * 128], identb[:]), ["zbf", "identb"], [tkm])
                dve(lambda h, mt=mt, tbm=tbm: h.tensor_copy(out=memT[:, :, mt * 128:(mt + 1) * 128], in_=tbm[:].rearrange("p (c m) -> p c m", m=128)), [tkm], [memTb])
            def p2m(mt):
                bk_, kk_ = bank()
                for k in range(8):
                    pe(lambda h, k=k, mt=mt, b=bk_: h.matmul(b[:, 0:256], lhsT=memT[:, k, mt * 128:(mt + 1) * 128], rhs=W_xk[1][:, k * 256:(k + 1) * 256], start=(k == 0), stop=(k == 7)), MEMT + [W_xk[0]], [kk_])
                for k in range(8):
                    pe(lambda h, k=k, mt=mt, b=bk_: h.matmul(b[:, 256:512], lhsT=memT[:, k, mt * 128:(mt + 1) * 128], rhs=W_xv[1][:, k * 256:(k + 1) * 256], start=(k == 0), stop=(k == 7)), MEMT + [W_xv[0]], [kk_])
                dve(lambda h, b=bk_: h.tensor_copy(out=tmpS[:, 0:512], in_=b[:, :]), [kk_], ["tmpS"])
                act(lambda h, b=bk_, mt=mt: h.copy(out=mva[:, mt, :, 0:64], in_=b[:, 256:512].rearrange("p (h d) -> p h d", d=64)), [kk_], ["mva"])
                dsp(omk[l, mt * 128:(mt + 1) * 128, :], tmpS[:, 0:256], ["tmpS"]); dsp(omv[l, mt * 128:(mt + 1) * 128, :], tmpS[:, 256:512], ["tmpS"])
            for mt in (range(2) if '2' in DBGP else []):
                p2m(mt)
            for hc in range(2):
                bk_, kk_ = bank()
                for k in range(8):
                    pe(lambda h, k=k, hc=hc, b=bk_: h.matmul(b[:, 0:256], lhsT=W_xk[1][:, k * 256 + hc * 128:k * 256 + (hc + 1) * 128], rhs=memT[:, k, :], start=(k == 0), stop=(k == 7)), MEMT + [W_xk[0]], [kk_])
                act(lambda h, hc=hc, b=bk_: h.copy(out=mkT[:, hc, :], in_=b[:, 0:256]), [kk_], ["kAtokZ"])
            qxT2 = hEQ[:].bitcast(BF16).rearrange("p (a b) -> p a b", b=128)
            PX2 = tmpS[:].bitcast(BF16).rearrange("p (a b c) -> p a b c", a=2, c=128)

            def p2(t):
                n = 128; c0 = t * 128
                par = t % 2
                qx = (qxT, qxT2)[par]; qk = (["kAb0", "kAb1"], ["hEQ", "hEQ"])[par]
                px = (PX, PX2)[par]; pk = (["PT0", "PT1"], ["tmpS", "tmpS"])[par]
                for hc in range(2):
                    bq, kq = bank()
                    for k in range(8):
                        pe(lambda h, k=k, hc=hc, bq=bq: h.matmul(bq[:, 0:n], lhsT=W_xq[1][:, k * 256 + hc * 128:k * 256 + (hc + 1) * 128], rhs=xT[:, k, c0:c0 + n], start=(k == 0), stop=(k == 7)), [f"xT{t}", W_xq[0]], [kq])
                    act(lambda h, hc=hc, bq=bq: h.copy(out=qx[:, hc, :], in_=bq[:, 0:n]), [kq], [qk[hc]])
                bsc = [bank(), bank()]
                for mt in range(2):
                    for hc in range(2):
                        for hp in range(2):
                            pe(lambda h, hc=hc, hp=hp, mt=mt: h.matmul(bsc[hp][0][:, (mt * 2 + hc) * 128:(mt * 2 + hc + 1) * 128], lhsT=mkT[64 * hp:64 * hp + 64, hc, mt * 128:(mt + 1) * 128], rhs=qx[64 * hp:64 * hp + 64, hc, :], start=True, stop=True), ["kAtokZ"] + qk, [bsc[hp][1]])
                for hp in range(2):
                    act(lambda h, hp=hp: h.activation(out=px[:, hp, :, :].rearrange("p a q -> p (a q)"), in_=bsc[hp][0][:, :], func=AF.Exp, scale=0.125), [bsc[hp][1]], [pk[hp]])

                def stage_b():
                    bo_, ko_ = bank()
                    for hh in range(4):
                        hc, hp = hh // 2, hh % 2
                        for mt in range(2):
                            pe(lambda h, hh=hh, hc=hc, hp=hp, mt=mt: h.matmul(bo_[:, hh * 65:(hh + 1) * 65], lhsT=px[:, hp, mt * 2 + hc, :], rhs=mva[:, mt, hh, :], start=(mt == 0), stop=(mt == 1)), [pk[hp], "mva"], [ko_])
                    ov = bo_[:, 0:260].rearrange("p (h e) -> p h e", e=65)
                    dve(lambda h: h.reciprocal(out=rec[:], in_=ov[:, :, 64]), [ko_], ["reca"])
                    dve(lambda h: h.tensor_tensor(out=oxt[:].rearrange("p (h d) -> p h d", d=64), in0=ov[:, :, 0:64], in1=rec[:].unsqueeze(2).to_broadcast([128, 4, 64]), op=ALU.mult), [ko_, "reca"], ["otok"])
                    tb, tk = tbank()
                    for hc in range(2):
                        pe(lambda h, hc=hc: h.transpose(tb[:, hc * 128:(hc + 1) * 128], oxt[:, hc * 128:(hc + 1) * 128], identb[:]), ["otok", "identb"], [tk])
                    dve(lambda h: h.tensor_copy(out=oxT[:].rearrange("p a b -> p (a b)"), in_=tb[:, 0:256]), [tk], ["otok2"])
                    bo = [bank(), bank()]
                    for nh in range(2):
                        for hc in range(2):
                            pe(lambda h, hc=hc, nh=nh: h.matmul(bo[nh][0][:, :], lhsT=oxT[:, hc, :], rhs=W_xo[1][:, hc * 1024 + nh * 512:hc * 1024 + (nh + 1) * 512], start=(hc == 0), stop=(hc == 1)), ["otok2", W_xo[0]], [bo[nh][1]])
                    ln_tile(t, 2 + 3 * l, bo)
                return stage_b
            prevB = None
            for t in (range(16) if '2' in DBGP else []):
                sb_ = p2(t)
                if prevB:
                    prevB()
                prevB = sb_
            if prevB:
                prevB()
            def p2s():
                t = 16; n = 64; c0 = 2048; PS_ = 28672
                v3 = lambda ap: ap.rearrange("p (b t) -> p b t", t=4)
                for hc in range(2):
                    bq, kq = bank()
                    for k_ in range(8):
                        pe(lambda h, k_=k_, hc=hc, bq=bq: h.matmul(bq[:, 0:n], lhsT=W_xq[1][:, k_ * 256 + hc * 128:k_ * 256 + (hc + 1) * 128], rhs=xT[:, k_, c0:c0 + n], start=(k_ == 0), stop=(k_ == 7)), ["xT16", W_xq[0]], [kq])
                    for hp in range(2):
                        act(lambda h, hc=hc, hp=hp, bq=bq: h.copy(out=qZ[64 * hp:64 * hp + 64, hc, hp, 0:n], in_=bq[64 * hp:64 * hp + 64, 0:n]), [kq], ["qZ"])
                def xhalf(hi_, half):
                    b0 = 8 * half
                    Kc_b, Kc_ = warena("xKc", 10240, 4096); Kc = Kc_.rearrange("p (b mt c) -> p b mt c", b=8, mt=2)
                    KcT_b, KcT_ = warena("xKcT", 14336, 4096); KcT = KcT_.rearrange("p (b hc m) -> p b hc m", b=8, hc=2)
                    Vx_b, Vx_ = warena("xVc", 18432, 4160); Vx = Vx_.rearrange("p (b mt hh e) -> p b mt hh e", b=8, mt=2, hh=4)
                    Px_b, Px_ = warena("xPb", 22592, 4096); Px = Px_.rearrange("p (mt hh b q) -> p mt hh b q", mt=2, hh=4, b=8)
                    pool(lambda h, Vx_=Vx_: h.memset(Vx_, 1.0), w=[Vx_b])
                    pool(lambda h, Px_=Px_: h.memset(Px_, 0.0), w=[Px_b])
                    for mt in range(2):
                        dpl(Kc[:, :, mt, :], cmk[l, b0:b0 + 8, mt * 128:(mt + 1) * 128, :].rearrange("b m c -> m b c"), w=[Kc_b])
                        for hh in range(4):
                            dpl(Vx[:, :, mt, hh, 0:64], cmv[l, b0:b0 + 8, mt * 128:(mt + 1) * 128, hh * 64:(hh + 1) * 64].rearrange("b m d -> m b d"), w=[Vx_b])
                    for bp in range(4):
                        tbk, tkk = tbank()
                        for j in range(8):
                            bl, hc, mt = bp * 2 + j // 4, (j // 2) % 2, j % 2
                            pe(lambda h, j=j, bl=bl, hc=hc, mt=mt, tbk=tbk, Kc=Kc: h.transpose(tbk[:, j * 128:(j + 1) * 128], Kc[:, bl, mt, hc * 128:(hc + 1) * 128], identb[:]), [Kc_b, "identb"], [tkk])
                        dve(lambda h, bp=bp, tbk=tbk, KcT_=KcT_: h.tensor_copy(out=KcT_[:, bp * 1024:(bp + 1) * 1024], in_=tbk[:, :]), [tkk], [KcT_b])
                    bs, ks = bank()
                    for bl in range(8):
                        for mt in range(2):
                            for hc in range(2):
                                pe(lambda h, bl=bl, mt=mt, hc=hc, KcT=KcT: h.matmul(bs[:, (mt * 2 + hc) * 64:(mt * 2 + hc + 1) * 64].rearrange("p (hp q) -> p hp q", hp=2)[:, :, bl * 4:bl * 4 + 4], lhsT=KcT[:, bl, hc, mt * 128:(mt + 1) * 128], rhs=qZ[:, hc, :, (b0 + bl) * 4:(b0 + bl) * 4 + 4], start=True, stop=True), [KcT_b, "qZ"], [ks])
                    for mt in range(2):
                        for hh in range(4):
                            act(lambda h, mt=mt, hh=hh, half=half: h.activation(out=bass.AP(WA, 22592 + (mt * 4 + hh) * 512 + half * 32, [[PS_, 128], [68, 8], [1, 4]]), in_=v3(bs[:, mt * 128 + hh * 32:mt * 128 + hh * 32 + 32]), func=AF.Exp, scale=0.125), [ks], [Px_b])
                    bo_, ko_ = bank()
                    for hh in range(4):
                        for bl in range(8):
                            for mt in range(2):
                                pe(lambda h, hh=hh, bl=bl, mt=mt, Px=Px, Vx=Vx: h.matmul(bo_[0:64, hh * 65:(hh + 1) * 65], lhsT=Px[:, mt, hh, bl, :], rhs=Vx[:, bl, mt, hh, :], start=(bl == 0 and mt == 0), stop=(bl == 7 and mt == 1)), [Px_b, Vx_b], [ko_])
                    if hi_ == 0:
                        dve(lambda h, bo_=bo_: h.tensor_copy(out=tmpS[0:64, 0:260], in_=bo_[0:64, 0:260]), [ko_], ["tmpS"])
                    else:
                        dve(lambda h, bo_=bo_: h.tensor_tensor(out=tmpS[0:64, 0:260], in0=tmpS[0:64, 0:260], in1=bo_[0:64, 0:260], op=ALU.add), [ko_, "tmpS"], ["tmpS"])
                for hi_, half in enumerate((0, 1)):
                    xhalf(hi_, half)
                ov = tmpS[0:64, 0:260].rearrange("p (h e) -> p h e", e=65)
                dve(lambda h: h.reciprocal(out=reca[0:64], in_=ov[:, :, 64]), ["tmpS"], ["reca"])
                dve(lambda h: h.tensor_tensor(out=otok[0:64].rearrange("p (h d) -> p h d", d=64), in0=ov[:, :, 0:64], in1=reca[0:64].unsqueeze(2).to_broadcast([64, 4, 64]), op=ALU.mult), ["tmpS", "reca"], ["otok"])
                if dbg:
                    dpl(dbgo[0:64, 0:256], otok[0:64, 0:256], ["otok"])
                tb, tk = tbank()
                for hc in range(2):
                    pe(lambda h, hc=hc: h.transpose(tb[:, hc * 128:hc * 128 + 64], otok[0:64, hc * 128:(hc + 1) * 128], identb[0:64, 0:64]), ["otok", "identb"], [tk])
                dve(lambda h: h.tensor_copy(out=oxT[:, :, 0:64], in_=tb[:, 0:256].rearrange("p (a b) -> p a b", b=128)[:, :, 0:64]), [tk], ["otok2"])
                bo = [bank(), bank()]
                for nh in range(2):
                    for hc in range(2):
                        pe(lambda h, hc=hc, nh=nh: h.matmul(bo[nh][0][0:64, :], lhsT=oxT[:, hc, 0:64], rhs=W_xo[1][:, hc * 1024 + nh * 512:hc * 1024 + (nh + 1) * 512], start=(hc == 0), stop=(hc == 1)), ["otok2", W_xo[0]], [bo[nh][1]])
                ln_tile(16, 2 + 3 * l, bo)
            if '2' in DBGP:
                p2s()
            load_ln(3 + 3 * l)
            def p3(j):
                r0 = ring[j % 4]
                Wg_ = warena(f"fg{j % 4}", r0, 2048); Wu_ = warena(f"fu{j % 4}", r0 + 2048, 2048); Wd_ = warena(f"fd{j % 4}", r0 + 4096, 2048)
                dpl(Wg_[1].rearrange("p (a b) -> p a b", b=256), wg[l, :, :, j * 256:(j + 1) * 256], w=[Wg_[0]])
                dpl(Wu_[1].rearrange("p (a b) -> p a b", b=256), wu[l, :, :, j * 256:(j + 1) * 256], w=[Wu_[0]])
                dpl(Wd_[1].rearrange("p (a b) -> p a b", b=1024), wd[l, :, 2 * j:2 * j + 2, :], w=[Wd_[0]])
                def p3t(t):
                    n = ntok(t); c0 = t * 128
                    for cc in range(2):
                        bg, kg = bank(); bu, ku = bank()
                        for k in range(8):
                            pe(lambda h, k=k, cc=cc, bg=bg: h.matmul(bg[:, 0:n], lhsT=Wg_[1][:, k * 256 + cc * 128:k * 256 + (cc + 1) * 128], rhs=xT[:, k, c0:c0 + n], start=(k == 0), stop=(k == 7)), [f"xT{t}", Wg_[0]], [kg])
                        for k in range(8):
                            pe(lambda h, k=k, cc=cc, bu=bu: h.matmul(bu[:, 0:n], lhsT=Wu_[1][:, k * 256 + cc * 128:k * 256 + (cc + 1) * 128], rhs=xT[:, k, c0:c0 + n], start=(k == 0), stop=(k == 7)), [f"xT{t}", Wu_[0]], [ku])
                        sg_, sgk = ((hE, "hE"), (hCU, "hCU"), (hEC, "hEC"), (hEN, "hEN"))[(t % 2) * 2 + cc]
                        hT_, hTk = ((hT, "hL1"), (hT2, "hL2"))[t % 2]
                        act(lambda h, bg=bg, sg_=sg_: h.activation(out=sg_[:, 0:n], in_=bg[:, 0:n], func=AF.Silu), [kg], [sgk])
                        dve(lambda h, bu=bu, cc=cc, sg_=sg_, hT_=hT_: h.tensor_tensor(out=hT_[:, cc, 0:n], in0=sg_[:, 0:n], in1=bu[:, 0:n], op=ALU.mult), [sgk, ku], [hTk])
                    def down():
                        bo = [bank(), bank()]
                        for nh in range(2):
                            for cc in range(2):
                                pe(lambda h, cc=cc, nh=nh: h.matmul(bo[nh][0][0:n, :], lhsT=hT_[:, cc, 0:n], rhs=Wd_[1][:, cc * 1024 + nh * 512:cc * 1024 + (nh + 1) * 512], start=(cc == 0), stop=(cc == 1)), [hTk, Wd_[0]], [bo[nh][1]])
                        ln_tile(t, (3 + 3 * l) if j == 10 else None, bo, first=(j == 0))
                    return down
                prevD = None
                for t in range(NT):
                    d_ = p3t(t)
                    if prevD:
                        prevD()
                    prevD = d_
                prevD()
            for j in (range(11) if '3' in DBGP else []):
                p3(j)
        for l_ in range(nl):
            layer(l_)
        for t in range(16):
            dsp(yp[t * 128:(t + 1) * 128, :], x_tok[:, t, :], [f"x{t}"])
        dsp(ys[:, :], x_tok[0:64, 16, :], ["x16"])
        S.final()
        S.emit(nc, sems, dsems, block)
    return nc


def _host_layout(inp, nl=4, cores=range(8)):
    f = lambda a: np.ascontiguousarray(np.asarray(a, dtype=np.float32))
    pk = lambda w: f(np.asarray(w)[:nl].reshape(nl, -1, 128, np.asarray(w).shape[-1]).transpose(0, 2, 1, 3))
    sh = {}
    sh["win"] = pk(np.asarray(inp["w_in"])[:, :, _perm()])
    sh["wo"] = pk(inp["w_o"]); sh["wxq"] = pk(inp["w_xq"]); sh["wxk"] = pk(inp["w_xk"]); sh["wxv"] = pk(inp["w_xv"])
    sh["wxo"] = pk(inp["w_xo"]); sh["wg"] = pk(inp["w_gate"]); sh["wu"] = pk(inp["w_up"]); sh["wd"] = pk(inp["w_down"])
    gs = [inp["emb_ln_g"]]; bs = [inp["emb_ln_b"]]
    for l in range(4):
        for k in (1, 2, 3):
            gs.append(np.asarray(inp[f"ln{k}_g"])[l]); bs.append(np.asarray(inp[f"ln{k}_b"])[l])
    rows = np.stack([np.stack([np.asarray(g), np.asarray(b)]) for g, b in zip(gs, bs)])
    sh["lnrows"] = f(rows)
    sh["lncols"] = f(rows.reshape(13, 2, 8, 128).transpose(3, 0, 1, 2).reshape(128, 13 * 16))
    sh["sink"] = f(np.asarray(inp["attn_sink"]).reshape(1, 16))
    sh["hlb"] = f(np.asarray(inp["hgrn_lb"]).reshape(4, 2, 128).transpose(2, 1, 0).reshape(128, 8))
    sh["hng"] = f(inp["hgrn_norm_g"])
    sh["cw"] = f(np.asarray(inp["conv_w"]).reshape(4, 3, 2, 128).transpose(3, 0, 2, 1).reshape(128, 24))
    sh["pw"] = f(inp["pool_w"])
    sh["psc"] = f(np.asarray(inp["pool_scale"]).reshape(4, 2, 128).transpose(2, 0, 1).reshape(128, 8))
    sh["ctab"] = _CT; sh["stab"] = _STAB
    maps = []
    for c in cores:
        m = dict(sh)
        b = slice(16 * c, 16 * c + 16)
        m["xp"] = f(inp["x_prompt"][c]); m["xs"] = f(np.asarray(inp["x_sample"])[b].reshape(64, D))
        m["ck"] = f(np.asarray(inp["cache_swa_k"])[:nl, b].reshape(nl, 16, 128, 128))
        m["cv"] = f(np.asarray(inp["cache_swa_v"])[:nl, b].reshape(nl, 16, 128, 128))
        m["sh"] = f(np.asarray(inp["state_hgrn"])[:nl, b])
        m["scv"] = f(np.asarray(inp["state_conv"])[:nl, b].reshape(nl, 32, 256))
        m["spl"] = f(np.asarray(inp["state_pool"])[:nl, b])
        m["cmk"] = f(np.asarray(inp["cache_mem_k"])[:nl, b].reshape(nl, 16, 256, 256))
        m["cmv"] = f(np.asarray(inp["cache_mem_v"])[:nl, b].reshape(nl, 16, 256, 256))
        m["memp"] = f(inp["mem_prompt"][c])
        maps.append(m)
    return maps


_NC = None


def kernel(**inputs):
    global _NC
    if _NC is None:
        _NC = build()
    maps = _host_layout(inputs, 4, range(8))
    res = run_bass_kernel_spmd(_NC, maps, core_ids=list(range(8))).results
    st = lambda k: np.stack([r[k] for r in res])
    cat = lambda k: np.concatenate([r[k] for r in res], 1)
    return (st("yp"), np.concatenate([r["ys"].reshape(16, 4, D) for r in res], 0),
            st("okp").transpose(1, 0, 2, 3).reshape(4, 8, 128, 2, 64), st("ovp").transpose(1, 0, 2, 3).reshape(4, 8, 128, 2, 64),
            st("ohp").transpose(1, 0, 2, 3, 4), st("ocp").transpose(1, 0, 2, 3), st("opp").transpose(1, 0, 2, 3),
            st("omk").transpose(1, 0, 2, 3).reshape(4, 8, 256, 4, 64), st("omv").transpose(1, 0, 2, 3).reshape(4, 8, 256, 4, 64),
            cat("oks").reshape(4, 128, 128, 2, 64), cat("ovs").reshape(4, 128, 128, 2, 64), cat("ohs"), cat("ocs"), cat("ops"))
```

```python
import numpy as np
import concourse.bass as bass
import concourse.mybir as mybir

F32 = mybir.dt.float32
BF16 = mybir.dt.bfloat16
ALU = mybir.AluOpType
AF = mybir.ActivationFunctionType
AX = mybir.AxisListType

import os as _os
SERIAL = bool(_os.environ.get('SERIAL'))
NOSAME = bool(_os.environ.get('NOSAME'))
WARSYNC = not bool(_os.environ.get('NOWAR'))
NS = 1
ND = 12


class Buf:
    __slots__ = ("lo", "hi", "name")

    def __init__(self, name, lo, hi):
        self.name, self.lo, self.hi = name, lo, hi


class _Op:
    __slots__ = ("eng", "fn", "idx", "is_dma", "waits", "needs_inc", "seq", "dsem", "dval")

    def __init__(self, eng, fn, is_dma):
        self.eng = eng
        self.fn = fn
        self.is_dma = is_dma
        self.waits = []
        self.needs_inc = False
        self.seq = -1
        self.idx = -1
        self.dsem = -1
        self.dval = 0


class Sched:
    ENGS = ("pe", "act", "dve", "pool", "sp")

    def __init__(self):
        self.ops = {e: [] for e in self.ENGS}
        self.last_w = {}
        self.readers = {}
        self.live = []
        self.known = {e: {f: -1 for f in self.ENGS} for e in self.ENGS}
        self.known_dma = {e: set() for e in self.ENGS}
        self.dma_count = {"sp": 0, "pool": 0}
        self.dma_hist = {"sp": {}, "pool": {}}

    def _overl(self, b):
        return [r for r in self.live if r is not b and r.lo < b.hi and b.lo < r.hi]

    def op(self, eng, fn, reads=(), writes=(), dma=False):
        o = _Op(eng, fn, dma)
        o.idx = len(self.ops[eng])
        deps = {}
        for k in reads:
            w = self.last_w.get(k)
            if w is not None:
                deps[w] = True
            if isinstance(k, str) and k[:2] in ("ps", "pT"):
                for r in self.readers.get(k, ()):
                    if r.eng != eng and r not in deps:
                        deps[r] = False
            if isinstance(k, Buf):
                for r in self._overl(k):
                    w = self.last_w.get(r)
                    if w is not None:
                        deps[w] = True
        for k in writes:
            ks = [k] + (self._overl(k) if isinstance(k, Buf) else [])
            for kk in ks:
                w = self.last_w.get(kk)
                if w is not None:
                    deps[w] = True
                for r in self.readers.get(kk, ()):
                    if r not in deps:
                        deps[r] = False
        if SERIAL:
            for e2 in self.ENGS:
                if self.ops[e2]:
                    deps.setdefault(self.ops[e2][-1], True)
        if dma:
            q = eng
            j = self.dma_count[q]
            self.dma_count[q] += 1
            o.dsem = j % ND
            o.dval = 16 * (j // ND + 1)
            prev = self.dma_hist[q].get(o.dsem)
            if prev is not None:
                deps.setdefault(prev, True)
            self.dma_hist[q][o.dsem] = o
            o.needs_inc = True
        kn = self.known[eng]
        latest = {}
        for d, hard in deps.items():
            if d is o:
                continue
            if d.is_dma:
                if d in self.known_dma[eng]:
                    continue
                self.known_dma[eng].add(d)
                o.waits.append(d)
                continue
            if d.eng == eng and (eng == "pe" or NOSAME or (not hard and not WARSYNC)):
                continue
            if d.idx <= kn[d.eng]:
                continue
            if d.eng not in latest or d.idx > latest[d.eng].idx:
                latest[d.eng] = d
        for e_, d in latest.items():
            kn[e_] = d.idx
            d.needs_inc = True
            o.waits.append(d)
        for k in reads:
            self.readers.setdefault(k, []).append(o)
            if isinstance(k, Buf) and k not in self.live:
                self.live.append(k)
        for k in writes:
            if isinstance(k, Buf):
                for r in self._overl(k):
                    self.live.remove(r)
                    self.last_w.pop(r, None)
                    self.readers.pop(r, None)
                if k not in self.live:
                    self.live.append(k)
            self.last_w[k] = o
            self.readers[k] = []
        self.ops[eng].append(o)
        return o

    def final(self):
        o = _Op("sp", lambda h: h.nop(), False)
        o.idx = len(self.ops["sp"])
        for q in self.dma_hist:
            for d in self.dma_hist[q].values():
                o.waits.append(d)
        self.ops["sp"].append(o)

    def emit(self, nc, sems, dsems, block):
        for e in self.ENGS:
            m = 0
            for o in self.ops[e]:
                if o.needs_inc and not o.is_dma:
                    o.seq = m
                    m += 1

        def run(e, h):
            for o in self.ops[e]:
                for d in o.waits:
                    if d.is_dma:
                        h.wait_ge(dsems[d.eng][d.dsem], d.dval)
                    else:
                        h.wait_ge(sems[d.eng][d.seq % NS], d.seq // NS + 1)
                ins = o.fn(h)
                if o.needs_inc:
                    if o.is_dma:
                        ins.then_inc(dsems[e][o.dsem], 16)
                    else:
                        ins.then_inc(sems[e][o.seq % NS], 1)

        @block.tensor
        def _(h):
            run("pe", h)

        @block.scalar
        def _(h):
            run("act", h)

        @block.vector
        def _(h):
            run("dve", h)

        @block.gpsimd
        def _(h):
            run("pool", h)

        @block.sync
        def _(h):
            run("sp", h)
from contextlib import ExitStack
import threading
from concourse.bass_utils import run_bass_kernel_spmd

import os
DBG_SKIP = bool(os.environ.get('DBG_SKIP'))
DBGP = os.environ.get('DBGP', '123')
D = 1024; DEPTH = 4; NT = 17; TS = 64; DFF = 2816
ALPHA = (2 * DEPTH) ** 0.25
LN_EPS = 1e-5; RMS_EPS = 1e-6
cA, cB, cBQ, cBF, cCB, cK, cV, cI, cG, cCC, cCH, cDV = 0, 128, 256, 512, 768, 1024, 1152, 1280, 1536, 1792, 2048, 2304


def _perm():
    q = np.arange(256).reshape(2, 2, 64)
    qa = q[:, 0, :].reshape(-1); qb = q[:, 1, :].reshape(-1)
    r = lambda a, b: np.arange(a, b)
    return np.concatenate([qa, qb, r(512, 768), r(768, 1024), r(1536, 1792), r(256, 384), r(384, 512),
                           r(1024, 1280), r(1280, 1536), r(1792, 2048), r(2048, 2304), r(2304, 2560)])


def _consts():
    c = {}
    s = np.arange(128)[:, None]; q = np.arange(128)[None, :]
    slopes = [2.0 ** (-2.0 * (h + 1)) for h in range(4)]
    NEG = -30000.0
    bt = np.zeros((128, 2, 2, 2, 128), np.float32)
    bsc = np.zeros((128, 2, 2, 64), np.float32)
    bsn = np.full((128, 2, 2, 64), NEG, np.float32)
    for g in range(2):
        for kv in range(2):
            sl = slopes[kv * 2 + g]
            bt[:, 0, g, kv, :] = np.where(s <= q, -sl * (q - s), NEG)
            bt[:, 1, g, kv, :] = np.where(s >= q, -sl * (128 + q - s), NEG)
            for b in range(16):
                for t in range(4):
                    col = b * 4 + t
                    bsc[:, g, kv, col] = np.where(np.arange(128) >= t, -sl * (t + 128 - np.arange(128)), NEG)
                    for t2 in range(t + 1):
                        bsn[b * 4 + t2, g, kv, col] = -sl * (t - t2)
    c["bt"] = bt.reshape(128, 1024)
    global _STAB
    _STAB = np.ascontiguousarray(np.concatenate([bsc.reshape(128, 256), bsn.reshape(128, 256)], 1))
    tri = (np.arange(64)[:, None] <= np.arange(64)[None, :]).astype(np.float32)
    c["tri2"] = np.concatenate([tri, tri], 0)
    bdc = np.zeros((128, 64), np.float32); mind = np.zeros((128, 16), np.float32)
    for b in range(16):
        mind[b * 4:(b + 1) * 4, b] = 1.0
        for t in range(4):
            bdc[b * 4:b * 4 + t + 1, b * 4 + t] = 1.0
    c["bdc"] = bdc; c["mind"] = mind
    ic = np.zeros((128, 2, 16), np.float32)
    for hc in range(2):
        for half in range(2):
            w = (2, 4, 8, 16)[hc * 2 + half]
            ic[half * 64:(half + 1) * 64, hc, :] = 1.0 / np.minimum(np.arange(16) + 1, w)
    c["icnt"] = ic.reshape(128, 32)
    c["ident"] = np.eye(128, dtype=np.float32)
    names = list(c.keys()); offs = {}; o = 0
    for n in names:
        offs[n] = (o, c[n].shape[1]); o += c[n].shape[1]
    return np.concatenate([c[n] for n in names], 1), offs


_STAB = None
_CT, _CO = _consts()


def build(nl=DEPTH, dbg=False):
    nc = bass.Bass("TRN2", target_bir_lowering=False)
    I = lambda n, s: nc.dram_tensor(n, list(s), F32, kind="ExternalInput").ap()
    O = lambda n, s: nc.dram_tensor(n, list(s), F32, kind="ExternalOutput").ap()
    xp = I("xp", (2048, D)); xs = I("xs", (64, D))
    ck = I("ck", (nl, 16, 128, 128)); cv = I("cv", (nl, 16, 128, 128))
    sh = I("sh", (nl, 16, 4, 64, 64)); scv = I("scv", (nl, 32, 256)); spl = I("spl", (nl, 16, 15, 256))
    cmk = I("cmk", (nl, 16, 256, 256)); cmv = I("cmv", (nl, 16, 256, 256)); memp = I("memp", (256, D))
    win = I("win", (nl, 128, 8, 2560)); wo = I("wo", (nl, 128, 8, 1024))
    wxq = I("wxq", (nl, 128, 8, 256)); wxk = I("wxk", (nl, 128, 8, 256)); wxv = I("wxv", (nl, 128, 8, 256))
    wxo = I("wxo", (nl, 128, 2, 1024)); wg = I("wg", (nl, 128, 8, DFF)); wu = I("wu", (nl, 128, 8, DFF))
    wd = I("wd", (nl, 128, 22, 1024))
    lncols = I("lncols", (128, 13 * 16)); lnrows = I("lnrows", (13, 2, 1024))
    sink = I("sink", (1, 16)); hlb = I("hlb", (128, 8)); hng = I("hng", (4, 256))
    cw = I("cw", (128, 24)); pw = I("pw", (4, 4, 64, 64)); psc = I("psc", (128, 8))
    ctab = I("ctab", _CT.shape); stab = I("stab", (128, 512))
    yp = O("yp", (2048, D)); ys = O("ys", (64, D))
    okp = O("okp", (nl, 128, 128)); ovp = O("ovp", (nl, 128, 128)); ohp = O("ohp", (nl, 4, 64, 64))
    ocp = O("ocp", (nl, 2, 256)); opp = O("opp", (nl, 15, 256)); omk = O("omk", (nl, 256, 256)); omv = O("omv", (nl, 256, 256))
    oks = O("oks", (nl, 16, 128, 128)); ovs = O("ovs", (nl, 16, 128, 128)); ohs = O("ohs", (nl, 16, 4, 64, 64))
    ocs = O("ocs", (nl, 16, 2, 256)); ops_ = O("ops", (nl, 16, 15, 256))
    dbgo = O("dbgo", (1024, 2112)) if dbg else None
    dbg2 = None
    S = Sched()
    es = ExitStack()
    with es:
        def sb(name, shape, dt=F32):
            return es.enter_context(nc.sbuf_tensor(name, list(shape), dt))
        x_tok = sb("x_tok", (128, NT, D)); xT = sb("xT", (128, 8, 2112), BF16)
        WA = sb("WA", (128, 28672), BF16)
        WK = None
        ct = sb("ct", (128, _CT.shape[1])); identb = sb("identb", (128, 128), BF16)
        lnc = sb("lnc", (128, 13, 2, 8)); Grow = sb("Grow", (128, D)); Brow = sb("Brow", (128, D))
        esink = sb("esink", (128, 16)); lbt = sb("lbt", (128, 2, 4)); oml = sb("oml", (128, 2, 4)); lbe = sb("lbe", (128, 2, 4)); lbs = sb("lbs", (128, 2, 1))
        normg = sb("normg", (128, 256)); cwt = sb("cwt", (128, 4, 2, 3)); psct = sb("psct", (128, 4, 2)); wbd = sb("wbd", (128, 2, 128), BF16)
        ones = sb("ones", (128, 64))
        ps = [es.enter_context(nc.psum_tensor(f"ps{i}", [128, 512], F32)) for i in range(8)]
        sems = {e: [es.enter_context(nc.semaphore(f"s_{e}{i}")) for i in range(NS)] for e in Sched.ENGS}
        dsems = {e: [es.enter_context(nc.semaphore(f"d_{e}{i}")) for i in range(ND)] for e in ("sp", "pool")}
        block = es.enter_context(nc.Block())

        C = lambda n: ct[:, _CO[n][0]:_CO[n][0] + _CO[n][1]]
        ident = C("ident")

        tl = threading.local()
        pools = {"ALL": list(range(8)), "A": [0, 1], "B": [2, 3], "Bt": [4], "C": [5], "D": [6], "E": [7], "P0": [0, 1, 2, 3], "P1": [4, 5, 6, 7]}
        ppos = {kk: 0 for kk in pools}

        def _next(pname):
            lst = pools[pname]; i = lst[ppos[pname] % len(lst)]; ppos[pname] += 1
            return i

        def bank():
            i = _next(getattr(tl, "pool", "ALL"))
            return ps[i], f"ps{i}"

        def tbank():
            i = _next(getattr(tl, "tpool", None) or getattr(tl, "pool", "ALL"))
            return ps[i][:].bitcast(BF16), f"ps{i}"

        _w = os.environ.get("WTS", "A1B2C1D1E1")
        wts = {_w[i]: int(_w[i + 1]) for i in range(0, len(_w), 2)}

        def run_interleaved(funcs, pnames, tpnames=None):
            nf = len(funcs); turn = [0]; alive = [True] * nf; cv = threading.Condition(); errs = []

            def handoff(i):
                j = (i + 1) % nf; c_ = 0
                while not alive[j] and c_ < nf:
                    j = (j + 1) % nf; c_ += 1
                turn[0] = j; cv.notify_all()

            cnt = [0] * nf

            def step(i):
                cnt[i] += 1
                if cnt[i] % wts.get(pnames[i], 1) != 0:
                    return
                with cv:
                    handoff(i)
                    while turn[0] != i:
                        cv.wait()

            def worker(i):
                tl.pool = pnames[i]; tl.tpool = tpnames[i] if tpnames else None
                tl.step = lambda: step(i)
                with cv:
                    while turn[0] != i:
                        cv.wait()
                try:
                    funcs[i]()
                except BaseException as e:
                    errs.append(e)
                finally:
                    with cv:
                        alive[i] = False
                        handoff(i)
            ths = [threading.Thread(target=worker, args=(i,)) for i in range(nf)]
            for th in ths:
                th.start()
            for th in ths:
                th.join()
            if errs:
                raise errs[0]

        def _y():
            st_ = getattr(tl, "step", None)
            if st_:
                st_()

        def warena(name, lo, n):
            return Buf(name, 2 * lo, 2 * (lo + n)), WA[:, lo:lo + n]

        def work(name, lo, n, dt=F32):
            v = WK[:, lo:lo + n]
            if dt == BF16:
                v = v.bitcast(BF16)
            return Buf(name, 1000000 + 4 * lo, 1000000 + 4 * (lo + n)), v

        def _mk(eng):
            def f(fn, r=(), w=()):
                S.op(eng, fn, r, w); _y()
            return f
        dve = _mk("dve"); act = _mk("act"); pe = _mk("pe"); pool = _mk("pool")

        def dsp(o, i, r=(), w=()):
            S.op("sp", lambda h: h.dma_start(out=o, in_=i), r, w, dma=True); _y()

        def dpl(o, i, r=(), w=()):
            S.op("pool", lambda h: h.dma_start(out=o, in_=i), r, w, dma=True); _y()

        dsp(ct[:], ctab[:], w=["ct"])
        dsp(lnc[:].rearrange("p a b c -> p (a b c)"), lncols[:], w=["lnc"])
        dsp(cwt[:].rearrange("p a b c -> p (a b c)"), cw[:], w=["cwt"])
        dsp(psct[:].rearrange("p a b -> p (a b)"), psc[:], w=["psct"])
        dsp(lbt[:].rearrange("p a b -> p (a b)"), hlb[:], w=["lbt"])
        dsp(esink[:], sink[:].partition_broadcast(128), w=["esink"])
        dve(lambda h: h.tensor_copy(out=identb[:], in_=ident), ["ct"], ["identb"])
        dve(lambda h: h.memset(ones[:], 1.0), w=["ones"])
        act(lambda h: h.activation(out=esink[:], in_=esink[:], func=AF.Exp), ["esink"], ["esink"])
        act(lambda h: h.activation(out=lbe[:], in_=lbt[:], func=AF.Exp), ["lbt"], ["lbe"])
        dve(lambda h: h.tensor_reduce(out=lbs[:], in_=lbe[:], axis=AX.X, op=ALU.add), ["lbe"], ["lbs"])
        dve(lambda h: h.reciprocal(out=lbs[:], in_=lbs[:]), ["lbs"], ["lbs"])
        dve(lambda h: h.tensor_tensor(out=lbe[:], in0=lbe[:], in1=lbs[:].to_broadcast([128, 2, 4]), op=ALU.mult), ["lbe", "lbs"], ["lbe"])
        dve(lambda h: h.memset(lbt[:, :, 0:1], 0.0), ["lbe"], ["lbt"])
        for l in range(1, 4):
            dve(lambda h, l=l: h.tensor_tensor(out=lbt[:, :, l:l + 1], in0=lbt[:, :, l - 1:l], in1=lbe[:, :, l:l + 1], op=ALU.add), ["lbt", "lbe"], ["lbt"])
        dve(lambda h: h.tensor_scalar(out=oml[:], in0=lbt[:], scalar1=-1.0, scalar2=1.0, op0=ALU.mult, op1=ALU.add), ["lbt"], ["oml"])

        for t in range(16):
            dsp(x_tok[:, t, :], xp[t * 128:(t + 1) * 128, :], w=[f"x{t}"])
        dsp(x_tok[0:64, 16, :], xs[:, :], w=["x16"])

        def ntok(t):
            return 64 if t == 16 else 128

        lnst = sb("lnst", (128, 2, 6)); lnmv = sb("lnmv", (128, 2)); lnr = sb("lnr", (128, 2)); zbf = sb("zbf", (128, D), BF16)

        def load_ln(i):
            dsp(Grow[:], lnrows[i, 0:1, :].partition_broadcast(128), w=["Grow"])
            dsp(Brow[:], lnrows[i, 1:2, :].partition_broadcast(128), w=["Brow"])

        def ln_tile(t, i, banks, first=True):
            n = ntok(t); xk = f"x{t}"
            xt = x_tok[0:n, t, :]
            if banks is not None:
                for nh, (b, bk) in enumerate(banks):
                    if first:
                        dve(lambda h, b=b, nh=nh: h.scalar_tensor_tensor(out=xt[:, nh * 512:(nh + 1) * 512], in0=xt[:, nh * 512:(nh + 1) * 512], scalar=ALPHA, in1=b[0:n, :], op0=ALU.mult, op1=ALU.add), [xk, bk], [xk])
                    else:
                        dve(lambda h, b=b, nh=nh: h.tensor_tensor(out=xt[:, nh * 512:(nh + 1) * 512], in0=xt[:, nh * 512:(nh + 1) * 512], in1=b[0:n, :], op=ALU.add), [xk, bk], [xk])
            if i is None:
                return
            for a in range(2):
                dve(lambda h, a=a: h.bn_stats(out=lnst[0:n, a, :], in_=xt[:, a * 512:(a + 1) * 512]), [xk], ["lnst"])
            dve(lambda h: h.bn_aggr(out=lnmv[0:n], in_=lnst[0:n].rearrange("p a b -> p (a b)")), ["lnst"], ["lnmv"])
            act(lambda h: h.activation(out=lnr[0:n, 0:1], in_=lnmv[0:n, 1:2], func=AF.Ln, bias=LN_EPS, scale=1.0), ["lnmv"], ["lnr"])
            act(lambda h: h.activation(out=lnr[0:n, 0:1], in_=lnr[0:n, 0:1], func=AF.Exp, scale=-0.5), ["lnr"], ["lnr"])
            dve(lambda h: h.scalar_tensor_tensor(out=lnr[0:n, 1:2], in0=lnmv[0:n, 0:1], scalar=-1.0, in1=lnr[0:n, 0:1], op0=ALU.mult, op1=ALU.mult), ["lnr", "lnmv"], ["lnr2"])
            act(lambda h: h.activation(out=zbf[0:n], in_=xt, func=AF.Identity, scale=lnr[0:n, 0:1], bias=lnr[0:n, 1:2]), [xk, "lnr", "lnr2"], ["zbf"])
            dve(lambda h: h.tensor_scalar(out=xt, in0=xt, scalar1=lnr[0:n, 0:1], scalar2=lnr[0:n, 1:2], op0=ALU.mult, op1=ALU.add), [xk, "lnr", "lnr2"], [xk])
            dve(lambda h: h.tensor_tensor(out=xt, in0=xt, in1=Grow[0:n], op=ALU.mult), [xk, "Grow"], [xk])
            pool(lambda h: h.tensor_tensor(out=xt, in0=xt, in1=Brow[0:n], op=ALU.add), [xk, "Brow"], [xk])
            tb, tk = tbank()
            for c in range(8):
                pe(lambda h, c=c: h.transpose(tb[:, c * 128:c * 128 + n], zbf[0:n, c * 128:(c + 1) * 128], identb[0:n, 0:n]), ["zbf", "identb"], [tk])
            for c in range(8):
                if c % 2 == 0:
                    act(lambda h, c=c: h.activation(out=xT[:, c, t * 128:t * 128 + n], in_=tb[:, c * 128:c * 128 + n], func=AF.Identity, scale=lnc[:, i, 0, c:c + 1], bias=lnc[:, i, 1, c:c + 1]), [tk, "lnc"], [f"xT{t}"])
                else:
                    dve(lambda h, c=c: h.tensor_scalar(out=xT[:, c, t * 128:t * 128 + n], in0=tb[:, c * 128:c * 128 + n], scalar1=lnc[:, i, 0, c:c + 1], scalar2=lnc[:, i, 1, c:c + 1], op0=ALU.mult, op1=ALU.add), [tk, "lnc"], [f"xT{t}"])

        W_in = [warena(f"win{k}", k * 2560, 2560) for k in range(8)]
        W_o = warena("wo", 20480, 8192)
        for k in range(8):
            dpl(W_in[k][1], win[0, :, k, :], w=[W_in[k][0]])
        dpl(W_o[1].rearrange("p (a b) -> p a b", b=1024), wo[0], w=[W_o[0]])
        load_ln(0)
        for t in range(NT):
            ln_tile(t, 0, None)

        W_xq = warena("wxq", 0, 2048); W_xo = warena("wxo", 2048, 2048); W_xk = warena("wxk", 4096, 2048); W_xv = warena("wxv", 6144, 2048)
        ring = [8192, 14336, 20480, 0]

        mixTs = [sb("mixT0", (128, 8, 128), BF16), sb("mixT1", (128, 8, 128), BF16)]
        qZ = sb("qZ", (128, 2, 2, 128), BF16); kTr = sb("kTr", (128, 2, 128), BF16); Vaug = sb("Vaug", (128, 2, 2, 65), BF16)
        tmpS = sb("tmpS", (128, 512)); PT = sb("PT", (128, 2, 512), BF16); otok = sb("otok", (128, 256), BF16)
        den = sb("den", (128, 4)); reca = sb("reca", (128, 4))
        uext = sb("uext", (128, 2, 130)); chs = sb("chs", (128, 128)); ycv = sb("ycv", (128, 128))
        ext = sb("ext", (128, 2, 144)); s2 = sb("s2", (128, 144)); s4 = sb("s4", (128, 144)); s8 = sb("s8", (128, 144)); s16 = sb("s16", (128, 144))
        pooled = sb("pooled", (128, 2, 128), BF16)
        hE = sb("hE", (128, 128)); hL1 = sb("hL1", (128, 128)); hL2 = sb("hL2", (128, 128)); hCU = sb("hCU", (128, 128))
        hEC = sb("hEC", (128, 128)); hEN = sb("hEN", (128, 128)); hEQ = sb("hEQ", (128, 128))
        qSZ = sb("qSZ", (128, 2, 2, 2, 64), BF16); kAb = sb("kAb", (128, 2, 128), BF16); kAtokZ = sb("kAtokZ", (128, 2, 256), BF16)
        Vh = sb("Vh", (128, 256), BF16); Sst = sb("Sst", (128, 2, 64)); Sbf = sb("Sbf", (128, 2, 2, 64), BF16); ECL = sb("ECL", (128, 2, 2))
        AtZ = sb("AtZ", (128, 2, 4, 64), BF16); sqb = sb("sqb", (128, 256)); sgb = sb("sgb", (128, 256)); otok2 = sb("otok2", (128, 256), BF16)
        ssq = sb("ssq", (128, 4)); rsq = sb("rsq", (128, 4))
        pool(lambda h: h.memset(qSZ[:], 0.0), w=["qSZ"])
        pool(lambda h: h.memset(kAtokZ[:], 0.0), w=["kAtokZ"])
        pool(lambda h: h.memset(AtZ[:], 0.0), w=["AtZ"])
        ptmp = sb("ptmp", (128, 16)); uxs = sb("uxs", (128, 2, 16, 6)); exs = sb("exs", (128, 2, 16, 20)); ECLs = sb("ECLs", (128, 2, 16))
        dbgb = None
        print("SBUF bytes remaining:", nc.sbuf_bytes_remaining)
        pool(lambda h: h.memset(qZ[:], 0.0), w=["qZ"])
        pool(lambda h: h.memset(Vaug[:], 1.0), w=["Va0", "Va1"])
        mva = sb("mva", (128, 2, 4, 65), BF16)
        sgt = hE; hT = hL1[:].bitcast(BF16).rearrange("p (a b) -> p a b", b=128); hT2 = hL2[:].bitcast(BF16).rearrange("p (a b) -> p a b", b=128)
        mkT = kAtokZ; qxT = kAb; oxt = otok; oxT = otok2[:].rearrange("p (a b) -> p a b", b=128); rec = reca
        PX = PT[:].rearrange("p a (b c) -> p a b c", c=128)
        memTb, memT_ = warena("memT", 8192, 2048)
        memT = memT_.rearrange("p (c m) -> p c m", m=256)
        dve(lambda h: h.memset(mva[:], 1.0), w=["mva"])
        MEMT = [memTb]

        def layer(l):
            if l > 0:
                for k in range(8):
                    dpl(W_in[k][1], win[l, :, k, :], w=[W_in[k][0]])
                dpl(W_o[1].rearrange("p (a b) -> p a b", b=1024), wo[l], w=[W_o[0]])
            load_ln(1 + 3 * l)
            WIN = [w[0] for w in W_in]
            pool(lambda h: h.memset(Sst[:], 0.0), w=["Sst"])
            pool(lambda h: h.memset(exs[:], 0.0), w=["exs0", "exs1"])
            pool(lambda h: h.memset(kAtokZ[:], 0.0), w=["kAtokZ"])
            pool(lambda h: h.memset(uext[:, :, 0:2], 0.0), w=["uext0", "uext1"])
            pool(lambda h: h.memset(ext[:, :, 0:16], 0.0), w=["ext0", "ext1"])
            pool(lambda h: h.memset(wbd[:], 0.0), w=["wbd"])
            for g_ in range(4):
                dpl(wbd[64 * (g_ % 2):64 * (g_ % 2) + 64, g_ // 2, 64 * (g_ % 2):64 * (g_ % 2) + 64], pw[l, g_], w=["wbd"])
            dsp(normg[:], hng[l:l + 1, :].partition_broadcast(128), w=["normg"])
            def p1(t, prev_post):
                n = ntok(t); c0 = t * 128
                mixT = mixTs[t % 2]; mk = f"mixT{t % 2}"
                b1, k1 = bank()
                for k in range(8):
                    pe(lambda h, k=k, b1=b1: h.matmul(b1[0:n, :], lhsT=xT[:, k, c0:c0 + n], rhs=W_in[k][1][:, cK:cK + 512], start=(k == 0), stop=(k == 7)), [f"xT{t}"] + WIN, [k1])
                act(lambda h: h.copy(out=Vh[0:n], in_=b1[0:n, 256:512]), [k1], ["Vh"])
                act(lambda h: h.copy(out=Vaug[0:n, t % 2, :, 0:64], in_=b1[0:n, 128:256].rearrange("p (kv d) -> p kv d", d=64)), [k1], [f"Va{t % 2}"])
                if t >= 15:
                    dve(lambda h, b1=b1: h.tensor_copy(out=sqb[0:n, 0:256], in_=b1[0:n, 0:256]), [k1], ["sqb"])
                    if t == 15:
                        dsp(okp[l], sqb[:, 0:128], ["sqb"]); dsp(ovp[l], sqb[:, 128:256], ["sqb"])
                    elif not DBG_SKIP:
                        dsp(oks[l, :, 0:124, :], ck[l, :, 4:128, :]); dsp(ovs[l, :, 0:124, :], cv[l, :, 4:128, :])
                        for tt in range(4):
                            dsp(oks[l, :, 124 + tt, :], sqb[tt:64:4, 0:128], ["sqb"]); dsp(ovs[l, :, 124 + tt, :], sqb[tt:64:4, 128:256], ["sqb"])
                    b2, k2 = bank(); b3, k3 = bank()
                    for k in range(8):
                        pe(lambda h, k=k, b2=b2: h.matmul(b2[0:n, :], lhsT=xT[:, k, c0:c0 + n], rhs=W_in[k][1][:, cCC:cCC + 512], start=(k == 0), stop=(k == 7)), [f"xT{t}"] + WIN, [k2])
                    for k in range(8):
                        pe(lambda h, k=k, b3=b3: h.matmul(b3[0:n, 0:256], lhsT=xT[:, k, c0:c0 + n], rhs=W_in[k][1][:, cDV:cDV + 256], start=(k == 0), stop=(k == 7)), [f"xT{t}"] + WIN, [k3])
                    act(lambda h, b2=b2: h.copy(out=sgb[0:n, :], in_=b2[0:n, 256:512]), [k2], ["sgb"])
                    dve(lambda h, b2=b2: h.tensor_tensor(out=sgb[0:n, :], in0=b2[0:n, 0:256], in1=sgb[0:n, :], op=ALU.mult), [k2, "sgb"], ["sgb"])
                    act(lambda h, b3=b3: h.copy(out=tmpS[0:n, 0:256], in_=b3[0:n, 0:256]), [k3], ["tmpS"])
                    if t == 15:
                        dsp(ocp[l], sgb[126:128, :], ["sgb"]); dsp(opp[l], tmpS[113:128, 0:256], ["tmpS"])
                    elif not DBG_SKIP:
                        dsp(ops_[l, :, 0:11, :], spl[l, :, 4:15, :])
                        for tt in range(4):
                            dsp(ops_[l, :, 11 + tt, :], tmpS[tt:64:4, 0:256], ["tmpS"])
                            if tt >= 2:
                                dsp(ocs[l, :, tt - 2, :], sgb[tt:64:4, :], ["sgb"])
                def fm(col):
                    b, bk = bank()
                    for k in range(8):
                        pe(lambda h, k=k: h.matmul(b[:, 0:n], lhsT=W_in[k][1][:, col:col + 128], rhs=xT[:, k, c0:c0 + n], start=(k == 0), stop=(k == 7)), [f"xT{t}"] + WIN, [bk])
                    return b, bk
                sl = t % 2
                v3 = lambda ap: ap.rearrange("p (b t) -> p b t", t=4)

                def attn_out(bo_, ko_):
                    ov = bo_[0:n, 0:260].rearrange("p (h e) -> p h e", e=65)
                    dve(lambda h: h.tensor_tensor(out=den[0:n], in0=ov[:, :, 64], in1=esink[0:n, l * 4:(l + 1) * 4], op=ALU.add), [ko_, "esink"], ["den"])
                    dve(lambda h: h.reciprocal(out=reca[0:n], in_=den[0:n]), ["den"], ["reca"])
                    dve(lambda h: h.tensor_tensor(out=otok[0:n].rearrange("p (h d) -> p h d", d=64), in0=ov[:, :, 0:64], in1=reca[0:n].unsqueeze(2).to_broadcast([n, 4, 64]), op=ALU.mult), [ko_, "reca"], ["otok"])

                def hgrn_gate():
                    bg, kg = bank()
                    for k_ in range(8):
                        pe(lambda h, k_=k_: h.matmul(bg[0:n, 0:256], lhsT=xT[:, k_, c0:c0 + n], rhs=W_in[k_][1][:, cG:cG + 256], start=(k_ == 0), stop=(k_ == 7)), [f"xT{t}"] + WIN, [kg])
                    act(lambda h: h.activation(out=sgb[0:n], in_=bg[0:n, 0:256], func=AF.Exp, scale=-1.0), [kg], ["sgb"])
                    act(lambda h: h.activation(out=sgb[0:n], in_=sgb[0:n], func=AF.Ln, scale=1.0, bias=1.0), ["sgb"], ["sgb"])
                    act(lambda h: h.activation(out=sgb[0:n], in_=sgb[0:n], func=AF.Exp, scale=-1.0), ["sgb"], ["sgb"])
                    dve(lambda h: h.tensor_tensor(out=sgb[0:n], in0=bg[0:n, 0:256], in1=sgb[0:n], op=ALU.mult), [kg, "sgb"], ["sgb"])

                def hgrn_out(bO, kO):
                    act(lambda h: h.activation(out=sqb[0:n], in_=bO[0:n, 0:256], func=AF.Square), [kO], ["sqb"])
                    dve(lambda h: h.tensor_reduce(out=ssq[0:n], in_=sqb[0:n].rearrange("p (a v) -> p a v", v=64), axis=AX.X, op=ALU.add), ["sqb"], ["ssq"])
                    act(lambda h: h.activation(out=rsq[0:n], in_=ssq[0:n], func=AF.Ln, scale=1.0 / 64, bias=RMS_EPS), ["ssq"], ["rsq"])
                    act(lambda h: h.activation(out=rsq[0:n], in_=rsq[0:n], func=AF.Exp, scale=-0.5), ["rsq"], ["rsq"])
                    dve(lambda h: h.tensor_tensor(out=sqb[0:n].rearrange("p (a v) -> p a v", v=64), in0=bO[0:n, 0:256].rearrange("p (a v) -> p a v", v=64), in1=rsq[0:n].unsqueeze(2).to_broadcast([n, 4, 64]), op=ALU.mult), [kO, "rsq", "sqb"], ["sqb"])
                    dve(lambda h: h.tensor_tensor(out=sqb[0:n], in0=sqb[0:n], in1=normg[0:n], op=ALU.mult), ["sqb", "normg"], ["sqb"])
                    dve(lambda h: h.tensor_tensor(out=otok2[0:n], in0=sqb[0:n], in1=sgb[0:n], op=ALU.mult), ["sqb", "sgb"], ["otok2"])
                if t == 16:
                    dsp(sqb[0:32, :], scv[l], w=["sqb"])
                    for hc in range(2):
                        bT, kT_ = bank()
                        pe(lambda h, hc=hc, bT=bT: h.transpose(bT[:, 0:32], sqb[0:32, hc * 128:(hc + 1) * 128], ident[0:32, 0:32]), ["sqb", "ct"], [kT_])
                        act(lambda h, hc=hc, bT=bT: h.copy(out=uxs[:, hc, :, 0:2], in_=bT[:, 0:32].rearrange("p (b j) -> p b j", j=2)), [kT_], [f"uxs{hc}"])
                    for half in range(2):
                        dsp(sqb[0:120, :], spl[l, 8 * half:8 * half + 8].rearrange("b j c -> (b j) c"), w=["sqb"])
                        for hc in range(2):
                            bT, kT_ = bank()
                            pe(lambda h, hc=hc, bT=bT: h.transpose(bT[:, 0:120], sqb[0:120, hc * 128:(hc + 1) * 128], ident[0:120, 0:120]), ["sqb", "ct"], [kT_])
                            act(lambda h, hc=hc, bT=bT, half=half: h.copy(out=exs[:, hc, 8 * half:8 * half + 8, 1:16], in_=bT[:, 0:120].rearrange("p (b j) -> p b j", j=15)), [kT_], [f"exs{hc}"])
                def secA():
                    for g, col in enumerate((cA, cB)):
                        b, bk = fm(col)
                        for kv in range(2):
                            act(lambda h, b=b, g=g, kv=kv: h.copy(out=qZ[64 * kv:64 * kv + 64, g, kv, 0:n], in_=b[64 * kv:64 * kv + 64, 0:n]), [bk], ["qZ"])
                    bK, kK_ = fm(cK)
                    act(lambda h: h.copy(out=kTr[:, sl, 0:n], in_=bK[:, 0:n]), [kK_], [f"kT{sl}"])
                    if t < 16:
                        blks = [0] if t == 0 else [1, 0]
                        for blk in blks:
                            ksl = sl if blk == 0 else 1 - sl
                            bs, ks = bank()
                            for g in range(2):
                                pe(lambda h, g=g, bs=bs, ksl=ksl: h.matmul(bs[:, g * 256:(g + 1) * 256], lhsT=kTr[:, ksl, :], rhs=qZ[:, g, :, :].rearrange("p a q -> p (a q)"), start=True, stop=True), [f"kT{ksl}", "qZ"], [ks])
                            dve(lambda h, bs=bs, blk=blk: h.scalar_tensor_tensor(out=tmpS[:], in0=bs[:, :], scalar=0.125, in1=C("bt")[:, blk * 512:(blk + 1) * 512], op0=ALU.mult, op1=ALU.add), [ks, "ct"], ["tmpS"])
                            act(lambda h, blk=blk: h.activation(out=PT[:, blk, :], in_=tmpS[:], func=AF.Exp), ["tmpS"], [f"PT{blk}"])
                        bo_, ko_ = bank()
                        for kv in range(2):
                            for g in range(2):
                                hh = kv * 2 + g; pc = (g * 2 + kv) * 128
                                if t > 0:
                                    pe(lambda h, hh=hh, pc=pc, kv=kv: h.matmul(bo_[:, hh * 65:(hh + 1) * 65], lhsT=PT[:, 1, pc:pc + 128], rhs=Vaug[:, 1 - sl, kv, :], start=True, stop=False), ["PT1", f"Va{1 - sl}"], [ko_])
                                pe(lambda h, hh=hh, pc=pc, kv=kv: h.matmul(bo_[:, hh * 65:(hh + 1) * 65], lhsT=PT[:, 0, pc:pc + 128], rhs=Vaug[:, sl, kv, :], start=(t == 0), stop=True), ["PT0", f"Va{sl}"], [ko_])
                        attn_out(bo_, ko_)
                def secB():
                    if t < 16:
                        hgrn_gate()
                        tbH, tkH = tbank()
                        for hc in range(2):
                            bq, kq = fm(cBQ + hc * 128); bf, kf = fm(cBF + hc * 128)
                            act(lambda h, bf=bf: h.activation(out=hE[:, 0:n], in_=bf[:, 0:n], func=AF.Exp, scale=-1.0), [kf], ["hE"])
                            act(lambda h, hc=hc: h.activation(out=hL1[:, 0:n], in_=hE[:, 0:n], func=AF.Ln, scale=lbt[:, hc, l:l + 1], bias=1.0), ["hE", "lbt"], ["hL1"])
                            act(lambda h: h.activation(out=hL2[:, 0:n], in_=hE[:, 0:n], func=AF.Ln, scale=1.0, bias=1.0), ["hE"], ["hL2"])
                            dve(lambda h: h.tensor_tensor(out=hL1[:, 0:n], in0=hL1[:, 0:n], in1=hL2[:, 0:n], op=ALU.subtract), ["hL1", "hL2"], ["hL1"])
                            act(lambda h: h.activation(out=hL2[:, 0:n], in_=hL2[:, 0:n], func=AF.Exp, scale=-1.0), ["hL2"], ["hL2"])
                            dve(lambda h, hc=hc: h.scalar_tensor_tensor(out=hE[:, 0:n], in0=hE[:, 0:n], scalar=oml[:, hc, l:l + 1], in1=hL2[:, 0:n], op0=ALU.mult, op1=ALU.mult), ["hE", "hL2", "oml"], ["hE"])
                            for c2 in range(2):
                                dve(lambda h, c2=c2: h.tensor_tensor_scan(out=hCU[:, c2 * 64:(c2 + 1) * 64], data0=ones[:, 0:64], data1=hL1[:, c2 * 64:(c2 + 1) * 64], initial=0.0, op0=ALU.mult, op1=ALU.add), ["hL1", "ones"], ["hCU"])
                            act(lambda h: h.activation(out=hEC[:, 0:n], in_=hCU[:, 0:n], func=AF.Exp), ["hCU"], ["hEC"])
                            act(lambda h: h.activation(out=hEN[:, 0:n], in_=hCU[:, 0:n], func=AF.Exp, scale=-1.0), ["hCU"], ["hEN"])
                            act(lambda h, bq=bq: h.activation(out=hEQ[:, 0:n], in_=bq[:, 0:n], func=AF.Exp, scale=-1.0), [kq], ["hEQ"])
                            act(lambda h: h.activation(out=hEQ[:, 0:n], in_=hEQ[:, 0:n], func=AF.Ln, scale=1.0, bias=1.0), ["hEQ"], ["hEQ"])
                            act(lambda h: h.activation(out=hEQ[:, 0:n], in_=hEQ[:, 0:n], func=AF.Exp, scale=-1.0), ["hEQ"], ["hEQ"])
                            dve(lambda h, bq=bq: h.tensor_tensor(out=hEQ[:, 0:n], in0=bq[:, 0:n], in1=hEQ[:, 0:n], op=ALU.mult), [kq, "hEQ"], ["hEQ"])
                            for hp in range(2):
                                rw = slice(64 * hp, 64 * hp + 64)
                                dve(lambda h, hc=hc, hp=hp, rw=rw: h.tensor_tensor(out=qSZ[rw, hc, :, hp, :], in0=hEQ[rw, :].rearrange("p (c t) -> p c t", t=64), in1=hEC[rw, :].rearrange("p (c t) -> p c t", t=64), op=ALU.mult), ["hEQ", "hEC"], ["qSZ"])
                            dve(lambda h, hc=hc: h.tensor_tensor(out=kAb[:, hc, :], in0=hE[:, 0:n], in1=hEN[:, 0:n], op=ALU.mult), ["hE", "hEN"], [f"kAb{hc}"])
                            act(lambda h, hc=hc: h.copy(out=ECL[:, hc, :], in_=hEC[:, 63:128:64]), ["hEC"], ["ECL"])
                            pe(lambda h, hc=hc: h.transpose(tbH[:, hc * 128:(hc + 1) * 128], kAb[:, hc, :], identb[:]), [f"kAb{hc}", "identb"], [tkH])
                        for c2 in range(2):
                            rw = slice(64 * c2, 64 * c2 + 64)
                            dve(lambda h, c2=c2, rw=rw: h.tensor_copy(out=kAtokZ[rw, c2, :], in_=tbH[rw, 0:256]), [tkH], ["kAtokZ"])
                        bU, kU = bank()
                        for c2 in range(2):
                            for hc in range(2):
                                pe(lambda h, c2=c2, hc=hc: h.matmul(bU[:, (c2 * 2 + hc) * 128:(c2 * 2 + hc + 1) * 128], lhsT=kAtokZ[:, c2, hc * 128:(hc + 1) * 128], rhs=Vh[:, hc * 128:(hc + 1) * 128], start=True, stop=True), ["kAtokZ", "Vh"], [kU])
                        for c2 in range(2):
                            act(lambda h, c2=c2: h.copy(out=Sbf[:, :, c2, :], in_=Sst[:]), ["Sst"], ["Sbf"])
                            for hp in range(2):
                                rw = slice(64 * hp, 64 * hp + 64)
                                dve(lambda h, c2=c2, hp=hp, rw=rw: h.tensor_tensor(out=Sst[rw], in0=Sst[rw], in1=bU[rw, c2 * 256:(c2 + 1) * 256].rearrange("p (hc hh v) -> p hc hh v", hc=2, hh=2)[:, :, hp, :], op=ALU.add), ["Sst", kU], ["Sst"])
                            dve(lambda h, c2=c2: h.tensor_tensor(out=Sst[:], in0=Sst[:], in1=ECL[:, :, c2:c2 + 1].to_broadcast([128, 2, 64]), op=ALU.mult), ["Sst", "ECL"], ["Sst"])
                        bA, kA_ = bank()
                        for c2 in range(2):
                            for hc in range(2):
                                pe(lambda h, c2=c2, hc=hc: h.matmul(bA[64 * c2:64 * c2 + 64, hc * 128:(hc + 1) * 128], lhsT=kAb[:, hc, c2 * 64:(c2 + 1) * 64], rhs=qSZ[:, hc, c2, :, :].rearrange("p a t -> p (a t)"), start=True, stop=True), [f"kAb{hc}", "qSZ"], [kA_])
                        for c2 in range(2):
                            rw = slice(64 * c2, 64 * c2 + 64)
                            dve(lambda h, c2=c2, rw=rw: h.tensor_tensor(out=AtZ[rw, c2, :, :], in0=bA[rw, 0:256].rearrange("p (a t) -> p a t", t=64), in1=C("tri2")[rw, :].unsqueeze(1).to_broadcast([64, 4, 64]), op=ALU.mult), [kA_, "ct"], ["AtZ"])
                        bO, kO = bank()
                        for c2 in range(2):
                            for hh in range(4):
                                hc, hp = hh // 2, hh % 2
                                pe(lambda h, c2=c2, hh=hh, hc=hc, hp=hp: h.matmul(bO[64 * c2:64 * c2 + 64, hh * 64:(hh + 1) * 64], lhsT=qSZ[:, hc, c2, hp, :], rhs=Sbf[:, hc, c2, :], start=True, stop=False), ["qSZ", "Sbf"], [kO])
                                pe(lambda h, c2=c2, hh=hh: h.matmul(bO[64 * c2:64 * c2 + 64, hh * 64:(hh + 1) * 64], lhsT=AtZ[:, c2, hh, :], rhs=Vh[:, hh * 64:(hh + 1) * 64], start=False, stop=True), ["AtZ", "Vh"], [kO])
                        hgrn_out(bO, kO)
                        if t == int(os.environ.get('DBG_ST', '15')):
                            dsp(ohp[l].rearrange("(hc hp) k v -> (hp k) hc v", hc=2), Sst[:], ["Sst"])
                    else:
                        tbH, tkH = tbank()
                        for hc in range(2):
                            bq, kq = fm(cBQ + hc * 128); bf, kf = fm(cBF + hc * 128)
                            act(lambda h, bf=bf: h.activation(out=hE[:, 0:n], in_=bf[:, 0:n], func=AF.Exp, scale=-1.0), [kf], ["hE"])
                            act(lambda h, hc=hc: h.activation(out=hL1[:, 0:n], in_=hE[:, 0:n], func=AF.Ln, scale=lbt[:, hc, l:l + 1], bias=1.0), ["hE", "lbt"], ["hL1"])
                            act(lambda h: h.activation(out=hL2[:, 0:n], in_=hE[:, 0:n], func=AF.Ln, scale=1.0, bias=1.0), ["hE"], ["hL2"])
                            dve(lambda h: h.tensor_tensor(out=hL1[:, 0:n], in0=hL1[:, 0:n], in1=hL2[:, 0:n], op=ALU.subtract), ["hL1", "hL2"], ["hL1"])
                            act(lambda h: h.activation(out=hL2[:, 0:n], in_=hL2[:, 0:n], func=AF.Exp, scale=-1.0), ["hL2"], ["hL2"])
                            dve(lambda h, hc=hc: h.scalar_tensor_tensor(out=hE[:, 0:n], in0=hE[:, 0:n], scalar=oml[:, hc, l:l + 1], in1=hL2[:, 0:n], op0=ALU.mult, op1=ALU.mult), ["hE", "hL2", "oml"], ["hE"])
                            dve(lambda h: h.tensor_copy(out=hCU[:, 0:n], in_=hL1[:, 0:n]), ["hL1"], ["hCU"])
                            for tt in range(1, 4):
                                dve(lambda h, tt=tt: h.tensor_tensor(out=v3(hCU[:, 0:n])[:, :, tt:tt + 1], in0=v3(hCU[:, 0:n])[:, :, tt - 1:tt], in1=v3(hL1[:, 0:n])[:, :, tt:tt + 1], op=ALU.add), ["hCU", "hL1"], ["hCU"])
                            act(lambda h: h.activation(out=hEC[:, 0:n], in_=hCU[:, 0:n], func=AF.Exp), ["hCU"], ["hEC"])
                            act(lambda h: h.activation(out=hEN[:, 0:n], in_=hCU[:, 0:n], func=AF.Exp, scale=-1.0), ["hCU"], ["hEN"])
                            act(lambda h, bq=bq: h.activation(out=hEQ[:, 0:n], in_=bq[:, 0:n], func=AF.Exp, scale=-1.0), [kq], ["hEQ"])
                            act(lambda h: h.activation(out=hEQ[:, 0:n], in_=hEQ[:, 0:n], func=AF.Ln, scale=1.0, bias=1.0), ["hEQ"], ["hEQ"])
                            act(lambda h: h.activation(out=hEQ[:, 0:n], in_=hEQ[:, 0:n], func=AF.Exp, scale=-1.0), ["hEQ"], ["hEQ"])
                            dve(lambda h, bq=bq: h.tensor_tensor(out=hEQ[:, 0:n], in0=bq[:, 0:n], in1=hEQ[:, 0:n], op=ALU.mult), [kq, "hEQ"], ["hEQ"])
                            for hp in range(2):
                                rw = slice(64 * hp, 64 * hp + 64)
                                dve(lambda h, hc=hc, hp=hp, rw=rw: h.tensor_tensor(out=qSZ[rw, hc, 0, hp, :], in0=hEQ[rw, 0:n], in1=hEC[rw, 0:n], op=ALU.mult), ["hEQ", "hEC"], ["qSZ"])
                            dve(lambda h, hc=hc: h.tensor_tensor(out=kAb[:, hc, 0:n], in0=hE[:, 0:n], in1=hEN[:, 0:n], op=ALU.mult), ["hE", "hEN"], [f"kAb{hc}"])
                            act(lambda h, hc=hc: h.copy(out=ECLs[:, hc, :], in_=v3(hEC[:, 0:n])[:, :, 3]), ["hEC"], ["ECLs"])
                            dve(lambda h, hc=hc: h.tensor_tensor(out=v3(kAb[:, hc, 64:128]), in0=v3(kAb[:, hc, 0:64]), in1=ECLs[:, hc, :].unsqueeze(2).to_broadcast([128, 16, 4]), op=ALU.mult), [f"kAb{hc}", "ECLs"], [f"kAb{hc}"])
                            pe(lambda h, hc=hc: h.transpose(tbH[0:64, hc * 128:(hc + 1) * 128], kAb[:, hc, 64:128], identb[:]), [f"kAb{hc}", "identb"], [tkH])
                        dve(lambda h: h.tensor_copy(out=kAtokZ[0:64, 0, :], in_=tbH[0:64, 0:256]), [tkH], ["kAtokZ"])
                        bA, kA_ = bank()
                        for hc in range(2):
                            pe(lambda h, hc=hc: h.matmul(bA[0:64, hc * 128:(hc + 1) * 128], lhsT=kAb[:, hc, 0:64], rhs=qSZ[:, hc, 0, :, :].rearrange("p a t -> p (a t)"), start=True, stop=True), [f"kAb{hc}", "qSZ"], [kA_])
                        dve(lambda h: h.tensor_tensor(out=AtZ[0:64, 0, :, :], in0=bA[0:64, 0:256].rearrange("p (a t) -> p a t", t=64), in1=C("bdc")[0:64, :].unsqueeze(1).to_broadcast([64, 4, 64]), op=ALU.mult), [kA_, "ct"], ["AtZ"])
                        hgrn_gate()
                def secC():
                    for hc in range(2):
                        if t < 16:
                            bch, kch = fm(cCH + hc * 128)
                            act(lambda h, bch=bch: h.copy(out=chs[:, 0:n], in_=bch[:, 0:n]), [kch], ["chs"])
                            bcc, kcc = fm(cCC + hc * 128)
                            dve(lambda h, bcc=bcc, hc=hc: h.tensor_tensor(out=uext[:, hc, 2:2 + n], in0=bcc[:, 0:n], in1=chs[:, 0:n], op=ALU.mult), [kcc, "chs"], [f"uext{hc}"])
                            dve(lambda h, hc=hc: h.tensor_scalar(out=ycv[:, 0:n], in0=uext[:, hc, 2:2 + n], scalar1=cwt[:, l, hc, 2:3], scalar2=None, op0=ALU.mult), [f"uext{hc}", "cwt"], ["ycv"])
                            for j in (1, 0):
                                dve(lambda h, hc=hc, j=j: h.scalar_tensor_tensor(out=ycv[:, 0:n], in0=uext[:, hc, j:j + n], scalar=cwt[:, l, hc, j:j + 1], in1=ycv[:, 0:n], op0=ALU.mult, op1=ALU.add), [f"uext{hc}", "cwt", "ycv"], ["ycv"])
                            bcb, kcb = fm(cCB + hc * 128)
                            dve(lambda h, hc=hc, bcb=bcb: h.tensor_tensor(out=mixT[:, 4 + hc, 0:n], in0=ycv[:, 0:n], in1=bcb[:, 0:n], op=ALU.mult), ["ycv", kcb], [mk])
                            act(lambda h, hc=hc: h.copy(out=uext[:, hc, 0:2], in_=uext[:, hc, n:n + 2]), [f"uext{hc}"], [f"uext{hc}"])
                        else:
                            v3 = lambda ap: ap.rearrange("p (b t) -> p b t", t=4)
                            bch, kch = fm(cCH + hc * 128)
                            act(lambda h, bch=bch: h.copy(out=chs[:, 0:n], in_=bch[:, 0:n]), [kch], ["chs"])
                            bcc, kcc = fm(cCC + hc * 128)
                            dve(lambda h, bcc=bcc, hc=hc: h.tensor_tensor(out=uxs[:, hc, :, 2:6], in0=v3(bcc[:, 0:n]), in1=v3(chs[:, 0:n]), op=ALU.mult), [kcc, "chs"], [f"uxs{hc}"])
                            dve(lambda h, hc=hc: h.tensor_scalar(out=v3(ycv[:, 0:n]), in0=uxs[:, hc, :, 2:6], scalar1=cwt[:, l, hc, 2:3], scalar2=None, op0=ALU.mult), [f"uxs{hc}", "cwt"], ["ycv"])
                            for j in (1, 0):
                                dve(lambda h, hc=hc, j=j: h.scalar_tensor_tensor(out=v3(ycv[:, 0:n]), in0=uxs[:, hc, :, j:j + 4], scalar=cwt[:, l, hc, j:j + 1], in1=v3(ycv[:, 0:n]), op0=ALU.mult, op1=ALU.add), [f"uxs{hc}", "cwt", "ycv"], ["ycv"])
                            bcb, kcb = fm(cCB + hc * 128)
                            dve(lambda h, hc=hc, bcb=bcb: h.tensor_tensor(out=mixT[:, 4 + hc, 0:n], in0=ycv[:, 0:n], in1=bcb[:, 0:n], op=ALU.mult), ["ycv", kcb], [mk])
                def secD():
                    for hc in range(2):
                        bdv, kdv = fm(cDV + hc * 128)
                        if t < 16:
                            W_ = 16 + n
                            act(lambda h, bdv=bdv, hc=hc: h.copy(out=ext[:, hc, 16:16 + n], in_=bdv[:, 0:n]), [kdv], [f"ext{hc}"])
                            pool(lambda h, hc=hc: h.tensor_tensor(out=s2[:, 1:W_], in0=ext[:, hc, 1:W_], in1=ext[:, hc, 0:W_ - 1], op=ALU.add), [f"ext{hc}"], ["s2"])
                            pool(lambda h: h.tensor_tensor(out=s4[:, 3:W_], in0=s2[:, 3:W_], in1=s2[:, 1:W_ - 2], op=ALU.add), ["s2"], ["s4"])
                            if hc == 1:
                                pool(lambda h: h.tensor_tensor(out=s8[:, 7:W_], in0=s4[:, 7:W_], in1=s4[:, 3:W_ - 4], op=ALU.add), ["s4"], ["s8"])
                                pool(lambda h: h.tensor_tensor(out=s16[:, 15:W_], in0=s8[:, 15:W_], in1=s8[:, 7:W_ - 8], op=ALU.add), ["s8"], ["s16"])
                            sel = [(s2, "s2", 2.0), (s4, "s4", 4.0)] if hc == 0 else [(s8, "s8", 8.0), (s16, "s16", 16.0)]
                            for half, (sv, sk, w_) in enumerate(sel):
                                rows = slice(64 * half, 64 * half + 64)
                                dve(lambda h, sv=sv, rows=rows, w_=w_, hc=hc: h.scalar_tensor_tensor(out=pooled[rows, hc, 0:n], in0=sv[rows, 16:W_], scalar=1.0 / w_, in1=ext[rows, hc, 16:W_], op0=ALU.mult, op1=ALU.subtract), [sk, f"ext{hc}"], [f"pooled{hc}"])
                                if t == 0:
                                    dve(lambda h, sv=sv, rows=rows, hc=hc: h.tensor_tensor(out=ptmp[rows, 0:16], in0=sv[rows, 16:32], in1=C("icnt")[rows, hc * 16:(hc + 1) * 16], op=ALU.mult), [sk, "ct"], ["ptmp"])
                                    dve(lambda h, rows=rows, hc=hc: h.tensor_tensor(out=pooled[rows, hc, 0:16], in0=ptmp[rows, 0:16], in1=ext[rows, hc, 16:32], op=ALU.subtract), ["ptmp", f"ext{hc}", f"pooled{hc}"], [f"pooled{hc}"])
                            by, ky = bank()
                            pe(lambda h, hc=hc, by=by: h.matmul(by[:, 0:n], lhsT=wbd[:, hc, :], rhs=pooled[:, hc, 0:n], start=True, stop=True), ["wbd", f"pooled{hc}"], [ky])
                            act(lambda h, hc=hc, by=by: h.activation(out=mixT[:, 6 + hc, 0:n], in_=by[:, 0:n], func=AF.Copy, scale=psct[:, l, hc:hc + 1]), [ky, "psct"], [mk])
                            act(lambda h, hc=hc: h.copy(out=ext[:, hc, 0:16], in_=ext[:, hc, n:n + 16]), [f"ext{hc}"], [f"ext{hc}"])
                        else:
                            v3 = lambda ap: ap.rearrange("p (b t) -> p b t", t=4)
                            act(lambda h, bdv=bdv, hc=hc: h.copy(out=exs[:, hc, :, 16:20], in_=v3(bdv[:, 0:n])), [kdv], [f"exs{hc}"])
                            for half in range(2):
                                w_ = (2, 4, 8, 16)[hc * 2 + half]; rows = slice(64 * half, 64 * half + 64)
                                dve(lambda h, rows=rows, hc=hc: h.tensor_copy(out=v3(s2[rows, 0:n]), in_=exs[rows, hc, :, 16:20]), [f"exs{hc}"], ["s2"])
                                for j in range(1, w_):
                                    dve(lambda h, rows=rows, hc=hc, j=j: h.tensor_tensor(out=v3(s2[rows, 0:n]), in0=v3(s2[rows, 0:n]), in1=exs[rows, hc, :, 16 - j:20 - j], op=ALU.add), ["s2", f"exs{hc}"], ["s2"])
                                dve(lambda h, rows=rows, hc=hc, w_=w_: h.scalar_tensor_tensor(out=v3(pooled[rows, hc, 0:n]), in0=v3(s2[rows, 0:n]), scalar=1.0 / w_, in1=exs[rows, hc, :, 16:20], op0=ALU.mult, op1=ALU.subtract), ["s2", f"exs{hc}"], [f"pooled{hc}"])
                            by, ky = bank()
                            pe(lambda h, hc=hc, by=by: h.matmul(by[:, 0:n], lhsT=wbd[:, hc, :], rhs=pooled[:, hc, 0:n], start=True, stop=True), ["wbd", f"pooled{hc}"], [ky])
                            act(lambda h, hc=hc, by=by: h.activation(out=mixT[:, 6 + hc, 0:n], in_=by[:, 0:n], func=AF.Copy, scale=psct[:, l, hc:hc + 1]), [ky, "psct"], [mk])
                if os.environ.get('NO_ILV'):
                    secA(); secB(); secC(); secD()
                    if prev_post:
                        prev_post()
                else:
                    order = os.environ.get("ORD", "EBDCA")
                    table = {"A": (secA, "A", None), "B": (secB, "B", "Bt"), "C": (secC, "C", None), "D": (secD, "D", None), "E": (prev_post, "E", None)}
                    sel = [table[c_] for c_ in order if table[c_][0] is not None]
                    run_interleaved([s_[0] for s_ in sel], [s_[1] for s_ in sel], [s_[2] for s_ in sel])
                if t == 16:
                    PS_ = 28672
                    kc_b, kc_ = warena("kc", 0, 2048); kc = kc_.rearrange("p (b c) -> p b c", c=128)
                    kcT_b, kcT_ = warena("kcT", 2048, 2048); kcT = kcT_.rearrange("p (b c) -> p b c", c=128)
                    Vc_b, Vc_ = warena("Vc", 4096, 2080); Vc = Vc_.rearrange("p (b kv e) -> p b kv e", kv=2, e=65)
                    Pb_b, Pb_ = warena("Pb", 6400, 4096); Pb = Pb_.rearrange("p (g kv b q) -> p g kv b q", g=2, kv=2, b=16)
                    tabs_b, tabs_ = warena("stab", 18944, 1024); tabs = tabs_.bitcast(F32)
                    dsp(tabs, stab[:, :], w=[tabs_b])
                    dpl(kc, ck[l].rearrange("b s c -> s b c"), w=[kc_b])
                    pool(lambda h: h.memset(Vc_, 1.0), w=[Vc_b])
                    for kv in range(2):
                        dpl(Vc[:, :, kv, 0:64], cv[l, :, :, kv * 64:(kv + 1) * 64].rearrange("b s d -> s b d"), w=[Vc_b])
                    pool(lambda h: h.memset(Pb_, 0.0), w=[Pb_b])
                    for half in range(2):
                        tbk, tkk = tbank()
                        for j in range(8):
                            pe(lambda h, j=j, half=half, tbk=tbk: h.transpose(tbk[:, j * 128:(j + 1) * 128], kc[:, half * 8 + j, :], identb[:]), [kc_b, "identb"], [tkk])
                        dve(lambda h, half=half, tbk=tbk: h.tensor_copy(out=kcT[:, half * 8:(half + 1) * 8, :], in_=tbk[:, :].rearrange("p (b c) -> p b c", c=128)), [tkk], [kcT_b])
                    bs, ks = bank()
                    for b_ in range(16):
                        for g in range(2):
                            pe(lambda h, b_=b_, g=g: h.matmul(bs[:, g * 128:(g + 1) * 128].rearrange("p (kv q) -> p kv q", kv=2)[:, :, b_ * 4:b_ * 4 + 4], lhsT=kcT[:, b_, :], rhs=qZ[:, g, :, b_ * 4:b_ * 4 + 4], start=True, stop=True), [kcT_b, "qZ"], [ks])
                    bn, kn = bank()
                    for g in range(2):
                        pe(lambda h, g=g: h.matmul(bn[0:64, g * 128:(g + 1) * 128].rearrange("p (kv q) -> p kv q", kv=2), lhsT=kTr[:, 0, 0:64], rhs=qZ[:, g, :, 0:64], start=True, stop=True), ["kT0", "qZ"], [kn])
                    dve(lambda h: h.scalar_tensor_tensor(out=tmpS[:, 0:256], in0=bs[:, 0:256], scalar=0.125, in1=tabs[:, 0:256], op0=ALU.mult, op1=ALU.add), [ks, tabs_b], ["tmpS"])
                    dve(lambda h: h.scalar_tensor_tensor(out=tmpS[0:64, 256:512], in0=bn[0:64, 0:256], scalar=0.125, in1=tabs[0:64, 256:512], op0=ALU.mult, op1=ALU.add), [kn, tabs_b, "tmpS"], ["tmpS"])
                    for gk in range(4):
                        act(lambda h, gk=gk: h.activation(out=bass.AP(WA, 6400 + gk * 1024, [[PS_, 128], [68, 16], [1, 4]]), in_=v3(tmpS[:, gk * 64:(gk + 1) * 64]), func=AF.Exp), ["tmpS"], [Pb_b])
                    act(lambda h: h.activation(out=PT[0:64, 0, 0:256], in_=tmpS[0:64, 256:512], func=AF.Exp), ["tmpS"], ["PT0"])
                    bo_, ko_ = bank()
                    for kv in range(2):
                        for g in range(2):
                            hh = kv * 2 + g; pc = (g * 2 + kv) * 64
                            for b_ in range(16):
                                pe(lambda h, hh=hh, g=g, kv=kv, b_=b_: h.matmul(bo_[0:64, hh * 65:(hh + 1) * 65], lhsT=Pb[:, g, kv, b_, :], rhs=Vc[:, b_, kv, :], start=(b_ == 0), stop=False), [Pb_b, Vc_b], [ko_])
                            pe(lambda h, hh=hh, pc=pc, kv=kv: h.matmul(bo_[0:64, hh * 65:(hh + 1) * 65], lhsT=PT[0:64, 0, pc:pc + 64], rhs=Vaug[0:64, 0, kv, :], start=False, stop=True), ["PT0", "Va0"], [ko_])
                    attn_out(bo_, ko_)
                    S0f_b, S0f_ = warena("S0f", 10496, 4096); S0f = S0f_.bitcast(F32).rearrange("p (hc b v) -> p hc b v", hc=2, b=16)
                    S0b_b, S0b_ = warena("S0b", 14592, 2048); S0b = S0b_.rearrange("p (hc b v) -> p hc b v", hc=2, b=16)
                    qSb_b, qSb_ = warena("qSb", 16640, 2048); qSbig = qSb_.rearrange("p (hp b q) -> p hp b q", hp=2, b=16)
                    Vbg_b, Vbg_ = warena("Vbg", 0, 4096); Vbig = Vbg_.rearrange("p (b c) -> p b c", c=256)
                    for hc in range(2):
                        for hp in range(2):
                            srcS = sh[l, :, hc * 2 + hp].rearrange("b k v -> k b v")
                            dsp(S0f[64 * hp:64 * hp + 64, hc], srcS, w=[S0f_b]); dpl(S0b[64 * hp:64 * hp + 64, hc], srcS, w=[S0b_b])
                    pool(lambda h: h.memset(qSb_, 0.0), w=[qSb_b])
                    dve(lambda h: h.tensor_tensor(out=Vbig[0:64], in0=Vh[0:64, :].unsqueeze(1).to_broadcast([64, 16, 256]), in1=C("mind")[0:64, :].unsqueeze(2).to_broadcast([64, 16, 256]), op=ALU.mult), ["Vh", "ct"], [Vbg_b])
                    bO, kO = bank()
                    for hc in range(2):
                        for hp in range(2):
                            rw = slice(64 * hp, 64 * hp + 64)
                            dve(lambda h, hc=hc, hp=hp, rw=rw: h.tensor_copy(out=bass.AP(WA, 64 * hp * PS_ + 16640 + hp * 1024, [[PS_, 64], [68, 16], [1, 4]]), in_=v3(qSZ[rw, hc, 0, hp, :])), ["qSZ"], [qSb_b])
                        for hp in range(2):
                            hh = hc * 2 + hp
                            for b_ in range(16):
                                pe(lambda h, hh=hh, hc=hc, hp=hp, b_=b_: h.matmul(bO[0:64, hh * 64:(hh + 1) * 64], lhsT=qSbig[:, hp, b_, :], rhs=S0b[:, hc, b_, :], start=(b_ == 0), stop=False), [qSb_b, S0b_b], [kO])
                            pe(lambda h, hh=hh: h.matmul(bO[0:64, hh * 64:(hh + 1) * 64], lhsT=AtZ[:, 0, hh, :], rhs=Vh[:, hh * 64:(hh + 1) * 64], start=False, stop=True), ["AtZ", "Vh"], [kO])
                    hgrn_out(bO, kO)
                    for hc in range(2):
                        for bg_ in range(4):
                            bU, kU = bank()
                            pe(lambda h, hc=hc, bg_=bg_, bU=bU: h.matmul(bU[:, 0:512].rearrange("p (b c) -> p b c", c=128), lhsT=kAtokZ[0:64, 0, hc * 128:(hc + 1) * 128], rhs=Vbig[0:64, bg_ * 4:(bg_ + 1) * 4, hc * 128:(hc + 1) * 128], start=True, stop=True), ["kAtokZ", Vbg_b], [kU])
                            for hp in range(2):
                                rw = slice(64 * hp, 64 * hp + 64); b4 = slice(bg_ * 4, bg_ * 4 + 4)
                                dve(lambda h, hc=hc, rw=rw, b4=b4: h.tensor_tensor(out=S0f[rw, hc, b4, :], in0=S0f[rw, hc, b4, :], in1=ECLs[rw, hc, b4].unsqueeze(2).to_broadcast([64, 4, 64]), op=ALU.mult), [S0f_b, "ECLs"], [S0f_b])
                                dve(lambda h, hc=hc, hp=hp, rw=rw, b4=b4, bU=bU: h.tensor_tensor(out=S0f[rw, hc, b4, :], in0=S0f[rw, hc, b4, :], in1=bU[rw, 0:512].rearrange("p (b c) -> p b c", c=128)[:, :, hp * 64:(hp + 1) * 64], op=ALU.add), [S0f_b, kU], [S0f_b])
                    for hc in range(2):
                        for hp in range(2):
                            dsp(ohs[l, :, hc * 2 + hp].rearrange("b k v -> k b v"), S0f[64 * hp:64 * hp + 64, hc], [S0f_b])
                def post():
                    tb, tk = tbank()
                    for c in range(2):
                        pe(lambda h, c=c: h.transpose(tb[:, c * 128:c * 128 + n], otok[0:n, c * 128:(c + 1) * 128], identb[0:n, 0:n]), ["otok", "identb"], [tk])
                    dve(lambda h: h.tensor_copy(out=mixT[:, 0:2, 0:n], in_=tb[:, 0:256].rearrange("p (a b) -> p a b", b=128)[:, :, 0:n]), [tk], [mk])
                    tb2, tk2 = tbank()
                    for c in range(2):
                        pe(lambda h, c=c: h.transpose(tb2[:, c * 128:c * 128 + n], otok2[0:n, c * 128:(c + 1) * 128], identb[0:n, 0:n]), ["otok2", "identb"], [tk2])
                    dve(lambda h: h.tensor_copy(out=mixT[:, 2:4, 0:n], in_=tb2[:, 0:256].rearrange("p (a b) -> p a b", b=128)[:, :, 0:n]), [tk2], [mk])
                    if dbg:
                        for c in range(8):
                            dpl(dbgo[c * 128:(c + 1) * 128, c0:c0 + n], mixT[:, c, 0:n], [mk])
                    for nh in range(2):
                        bo_h, bok_h = bank()
                        for k in range(8):
                            pe(lambda h, k=k, nh=nh, bo_h=bo_h: h.matmul(bo_h[0:n, :], lhsT=mixT[:, k, 0:n], rhs=W_o[1][:, k * 1024 + nh * 512:k * 1024 + (nh + 1) * 512], start=(k == 0), stop=(k == 7)), [mk, W_o[0]], [bok_h])
                        dve(lambda h, nh=nh, bo_h=bo_h: h.scalar_tensor_tensor(out=x_tok[0:n, t, nh * 512:(nh + 1) * 512], in0=x_tok[0:n, t, nh * 512:(nh + 1) * 512], scalar=ALPHA, in1=bo_h[0:n, :], op0=ALU.mult, op1=ALU.add), [f"x{t}", bok_h], [f"x{t}"])
                    ln_tile(t, 1 + 3 * l, None)
                return post
            prev = None
            for t in (range(NT) if '1' in DBGP else []):
                prev = p1(t, prev)
            if prev:
                prev()
            dpl(W_xq[1].rearrange("p (a b) -> p a b", b=256), wxq[l], w=[W_xq[0]])
            dpl(W_xo[1].rearrange("p (a b) -> p a b", b=1024), wxo[l], w=[W_xo[0]])
            dpl(W_xk[1].rearrange("p (a b) -> p a b", b=256), wxk[l], w=[W_xk[0]])
            dpl(W_xv[1].rearrange("p (a b) -> p a b", b=256), wxv[l], w=[W_xv[0]])
            load_ln(2 + 3 * l)
            for mt in range(2):
                for hf in range(2):
                    dsp(tmpS[:], memp[mt * 128:(mt + 1) * 128, hf * 512:(hf + 1) * 512], w=["tmpS"])
                    act(lambda h, hf=hf: h.copy(out=zbf[:, hf * 512:(hf + 1) * 512], in_=tmpS[:]), ["tmpS"], ["zbf"])
                tbm, tkm = tbank()
                for c in range(8):
                    pe(lambda h, c=c, tbm=tbm: h.transpose(tbm[:, c * 128:(c + 1) * 128], zbf[:, c * 128:(c + 1) * 128], identb[:]), ["zbf", "identb"], [tkm])
                dve(lambda h, mt=mt, tbm=tbm: h.tensor_copy(out=memT[:, :, mt * 128:(mt + 1) * 128], in_=tbm[:].rearrange("p (c m) -> p c m", m=128)), [tkm], [memTb])
            def p2m(mt):
                bk_, kk_ = bank()
                for k in range(8):
                    pe(lambda h, k=k, mt=mt, b=bk_: h.matmul(b[:, 0:256], lhsT=memT[:, k, mt * 128:(mt + 1) * 128], rhs=W_xk[1][:, k * 256:(k + 1) * 256], start=(k == 0), stop=(k == 7)), MEMT + [W_xk[0]], [kk_])
                for k in range(8):
                    pe(lambda h, k=k, mt=mt, b=bk_: h.matmul(b[:, 256:512], lhsT=memT[:, k, mt * 128:(mt + 1) * 128], rhs=W_xv[1][:, k * 256:(k + 1) * 256], start=(k == 0), stop=(k == 7)), MEMT + [W_xv[0]], [kk_])
                dve(lambda h, b=bk_: h.tensor_copy(out=tmpS[:, 0:512], in_=b[:, :]), [kk_], ["tmpS"])
                act(lambda h, b=bk_, mt=mt: h.copy(out=mva[:, mt, :, 0:64], in_=b[:, 256:512].rearrange("p (h d) -> p h d", d=64)), [kk_], ["mva"])
                dsp(omk[l, mt * 128:(mt + 1) * 128, :], tmpS[:, 0:256], ["tmpS"]); dsp(omv[l, mt * 128:(mt + 1) * 128, :], tmpS[:, 256:512], ["tmpS"])
            for mt in (range(2) if '2' in DBGP else []):
                p2m(mt)
            for hc in range(2):
                bk_, kk_ = bank()
                for k in range(8):
                    pe(lambda h, k=k, hc=hc, b=bk_: h.matmul(b[:, 0:256], lhsT=W_xk[1][:, k * 256 + hc * 128:k * 256 + (hc + 1) * 128], rhs=memT[:, k, :], start=(k == 0), stop=(k == 7)), MEMT + [W_xk[0]], [kk_])
                act(lambda h, hc=hc, b=bk_: h.copy(out=mkT[:, hc, :], in_=b[:, 0:256]), [kk_], ["kAtokZ"])
            qxT2 = hEQ[:].bitcast(BF16).rearrange("p (a b) -> p a b", b=128)
            PX2 = tmpS[:].bitcast(BF16).rearrange("p (a b c) -> p a b c", a=2, c=128)

            def p2(t):
                n = 128; c0 = t * 128
                par = t % 2
                qx = (qxT, qxT2)[par]; qk = (["kAb0", "kAb1"], ["hEQ", "hEQ"])[par]
                px = (PX, PX2)[par]; pk = (["PT0", "PT1"], ["tmpS", "tmpS"])[par]
                for hc in range(2):
                    bq, kq = bank()
                    for k in range(8):
                        pe(lambda h, k=k, hc=hc, bq=bq: h.matmul(bq[:, 0:n], lhsT=W_xq[1][:, k * 256 + hc * 128:k * 256 + (hc + 1) * 128], rhs=xT[:, k, c0:c0 + n], start=(k == 0), stop=(k == 7)), [f"xT{t}", W_xq[0]], [kq])
                    act(lambda h, hc=hc, bq=bq: h.copy(out=qx[:, hc, :], in_=bq[:, 0:n]), [kq], [qk[hc]])
                bsc = [bank(), bank()]
                for mt in range(2):
                    for hc in range(2):
                        for hp in range(2):
                            pe(lambda h, hc=hc, hp=hp, mt=mt: h.matmul(bsc[hp][0][:, (mt * 2 + hc) * 128:(mt * 2 + hc + 1) * 128], lhsT=mkT[64 * hp:64 * hp + 64, hc, mt * 128:(mt + 1) * 128], rhs=qx[64 * hp:64 * hp + 64, hc, :], start=True, stop=True), ["kAtokZ"] + qk, [bsc[hp][1]])
                for hp in range(2):
                    act(lambda h, hp=hp: h.activation(out=px[:, hp, :, :].rearrange("p a q -> p (a q)"), in_=bsc[hp][0][:, :], func=AF.Exp, scale=0.125), [bsc[hp][1]], [pk[hp]])

                def stage_b():
                    bo_, ko_ = bank()
                    for hh in range(4):
                        hc, hp = hh // 2, hh % 2
                        for mt in range(2):
                            pe(lambda h, hh=hh, hc=hc, hp=hp, mt=mt: h.matmul(bo_[:, hh * 65:(hh + 1) * 65], lhsT=px[:, hp, mt * 2 + hc, :], rhs=mva[:, mt, hh, :], start=(mt == 0), stop=(mt == 1)), [pk[hp], "mva"], [ko_])
                    ov = bo_[:, 0:260].rearrange("p (h e) -> p h e", e=65)
                    dve(lambda h: h.reciprocal(out=rec[:], in_=ov[:, :, 64]), [ko_], ["reca"])
                    dve(lambda h: h.tensor_tensor(out=oxt[:].rearrange("p (h d) -> p h d", d=64), in0=ov[:, :, 0:64], in1=rec[:].unsqueeze(2).to_broadcast([128, 4, 64]), op=ALU.mult), [ko_, "reca"], ["otok"])
                    tb, tk = tbank()
                    for hc in range(2):
                        pe(lambda h, hc=hc: h.transpose(tb[:, hc * 128:(hc + 1) * 128], oxt[:, hc * 128:(hc + 1) * 128], identb[:]), ["otok", "identb"], [tk])
                    dve(lambda h: h.tensor_copy(out=oxT[:].rearrange("p a b -> p (a b)"), in_=tb[:, 0:256]), [tk], ["otok2"])
                    bo = [bank(), bank()]
                    for nh in range(2):
                        for hc in range(2):
                            pe(lambda h, hc=hc, nh=nh: h.matmul(bo[nh][0][:, :], lhsT=oxT[:, hc, :], rhs=W_xo[1][:, hc * 1024 + nh * 512:hc * 1024 + (nh + 1) * 512], start=(hc == 0), stop=(hc == 1)), ["otok2", W_xo[0]], [bo[nh][1]])
                    ln_tile(t, 2 + 3 * l, bo)
                return stage_b
            prevB = None
            for t in (range(16) if '2' in DBGP else []):
                sb_ = p2(t)
                if prevB:
                    prevB()
                prevB = sb_
            if prevB:
                prevB()
            def p2s():
                t = 16; n = 64; c0 = 2048; PS_ = 28672
                v3 = lambda ap: ap.rearrange("p (b t) -> p b t", t=4)
                for hc in range(2):
                    bq, kq = bank()
                    for k_ in range(8):
                        pe(lambda h, k_=k_, hc=hc, bq=bq: h.matmul(bq[:, 0:n], lhsT=W_xq[1][:, k_ * 256 + hc * 128:k_ * 256 + (hc + 1) * 128], rhs=xT[:, k_, c0:c0 + n], start=(k_ == 0), stop=(k_ == 7)), ["xT16", W_xq[0]], [kq])
                    for hp in range(2):
                        act(lambda h, hc=hc, hp=hp, bq=bq: h.copy(out=qZ[64 * hp:64 * hp + 64, hc, hp, 0:n], in_=bq[64 * hp:64 * hp + 64, 0:n]), [kq], ["qZ"])
                def xhalf(hi_, half):
                    b0 = 8 * half
                    Kc_b, Kc_ = warena("xKc", 10240, 4096); Kc = Kc_.rearrange("p (b mt c) -> p b mt c", b=8, mt=2)
                    KcT_b, KcT_ = warena("xKcT", 14336, 4096); KcT = KcT_.rearrange("p (b hc m) -> p b hc m", b=8, hc=2)
                    Vx_b, Vx_ = warena("xVc", 18432, 4160); Vx = Vx_.rearrange("p (b mt hh e) -> p b mt hh e", b=8, mt=2, hh=4)
                    Px_b, Px_ = warena("xPb", 22592, 4096); Px = Px_.rearrange("p (mt hh b q) -> p mt hh b q", mt=2, hh=4, b=8)
                    pool(lambda h, Vx_=Vx_: h.memset(Vx_, 1.0), w=[Vx_b])
                    pool(lambda h, Px_=Px_: h.memset(Px_, 0.0), w=[Px_b])
                    for mt in range(2):
                        dpl(Kc[:, :, mt, :], cmk[l, b0:b0 + 8, mt * 128:(mt + 1) * 128, :].rearrange("b m c -> m b c"), w=[Kc_b])
                        for hh in range(4):
                            dpl(Vx[:, :, mt, hh, 0:64], cmv[l, b0:b0 + 8, mt * 128:(mt + 1) * 128, hh * 64:(hh + 1) * 64].rearrange("b m d -> m b d"), w=[Vx_b])
                    for bp in range(4):
                        tbk, tkk = tbank()
                        for j in range(8):
                            bl, hc, mt = bp * 2 + j // 4, (j // 2) % 2, j % 2
                            pe(lambda h, j=j, bl=bl, hc=hc, mt=mt, tbk=tbk, Kc=Kc: h.transpose(tbk[:, j * 128:(j + 1) * 128], Kc[:, bl, mt, hc * 128:(hc + 1) * 128], identb[:]), [Kc_b, "identb"], [tkk])
                        dve(lambda h, bp=bp, tbk=tbk, KcT_=KcT_: h.tensor_copy(out=KcT_[:, bp * 1024:(bp + 1) * 1024], in_=tbk[:, :]), [tkk], [KcT_b])
                    bs, ks = bank()
                    for bl in range(8):
                        for mt in range(2):
                            for hc in range(2):
                                pe(lambda h, bl=bl, mt=mt, hc=hc, KcT=KcT: h.matmul(bs[:, (mt * 2 + hc) * 64:(mt * 2 + hc + 1) * 64].rearrange("p (hp q) -> p hp q", hp=2)[:, :, bl * 4:bl * 4 + 4], lhsT=KcT[:, bl, hc, mt * 128:(mt + 1) * 128], rhs=qZ[:, hc, :, (b0 + bl) * 4:(b0 + bl) * 4 + 4], start=True, stop=True), [KcT_b, "qZ"], [ks])
                    for mt in range(2):
                        for hh in range(4):
                            act(lambda h, mt=mt, hh=hh, half=half: h.activation(out=bass.AP(WA, 22592 + (mt * 4 + hh) * 512 + half * 32, [[PS_, 128], [68, 8], [1, 4]]), in_=v3(bs[:, mt * 128 + hh * 32:mt * 128 + hh * 32 + 32]), func=AF.Exp, scale=0.125), [ks], [Px_b])
                    bo_, ko_ = bank()
                    for hh in range(4):
                        for bl in range(8):
                            for mt in range(2):
                                pe(lambda h, hh=hh, bl=bl, mt=mt, Px=Px, Vx=Vx: h.matmul(bo_[0:64, hh * 65:(hh + 1) * 65], lhsT=Px[:, mt, hh, bl, :], rhs=Vx[:, bl, mt, hh, :], start=(bl == 0 and mt == 0), stop=(bl == 7 and mt == 1)), [Px_b, Vx_b], [ko_])
                    if hi_ == 0:
                        dve(lambda h, bo_=bo_: h.tensor_copy(out=tmpS[0:64, 0:260], in_=bo_[0:64, 0:260]), [ko_], ["tmpS"])
                    else:
                        dve(lambda h, bo_=bo_: h.tensor_tensor(out=tmpS[0:64, 0:260], in0=tmpS[0:64, 0:260], in1=bo_[0:64, 0:260], op=ALU.add), [ko_, "tmpS"], ["tmpS"])
                for hi_, half in enumerate((0, 1)):
                    xhalf(hi_, half)
                ov = tmpS[0:64, 0:260].rearrange("p (h e) -> p h e", e=65)
                dve(lambda h: h.reciprocal(out=reca[0:64], in_=ov[:, :, 64]), ["tmpS"], ["reca"])
                dve(lambda h: h.tensor_tensor(out=otok[0:64].rearrange("p (h d) -> p h d", d=64), in0=ov[:, :, 0:64], in1=reca[0:64].unsqueeze(2).to_broadcast([64, 4, 64]), op=ALU.mult), ["tmpS", "reca"], ["otok"])
                if dbg:
                    dpl(dbgo[0:64, 0:256], otok[0:64, 0:256], ["otok"])
                tb, tk = tbank()
                for hc in range(2):
                    pe(lambda h, hc=hc: h.transpose(tb[:, hc * 128:hc * 128 + 64], otok[0:64, hc * 128:(hc + 1) * 128], identb[0:64, 0:64]), ["otok", "identb"], [tk])
                dve(lambda h: h.tensor_copy(out=oxT[:, :, 0:64], in_=tb[:, 0:256].rearrange("p (a b) -> p a b", b=128)[:, :, 0:64]), [tk], ["otok2"])
                bo = [bank(), bank()]
                for nh in range(2):
                    for hc in range(2):
                        pe(lambda h, hc=hc, nh=nh: h.matmul(bo[nh][0][0:64, :], lhsT=oxT[:, hc, 0:64], rhs=W_xo[1][:, hc * 1024 + nh * 512:hc * 1024 + (nh + 1) * 512], start=(hc == 0), stop=(hc == 1)), ["otok2", W_xo[0]], [bo[nh][1]])
                ln_tile(16, 2 + 3 * l, bo)
            if '2' in DBGP:
                p2s()
            load_ln(3 + 3 * l)
            def p3(j):
                r0 = ring[j % 4]
                Wg_ = warena(f"fg{j % 4}", r0, 2048); Wu_ = warena(f"fu{j % 4}", r0 + 2048, 2048); Wd_ = warena(f"fd{j % 4}", r0 + 4096, 2048)
                dpl(Wg_[1].rearrange("p (a b) -> p a b", b=256), wg[l, :, :, j * 256:(j + 1) * 256], w=[Wg_[0]])
                dpl(Wu_[1].rearrange("p (a b) -> p a b", b=256), wu[l, :, :, j * 256:(j + 1) * 256], w=[Wu_[0]])
                dpl(Wd_[1].rearrange("p (a b) -> p a b", b=1024), wd[l, :, 2 * j:2 * j + 2, :], w=[Wd_[0]])
                def p3t(t):
                    n = ntok(t); c0 = t * 128
                    for cc in range(2):
                        bg, kg = bank(); bu, ku = bank()
                        for k in range(8):
                            pe(lambda h, k=k, cc=cc, bg=bg: h.matmul(bg[:, 0:n], lhsT=Wg_[1][:, k * 256 + cc * 128:k * 256 + (cc + 1) * 128], rhs=xT[:, k, c0:c0 + n], start=(k == 0), stop=(k == 7)), [f"xT{t}", Wg_[0]], [kg])
                        for k in range(8):
                            pe(lambda h, k=k, cc=cc, bu=bu: h.matmul(bu[:, 0:n], lhsT=Wu_[1][:, k * 256 + cc * 128:k * 256 + (cc + 1) * 128], rhs=xT[:, k, c0:c0 + n], start=(k == 0), stop=(k == 7)), [f"xT{t}", Wu_[0]], [ku])
                        sg_, sgk = ((hE, "hE"), (hCU, "hCU"), (hEC, "hEC"), (hEN, "hEN"))[(t % 2) * 2 + cc]
                        hT_, hTk = ((hT, "hL1"), (hT2, "hL2"))[t % 2]
                        act(lambda h, bg=bg, sg_=sg_: h.activation(out=sg_[:, 0:n], in_=bg[:, 0:n], func=AF.Silu), [kg], [sgk])
                        dve(lambda h, bu=bu, cc=cc, sg_=sg_, hT_=hT_: h.tensor_tensor(out=hT_[:, cc, 0:n], in0=sg_[:, 0:n], in1=bu[:, 0:n], op=ALU.mult), [sgk, ku], [hTk])
                    def down():
                        bo = [bank(), bank()]
                        for nh in range(2):
                            for cc in range(2):
                                pe(lambda h, cc=cc, nh=nh: h.matmul(bo[nh][0][0:n, :], lhsT=hT_[:, cc, 0:n], rhs=Wd_[1][:, cc * 1024 + nh * 512:cc * 1024 + (nh + 1) * 512], start=(cc == 0), stop=(cc == 1)), [hTk, Wd_[0]], [bo[nh][1]])
                        ln_tile(t, (3 + 3 * l) if j == 10 else None, bo, first=(j == 0))
                    return down
                prevD = None
                for t in range(NT):
                    d_ = p3t(t)
                    if prevD:
                        prevD()
                    prevD = d_
                prevD()
            for j in (range(11) if '3' in DBGP else []):
                p3(j)
        for l_ in range(nl):
            layer(l_)
        for t in range(16):
            dsp(yp[t * 128:(t + 1) * 128, :], x_tok[:, t, :], [f"x{t}"])
        dsp(ys[:, :], x_tok[0:64, 16, :], ["x16"])
        S.final()
        S.emit(nc, sems, dsems, block)
    return nc


def _host_layout(inp, nl=4, cores=range(8)):
    f = lambda a: np.ascontiguousarray(np.asarray(a, dtype=np.float32))
    pk = lambda w: f(np.asarray(w)[:nl].reshape(nl, -1, 128, np.asarray(w).shape[-1]).transpose(0, 2, 1, 3))
    sh = {}
    sh["win"] = pk(np.asarray(inp["w_in"])[:, :, _perm()])
    sh["wo"] = pk(inp["w_o"]); sh["wxq"] = pk(inp["w_xq"]); sh["wxk"] = pk(inp["w_xk"]); sh["wxv"] = pk(inp["w_xv"])
    sh["wxo"] = pk(inp["w_xo"]); sh["wg"] = pk(inp["w_gate"]); sh["wu"] = pk(inp["w_up"]); sh["wd"] = pk(inp["w_down"])
    gs = [inp["emb_ln_g"]]; bs = [inp["emb_ln_b"]]
    for l in range(4):
        for k in (1, 2, 3):
            gs.append(np.asarray(inp[f"ln{k}_g"])[l]); bs.append(np.asarray(inp[f"ln{k}_b"])[l])
    rows = np.stack([np.stack([np.asarray(g), np.asarray(b)]) for g, b in zip(gs, bs)])
    sh["lnrows"] = f(rows)
    sh["lncols"] = f(rows.reshape(13, 2, 8, 128).transpose(3, 0, 1, 2).reshape(128, 13 * 16))
    sh["sink"] = f(np.asarray(inp["attn_sink"]).reshape(1, 16))
    sh["hlb"] = f(np.asarray(inp["hgrn_lb"]).reshape(4, 2, 128).transpose(2, 1, 0).reshape(128, 8))
    sh["hng"] = f(inp["hgrn_norm_g"])
    sh["cw"] = f(np.asarray(inp["conv_w"]).reshape(4, 3, 2, 128).transpose(3, 0, 2, 1).reshape(128, 24))
    sh["pw"] = f(inp["pool_w"])
    sh["psc"] = f(np.asarray(inp["pool_scale"]).reshape(4, 2, 128).transpose(2, 0, 1).reshape(128, 8))
    sh["ctab"] = _CT; sh["stab"] = _STAB
    maps = []
    for c in cores:
        m = dict(sh)
        b = slice(16 * c, 16 * c + 16)
        m["xp"] = f(inp["x_prompt"][c]); m["xs"] = f(np.asarray(inp["x_sample"])[b].reshape(64, D))
        m["ck"] = f(np.asarray(inp["cache_swa_k"])[:nl, b].reshape(nl, 16, 128, 128))
        m["cv"] = f(np.asarray(inp["cache_swa_v"])[:nl, b].reshape(nl, 16, 128, 128))
        m["sh"] = f(np.asarray(inp["state_hgrn"])[:nl, b])
        m["scv"] = f(np.asarray(inp["state_conv"])[:nl, b].reshape(nl, 32, 256))
        m["spl"] = f(np.asarray(inp["state_pool"])[:nl, b])
        m["cmk"] = f(np.asarray(inp["cache_mem_k"])[:nl, b].reshape(nl, 16, 256, 256))
        m["cmv"] = f(np.asarray(inp["cache_mem_v"])[:nl, b].reshape(nl, 16, 256, 256))
        m["memp"] = f(inp["mem_prompt"][c])
        maps.append(m)
    return maps


_NC = None


def kernel(**inputs):
    global _NC
    if _NC is None:
        _NC = build()
    maps = _host_layout(inputs, 4, range(8))
    res = run_bass_kernel_spmd(_NC, maps, core_ids=list(range(8))).results
    st = lambda k: np.stack([r[k] for r in res])
    cat = lambda k: np.concatenate([r[k] for r in res], 1)
    return (st("yp"), np.concatenate([r["ys"].reshape(16, 4, D) for r in res], 0),
            st("okp").transpose(1, 0, 2, 3).reshape(4, 8, 128, 2, 64), st("ovp").transpose(1, 0, 2, 3).reshape(4, 8, 128, 2, 64),
            st("ohp").transpose(1, 0, 2, 3, 4), st("ocp").transpose(1, 0, 2, 3), st("opp").transpose(1, 0, 2, 3),
            st("omk").transpose(1, 0, 2, 3).reshape(4, 8, 256, 4, 64), st("omv").transpose(1, 0, 2, 3).reshape(4, 8, 256, 4, 64),
            cat("oks").reshape(4, 128, 128, 2, 64), cat("ovs").reshape(4, 128, 128, 2, 64), cat("ohs"), cat("ocs"), cat("ops"))
```
